# Optimizing a Trainium2 kernel written in Bass

```python
import math
import jax
import jax.numpy as jnp
from jax import lax
import numpy as np

D_MODEL = 1024
BATCH = 8
SEQ = 4096
DEPTH = 4

N_EVEN = (DEPTH + 1) // 2
N_ODD = DEPTH // 2
HEAD_DIM = 64
NORM_EPS = 1e-6
BLOCK = 128

RW_WIDTH = D_MODEL
RW_HEADS = RW_WIDTH // HEAD_DIM
RW_DECAY_LORA = 64
RW_ICLR_LORA = 64
RW_PROJ = 3 * RW_WIDTH + RW_DECAY_LORA + RW_ICLR_LORA
RW_GN_EPS = 64e-5

S5_WIDTH = D_MODEL
S5_GROUP = 16
S5_GROUPS = S5_WIDTH // S5_GROUP
S5_STATE = 64

M2_WIDTH = D_MODEL
M2_HEADS = M2_WIDTH // HEAD_DIM
M2_GROUPS = 2
M2_STATE = 128
M2_CONV = 4
M2_CONV_DIM = M2_WIDTH + 2 * M2_GROUPS * M2_STATE

MLA_HEADS = 8
MLA_NOPE = 128
MLA_ROPE = 64
MLA_V = 128
MLA_Q_RANK = 384
MLA_KV_RANK = 256
MLA_WIDTH = MLA_HEADS * MLA_V
ROPE_THETA = 10000.0

N_MEM = 256
MEM_HEADS = 4
MEM_WIDTH = MEM_HEADS * HEAD_DIM

EVEN_WIDTH = RW_WIDTH + S5_WIDTH + MEM_WIDTH
ODD_WIDTH = M2_WIDTH + MLA_WIDTH + MEM_WIDTH
EVEN_SIZES = (RW_PROJ, S5_WIDTH, MEM_WIDTH, EVEN_WIDTH)
ODD_SIZES = (M2_CONV_DIM, M2_HEADS, MLA_Q_RANK, MLA_KV_RANK, MLA_ROPE, MEM_WIDTH, ODD_WIDTH)
EVEN_PROJ = sum(EVEN_SIZES)
ODD_PROJ = sum(ODD_SIZES)

kernel_name = 'hybrid_rwkv7_s5_mamba2_mla_memory'


def split_last(x, sizes):
    out, start = [], 0
    for n in sizes:
        out.append(x[..., start:start + n])
        start += n
    return out


def rms_norm(x, w, eps=NORM_EPS):
    xf = x.astype(jnp.float32)
    ms = jnp.mean(xf * xf, axis=-1, keepdims=True)
    return (xf * lax.rsqrt(ms + eps)).astype(x.dtype) * w


def token_shift(x):
    return jnp.pad(x, ((0, 0), (1, 0), (0, 0)))[:, :-1]


def rope_tables(positions):
    inv = 1.0 / (ROPE_THETA ** (jnp.arange(0, MLA_ROPE, 2, dtype=jnp.float32) / MLA_ROPE))
    ang = positions.astype(jnp.float32)[..., None] * inv
    return jnp.cos(ang), jnp.sin(ang)


def apply_rope(x, cos, sin):
    half = x.shape[-1] // 2
    x1, x2 = x[..., :half], x[..., half:]
    return jnp.concatenate([x1 * cos - x2 * sin, x1 * sin + x2 * cos], -1).astype(x.dtype)


def segsum(a):
    t = a.shape[-1]
    a_rep = jnp.broadcast_to(a[..., :, None], a.shape + (t,))
    strict = jnp.tril(jnp.ones((t, t), dtype=bool), -1)
    cs = jnp.cumsum(jnp.where(strict, a_rep, 0), axis=-2)
    return jnp.where(jnp.tril(jnp.ones((t, t), dtype=bool)), cs, -jnp.inf)


def causal_depthwise_conv(x, w, b):
    k, c = w.shape
    y = lax.conv_general_dilated(x, w[:, None, :], window_strides=(1,), padding=[(k - 1, 0)],
                                 dimension_numbers=('NWC', 'WIO', 'NWC'), feature_group_count=c)
    return y + b


def rwkv7_time_mix(p, mu, w0, w2, a0, a2, k_k, k_a, r_k, ln_w, ln_b):
    b, s, _ = p.shape
    h, n = RW_HEADS, HEAD_DIM
    p = p + (token_shift(p) - p) * mu
    r, k, v, w_lat, a_lat = split_last(p, (RW_WIDTH, RW_WIDTH, RW_WIDTH, RW_DECAY_LORA, RW_ICLR_LORA))
    log_w = -jax.nn.softplus(-(w0 + jnp.tanh(w_lat) @ w2)) - 0.5
    decay = jnp.exp(-jnp.exp(log_w))
    iclr = jax.nn.sigmoid(a0 + a_lat @ a2)
    kk = (k * k_k).reshape(b, s, h, n).astype(jnp.float32)
    kk = (kk * lax.rsqrt(jnp.maximum(jnp.sum(kk * kk, -1, keepdims=True), 1e-24))).astype(p.dtype)
    k = k * (1.0 + (iclr - 1.0) * k_a)
    heads = lambda t: t.reshape(b, s, h, n)
    r, decay, k, v, iclr = map(heads, (r, decay, k, v, iclr))
    removal = kk * iclr

    def step(state, inp):
        r_t, w_t, k_t, v_t, kk_t, b_t = inp
        sa = -jnp.einsum('bhij,bhj->bhi', state, kk_t)
        state = (state * w_t[:, :, None, :] + sa[..., None] * b_t[:, :, None, :]
                 + v_t[..., None] * k_t[:, :, None, :])
        return state, jnp.einsum('bhij,bhj->bhi', state, r_t)

    seq_major = lambda t: jnp.swapaxes(t, 0, 1)
    state0 = jnp.zeros((b, h, n, n), p.dtype)
    _, y = lax.scan(step, state0, tuple(map(seq_major, (r, decay, k, v, kk, removal))))
    y = seq_major(y).astype(jnp.float32)
    mean = jnp.mean(y, -1, keepdims=True)
    var = jnp.mean(jnp.square(y - mean), -1, keepdims=True)
    y = ((y - mean) * lax.rsqrt(var + RW_GN_EPS)).astype(p.dtype).reshape(b, s, h * n) * ln_w + ln_b
    bonus = jnp.sum(r * k * r_k, -1, keepdims=True) * v
    return y + bonus.reshape(b, s, h * n)


def s5_ssm(u, lam_re, lam_im, b_re, b_im, c_re, c_im, d_skip, log_dt, glu_w, glu_b):
    bsz, s, _ = u.shape
    g, n, pch, l = S5_GROUPS, S5_STATE, S5_GROUP, BLOCK
    nc = s // l
    dt = jnp.exp(log_dt)[:, None]
    mag = jnp.exp(lam_re * dt)
    ab_re, ab_im = mag * jnp.cos(lam_im * dt), mag * jnp.sin(lam_im * dt)
    den = lam_re * lam_re + lam_im * lam_im
    nr, ni = ab_re - 1.0, ab_im
    f_re = (nr * lam_re + ni * lam_im) / den
    f_im = (ni * lam_re - nr * lam_im) / den
    bb_re = f_re[..., None] * b_re - f_im[..., None] * b_im
    bb_im = f_re[..., None] * b_im + f_im[..., None] * b_re
    steps = jnp.arange(1, l + 1, dtype=lam_re.dtype)[:, None, None]
    pmag = jnp.exp(steps * lam_re * dt)
    pw_re, pw_im = pmag * jnp.cos(steps * lam_im * dt), pmag * jnp.sin(steps * lam_im * dt)
    a_re = jnp.broadcast_to(ab_re, (bsz, l, g, n))
    a_im = jnp.broadcast_to(ab_im, (bsz, l, g, n))

    def combine(e1, e2):
        a1r, a1i, b1r, b1i = e1
        a2r, a2i, b2r, b2i = e2
        return (a2r * a1r - a2i * a1i, a2r * a1i + a2i * a1r,
                a2r * b1r - a2i * b1i + b2r, a2r * b1i + a2i * b1r + b2i)

    def chunk_step(carry, u_c):
        h_re, h_im = carry
        bu_re = jnp.einsum('blgp,gnp->blgn', u_c, bb_re)
        bu_im = jnp.einsum('blgp,gnp->blgn', u_c, bb_im)
        _, _, s_re, s_im = lax.associative_scan(combine, (a_re, a_im, bu_re, bu_im), axis=1)
        s_re = s_re + pw_re * h_re[:, None] - pw_im * h_im[:, None]
        s_im = s_im + pw_re * h_im[:, None] + pw_im * h_re[:, None]
        y = jnp.einsum('blgn,gpn->blgp', s_re, c_re) - jnp.einsum('blgn,gpn->blgp', s_im, c_im)
        return (s_re[:, -1], s_im[:, -1]), y

    uc = u.reshape(bsz, nc, l, g, pch).transpose(1, 0, 2, 3, 4)
    h0 = jnp.zeros((bsz, g, n), u.dtype)
    _, y = lax.scan(chunk_step, (h0, h0), uc)
    y = y.transpose(1, 0, 2, 3, 4).reshape(bsz, s, g * pch) + d_skip * u
    y = jax.nn.gelu(y)
    return y * jax.nn.sigmoid(y @ glu_w + glu_b)


def mamba2_ssd(xbc, dt_raw, z, conv_w, conv_b, dt_bias, a_log, d_skip, norm_w):
    bsz, s, _ = xbc.shape
    h, p, g, n, l = M2_HEADS, HEAD_DIM, M2_GROUPS, M2_STATE, BLOCK
    j, nc = h // g, s // l
    xbc = jax.nn.silu(causal_depthwise_conv(xbc, conv_w, conv_b))
    xs, bm, cm = split_last(xbc, (M2_WIDTH, g * n, g * n))
    dt = jax.nn.softplus(dt_raw + dt_bias)
    a_dt = (dt * -jnp.exp(a_log)).reshape(bsz, nc, l, g, j).transpose(0, 3, 4, 1, 2)
    xh = xs.reshape(bsz, s, h, p)
    xdt = (xh * dt[..., None]).reshape(bsz, nc, l, g, j, p)
    bc = bm.reshape(bsz, nc, l, g, n)
    cc = cm.reshape(bsz, nc, l, g, n)
    a_cum = jnp.cumsum(a_dt, axis=-1)
    lmat = jnp.exp(segsum(a_dt))
    cb = jnp.einsum('bclgn,bcsgn->bgcls', cc, bc)
    y_diag = jnp.einsum('bgcls,bgjcls,bcsgjp->bclgjp', cb, lmat, xdt)
    decay_states = jnp.exp(a_cum[..., -1:] - a_cum)
    states = jnp.einsum('bcsgn,bgjcs,bcsgjp->bcgjpn', bc, decay_states, xdt)
    states = jnp.concatenate([jnp.zeros_like(states[:, :1]), states], axis=1)
    chunk_tot = jnp.pad(a_cum[..., -1], ((0, 0), (0, 0), (0, 0), (1, 0)))
    decay_chunk = jnp.exp(segsum(chunk_tot))
    states = jnp.einsum('bgjzc,bcgjpn->bzgjpn', decay_chunk, states)[:, :-1]
    y_off = jnp.einsum('bclgn,bcgjpn,bgjcl->bclgjp', cc, states, jnp.exp(a_cum))
    y = (y_diag + y_off).reshape(bsz, s, h, p) + xh * d_skip[:, None]
    y = y.reshape(bsz, s, h * p) * jax.nn.silu(z)
    y = rms_norm(y.reshape(bsz, s, g, h * p // g), norm_w.reshape(g, -1))
    return y.reshape(bsz, s, h * p)


def mla_attend(cq, ckv, k_rope, cos, sin, q_norm_w, wq_up, kv_norm_w, wkv_up):
    bsz, s, _ = cq.shape
    h = MLA_HEADS
    q = (rms_norm(cq, q_norm_w) @ wq_up).reshape(bsz, s, h, MLA_NOPE + MLA_ROPE)
    q_nope = q[..., :MLA_NOPE]
    q_pe = apply_rope(q[..., MLA_NOPE:], cos[:, :, None], sin[:, :, None])
    kv = (rms_norm(ckv, kv_norm_w) @ wkv_up).reshape(bsz, s, h, MLA_NOPE + MLA_V)
    k_nope, v = kv[..., :MLA_NOPE], kv[..., MLA_NOPE:]
    k_pe = apply_rope(k_rope, cos, sin)
    scale = (MLA_NOPE + MLA_ROPE) ** -0.5
    outs = []
    for start in range(0, s, BLOCK):
        end = start + BLOCK
        sc = (jnp.einsum('bqhd,bkhd->bhqk', q_nope[:, start:end], k_nope[:, :end])
              + jnp.einsum('bqhr,bkr->bhqk', q_pe[:, start:end], k_pe[:, :end]))
        sc = sc.astype(jnp.float32) * scale
        causal = (start + jnp.arange(BLOCK))[:, None] >= jnp.arange(end)[None, :]
        pr = jax.nn.softmax(jnp.where(causal, sc, -jnp.inf), axis=-1).astype(v.dtype)
        outs.append(jnp.einsum('bhqk,bkhd->bqhd', pr, v[:, :end]))
    return jnp.concatenate(outs, axis=1).reshape(bsz, s, h * MLA_V)


def memory_attend(q, mem_n, mem_kv_w):
    bsz, s, _ = q.shape
    kv = mem_n @ mem_kv_w
    k = kv[..., :MEM_WIDTH].reshape(bsz, -1, MEM_HEADS, HEAD_DIM)
    v = kv[..., MEM_WIDTH:].reshape(bsz, -1, MEM_HEADS, HEAD_DIM)
    q = q.reshape(bsz, s, MEM_HEADS, HEAD_DIM)
    sc = jnp.einsum('bqhd,bmhd->bhqm', q, k).astype(jnp.float32) * HEAD_DIM ** -0.5
    pr = jax.nn.softmax(sc, axis=-1).astype(v.dtype)
    return jnp.einsum('bhqm,bmhd->bqhd', pr, v).reshape(bsz, s, MEM_WIDTH)


def even_mixer(xn, mem_n, in_w, out_w, mem_kv_w, rw_mu, rw_w0, rw_w2, rw_a0, rw_a2, rw_k_k, rw_k_a,
               rw_r_k, rw_ln_w, rw_ln_b, lam_re, lam_im, b_re, b_im, c_re, c_im, s5_d, log_dt,
               glu_w, glu_b):
    proj = xn @ in_w
    rw_in, u, q_mem, z = split_last(proj, EVEN_SIZES)
    a_out = rwkv7_time_mix(rw_in, rw_mu, rw_w0, rw_w2, rw_a0, rw_a2, rw_k_k, rw_k_a, rw_r_k,
                           rw_ln_w, rw_ln_b)
    b_out = s5_ssm(u, lam_re, lam_im, b_re, b_im, c_re, c_im, s5_d, log_dt, glu_w, glu_b)
    m_out = memory_attend(q_mem, mem_n, mem_kv_w)
    y = jnp.concatenate([a_out, b_out, m_out], axis=-1) * jax.nn.silu(z)
    return y @ out_w


def odd_mixer(xn, mem_n, cos, sin, in_w, out_w, mem_kv_w, conv_w, conv_b, dt_bias, a_log, m2_d,
              m2_norm_w, q_norm_w, wq_up, kv_norm_w, wkv_up):
    proj = xn @ in_w
    xbc, dt_raw, cq, ckv, k_rope, q_mem, z = split_last(proj, ODD_SIZES)
    z_c, z_d, z_m = split_last(z, (M2_WIDTH, MLA_WIDTH, MEM_WIDTH))
    c_out = mamba2_ssd(xbc, dt_raw, z_c, conv_w, conv_b, dt_bias, a_log, m2_d, m2_norm_w)
    d_out = mla_attend(cq, ckv, k_rope, cos, sin, q_norm_w, wq_up, kv_norm_w, wkv_up) * jax.nn.silu(z_d)
    m_out = memory_attend(q_mem, mem_n, mem_kv_w) * jax.nn.silu(z_m)
    return jnp.concatenate([c_out, d_out, m_out], axis=-1) @ out_w


def setup_inputs(seed: int = 0) -> dict:
    key = jax.random.key(seed)
    ks = iter(jax.random.split(key, 64))
    f32 = jnp.float32
    nrm = lambda shape, scale: scale * jax.random.normal(next(ks), shape, f32)
    gain = lambda shape: 1.0 + nrm(shape, 0.02)
    unif = lambda shape, lo, hi: jax.random.uniform(next(ks), shape, f32, lo, hi)
    e, o = N_EVEN, N_ODD
    m2_dt = jnp.exp(unif((o, M2_HEADS), math.log(1e-3), math.log(1e-1)))
    return {
        'x': nrm((BATCH, SEQ, D_MODEL), 1.0),
        'mem': nrm((BATCH, N_MEM, D_MODEL), 1.0),
        'positions': jnp.tile(jnp.arange(SEQ, dtype=jnp.int32)[None, :], (BATCH, 1)),
        'norm_w': gain((DEPTH, D_MODEL)),
        'mem_norm_w': gain((D_MODEL,)),
        'final_norm_w': gain((D_MODEL,)),
        'mem_kv_w': nrm((DEPTH, D_MODEL, 2 * MEM_WIDTH), D_MODEL ** -0.5),
        'ev_in_w': nrm((e, D_MODEL, EVEN_PROJ), D_MODEL ** -0.5),
        'ev_out_w': nrm((e, EVEN_WIDTH, D_MODEL), EVEN_WIDTH ** -0.5),
        'rw_mu': unif((e, RW_PROJ), 0.0, 1.0),
        'rw_w0': unif((e, RW_WIDTH), -6.0, 1.0),
        'rw_w2': nrm((e, RW_DECAY_LORA, RW_WIDTH), 0.5 * RW_DECAY_LORA ** -0.5),
        'rw_a0': nrm((e, RW_WIDTH), 0.1),
        'rw_a2': nrm((e, RW_ICLR_LORA, RW_WIDTH), 0.5 * RW_ICLR_LORA ** -0.5),
        'rw_k_k': 0.85 + nrm((e, RW_WIDTH), 0.02),
        'rw_k_a': gain((e, RW_WIDTH)),
        'rw_r_k': nrm((e, RW_HEADS, HEAD_DIM), 0.1),
        'rw_ln_w': gain((e, RW_WIDTH)),
        'rw_ln_b': nrm((e, RW_WIDTH), 0.02),
        's5_lambda_re': -0.5 + nrm((e, S5_GROUPS, S5_STATE), 0.01),
        's5_lambda_im': math.pi * jnp.arange(S5_STATE, dtype=f32) + nrm((e, S5_GROUPS, S5_STATE), 0.01),
        's5_b_re': nrm((e, S5_GROUPS, S5_STATE, S5_GROUP), (2 * S5_GROUP) ** -0.5),
        's5_b_im': nrm((e, S5_GROUPS, S5_STATE, S5_GROUP), (2 * S5_GROUP) ** -0.5),
        's5_c_re': nrm((e, S5_GROUPS, S5_GROUP, S5_STATE), (2 * S5_STATE) ** -0.5),
        's5_c_im': nrm((e, S5_GROUPS, S5_GROUP, S5_STATE), (2 * S5_STATE) ** -0.5),
        's5_d': nrm((e, S5_WIDTH), 0.5),
        's5_log_dt': unif((e, S5_GROUPS), math.log(1e-3), math.log(1e-1)),
        's5_glu_w': nrm((e, S5_WIDTH, S5_WIDTH), S5_WIDTH ** -0.5),
        's5_glu_b': nrm((e, S5_WIDTH), 0.02),
        'od_in_w': nrm((o, D_MODEL, ODD_PROJ), D_MODEL ** -0.5),
        'od_out_w': nrm((o, ODD_WIDTH, D_MODEL), ODD_WIDTH ** -0.5),
        'm2_conv_w': nrm((o, M2_CONV, M2_CONV_DIM), M2_CONV ** -0.5),
        'm2_conv_b': nrm((o, M2_CONV_DIM), 0.02),
        'm2_dt_bias': m2_dt + jnp.log(-jnp.expm1(-m2_dt)),
        'm2_a_log': jnp.log(unif((o, M2_HEADS), 1.0, 16.0)),
        'm2_d': gain((o, M2_HEADS)),
        'm2_norm_w': gain((o, M2_WIDTH)),
        'mla_q_norm_w': gain((o, MLA_Q_RANK)),
        'mla_wq_up': nrm((o, MLA_Q_RANK, MLA_HEADS * (MLA_NOPE + MLA_ROPE)), MLA_Q_RANK ** -0.5),
        'mla_kv_norm_w': gain((o, MLA_KV_RANK)),
        'mla_wkv_up': nrm((o, MLA_KV_RANK, MLA_HEADS * (MLA_NOPE + MLA_V)), MLA_KV_RANK ** -0.5),
    }


def reference(x, mem, positions, norm_w, mem_norm_w, final_norm_w, mem_kv_w, ev_in_w, ev_out_w,
              rw_mu, rw_w0, rw_w2, rw_a0, rw_a2, rw_k_k, rw_k_a, rw_r_k, rw_ln_w, rw_ln_b,
              s5_lambda_re, s5_lambda_im, s5_b_re, s5_b_im, s5_c_re, s5_c_im, s5_d, s5_log_dt,
              s5_glu_w, s5_glu_b, od_in_w, od_out_w, m2_conv_w, m2_conv_b, m2_dt_bias, m2_a_log,
              m2_d, m2_norm_w, mla_q_norm_w, mla_wq_up, mla_kv_norm_w, mla_wkv_up):
    mem_n = rms_norm(mem, mem_norm_w)
    cos, sin = rope_tables(positions)
    h = x
    for layer in range(DEPTH):
        i = layer // 2
        xn = rms_norm(h, norm_w[layer])
        if layer % 2 == 0:
            h = h + even_mixer(xn, mem_n, ev_in_w[i], ev_out_w[i], mem_kv_w[layer], rw_mu[i], rw_w0[i],
                               rw_w2[i], rw_a0[i], rw_a2[i], rw_k_k[i], rw_k_a[i], rw_r_k[i],
                               rw_ln_w[i], rw_ln_b[i], s5_lambda_re[i], s5_lambda_im[i], s5_b_re[i],
                               s5_b_im[i], s5_c_re[i], s5_c_im[i], s5_d[i], s5_log_dt[i],
                               s5_glu_w[i], s5_glu_b[i])
        else:
            h = h + odd_mixer(xn, mem_n, cos, sin, od_in_w[i], od_out_w[i], mem_kv_w[layer],
                              m2_conv_w[i], m2_conv_b[i], m2_dt_bias[i], m2_a_log[i], m2_d[i],
                              m2_norm_w[i], mla_q_norm_w[i], mla_wq_up[i], mla_kv_norm_w[i],
                              mla_wkv_up[i])
    return rms_norm(h, final_norm_w)
```

```python
import contextlib
import math
import numpy as np
import concourse.bass as bass
import concourse.mybir as mybir
from concourse.bass_utils import run_bass_kernel_spmd

F32 = mybir.dt.float32
BF16 = mybir.dt.bfloat16
I32 = mybir.dt.int32
AF = mybir.ActivationFunctionType
ALU = mybir.AluOpType
AX = mybir.AxisListType

ENGS = ['tensor', 'vector', 'scalar', 'gpsimd', 'sync']
NDMASEM = 12

S = 4096
D = 1024
EVEN_PROJ = 6784
ODD_PROJ = 4816
C0 = math.exp(-0.5)


class Prog:
    def __init__(self):
        self.nc = bass.Bass("TRN2", target_bir_lowering=False)
        self.stack = contextlib.ExitStack()
        self.ops = {e: [] for e in ENGS}
        self.cnt = {e: 0 for e in ENGS}
        self.lastw = {}
        self.readers = {}
        self.dma_rr = {'sync': 0, 'gpsimd': 0}
        self.dma_val = {}
        self.dma_last = {}
        self.all_dma_tokens = []
        self.alias = {'b5a': 'bank5', 'b5b': 'bank5', 'b6a': 'bank7', 'b6b': 'bank6', 'b6c': 'bank7', 'b6d': 'bank7',
                      'b7a': 'bank6', 'b7b': 'bank2', 'b7c': 'bank2', 'b7d': 'bank2', 'b7e': 'bank2'}

    def dram(self, name, shape, dt, kind="Internal"):
        return self.nc.dram_tensor(name, list(shape), dt, kind=kind)

    def sb(self, name, shape, dt=F32):
        return self.stack.enter_context(self.nc.sbuf_tensor("t_" + name, list(shape), dt))

    def ps(self, name, shape, dt=F32):
        return self.stack.enter_context(self.nc.psum_tensor("p_" + name, list(shape), dt))

    def _deps(self, r, w):
        r = [self.alias.get(k, k) for k in r]
        w = [self.alias.get(k, k) for k in w]
        deps = []
        for k in r:
            t = self.lastw.get(k)
            if t is not None:
                deps.append(t)
        for k in w:
            t = self.lastw.get(k)
            if t is not None:
                deps.append(t)
            deps.extend(self.readers.get(k, []))
        return deps

    def _commit(self, tok, r, w):
        r = [self.alias.get(k, k) for k in r]
        w = [self.alias.get(k, k) for k in w]
        for k in w:
            self.lastw[k] = tok
            self.readers[k] = []
        for k in r:
            if k in w:
                continue
            lst = self.readers.setdefault(k, [])
            if tok[0] == 'c':
                lst[:] = [t for t in lst if not (t[0] == 'c' and t[1] == tok[1])]
            lst.append(tok)

    def op(self, eng, fn, r=(), w=()):
        r = [self.alias.get(k, k) for k in r]
        w = [self.alias.get(k, k) for k in w]
        pb = [k for k in list(r) + list(w) if k.startswith('bank')]
        r = list(r) + pb
        w = list(w) + [k for k in pb if k not in w]
        deps = self._deps(r, w) + list(getattr(self, '_bar_extra', []))
        self.cnt[eng] += 1
        tok = ('c', eng, self.cnt[eng])
        self.ops[eng].append((fn, deps, 'c', tok))
        self._commit(tok, r, w)
        return tok

    def barrier(self, scratch):
        toks = [('c', e, self.cnt[e]) for e in ENGS if self.cnt[e] > 0]
        last_d = {}
        for t in self.all_dma_tokens:
            last_d[t[1]] = t
        toks += list(last_d.values())
        for i, eng in enumerate(['vector', 'scalar', 'gpsimd']):
            self.cnt[eng] += 1
            tok = ('c', eng, self.cnt[eng])
            if eng == 'scalar':
                fn = lambda e, i=i: e.copy(scratch[:, i:i + 1], scratch[:, i + 4:i + 5])
            else:
                fn = lambda e, i=i: e.tensor_copy(scratch[:, i:i + 1], scratch[:, i + 4:i + 5])
            self.ops[eng].append((fn, list(toks), 'c', tok))
        self.bar_toks = [('c', e, self.cnt[e]) for e in ['vector', 'scalar', 'gpsimd']]
        for k in list(self.lastw.keys()):
            pass
        self.lastw['__bar__'] = self.bar_toks[0]
        self._bar_extra = list(self.bar_toks)

    def dma(self, out, in_, r=(), w=(), q='sync', **kw):
        deps = self._deps(r, w) + list(getattr(self, '_bar_extra', []))
        i = self.dma_rr[q]
        self.dma_rr[q] = (i + 1) % NDMASEM
        key = (q, i)
        prev = self.dma_last.get(key)
        if prev is not None:
            deps.append(prev)
        v = self.dma_val.get(key, 0) + 16
        self.dma_val[key] = v
        tok = ('d', key, v)
        self.dma_last[key] = tok

        def fn(e, out=out, in_=in_, kw=kw):
            return e.dma_start(out=out, in_=in_, **kw)
        self.ops[q].append((fn, deps, 'd', tok))
        self._commit(tok, r, w)
        self.all_dma_tokens.append(tok)
        return tok

    def emit(self):
        nc = self.nc
        EPOCH = 16000
        sems = {}
        for e in ENGS:
            nep = max(1, (self.cnt[e] + EPOCH - 1) // EPOCH)
            for ep in range(nep):
                sems[(e, ep)] = self.stack.enter_context(nc.semaphore(f"s_{e}_{ep}"))
        dsems = {}
        for q in ('sync', 'gpsimd'):
            for i in range(NDMASEM):
                dsems[(q, i)] = self.stack.enter_context(nc.semaphore(f"d_{q}_{i}"))
        prog = self

        def csem(eng, n):
            return sems[(eng, (n - 1) // EPOCH)], (n - 1) % EPOCH + 1

        def run(ename, e):
            waited = {}
            for (fn, deps, kind, tok) in prog.ops[ename]:
                need = {}
                for d in deps:
                    if d[0] == 'c':
                        if d[1] == ename and ename == 'tensor':
                            continue
                        k = ('c', d[1])
                    else:
                        k = ('d', d[1])
                    if d[2] > need.get(k, 0):
                        need[k] = d[2]
                for k, v in need.items():
                    if waited.get(k, 0) >= v:
                        continue
                    waited[k] = v
                    if k[0] == 'c':
                        s_, val = csem(k[1], v)
                        e.wait_ge(s_, val)
                    else:
                        e.wait_ge(dsems[k[1]], v)
                ins = fn(e)
                if kind == 'c':
                    s_, _ = csem(ename, tok[2])
                    ins.then_inc(s_, 1)
                else:
                    ins.then_inc(dsems[tok[1]], 16)
            if ename == 'sync':
                fin = {}
                for t in prog.all_dma_tokens:
                    fin[t[1]] = max(fin.get(t[1], 0), t[2])
                for k, v in fin.items():
                    e.wait_ge(dsems[k], v)
                for en in ENGS:
                    if en != 'sync' and prog.cnt[en] > 0:
                        s_, val = csem(en, prog.cnt[en])
                        e.wait_ge(s_, val)

        with nc.Block() as block:
            @block.sync
            def _(e):
                run('sync', e)

            @block.tensor
            def _(e):
                run('tensor', e)

            @block.vector
            def _(e):
                run('vector', e)

            @block.scalar
            def _(e):
                run('scalar', e)

            @block.gpsimd
            def _(e):
                run('gpsimd', e)
        self.stack.close()
        return nc


def host_consts():
    c = {}
    c['ident'] = np.eye(128, dtype=np.float32)
    s = np.arange(128)[:, None]
    t = np.arange(128)[None, :]
    same = (s // 64) == (t // 64)
    MU = (same & (s < t)).astype(np.float32)
    MUI = (same & (s <= t)).astype(np.float32)
    c['mm'] = np.concatenate([MU, MUI], axis=1)
    c['ml'] = (same & (s > t)).astype(np.float32)
    m01 = np.ones((128, 512), np.float32)
    m01[:, ::64] = 0.0
    c['m01'] = m01
    m128 = np.ones((128, 512), np.float32)
    m128[:, ::128] = 0.0
    c['m128'] = m128
    c['iota'] = np.tile(np.arange(128, dtype=np.float32)[None, :], (128, 1))
    sel = np.zeros((16, 16, 128), np.float32)
    for h in range(16):
        sel[h, h, :] = 1.0
    c['sel16'] = sel.reshape(16, 2048)
    c['mui128'] = (np.arange(128)[:, None] <= np.arange(128)[None, :]).astype(np.float32)
    c['mneg128'] = np.where(np.arange(128)[None, :] > np.arange(128)[:, None], -30000.0, 0.0).astype(np.float32)
    inv = 1.0 / (10000.0 ** (np.arange(0, 64, 2, dtype=np.float32) / 64.0))
    c['invf64'] = np.concatenate([inv, inv]).astype(np.float32).reshape(64, 1)
    return c


WNAMES = ['norm_w', 'mem_norm_w', 'final_norm_w', 'mem_kv_w', 'ev_in_w', 'ev_out_w', 'rw_mu', 'rw_w0',
          'rw_w2', 'rw_a0', 'rw_a2', 'rw_k_k', 'rw_k_a', 'rw_r_k', 'rw_ln_w', 'rw_ln_b',
          's5_lambda_re', 's5_lambda_im', 's5_b_re', 's5_b_im', 's5_c_re', 's5_c_im', 's5_d',
          's5_log_dt', 's5_glu_w', 's5_glu_b', 'od_in_w', 'od_out_w', 'm2_conv_w', 'm2_conv_b',
          'm2_dt_bias', 'm2_a_log', 'm2_d', 'm2_norm_w', 'mla_q_norm_w', 'mla_wq_up',
          'mla_kv_norm_w', 'mla_wkv_up']


class K:
    def __init__(self, shapes, nlayers=4, debug=None):
        self.P = P = Prog()
        self.nlayers = nlayers
        self.debug = debug
        self.dbgsel = None
        self.heads = range(16)
        self.segs = range(8)
        self.stop = None
        self.I = {}
        for n, (shp, dt) in shapes.items():
            self.I[n] = P.dram(n, shp, dt, kind="ExternalInput").ap()
        self.out = P.dram("out", [S, D], F32, kind="ExternalOutput").ap()
        self.hT = P.dram("hT", [D, S], F32).ap()
        self.ycT = P.dram("ycT", [2304, S], BF16).ap()
        if debug:
            self.dbg = P.dram("dbg", list(debug), F32, kind="ExternalOutput").ap()
        self.xnT = P.sb("xnT", [128, 8, S], BF16)
        self.ident = P.sb("ident", [128, 128])
        self.mm = P.sb("mm", [128, 256])
        self.ml = P.sb("ml", [128, 128])
        self.m01 = P.sb("m01", [128, 512])
        self.ones = P.sb("ones", [128, 128])
        self.bank = [P.ps(f"bank{i}", [128, 512]) for i in range(8)]
        P.dma(self.ident[:], self.I['ident'], w=['ident'])
        P.dma(self.mm[:], self.I['mm'], w=['mm'])
        P.dma(self.ml[:], self.I['ml'], w=['ml'])
        P.dma(self.m01[:], self.I['m01'], w=['m01'])
        self.m128 = P.sb("m128", [128, 512])
        self.iota = P.sb("iota", [128, 128])
        P.dma(self.m128[:], self.I['m128'], w=['m128'])
        P.dma(self.iota[:], self.I['iota'], w=['iota'])
        P.op('gpsimd', lambda e: e.memset(self.ones[:], 1.0), w=['ones'])
        self.wst = [P.sb(f"wst{i}", [128, 8, 128]) for i in range(2)]
        self.g = [P.sb(f"g{i}", [128, 512]) for i in range(24)]
        self.big = P.sb("big", [128, S + 1])
        self.ARENA = 9728
        self.arena = P.sb("arena", [128, self.ARENA])
        self.arena_off = 0
        self.ybig = self.big[:, 0:4096].bitcast(BF16)
        self.wst_i = 0
        self.pj_i = 0

    def carve(self, name, shape, dt=F32):
        p = shape[0]
        n = 1
        for d_ in shape[1:]:
            n *= d_
        words = n if dt in (F32, I32) else (n + 1) // 2
        off = self.arena_off
        self.arena_off += words
        assert self.arena_off <= self.ARENA, (name, self.arena_off)
        ap = self.arena[0:p, off:off + words]
        if dt != F32:
            ap = ap.bitcast(dt)[:, 0:n]
        if len(shape) == 3:
            ap = ap.rearrange("p (a b) -> p a b", b=shape[2])
        elif len(shape) == 4:
            ap = ap.rearrange("p (a b c) -> p a b c", b=shape[2], c=shape[3])
        return ap

    def phase(self):
        self.P.barrier(self.barscr)
        self.arena_off = 0

    def load_w(self, wap, c0, M, dst, key):
        P = self.P
        i = self.wst_i
        self.wst_i ^= 1
        st = self.wst[i]
        P.dma(st[:, :, 0:M], wap[:, c0:c0 + M].rearrange("(kt p) m -> p kt m", p=128),
              w=[f'wst{i}'])
        P.op('gpsimd', lambda e: e.tensor_copy(dst[:, :, 0:M], st[:, :, 0:M]), r=[f'wst{i}'], w=[key])

    def proj(self, wt, wkey, M, t0):
        P = self.P
        i = self.pj_i
        self.pj_i ^= 1
        bk = self.bank[i]
        xn = self.xnT

        def fn(e):
            ins = None
            for kt in range(8):
                ins = e.matmul(bk[0:M, :], wt[:, kt, 0:M], xn[:, kt, t0:t0 + 512],
                               start=(kt == 0), stop=(kt == 7))
            return ins
        P.op('tensor', fn, r=[wkey, 'xnT'], w=[f'bank{i}'])
        return bk[0:M, :], f'bank{i}'

    def stage0(self):
        P = self.P
        x = self.I['x']
        self.hb = [P.sb(f"hb{i}", [128, 8, 256]) for i in range(2)]
        xin = [self.hb[i][:, 0:4, :].rearrange("p a b -> p (a b)") for i in range(2)]
        xo = [self.hb[i][:, 4:8, :].rearrange("p a (b c) -> p (a b) c", c=128) for i in range(2)]
        for tb in range(S // 128):
            i = tb % 2
            P.dma(xin[i], x[tb * 128:(tb + 1) * 128, :], w=[f'hb{i}'])
            for half in range(2):
                bk = self.bank[2 + half]

                def fn(e, i=i, half=half, bk=bk):
                    ins = None
                    for q in range(4):
                        kt = half * 4 + q
                        ins = e.transpose(bk[:, q * 128:(q + 1) * 128], xin[i][:, kt * 128:(kt + 1) * 128],
                                          self.ident[:])
                    return ins
                P.op('tensor', fn, r=[f'hb{i}', 'ident'], w=[f'bank{2 + half}'])
                eng = 'vector' if half == 0 else 'scalar'
                if half == 0:
                    P.op('vector', lambda e, i=i, bk=bk: e.tensor_copy(
                        xo[i][:, 0:4, :], bk[:].rearrange("p (q t) -> p q t", t=128)),
                        r=['bank2'], w=[f'hb{i}'])
                else:
                    P.op('scalar', lambda e, i=i, bk=bk: e.copy(
                        xo[i][:, 4:8, :], bk[:].rearrange("p (q t) -> p q t", t=128)),
                        r=['bank3'], w=[f'hb{i}'])
            P.dma(self.hT[:, tb * 128:(tb + 1) * 128].rearrange("(kt p) t -> p kt t", p=128), xo[i],
                  r=[f'hb{i}'], w=['hT'])

    def norm_phase(self, layer):
        P = self.P
        if not hasattr(self, 'nw'):
            self.nw = P.sb("nw", [128, 4, 8])
            P.dma(self.nw[:], self.I['norm_w'].rearrange("l (kt p) -> p l kt", p=128), w=['nw'],
                  allow_slow_non_contiguous=True)
            self.sq = [self.g[0], self.g[1]]
            self.rinv = self.g[2]
        NB = 256
        for blk in range(S // NB):
            i = blk % 2
            t0 = blk * NB
            hb = self.hb[i]
            P.dma(hb[:], self.hT[:, t0:t0 + NB].rearrange("(kt p) t -> p kt t", p=128), r=['hT'],
                  w=[f'hb{i}'])
            bk = self.bank[2]
            for kt in range(8):
                j = kt % 2
                P.op('scalar', lambda e, kt=kt, j=j, hb=hb: e.activation(
                    out=self.sq[j][:, 0:NB], in_=hb[:, kt, :], func=AF.Square), r=[f'hb{i}'], w=[f'g{j}'])
                P.op('tensor', lambda e, kt=kt, j=j, bk=bk: e.matmul(
                    bk[:, 0:NB], self.ones[:], self.sq[j][:, 0:NB], start=(kt == 0), stop=(kt == 7)),
                    r=[f'g{j}', 'ones'], w=['bank2'])
            P.op('scalar', lambda e, bk=bk: e.activation(out=self.rinv[:, 0:NB], in_=bk[:, 0:NB], func=AF.Sqrt,
                                                         scale=1.0 / D, bias=self.eps6[:, 0:1]),
                 r=['bank2', 'eps'], w=['g2'])
            P.op('vector', lambda e: e.reciprocal(self.rinv[:, 0:NB], self.rinv[:, 0:NB]), r=['g2'], w=['g2'])
            for kt in range(8):
                eng = 'vector' if kt % 2 == 0 else 'gpsimd'
                if eng == 'vector':
                    P.op('vector', lambda e, kt=kt, hb=hb, t0=t0: e.scalar_tensor_tensor(
                        out=self.xnT[:, kt, t0:t0 + NB], in0=hb[:, kt, :], scalar=self.nw[:, layer, kt:kt + 1],
                        in1=self.rinv[:, 0:NB], op0=ALU.mult, op1=ALU.mult), r=[f'hb{i}', 'g2', 'nw'], w=['xnT'])
                else:
                    P.op('gpsimd', lambda e, kt=kt, hb=hb: e.tensor_scalar(
                        out=hb[:, kt, :], in0=hb[:, kt, :], scalar1=self.nw[:, layer, kt:kt + 1], scalar2=None,
                        op0=ALU.mult), r=[f'hb{i}', 'nw'], w=[f'hb{i}'])
                    P.op('gpsimd', lambda e, kt=kt, hb=hb, t0=t0: e.tensor_tensor(
                        out=self.xnT[:, kt, t0:t0 + NB], in0=hb[:, kt, :], in1=self.rinv[:, 0:NB], op=ALU.mult),
                        r=[f'hb{i}', 'g2'], w=['xnT'])

    def consts_small(self):
        P = self.P
        self.eps6 = P.sb("eps6", [128, 4])
        self.barscr = P.sb("barscr", [128, 8])
        P.op('gpsimd', lambda e: e.memset(self.barscr[:], 0.0), w=['barscr'])
        P.op('gpsimd', lambda e: e.memset(self.eps6[:, 0:1], 1e-6), w=['eps'])
        P.op('gpsimd', lambda e: e.memset(self.eps6[:, 1:2], 64e-5), w=['eps'])
        P.op('gpsimd', lambda e: e.memset(self.eps6[:, 2:3], 0.0), w=['eps'])
        P.op('gpsimd', lambda e: e.memset(self.eps6[:, 3:4], 1.0), w=['eps'])

    def rwkv_alloc(self):
        P = self.P
        a = self.rw = {}
        self.rwk = {}
        for n in ['rr', 'kr', 'vr']:
            a[n] = self.carve("rw_" + n, [64, 513])
        for gi, n in enumerate(['rp', 'kp', 'vp', 'sg', 'ic', 'kk', 'k2', 'bb', 'csg', 'e1', 'e2', 'e3', 'e4', 'tmp',
                                'Af', 'Rf', 'Kf', 'Bf', 'KHf', 'BHf', 'bon']):
            a[n] = self.g[gi][0:64, :]
            self.rwk[n] = f'g{gi}'
            P.alias['rw_' + n] = f'g{gi}'
        a['zs'] = a['e3']
        a['yf'] = a['e4']
        P.alias['rw_zs'] = P.alias['rw_e3']
        P.alias['rw_yf'] = P.alias['rw_e4']
        P.alias['rw_latraw'] = 'big'
        P.alias['rw_lat'] = 'big'
        a['yo'] = self.carve("rw_yo", [64, 512], BF16)
        a['latraw'] = self.big
        a['lat'] = self.big[:, 1:S + 1]
        a['Zt'] = self.carve("rw_Zt", [128, 4, 128])
        a['KBt'] = self.carve("rw_KBt", [128, 2, 4, 64])
        a['Vt'] = self.carve("rw_Vt", [128, 4, 64])
        v4g = lambda gi: self.g[gi][:].rearrange("p (j t) -> p j t", t=128)
        for n, gi in [('X0', 0), ('X1', 1), ('XT0', 2), ('XT1', 3), ('RbT', 4), ('AkT', 5), ('RkT', 6)]:
            a[n] = v4g(gi)
            P.alias['rw_' + n] = f'g{gi}'
        a['QT'] = self.g[7][0:64, :]
        a['GT'] = self.g[10][0:64, :].rearrange("p (c n) -> p c n", n=64)
        a['H'] = self.g[13][0:64, :].rearrange("p (c n) -> p c n", n=64)
        P.alias['rw_QT'] = 'g7'
        P.alias['rw_GT'] = 'g10'
        P.alias['rw_H'] = 'g13'
        a['St'] = self.carve("rw_St", [64, 2, 64])
        a['Yt'] = self.g[21][0:64, :].rearrange("p (c i) -> p c i", i=64)
        a['Ysq'] = self.g[22][0:64, :].rearrange("p (c i) -> p c i", i=64)
        P.alias['rw_Yt'] = 'g21'
        P.alias['rw_Ysq'] = 'g22'
        a['st'] = self.carve("rw_st", [64, 4, 8])
        a['wr'] = self.carve("rw_wr", [128, 8, 64], BF16)
        a['wk'] = self.carve("rw_wk", [128, 8, 64], BF16)
        a['wv'] = self.carve("rw_wv", [128, 8, 64], BF16)
        a['wz'] = self.carve("rw_wz", [128, 8, 64], BF16)
        a['wlat'] = self.carve("rw_wlat", [128, 8, 128], BF16)
        a['w2a2'] = self.carve("rw_w2a2", [128, 1024])
        a['mu'] = self.carve("rw_mu", [64, 50])
        a['omu'] = self.carve("rw_omu", [64, 50])
        a['mulat'] = self.carve("rw_mulat", [128, 1])
        a['pv'] = self.carve("rw_pv", [64, 7, 16])

    def rwkv(self, li):
        P = self.P
        a = self.rw
        I = self.I
        B = self.bank
        V, A, G, T = 'vector', 'scalar', 'gpsimd', 'tensor'
        inw = I['ev_in_w'][li]
        P.dma(a['mu'][:], I['rw_mu'][li].rearrange("(c p) -> p c", p=64), w=['rw_mu'],
              allow_slow_non_contiguous=True)
        P.op(V, lambda e: e.tensor_scalar(out=a['omu'][:], in0=a['mu'][:], scalar1=-1.0, scalar2=1.0,
                                          op0=ALU.mult, op1=ALU.add), r=['rw_mu'], w=['rw_omu'])
        P.dma(a['mulat'][:], I['rw_mu'][li][3072:3200].rearrange("(p o) -> p o", o=1), w=['rw_mulat'],
              allow_slow_non_contiguous=True)
        for j, n in enumerate(['rw_w0', 'rw_a0', 'rw_k_k', 'rw_k_a', 'rw_ln_w', 'rw_ln_b']):
            P.dma(a['pv'][:, j, :], I[n][li].rearrange("(h p) -> p h", p=64), w=['rw_pv'],
                  allow_slow_non_contiguous=True)
        P.dma(a['pv'][:, 6, :], I['rw_r_k'][li].rearrange("h p -> p h"), w=['rw_pv'],
              allow_slow_non_contiguous=True)
        P.dma(a['w2a2'][0:64, :], I['rw_w2'][li], w=['rw_w2a2'])
        P.dma(a['w2a2'][64:128, :], I['rw_a2'][li], w=['rw_w2a2'])
        P.op(G, lambda e: e.memset(a['latraw'][:, 0:1], 0.0), w=['rw_latraw'])
        self.load_w(inw, 3072, 128, a['wlat'], 'rw_wlat')
        for blk in range(8):
            t0 = blk * 512
            ps, pk = self.proj(a['wlat'], 'rw_wlat', 128, t0)
            P.op(A, lambda e, ps=ps, t0=t0: e.copy(a['latraw'][:, 1 + t0:1 + t0 + 512], ps), r=[pk],
                 w=['rw_latraw'])
        tmpd = self.g[23]
        for blk in reversed(range(8)):
            t0 = blk * 512
            P.op(V, lambda e, t0=t0: e.tensor_tensor(out=tmpd[:], in0=a['latraw'][:, t0:t0 + 512],
                                                     in1=a['latraw'][:, t0 + 1:t0 + 513], op=ALU.subtract),
                 r=['big'], w=['g23'])
            P.op(V, lambda e, t0=t0: e.scalar_tensor_tensor(
                out=a['latraw'][:, t0 + 1:t0 + 513], in0=tmpd[:], scalar=a['mulat'][:, 0:1],
                in1=a['latraw'][:, t0 + 1:t0 + 513], op0=ALU.mult, op1=ALU.add),
                r=['g23', 'big', 'rw_mulat'], w=['big'])
            P.op(A, lambda e, t0=t0: e.activation(out=a['lat'][0:64, t0:t0 + 512], in_=a['lat'][0:64, t0:t0 + 512],
                                                  func=AF.Tanh), r=['big'], w=['big'])
        for h in self.heads:
            self.load_w(inw, h * 64, 64, a['wr'], 'rw_wr')
            self.load_w(inw, 1024 + h * 64, 64, a['wk'], 'rw_wk')
            self.load_w(inw, 2048 + h * 64, 64, a['wv'], 'rw_wv')
            self.load_w(inw, 4480 + h * 64, 64, a['wz'], 'rw_wz')
            P.op(G, lambda e: e.memset(a['St'][:, 0, :], 0.0), w=['rw_St0'])
            for n in ['rr', 'kr', 'vr']:
                P.op(G, lambda e, n=n: e.memset(a[n][:, 512:513], 0.0), w=['rw_' + n])
            for seg in self.segs:
                self.rwkv_seg(li, h, seg)

    def rwkv_seg(self, li, h, seg):
        P = self.P
        a = self.rw
        B = self.bank
        V, A, G, T = 'vector', 'scalar', 'gpsimd', 'tensor'
        t0 = seg * 512
        pv = a['pv']
        w0, a0, k_k, k_a, ln_w, ln_b, r_k = [pv[:, j, h:h + 1] for j in range(7)]
        ident = self.ident
        hs = slice(h * 64, (h + 1) * 64)

        def tt(eng, out, in0, in1, op, r, w):
            P.op(eng, lambda e: e.tensor_tensor(out=out, in0=in0, in1=in1, op=op), r=r, w=w)

        def stt(eng, out, in0, sc, in1, op0, op1, r, w):
            P.op(eng, lambda e: e.scalar_tensor_tensor(out=out, in0=in0, scalar=sc, in1=in1, op0=op0, op1=op1),
                 r=r, w=w)

        def act(out, in_, func, r, w, scale=None, bias=None):
            kw = {}
            if scale is not None:
                kw['scale'] = scale
            if bias is not None:
                kw['bias'] = bias
            P.op(A, lambda e: e.activation(out=out, in_=in_, func=func, **kw), r=r, w=w)

        for n, wn, mc, tmn in [('r', 'wr', h, 'Af'), ('k', 'wk', 16 + h, 'Rf'), ('v', 'wv', 32 + h, 'Kf')]:
            raw = a[n + 'r']
            rk = 'rw_' + n + 'r'
            ltmp, ltk = a[tmn], 'rw_' + tmn
            P.op(G, lambda e, raw=raw: e.tensor_copy(raw[:, 0:1], raw[:, 512:513]), r=[rk], w=[rk])
            ps, pk = self.proj(a[wn], 'rw_' + wn, 64, t0)
            P.op(A, lambda e, raw=raw, ps=ps: e.copy(raw[:, 1:513], ps), r=[pk], w=[rk])
            P.op(G, lambda e, raw=raw, mc=mc, ltmp=ltmp: e.tensor_scalar(out=ltmp[:], in0=raw[:, 0:512],
                                                                        scalar1=a['mu'][:, mc:mc + 1], scalar2=None,
                                                                        op0=ALU.mult), r=[rk, 'rw_mu'], w=[ltk])
            stt(V, a[n + 'p'][:], ps, a['omu'][:, mc:mc + 1], ltmp[:], ALU.mult, ALU.add,
                [pk, 'rw_omu', ltk], ['rw_' + n + 'p'])
        if self.stop == 'lerp':
            return
        bz = B[2]
        P.op(T, lambda e: e.matmul(bz[0:64, :], a['w2a2'][0:64, hs], a['lat'][0:64, t0:t0 + 512], start=True,
                                   stop=True), r=['rw_w2a2', 'rw_lat'], w=['bank2'])
        act(a['sg'][:], bz[0:64, :], AF.Sigmoid, ['bank2', 'rw_pv'], ['rw_sg'], bias=w0)
        P.op(T, lambda e: e.matmul(bz[0:64, :], a['w2a2'][64:128, hs], a['lat'][64:128, t0:t0 + 512], start=True,
                                   stop=True), r=['rw_w2a2', 'rw_lat'], w=['bank2'])
        act(a['ic'][:], bz[0:64, :], AF.Sigmoid, ['bank2', 'rw_pv'], ['rw_ic'], bias=a0)
        P.op(V, lambda e: e.tensor_scalar(out=a['kk'][:], in0=a['kp'][:], scalar1=k_k, scalar2=None, op0=ALU.mult),
             r=['rw_kp', 'rw_pv'], w=['rw_kk'])
        tt(G, a['tmp'][:], a['kk'][:], a['kk'][:], ALU.mult, ['rw_kk'], ['rw_tmp'])
        P.op(T, lambda e: e.matmul(bz[0:64, :], self.ones[0:64, 0:64], a['tmp'][:], start=True, stop=True),
             r=['ones', 'rw_tmp'], w=['bank2'])
        P.op(V, lambda e: e.tensor_scalar(out=a['tmp'][:], in0=bz[0:64, :], scalar1=1e-24, scalar2=None,
                                          op0=ALU.max), r=['bank2'], w=['rw_tmp'])
        act(a['tmp'][:], a['tmp'][:], AF.Sqrt, ['rw_tmp'], ['rw_tmp'])
        P.op(V, lambda e: e.reciprocal(a['tmp'][:], a['tmp'][:]), r=['rw_tmp'], w=['rw_tmp'])
        tt(G, a['kk'][:], a['kk'][:], a['tmp'][:], ALU.mult, ['rw_kk', 'rw_tmp'], ['rw_kk'])
        P.op(V, lambda e: e.tensor_scalar(out=a['k2'][:], in0=a['ic'][:], scalar1=-1.0, scalar2=k_a, op0=ALU.add,
                                          op1=ALU.mult), r=['rw_ic', 'rw_pv'], w=['rw_k2'])
        stt(V, a['k2'][:], a['k2'][:], 1.0, a['kp'][:], ALU.add, ALU.mult, ['rw_k2', 'rw_kp'], ['rw_k2'])
        tt(G, a['bb'][:], a['kk'][:], a['ic'][:], ALU.mult, ['rw_kk', 'rw_ic'], ['rw_bb'])
        if self.stop == 'kk':
            return
        P.op(V, lambda e: e.tensor_tensor_scan(out=a['csg'][:], data0=self.m01[0:64, :], data1=a['sg'][:],
                                               initial=0.0, op0=ALU.mult, op1=ALU.add),
             r=['m01', 'rw_sg'], w=['rw_csg'])
        act(a['e1'][:], a['csg'][:], AF.Exp, ['rw_csg'], ['rw_e1'], scale=-C0)
        act(a['e2'][:], a['csg'][:], AF.Exp, ['rw_csg'], ['rw_e2'], scale=C0)
        tt(G, a['e3'][:], a['csg'][:], a['sg'][:], ALU.subtract, ['rw_csg', 'rw_sg'], ['rw_e3'])
        act(a['e3'][:], a['e3'][:], AF.Exp, ['rw_e3'], ['rw_e3'], scale=-C0)
        c3 = a['csg'][:].rearrange("p (c t) -> p c t", t=64)
        tt(V, a['e4'][:].rearrange("p (c t) -> p c t", t=64), c3[:, :, 63:64].to_broadcast([64, 8, 64]), c3,
           ALU.subtract, ['rw_csg'], ['rw_e4'])
        act(a['e4'][:], a['e4'][:], AF.Exp, ['rw_e4'], ['rw_e4'], scale=-C0)
        stt(V, a['Af'][:], a['kk'][:], -1.0, a['e3'][:], ALU.mult, ALU.mult, ['rw_kk', 'rw_e3'], ['rw_Af'])
        tt(G, a['Rf'][:], a['rp'][:], a['e1'][:], ALU.mult, ['rw_rp', 'rw_e1'], ['rw_Rf'])
        tt(V, a['Kf'][:], a['k2'][:], a['e2'][:], ALU.mult, ['rw_k2', 'rw_e2'], ['rw_Kf'])
        tt(G, a['Bf'][:], a['bb'][:], a['e2'][:], ALU.mult, ['rw_bb', 'rw_e2'], ['rw_Bf'])
        tt(V, a['KHf'][:], a['k2'][:], a['e4'][:], ALU.mult, ['rw_k2', 'rw_e4'], ['rw_KHf'])
        tt(G, a['BHf'][:], a['bb'][:], a['e4'][:], ALU.mult, ['rw_bb', 'rw_e4'], ['rw_BHf'])
        stt(V, a['tmp'][:], a['rp'][:], r_k, a['k2'][:], ALU.mult, ALU.mult, ['rw_rp', 'rw_k2', 'rw_pv'], ['rw_tmp'])
        P.op(T, lambda e: e.matmul(bz[0:64, :], self.ones[0:64, 0:64], a['tmp'][:], start=True, stop=True),
             r=['ones', 'rw_tmp'], w=['bank2'])
        tt(V, a['bon'][:], bz[0:64, :], a['vp'][:], ALU.mult, ['bank2', 'rw_vp'], ['rw_bon'])
        if self.stop == 'prep':
            return
        def trn(e, srcs, bk):
            ins = None
            for q, src in enumerate(srcs):
                for j in range(4):
                    c = (q * 4 + j) * 64
                    ins = e.transpose(bk[:, c:c + 64], src[0:64, j * 128:(j + 1) * 128], ident[0:64, 0:64])
            return ins
        P.op(T, lambda e: trn(e, [a['Af'], a['vp']], B[3]), r=['rw_Af', 'rw_vp', 'ident'], w=['bank3'])
        if self.stop == 'trans1':
            return
        P.op(T, lambda e: trn(e, [a['KHf'], a['BHf']], B[4]), r=['rw_KHf', 'rw_BHf', 'ident'], w=['bank4'])
        P.op(V, lambda e: e.tensor_copy(a['Zt'][:, :, 0:64], B[3][:, 0:256].rearrange("p (j c) -> p j c", c=64)),
             r=['bank3'], w=['rw_Zt'])
        if self.stop == 'trans2':
            return
        P.op(V, lambda e: e.tensor_copy(a['Vt'][:], B[3][:, 256:512].rearrange("p (j c) -> p j c", c=64)),
             r=['bank3'], w=['rw_Vt'])
        if self.stop == 'trans3':
            return
        P.op(A, lambda e: e.copy(a['KBt'][:], B[4][:, :].rearrange("p (q j c) -> p q j c", q=2, c=64)),
             r=['bank4'], w=['rw_KBt'])
        if self.stop == 'trans':
            return
        Zt = a['Zt']
        mu_b = self.mm[:, 0:128].unsqueeze(1).to_broadcast([128, 4, 128])
        mui_b = self.mm[:, 128:256].unsqueeze(1).to_broadcast([128, 4, 128])
        ml_b = self.ml[:].unsqueeze(1).to_broadcast([128, 4, 128])
        b3v = lambda bk: bk[:, :].rearrange("p (j t) -> p j t", t=128)
        blk = lambda j: slice(j * 128, (j + 1) * 128)

        for (bi_, L, R) in [(5, 'Bf', 'Af'), (3, 'Bf', 'Rf'), (4, 'Kf', 'Af'), (6, 'Kf', 'Rf'), (7, 'Af', 'Bf')]:
            def sc(e, bi_=bi_, L=L, R=R):
                ins = None
                for j in range(4):
                    ins = e.matmul(B[bi_][:, blk(j)], a[L][:, blk(j)], a[R][:, blk(j)], start=True, stop=True)
                return ins
            P.op(T, sc, r=['rw_' + L, 'rw_' + R], w=[f'bank{bi_}'])
        tt(V, a['X0'], b3v(B[5]), mu_b, ALU.mult, ['bank5', 'mm'], ['rw_X0'])
        tt(V, a['RbT'], b3v(B[3]), mui_b, ALU.mult, ['bank3', 'mm'], ['rw_RbT'])
        tt(V, a['AkT'], b3v(B[4]), mu_b, ALU.mult, ['bank4', 'mm'], ['rw_AkT'])
        tt(V, a['RkT'], b3v(B[6]), mui_b, ALU.mult, ['bank6', 'mm'], ['rw_RkT'])
        tt(V, a['XT0'], b3v(B[7]), ml_b, ALU.mult, ['bank7', 'ml'], ['rw_XT0'])

        if self.stop == 'c1':
            return

        def akv(e):
            ins = None
            for j in range(4):
                ins = e.matmul(B[5][:, j * 128:j * 128 + 64], a['AkT'][:, j, :], a['Vt'][:, j, :], start=True, stop=True)
            return ins
        P.op(T, akv, r=['rw_AkT', 'rw_Vt'], w=['bank5'])
        P.op(V, lambda e: e.tensor_copy(Zt[:, :, 64:128], b3v(B[5])[:, :, 0:64]), r=['bank5'], w=['rw_Zt'])
        for lv in range(6):
            Xc, XTc = a[f'X{lv % 2}'], a[f'XT{lv % 2}']
            Xn, XTn = a[f'X{(lv + 1) % 2}'], a[f'XT{(lv + 1) % 2}']
            kc, ktc = f'rw_X{lv % 2}', f'rw_XT{lv % 2}'
            kn, ktn = f'rw_X{(lv + 1) % 2}', f'rw_XT{(lv + 1) % 2}'

            if lv < 5:
                def fx(e, Xc=Xc, XTc=XTc):
                    ins = None
                    for j in range(4):
                        ins = e.matmul(B[6][:, blk(j)], XTc[:, j, :], Xc[:, j, :], start=True, stop=True)
                    return ins

                def fxt(e, Xc=Xc, XTc=XTc):
                    ins = None
                    for j in range(4):
                        ins = e.matmul(B[7][:, blk(j)], Xc[:, j, :], XTc[:, j, :], start=True, stop=True)
                    return ins
                P.op(T, fx, r=[kc, ktc], w=['bank6'])
                P.op(T, fxt, r=[kc, ktc], w=['bank7'])
                P.op(A, lambda e, Xn=Xn: e.copy(Xn, b3v(B[6])), r=['bank6'], w=[kn])
                P.op(A, lambda e, XTn=XTn: e.copy(XTn, b3v(B[7])), r=['bank7'], w=[ktn])

            def fz(e, Xc=Xc):
                ins = None
                for j in range(4):
                    ins = e.matmul(B[5][:, blk(j)], Xc[:, j, :], Zt[:, j, :], start=True, stop=True)
                return ins
            P.op(T, fz, r=[kc, 'rw_Zt'], w=['bank5'])
            tt(V, Zt[:], Zt[:], b3v(B[5]), ALU.add, ['rw_Zt', 'bank5'], ['rw_Zt'])

        if self.stop == 'c2':
            return

        def fq(e):
            ins = None
            for j in range(4):
                ins = e.matmul(B[3][0:64, blk(j)], Zt[:, j, 0:64], a['RbT'][:, j, :], start=True, stop=True)
            return ins
        P.op(T, fq, r=['rw_Zt', 'rw_RbT'], w=['bank3'])
        tt(V, a['QT'], B[3][0:64, :], a['Rf'][:], ALU.add, ['bank3', 'rw_Rf'], ['rw_QT'])

        if self.stop == 'c2a':
            return

        def fg(e):
            ins = None
            for par, bk in [(0, B[4]), (1, B[7])]:
                rs_ = slice(par * 64, par * 64 + 64)
                for j in range(4):
                    ins = e.matmul(bk[0:64, j * 64:(j + 1) * 64], Zt[rs_, j, 0:64], a['KBt'][rs_, 1, j, :], start=True, stop=True)
            return ins
        P.op(T, fg, r=['rw_Zt', 'rw_KBt'], w=['bank4', 'bank7'])
        if self.stop == 'c2f':
            return
        for ci in range(8):
            ce = ci * 64 + 63
            bk = B[4] if ci % 2 == 0 else B[7]
            j = ci // 2
            stt(V, a['GT'][:, ci, :], ident[0:64, 0:64], a['e1'][:, ce:ce + 1], bk[0:64, j * 64:(j + 1) * 64], ALU.mult, ALU.add,
                ['ident', 'rw_e1', 'bank4' if ci % 2 == 0 else 'bank7'], ['rw_GT'])

        if self.stop == 'c2b':
            return

        def fh(e):
            ins = None
            for par, bk in [(0, B[6]), (1, B[5])]:
                rs_ = slice(par * 64, par * 64 + 64)
                for j in range(4):
                    e.matmul(bk[0:64, j * 64:(j + 1) * 64], a['KBt'][rs_, 0, j, :], a['Vt'][rs_, j, :], start=True, stop=False)
                    ins = e.matmul(bk[0:64, j * 64:(j + 1) * 64], a['KBt'][rs_, 1, j, :], Zt[rs_, j, 64:128], start=False, stop=True)
            return ins
        P.op(T, fh, r=['rw_KBt', 'rw_Vt', 'rw_Zt'], w=['bank6', 'bank5'])
        H4 = self.g[13][0:64, :].rearrange("p (j par n) -> p j par n", par=2, n=64)
        P.op(A, lambda e: e.copy(H4[:, :, 0, :], B[6][0:64, 0:256].rearrange("p (j n) -> p j n", n=64)), r=['bank6'], w=['rw_H'])
        P.op(A, lambda e: e.copy(H4[:, :, 1, :], B[5][0:64, 0:256].rearrange("p (j n) -> p j n", n=64)), r=['bank5'], w=['rw_H'])
        if self.stop == 'c3':
            return
        for ci in range(8):
            j, ccs = ci // 2, slice((ci % 2) * 64, (ci % 2) * 64 + 64)
            qcs = slice(ci * 64, (ci + 1) * 64)
            St, Sn = a['St'][:, ci % 2, :], a['St'][:, (ci + 1) % 2, :]
            kS, kSn = f'rw_St{ci % 2}', f'rw_St{(ci + 1) % 2}'
            P.op(T, lambda e, ci=ci, St=St: e.matmul(B[2][0:64, 0:64], a['GT'][:, ci, :], St, start=True, stop=True),
                 r=['rw_GT', kS], w=['bank2'])
            tt(V, Sn, B[2][0:64, 0:64], a['H'][:, ci, :], ALU.add, ['bank2', 'rw_H'], [kSn])

            def ym(e, j=j, ccs=ccs, qcs=qcs, ci=ci, St=St):
                e.matmul(B[3][0:64, ci * 64:(ci + 1) * 64], a['RkT'][:, j, ccs], a['Vt'][:, j, :], start=True, stop=False)
                e.matmul(B[3][0:64, ci * 64:(ci + 1) * 64], a['RbT'][:, j, ccs], Zt[:, j, 64:128], start=False, stop=False)
                return e.matmul(B[3][0:64, ci * 64:(ci + 1) * 64], a['QT'][:, qcs], St, start=False, stop=True)
            P.op(T, ym, r=['rw_RkT', 'rw_RbT', 'rw_Vt', 'rw_Zt', 'rw_QT', kS], w=['bank3'])
        P.op(V, lambda e: e.tensor_copy(a['Yt'], B[3][0:64, :].rearrange("p (c i) -> p c i", i=64)), r=['bank3'], w=['rw_Yt'])
        if self.stop == 'chunk':
            return
        Yt, Ysq, st = a['Yt'], a['Ysq'], a['st']
        P.op(V, lambda e: e.tensor_reduce(out=st[:, 0, :], in_=Yt[:], axis=AX.X, op=ALU.add), r=['rw_Yt'], w=['rw_st'])
        tt(G, Ysq[:], Yt[:], Yt[:], ALU.mult, ['rw_Yt'], ['rw_Ysq'])
        P.op(V, lambda e: e.tensor_reduce(out=st[:, 1, :], in_=Ysq[:], axis=AX.X, op=ALU.add), r=['rw_Ysq'], w=['rw_st'])
        P.op(V, lambda e: e.tensor_scalar(out=st[:, 0, :], in0=st[:, 0, :], scalar1=1.0 / 64, scalar2=None, op0=ALU.mult),
             r=['rw_st'], w=['rw_st'])
        tt(V, st[:, 2, :], st[:, 0, :], st[:, 0, :], ALU.mult, ['rw_st'], ['rw_st'])
        stt(V, st[:, 2, :], st[:, 1, :], 1.0 / 64, st[:, 2, :], ALU.mult, ALU.subtract, ['rw_st'], ['rw_st'])
        act(st[:, 2, :], st[:, 2, :], AF.Sqrt, ['rw_st', 'eps'], ['rw_st'], bias=self.eps6[0:64, 1:2])
        P.op(V, lambda e: e.reciprocal(st[:, 2, :], st[:, 2, :]), r=['rw_st'], w=['rw_st'])
        tt(V, Yt[:], Yt[:], st[:, 0, :].unsqueeze(2).to_broadcast([64, 8, 64]), ALU.subtract, ['rw_Yt', 'rw_st'], ['rw_Yt'])
        tt(V, Yt[:], Yt[:], st[:, 2, :].unsqueeze(2).to_broadcast([64, 8, 64]), ALU.mult, ['rw_Yt', 'rw_st'], ['rw_Yt'])

        def ytr(e):
            ins = None
            for ci in range(8):
                ins = e.transpose(B[3][0:64, ci * 64:(ci + 1) * 64], Yt[:, ci, :], ident[0:64, 0:64])
            return ins
        P.op(T, ytr, r=['rw_Yt', 'ident'], w=['bank3'])
        act(a['yf'][:], B[3][0:64, :], AF.Identity, ['bank3', 'rw_pv'], ['rw_yf'], scale=ln_w, bias=ln_b)
        tt(V, a['yf'][:], a['yf'][:], a['bon'][:], ALU.add, ['rw_yf', 'rw_bon'], ['rw_yf'])
        if self.debug and li == 0 and self.dbgsel == 'a_out':
            P.dma(self.dbg[h * 64:(h + 1) * 64, t0:t0 + 512], a['yf'][:], r=['rw_yf'])
        ps, pk = self.proj(a['wz'], 'rw_wz', 64, t0)
        act(a['zs'][:], ps, AF.Silu, [pk], ['rw_zs'])
        tt(V, a['yo'][:], a['yf'][:], a['zs'][:], ALU.mult, ['rw_yf', 'rw_zs'], ['rw_yo'])
        P.dma(self.ycT[h * 64:(h + 1) * 64, t0:t0 + 512], a['yo'][:], r=['rw_yo'], w=['ycT'])

    def mem_setup(self):
        P = self.P
        V, A, G, T = 'vector', 'scalar', 'gpsimd', 'tensor'
        self.memT = P.sb("memT", [128, 8, 256], BF16)
        self.mnw = P.sb("mnw", [128, 8])
        self.wkv = self.ybig[:, 4096:8192].rearrange("p (k m) -> p k m", m=512)
        P.alias['wkv'] = 'big'
        self.mst = P.sb("mst", [128, 8])
        P.dma(self.mnw[:], self.I['mem_norm_w'].rearrange("(kt p) -> p kt", p=128), w=['mnw'],
              allow_slow_non_contiguous=True)
        mt_ = [self.g[0], self.g[1]]
        for mt in range(2):
            m2 = self.hb[mt][:, 0:4, :].rearrange("p a b -> p (a b)")
            P.dma(m2, self.I['mem'][mt * 128:(mt + 1) * 128, :], w=[f'hb{mt}'])
            P.op(A, lambda e, m2=m2, mt=mt: e.activation(out=self.hb[mt][:, 4:8, :].rearrange("p a b -> p (a b)"), in_=m2,
                                                       func=AF.Square, accum_out=self.mst[:, mt:mt + 1]),
                 r=[f'hb{mt}'], w=[f'hb{mt}', 'mst'])
            P.op(A, lambda e, mt=mt: e.activation(out=self.mst[:, mt:mt + 1], in_=self.mst[:, mt:mt + 1], func=AF.Sqrt,
                                                  scale=1.0 / D, bias=self.eps6[:, 0:1]), r=['mst', 'eps'], w=['mst'])
            P.op(V, lambda e, mt=mt: e.reciprocal(self.mst[:, mt:mt + 1], self.mst[:, mt:mt + 1]), r=['mst'], w=['mst'])
            P.op(V, lambda e, m2=m2, mt=mt: e.tensor_scalar(out=m2, in0=m2, scalar1=self.mst[:, mt:mt + 1], scalar2=None,
                                                          op0=ALU.mult), r=[f'hb{mt}', 'mst'], w=[f'hb{mt}'])
            for half in range(2):
                bk = self.bank[3 + half]

                def fn(e, m2=m2, half=half, bk=bk):
                    ins = None
                    for q in range(4):
                        kt = half * 4 + q
                        ins = e.transpose(bk[:, q * 128:(q + 1) * 128], m2[:, kt * 128:(kt + 1) * 128], self.ident[:])
                    return ins
                P.op(T, fn, r=[f'hb{mt}', 'ident'], w=[f'bank{3 + half}'])
                for q in range(4):
                    kt = half * 4 + q
                    P.op(V, lambda e, kt=kt, q=q, bk=bk, mt=mt: e.tensor_scalar(
                        out=self.memT[:, kt, mt * 128:(mt + 1) * 128], in0=bk[:, q * 128:(q + 1) * 128],
                        scalar1=self.mnw[:, kt:kt + 1], scalar2=None, op0=ALU.mult),
                        r=[f'bank{3 + half}', 'mnw'], w=['memT'])

    def load_w_to(self, wap, c0, M, dst_ap, key):
        P = self.P
        i = self.wst_i
        self.wst_i ^= 1
        st = self.wst[i]
        P.dma(st[:, :, 0:M], wap[:, c0:c0 + M].rearrange("(kt p) m -> p kt m", p=128), w=[f'wst{i}'])
        P.op('gpsimd', lambda e: e.tensor_copy(dst_ap, st[:, :, 0:M]), r=[f'wst{i}'], w=[key])

    def mem_attn(self, layer, inw, qcol, zcol, ycrow):
        P = self.P
        V, A, G, T = 'vector', 'scalar', 'gpsimd', 'tensor'
        B = self.bank
        self.phase()
        self.kT = self.carve("kT", [64, 4, 256])
        self.vm = self.carve("vm", [128, 2, 256])
        self.wq = self.carve("wq", [128, 8, 64], BF16)
        self.wzm = self.carve("wzm", [128, 8, 64], BF16)
        self.mo = self.carve("mo", [64, 512], BF16)
        wkvd = self.I['mem_kv_w'][layer]
        for c in range(4):
            self.load_w_to(wkvd, c * 128, 128, self.wkv[:, :, c * 128:(c + 1) * 128], 'wkv')
        for h in range(4):
            def fk(e, h=h):
                ins = None
                for kt in range(8):
                    ins = e.matmul(B[2][0:64, 0:256], self.wkv[:, kt, h * 64:(h + 1) * 64], self.memT[:, kt, :],
                                   start=(kt == 0), stop=(kt == 7))
                return ins
            P.op(T, fk, r=['wkv', 'memT'], w=['bank2'])
            P.op(V, lambda e, h=h: e.tensor_copy(self.kT[:, h, :], B[2][0:64, 0:256]), r=['bank2'], w=['kT'])
        for mt in range(2):
            def fv(e, mt=mt):
                ins = None
                for kt in range(8):
                    ins = e.matmul(B[2][:, 0:256], self.memT[:, kt, mt * 128:(mt + 1) * 128], self.wkv[:, kt, 256:512],
                                   start=(kt == 0), stop=(kt == 7))
                return ins
            P.op(T, fv, r=['wkv', 'memT'], w=['bank2'])
            P.op(V, lambda e, mt=mt: e.tensor_copy(self.vm[:, mt, :], B[2][:, 0:256]), r=['bank2'], w=['vm'])
        qf, pr, prT, zs = self.g[0], self.g[1], self.g[2], self.g[3]
        for h in range(4):
            self.load_w(inw, qcol + h * 64, 64, self.wq, 'wq')
            self.load_w(inw, zcol + h * 64, 64, self.wzm, 'wzm')
            for blk in range(8):
                t0 = blk * 512
                ps, pk = self.proj(self.wq, 'wq', 64, t0)
                P.op(A, lambda e, ps=ps: e.copy(qf[0:64, :], ps), r=[pk], w=['g0'])
                for sb in range(4):
                    ts = slice(sb * 128, (sb + 1) * 128)
                    P.op(T, lambda e, ts=ts, h=h: e.matmul(B[3][:, 0:256], qf[0:64, ts], self.kT[:, h, :], start=True,
                                                           stop=True), r=['g0', 'kT'], w=['bank3'])
                    P.op(V, lambda e: e.tensor_reduce(out=self.mst[:, 2:3], in_=B[3][:, 0:256], axis=AX.X, op=ALU.max),
                         r=['bank3'], w=['mst'])
                    P.op(V, lambda e: e.tensor_scalar(out=self.mst[:, 2:3], in0=self.mst[:, 2:3], scalar1=-0.125,
                                                      scalar2=None, op0=ALU.mult), r=['mst'], w=['mst'])
                    P.op(A, lambda e: e.activation(out=pr[:, 0:256], in_=B[3][:, 0:256], func=AF.Exp, scale=0.125,
                                                   bias=self.mst[:, 2:3], accum_out=self.mst[:, 3:4]),
                         r=['bank3', 'mst'], w=['g1', 'mst'])
                    P.op(V, lambda e: e.reciprocal(self.mst[:, 3:4], self.mst[:, 3:4]), r=['mst'], w=['mst'])
                    P.op(V, lambda e: e.tensor_scalar(out=pr[:, 0:256], in0=pr[:, 0:256], scalar1=self.mst[:, 3:4],
                                                      scalar2=None, op0=ALU.mult), r=['g1', 'mst'], w=['g1'])

                    def ftr(e):
                        e.transpose(B[4][:, 0:128], pr[:, 0:128], self.ident[:])
                        return e.transpose(B[4][:, 128:256], pr[:, 128:256], self.ident[:])
                    P.op(T, ftr, r=['g1', 'ident'], w=['bank4'])
                    P.op(V, lambda e: e.tensor_copy(prT[:, 0:256], B[4][:, 0:256]), r=['bank4'], w=['g2'])

                    def fpv(e, ts=ts, h=h):
                        e.matmul(B[5][0:64, ts], self.vm[:, 0, h * 64:(h + 1) * 64], prT[:, 0:128], start=True, stop=False)
                        return e.matmul(B[5][0:64, ts], self.vm[:, 1, h * 64:(h + 1) * 64], prT[:, 128:256], start=False,
                                        stop=True)
                    P.op(T, fpv, r=['vm', 'g2'], w=['bank5'])
                ps, pk = self.proj(self.wzm, 'wzm', 64, t0)
                P.op(A, lambda e, ps=ps: e.activation(out=zs[0:64, :], in_=ps, func=AF.Silu), r=[pk], w=['g3'])
                if self.debug and self.dbgsel == 'm_out' and layer == 0:
                    P.op(V, lambda e: e.tensor_copy(qf[0:64, :], B[5][0:64, :]), r=['bank5'], w=['g0'])
                    P.dma(self.dbg[h * 64:(h + 1) * 64, t0:t0 + 512], qf[0:64, :], r=['g0'])
                P.op(V, lambda e: e.tensor_tensor(out=self.mo[:], in0=B[5][0:64, :], in1=zs[0:64, :], op=ALU.mult),
                     r=['bank5', 'g3'], w=['mo'])
                P.dma(self.ycT[ycrow + h * 64:ycrow + (h + 1) * 64, t0:t0 + 512], self.mo[:], r=['mo'], w=['ycT'])

    def out_proj(self, wout):
        P = self.P
        V, A, G, T = 'vector', 'scalar', 'gpsimd', 'tensor'
        B = self.bank
        self.phase()
        wo = self.carve("wo", [128, 18, 1024], BF16)
        yb = self.ybig[:, 0:18 * 256].rearrange("p (k t) -> p k t", t=256)
        for c in range(6):
            for q in range(8):
                st = self.wst[(c * 8 + q) % 2]
                sk = f'wst{(c * 8 + q) % 2}'
                P.dma(st[:, 0:3, :], wout[c * 384:(c + 1) * 384, q * 128:(q + 1) * 128].rearrange("(kt p) m -> p kt m", p=128),
                      w=[sk])
                P.op(G, lambda e, st=st, c=c, q=q: e.tensor_copy(wo[:, c * 3:(c + 1) * 3, q * 128:(q + 1) * 128], st[:, 0:3, :]),
                     r=[sk], w=['wo'])
        for blk in range(16):
            t0 = blk * 256
            P.dma(yb, self.ycT[:, t0:t0 + 256].rearrange("(kt p) t -> p kt t", p=128), r=['ycT'], w=['big'])
            for dt_ in range(8):
                i = self.pj_i
                self.pj_i ^= 1

                def fn(e, i=i, dt_=dt_):
                    ins = None
                    for kt in range(18):
                        ins = e.matmul(B[i][:, 0:256], wo[:, kt, dt_ * 128:(dt_ + 1) * 128], yb[:, kt, :], start=(kt == 0),
                                       stop=(kt == 17))
                    return ins
                P.op(T, fn, r=['wo', 'big'], w=[f'bank{i}'])
                gi = 8 + (dt_ % 4)
                hs = self.g[gi]
                hk = f'hT_{dt_}_{blk}'
                P.dma(hs[:, 0:256], self.hT[dt_ * 128:(dt_ + 1) * 128, t0:t0 + 256], r=[hk], w=[f'g{gi}'])
                P.op(V, lambda e, hs=hs, i=i: e.tensor_tensor(out=hs[:, 0:256], in0=hs[:, 0:256], in1=B[i][:, 0:256], op=ALU.add),
                     r=[f'g{gi}', f'bank{i}'], w=[f'g{gi}'])
                P.dma(self.hT[dt_ * 128:(dt_ + 1) * 128, t0:t0 + 256], hs[:, 0:256], r=[f'g{gi}'], w=[hk], q='gpsimd')

    def sin_rr(self, out, x, shape, tmp, tmpi, keys_r, key_out, key_tmp, key_tmpi, shift=0.0):
        P = self.P
        V, A = 'vector', 'scalar'
        TWO_PI = 2.0 * math.pi
        C1 = 6.28125
        C2 = TWO_PI - C1
        P.op(V, lambda e: e.tensor_scalar(out=out, in0=x, scalar1=shift, scalar2=None, op0=ALU.add), r=keys_r, w=[key_out])
        P.op(V, lambda e: e.tensor_scalar(out=tmp, in0=out, scalar1=1.0 / TWO_PI, scalar2=None, op0=ALU.mult),
             r=[key_out], w=[key_tmp])
        P.op(V, lambda e: e.tensor_copy(tmpi, tmp), r=[key_tmp], w=[key_tmpi])
        P.op(V, lambda e: e.tensor_copy(tmp, tmpi), r=[key_tmpi], w=[key_tmp])
        P.op(V, lambda e: e.scalar_tensor_tensor(out=out, in0=tmp, scalar=-C1, in1=out, op0=ALU.mult, op1=ALU.add),
             r=[key_tmp, key_out], w=[key_out])
        P.op(V, lambda e: e.scalar_tensor_tensor(out=out, in0=tmp, scalar=-C2, in1=out, op0=ALU.mult, op1=ALU.add),
             r=[key_tmp, key_out], w=[key_out])
        P.op(V, lambda e: e.tensor_scalar(out=tmp, in0=out, scalar1=math.pi, scalar2=None, op0=ALU.is_gt),
             r=[key_out], w=[key_tmp])
        P.op(V, lambda e: e.scalar_tensor_tensor(out=out, in0=tmp, scalar=-TWO_PI, in1=out, op0=ALU.mult, op1=ALU.add),
             r=[key_tmp, key_out], w=[key_out])
        P.op(V, lambda e: e.tensor_scalar(out=tmp, in0=out, scalar1=-math.pi, scalar2=None, op0=ALU.is_lt),
             r=[key_out], w=[key_tmp])
        P.op(V, lambda e: e.scalar_tensor_tensor(out=out, in0=tmp, scalar=TWO_PI, in1=out, op0=ALU.mult, op1=ALU.add),
             r=[key_tmp, key_out], w=[key_out])
        P.op(V, lambda e: e.tensor_scalar(out=out, in0=out, scalar1=-3.1415925, scalar2=3.1415925, op0=ALU.max,
                                          op1=ALU.min), r=[key_out], w=[key_out])
        P.op(A, lambda e: e.activation(out=out, in_=out, func=AF.Sin), r=[key_out], w=[key_out])

    def s5_alloc(self):
        P = self.P
        s = self.s5d = {}
        for n in ['LR', 'LI', 'LD', 'lrd', 'lid', 'mag', 'abr', 'abi', 'den', 'fr', 'fi', 'a128r', 'a128i', 'na128i',
                  'nabi', 't0', 't1', 't2']:
            s[n] = self.carve("s5_" + n, [128, 32])
        s['ti'] = self.carve("s5_ti", [128, 512], I32)
        v16 = lambda t: t[:].rearrange("p (j q) -> p j q", q=16)
        s['br'], s['bi'], s['bt'], s['bbr'], s['bbi'] = v16(self.g[8]), v16(self.g[9]), v16(self.g[10]), v16(self.g[22]), v16(self.g[23])
        s['cn'] = self.carve("s5_cn", [128, 2, 64])
        s['H'] = self.carve("s5_H", [128, 2, 4, 5])
        s['e4'] = self.carve("s5_e4", [128, 2, 4, 4])
        s['gg4'] = self.carve("s5_gg4", [128, 2, 4, 4])
        s['hm'] = self.carve("s5_hm", [128, 4, 4])
        s['hm2'] = self.carve("s5_hm2", [128, 2, 4, 4])
        s['x'] = [self.carve(f"s5_x{i}", [128, 512]) for i in range(6)]
        s['p127'] = self.carve("s5_p127", [128, 4, 4])
        s['dsk'] = self.carve("s5_dsk", [128, 8])
        s['glb'] = self.carve("s5_glb", [128, 8])
        s['wu'] = self.carve("s5_wu", [128, 8, 128], BF16)
        s['wg'] = self.carve("s5_wg", [128, 8, 128], BF16)
        s['yo'] = self.carve("s5_yo", [128, 512], BF16)
        if not hasattr(self, 's5yT'):
            self.s5yT = P.dram("s5yT", [1024, S], BF16).ap()

    def s5(self, li):
        P = self.P
        s = self.s5d
        I = self.I
        B = self.bank
        g = self.g
        V, A, G, T = 'vector', 'scalar', 'gpsimd', 'tensor'
        inw = I['ev_in_w'][li]

        def ts(eng, out, in0, s1, s2, op0, op1, r, w):
            if op1 is None:
                P.op(eng, lambda e: e.tensor_scalar(out=out, in0=in0, scalar1=s1, scalar2=None, op0=op0), r=r, w=w)
            else:
                P.op(eng, lambda e: e.tensor_scalar(out=out, in0=in0, scalar1=s1, scalar2=s2, op0=op0, op1=op1), r=r, w=w)

        def tt(eng, out, in0, in1, op, r, w):
            P.op(eng, lambda e: e.tensor_tensor(out=out, in0=in0, in1=in1, op=op), r=r, w=w)

        def stt(eng, out, in0, sc, in1, op0, op1, r, w):
            P.op(eng, lambda e: e.scalar_tensor_tensor(out=out, in0=in0, scalar=sc, in1=in1, op0=op0, op1=op1), r=r, w=w)

        def act(out, in_, func, r, w, **kw):
            P.op(A, lambda e: e.activation(out=out, in_=in_, func=func, **kw), r=r, w=w)
        K = 's5p'
        for kk_ in ['g8', 'g9', 'g10', 'g22', 'g23']:
            pass
        P.dma(s['LR'][:], I['s5_lambda_re'][li].rearrange("(j gl) n -> (gl n) j", gl=2), w=[K], allow_slow_non_contiguous=True)
        P.dma(s['LI'][:], I['s5_lambda_im'][li].rearrange("(j gl) n -> (gl n) j", gl=2), w=[K], allow_slow_non_contiguous=True)
        ld2 = I['s5_log_dt'][li].rearrange("(j gl) -> gl j", gl=2)
        for gl in range(2):
            P.dma(s['LD'][gl * 64:(gl + 1) * 64, :], ld2[gl:gl + 1, :].to_broadcast([64, 32]), w=[K],
                  allow_slow_non_contiguous=True)
        P.dma(s['br'][:], I['s5_b_re'][li].rearrange("(j gl) n q -> (gl n) j q", gl=2), w=[K, 'g8', 'g9', 'g10', 'g22', 'g23'], allow_slow_non_contiguous=True)
        P.dma(s['bi'][:], I['s5_b_im'][li].rearrange("(j gl) n q -> (gl n) j q", gl=2), w=[K, 'g8', 'g9', 'g10', 'g22', 'g23'], allow_slow_non_contiguous=True)
        P.dma(s['dsk'][:], I['s5_d'][li].rearrange("(c p) -> p c", p=128), w=[K], allow_slow_non_contiguous=True)
        P.dma(s['glb'][:], I['s5_glu_b'][li].rearrange("(c p) -> p c", p=128), w=[K], allow_slow_non_contiguous=True)
        act(s['LD'][:], s['LD'][:], AF.Exp, [K], [K])
        tt(V, s['lrd'][:], s['LR'][:], s['LD'][:], ALU.mult, [K], [K])
        tt(V, s['lid'][:], s['LI'][:], s['LD'][:], ALU.mult, [K], [K])
        act(s['mag'][:], s['lrd'][:], AF.Exp, [K], [K])
        ti32 = s['ti'][:, 0:32]
        self.sin_rr(s['abi'][:], s['lid'][:], None, s['t0'][:], ti32, [K], K, K, K)
        self.sin_rr(s['abr'][:], s['lid'][:], None, s['t0'][:], ti32, [K], K, K, K, shift=math.pi / 2)
        tt(V, s['abr'][:], s['abr'][:], s['mag'][:], ALU.mult, [K], [K])
        tt(V, s['abi'][:], s['abi'][:], s['mag'][:], ALU.mult, [K], [K])
        ts(V, s['nabi'][:], s['abi'][:], -1.0, None, ALU.mult, None, [K], [K])
        tt(V, s['den'][:], s['LR'][:], s['LR'][:], ALU.mult, [K], [K])
        tt(V, s['t1'][:], s['LI'][:], s['LI'][:], ALU.mult, [K], [K])
        tt(V, s['den'][:], s['den'][:], s['t1'][:], ALU.add, [K], [K])
        P.op(V, lambda e: e.reciprocal(s['den'][:], s['den'][:]), r=[K], w=[K])
        ts(V, s['t1'][:], s['abr'][:], -1.0, None, ALU.add, None, [K], [K])
        tt(V, s['fr'][:], s['t1'][:], s['LR'][:], ALU.mult, [K], [K])
        tt(V, s['t2'][:], s['abi'][:], s['LI'][:], ALU.mult, [K], [K])
        tt(V, s['fr'][:], s['fr'][:], s['t2'][:], ALU.add, [K], [K])
        tt(V, s['fr'][:], s['fr'][:], s['den'][:], ALU.mult, [K], [K])
        tt(V, s['fi'][:], s['abi'][:], s['LR'][:], ALU.mult, [K], [K])
        tt(V, s['t2'][:], s['t1'][:], s['LI'][:], ALU.mult, [K], [K])
        tt(V, s['fi'][:], s['fi'][:], s['t2'][:], ALU.subtract, [K], [K])
        tt(V, s['fi'][:], s['fi'][:], s['den'][:], ALU.mult, [K], [K])
        bc = lambda t: t[:].unsqueeze(2).to_broadcast([128, 32, 16])
        tt(V, s['bbr'][:], s['br'][:], bc(s['fr']), ALU.mult, [K], [K, 'g8', 'g9', 'g10', 'g22', 'g23'])
        tt(V, s['bt'][:], s['bi'][:], bc(s['fi']), ALU.mult, [K], [K, 'g8', 'g9', 'g10', 'g22', 'g23'])
        tt(V, s['bbr'][:], s['bbr'][:], s['bt'][:], ALU.subtract, [K], [K, 'g8', 'g9', 'g10', 'g22', 'g23'])
        tt(V, s['bbi'][:], s['bi'][:], bc(s['fr']), ALU.mult, [K], [K, 'g8', 'g9', 'g10', 'g22', 'g23'])
        tt(V, s['bt'][:], s['br'][:], bc(s['fi']), ALU.mult, [K], [K, 'g8', 'g9', 'g10', 'g22', 'g23'])
        tt(V, s['bbi'][:], s['bbi'][:], s['bt'][:], ALU.add, [K], [K, 'g8', 'g9', 'g10', 'g22', 'g23'])
        ts(V, s['t1'][:], s['lrd'][:], 128.0, None, ALU.mult, None, [K], [K])
        act(s['t1'][:], s['t1'][:], AF.Exp, [K], [K])
        ts(V, s['t2'][:], s['lid'][:], 128.0, None, ALU.mult, None, [K], [K])
        self.sin_rr(s['a128i'][:], s['t2'][:], None, s['t0'][:], ti32, [K], K, K, K)
        self.sin_rr(s['a128r'][:], s['t2'][:], None, s['t0'][:], ti32, [K], K, K, K, shift=math.pi / 2)
        tt(V, s['a128r'][:], s['a128r'][:], s['t1'][:], ALU.mult, [K], [K])
        tt(V, s['a128i'][:], s['a128i'][:], s['t1'][:], ALU.mult, [K], [K])
        ts(V, s['na128i'][:], s['a128i'][:], -1.0, None, ALU.mult, None, [K], [K])

        PiR, PiI, PoR, PoI, LBr, LBi, CLr, CLi = [g[i] for i in range(8)]
        gk = lambda i: f'g{i}'
        v4 = lambda t: t[:].rearrange("p (j t) -> p j t", t=128)
        iota = self.iota
        for sl in getattr(self, 's5_slices', range(8)):
            j0 = sl * 4
            if getattr(self, 'use_bar', False):
                P.barrier(self.barscr)
            self.load_w(inw, 3200 + sl * 128, 128, s['wu'], 's5_wu')
            for blk in range(8):
                ps, pk = self.proj(s['wu'], 's5_wu', 128, blk * 512)
                P.op(A, lambda e, ps=ps, blk=blk: e.copy(self.big[:, blk * 512:(blk + 1) * 512], ps), r=[pk], w=['big'])
            ang, Sn, Cs, mo, mi, tmp = [g[i] for i in range(8, 14)]
            iob = iota[:].unsqueeze(1).to_broadcast([128, 4, 128])
            tt(V, v4(ang), iob, s['lid'][:, j0:j0 + 4].unsqueeze(2).to_broadcast([128, 4, 128]), ALU.mult,
               ['iota', K], [gk(8)])
            self.sin_rr(Sn[:], ang[:], None, tmp[:], s['ti'][:], [gk(8)], gk(9), gk(13), K)
            self.sin_rr(Cs[:], ang[:], None, tmp[:], s['ti'][:], [gk(8)], gk(10), gk(13), K, shift=math.pi / 2)
            tt(V, v4(ang), iob, s['lrd'][:, j0:j0 + 4].unsqueeze(2).to_broadcast([128, 4, 128]), ALU.mult,
               ['iota', K], [gk(8)])
            act(mo[:], ang[:], AF.Exp, [gk(8)], [gk(11)])
            act(mi[:], ang[:], AF.Exp, [gk(8)], [gk(12)], scale=-1.0)
            tt(V, PoR[:], mo[:], Cs[:], ALU.mult, [gk(11), gk(10)], [gk(2)])
            tt(G, PoI[:], mo[:], Sn[:], ALU.mult, [gk(11), gk(9)], [gk(3)])
            tt(V, PiR[:], mi[:], Cs[:], ALU.mult, [gk(12), gk(10)], [gk(0)])
            stt(V, PiI[:], mi[:], -1.0, Sn[:], ALU.mult, ALU.mult, [gk(12), gk(9)], [gk(1)])
            P.op(V, lambda e: e.tensor_copy(s['p127'][:, :, 0], v4(PoR)[:, :, 127]), r=[gk(2)], w=['s5_p127'])
            P.op(V, lambda e: e.tensor_copy(s['p127'][:, :, 1], v4(PoI)[:, :, 127]), r=[gk(3)], w=['s5_p127'])
            ts(V, s['p127'][:, :, 2], s['p127'][:, :, 1], -1.0, None, ALU.mult, None, ['s5_p127'], ['s5_p127'])
            Xr, Xi = g[8], g[9]
            P.op(G, lambda e: e.memset(Xr[:], 0.0), w=[gk(8)])
            P.op(G, lambda e: e.memset(Xi[:], 0.0), w=[gk(9)])
            for jj in range(4):
                for gl in range(2):
                    off = (2 * jj + gl) * 16
                    ps_ = slice(gl * 64, (gl + 1) * 64)
                    P.op(V, lambda e, jj=jj, off=off, ps_=ps_, j0=j0: e.tensor_copy(v4(Xr)[ps_, jj, off:off + 16], s['bbr'][ps_, j0 + jj, :]),
                         r=[K, 'g22', 'g23'], w=[gk(8)])
                    P.op(V, lambda e, jj=jj, off=off, ps_=ps_, j0=j0: e.tensor_copy(v4(Xi)[ps_, jj, off:off + 16], s['bbi'][ps_, j0 + jj, :]),
                         r=[K, 'g22', 'g23'], w=[gk(9)])
            for X, LBx, kx, ko in [(Xr, LBr, gk(8), gk(4)), (Xi, LBi, gk(9), gk(5))]:
                def ftr(e, X=X):
                    ins = None
                    for jj in range(4):
                        ins = e.transpose(B[4][:, jj * 128:(jj + 1) * 128], v4(X)[:, jj, :], self.ident[:])
                    return ins
                P.op(T, ftr, r=[kx, 'ident'], w=['bank4'])
                P.op(V, lambda e, LBx=LBx: e.tensor_copy(LBx[:], B[4][:, :]), r=['bank4'], w=[ko])
            P.op(G, lambda e: e.memset(CLr[:], 0.0), w=[gk(6)])
            P.op(G, lambda e: e.memset(CLi[:], 0.0), w=[gk(7)])
            for ci, (cname, CLx, kc, sgn) in enumerate([('s5_c_re', CLr, gk(6), 1.0), ('s5_c_im', CLi, gk(7), -1.0)]):
                P.dma(s['cn'][:, ci, :], I[cname][li].rearrange("g p n -> (g p) n")[sl * 128:(sl + 1) * 128, :], w=['s5_cn'])
                P.op(T, lambda e, ci=ci: e.transpose(B[4][0:64, 0:128], s['cn'][:, ci, :], self.ident[:]),
                     r=['s5_cn', 'ident'], w=['bank4'])
                for jj in range(4):
                    c0 = 2 * jj * 16
                    ts(V, v4(CLx)[0:64, jj, c0:c0 + 16], B[4][0:64, c0:c0 + 16], sgn, None, ALU.mult, None, ['bank4'], [kc])
                    ts(V, v4(CLx)[64:128, jj, c0 + 16:c0 + 32], B[4][0:64, c0 + 16:c0 + 32], sgn, None, ALU.mult, None,
                       ['bank4'], [kc])
            P.op(G, lambda e: e.memset(s['H'][:], 0.0), w=['s5_H'])
            AR, AI = s['a128r'][:, j0:j0 + 4], s['a128i'][:, j0:j0 + 4]
            ABR, ABI = s['abr'][:, j0:j0 + 4], s['abi'][:, j0:j0 + 4]
            H = s['H']
            E = s['e4']
            GGt = s['gg4']
            m_ = s['hm']
            for blk in range(8):
                t0 = blk * 512
                ub = self.big[:, t0:t0 + 512]
                sets = [([g[16 + i] for i in range(6)], [gk(16 + i) for i in range(6)]),
                        (s['x'], [f's5_x{i}' for i in range(6)])]
                for jj in range(4):
                    (t1, t2, u1, u2, zr, zi), (k1, k2, ku1, ku2, kzr, kzi) = sets[jj % 2]
                    sr, si = g[8 + 2 * jj], g[9 + 2 * jj]
                    ksr, ksi = gk(8 + 2 * jj), gk(9 + 2 * jj)
                    bR, bI = (B[2], B[3]) if jj % 2 == 0 else (B[4], B[5])
                    kR, kI = ('bank2', 'bank3') if jj % 2 == 0 else ('bank4', 'bank5')
                    P.op(T, lambda e, jj=jj, ub=ub, bR=bR: e.matmul(bR[:, :], v4(LBr)[:, jj, :], ub, start=True, stop=True),
                         r=[gk(4), 'big'], w=[kR])
                    P.op(T, lambda e, jj=jj, ub=ub, bI=bI: e.matmul(bI[:, :], v4(LBi)[:, jj, :], ub, start=True, stop=True),
                         r=[gk(5), 'big'], w=[kI])
                    b2 = bR[:, :].rearrange("p (c t) -> p c t", t=128)
                    b3 = bI[:, :].rearrange("p (c t) -> p c t", t=128)
                    tb = lambda tab, jj=jj: v4(tab)[:, jj, :].unsqueeze(1).to_broadcast([128, 4, 128])
                    tt(V, v4(t1), b2, tb(PiR), ALU.mult, [kR, gk(0)], [k1])
                    tt(V, v4(t2), b3, tb(PiI), ALU.mult, [kI, gk(1)], [k2])
                    tt(G, zr[:], t1[:], t2[:], ALU.subtract, [k1, k2], [kzr])
                    tt(V, v4(u1), b2, tb(PiI), ALU.mult, [kR, gk(1)], [ku1])
                    tt(V, v4(u2), b3, tb(PiR), ALU.mult, [kI, gk(0)], [ku2])
                    tt(G, zi[:], u1[:], u2[:], ALU.add, [ku1, ku2], [kzi])
                    P.op(V, lambda e, zr=zr, sr=sr: e.tensor_tensor_scan(out=sr[:], data0=self.m128[:], data1=zr[:], initial=0.0,
                                                                       op0=ALU.mult, op1=ALU.add), r=['m128', kzr], w=[ksr])
                    P.op(V, lambda e, zi=zi, si=si: e.tensor_tensor_scan(out=si[:], data0=self.m128[:], data1=zi[:], initial=0.0,
                                                                       op0=ALU.mult, op1=ALU.add), r=['m128', kzi], w=[ksi])
                    pr_, pi_, npi_ = s['p127'][:, jj, 0:1], s['p127'][:, jj, 1:2], s['p127'][:, jj, 2:3]
                    ts(V, E[:, 0, jj, :], v4(sr)[:, :, 127], pr_, None, ALU.mult, None, [ksr, 's5_p127'], ['s5_e'])
                    stt(V, E[:, 0, jj, :], v4(si)[:, :, 127], npi_, E[:, 0, jj, :], ALU.mult, ALU.add, [ksi, 's5_p127', 's5_e'], ['s5_e'])
                    ts(V, E[:, 1, jj, :], v4(sr)[:, :, 127], pi_, None, ALU.mult, None, [ksr, 's5_p127'], ['s5_e'])
                    stt(V, E[:, 1, jj, :], v4(si)[:, :, 127], pr_, E[:, 1, jj, :], ALU.mult, ALU.add, [ksi, 's5_p127', 's5_e'], ['s5_e'])
                KH = ['s5_H', 's5_e', K, 's5_hm']
                P.op(V, lambda e: e.tensor_copy(H[:, :, :, 0], H[:, :, :, 4]), r=['s5_H'], w=['s5_H'])
                for c_ in range(4):
                    hr, hi = H[:, 0, :, c_], H[:, 1, :, c_]
                    hrn, hin = H[:, 0, :, c_ + 1], H[:, 1, :, c_ + 1]
                    tt(V, m_[:, 0, :], hr, AR, ALU.mult, KH, ['s5_hm'])
                    tt(V, m_[:, 1, :], hi, AI, ALU.mult, KH, ['s5_hm'])
                    tt(V, m_[:, 2, :], hi, AR, ALU.mult, KH, ['s5_hm'])
                    tt(V, m_[:, 3, :], hr, AI, ALU.mult, KH, ['s5_hm'])
                    tt(V, hrn, m_[:, 0, :], m_[:, 1, :], ALU.subtract, KH, ['s5_H'])
                    tt(V, hrn, hrn, E[:, 0, :, c_], ALU.add, KH, ['s5_H'])
                    tt(V, hin, m_[:, 2, :], m_[:, 3, :], ALU.add, KH, ['s5_H'])
                    tt(V, hin, hin, E[:, 1, :, c_], ALU.add, KH, ['s5_H'])
                bc4 = lambda t: t.unsqueeze(2).to_broadcast([128, 4, 4])
                KG = ['s5_H', K, 's5_gg', 's5_hm2']
                m2_ = s['hm2']
                tt(V, GGt[:, 0, :, :], H[:, 0, :, 0:4], bc4(ABR), ALU.mult, KG, ['s5_gg'])
                tt(V, m2_[:, 0, :, :], H[:, 1, :, 0:4], bc4(ABI), ALU.mult, KG, ['s5_hm2'])
                tt(V, GGt[:, 0, :, :], GGt[:, 0, :, :], m2_[:, 0, :, :], ALU.subtract, KG, ['s5_gg'])
                tt(V, GGt[:, 1, :, :], H[:, 1, :, 0:4], bc4(ABR), ALU.mult, KG, ['s5_gg'])
                tt(V, m2_[:, 1, :, :], H[:, 0, :, 0:4], bc4(ABI), ALU.mult, KG, ['s5_hm2'])
                tt(V, GGt[:, 1, :, :], GGt[:, 1, :, :], m2_[:, 1, :, :], ALU.add, KG, ['s5_gg'])
                for jj in range(4):
                    (t1, t2, u1, u2, zr, zi), (k1, k2, ku1, ku2, kzr, kzi) = sets[jj % 2]
                    sr, si = g[8 + 2 * jj], g[9 + 2 * jj]
                    ksr, ksi = gk(8 + 2 * jj), gk(9 + 2 * jj)
                    tb = lambda tab, jj=jj: v4(tab)[:, jj, :].unsqueeze(1).to_broadcast([128, 4, 128])
                    tt(G, v4(sr), v4(sr), GGt[:, 0, jj, :].unsqueeze(2).to_broadcast([128, 4, 128]), ALU.add, [ksr, 's5_gg'], [ksr])
                    tt(G, v4(si), v4(si), GGt[:, 1, jj, :].unsqueeze(2).to_broadcast([128, 4, 128]), ALU.add, [ksi, 's5_gg'], [ksi])
                    tt(V, v4(t1), v4(sr), tb(PoR), ALU.mult, [ksr, gk(2)], [k1])
                    tt(G, v4(t2), v4(si), tb(PoI), ALU.mult, [ksi, gk(3)], [k2])
                    tt(V, zr[:], t1[:], t2[:], ALU.subtract, [k1, k2], [kzr])
                    tt(V, v4(u1), v4(sr), tb(PoI), ALU.mult, [ksr, gk(3)], [ku1])
                    tt(G, v4(u2), v4(si), tb(PoR), ALU.mult, [ksi, gk(2)], [ku2])
                    tt(V, zi[:], u1[:], u2[:], ALU.add, [ku1, ku2], [kzi])

                    def fy(e, jj=jj, zr=zr, zi=zi):
                        e.matmul(B[6][:, :], v4(CLr)[:, jj, :], zr[:], start=(jj == 0), stop=False)
                        return e.matmul(B[6][:, :], v4(CLi)[:, jj, :], zi[:], start=False, stop=(jj == 3))
                    P.op(T, fy, r=[gk(6), gk(7), kzr, kzi], w=['bank6'])
                y, y2 = g[16], g[17]
                stt(V, y[:], ub, s['dsk'][:, sl:sl + 1], B[6][:, :], ALU.mult, ALU.add, ['big', K, 'bank6'], [gk(16)])
                if self.debug and self.dbgsel == 's5y' and li == 0:
                    P.dma(self.dbg[sl * 128:(sl + 1) * 128, t0:t0 + 512], y[:], r=[gk(16)])
                tt(G, y2[:], y[:], y[:], ALU.mult, [gk(16)], [gk(17)])
                ts(V, y2[:], y2[:], 0.044715, 1.0, ALU.mult, ALU.add, [gk(17)], [gk(17)])
                tt(G, y2[:], y2[:], y[:], ALU.mult, [gk(17), gk(16)], [gk(17)])
                act(y2[:], y2[:], AF.Sigmoid, [gk(17)], [gk(17)], scale=2.0 * math.sqrt(2.0 / math.pi))
                tt(V, s['yo'][:], y[:], y2[:], ALU.mult, [gk(16), gk(17)], ['s5_yo'])
                P.dma(self.s5yT[sl * 128:(sl + 1) * 128, t0:t0 + 512], s['yo'][:], r=['s5_yo'], w=['s5yT'])
        yb = self.ybig[:, 0:4096].rearrange("p (k t) -> p k t", t=512)
        for f in range(8):
            self.load_w(I['s5_glu_w'][li], f * 128, 128, s['wg'], 's5_wg')
            self.load_w(inw, 5504 + f * 128, 128, s['wu'], 's5_wu')
            for blk in range(8):
                t0 = blk * 512
                P.dma(yb, self.s5yT[:, t0:t0 + 512].rearrange("(kt p) t -> p kt t", p=128), r=['s5yT'], w=['big'])

                def fg(e):
                    ins = None
                    for kt in range(8):
                        ins = e.matmul(B[2][:, :], s['wg'][:, kt, :], yb[:, kt, :], start=(kt == 0), stop=(kt == 7))
                    return ins
                P.op(T, fg, r=['s5_wg', 'big'], w=['bank2'])
                sg, zs = g[14], g[15]
                act(sg[:], B[2][:, :], AF.Sigmoid, ['bank2', K], [gk(14)], bias=s['glb'][:, f:f + 1])
                tt(V, sg[:], sg[:], yb[:, f, :], ALU.mult, [gk(14), 'big'], [gk(14)])
                ps, pk = self.proj(s['wu'], 's5_wu', 128, t0)
                act(zs[:], ps, AF.Silu, [pk], [gk(15)])
                if self.debug and self.dbgsel == 'b_out' and li == 0:
                    P.dma(self.dbg[f * 128:(f + 1) * 128, t0:t0 + 512], sg[:], r=[gk(14)])
                tt(V, s['yo'][:], sg[:], zs[:], ALU.mult, [gk(14), gk(15)], ['s5_yo'])
                P.dma(self.ycT[1024 + f * 128:1024 + (f + 1) * 128, t0:t0 + 512], s['yo'][:], r=['s5_yo'], w=['ycT'])

    def final_phase(self):
        P = self.P
        V, A, G, T = 'vector', 'scalar', 'gpsimd', 'tensor'
        B = self.bank
        fw = P.sb("fnw", [128, 8])
        P.dma(fw[:], self.I['final_norm_w'].rearrange("(kt p) -> p kt", p=128), w=['fnw'], allow_slow_non_contiguous=True)
        sq = [self.g[0], self.g[1]]
        rinv = self.g[2]
        ot = [self.g[4 + i] for i in range(4)]
        NB = 256
        for blk in range(S // NB):
            i = blk % 2
            t0 = blk * NB
            hb = self.hb[i]
            P.dma(hb[:], self.hT[:, t0:t0 + NB].rearrange("(kt p) t -> p kt t", p=128), r=['hT'], w=[f'hb{i}'])
            for kt in range(8):
                j = kt % 2
                P.op(A, lambda e, kt=kt, j=j, hb=hb: e.activation(out=sq[j][:, 0:NB], in_=hb[:, kt, :], func=AF.Square),
                     r=[f'hb{i}'], w=[f'g{j}'])
                P.op(T, lambda e, kt=kt, j=j: e.matmul(B[2][:, 0:NB], self.ones[:], sq[j][:, 0:NB], start=(kt == 0),
                                                       stop=(kt == 7)), r=[f'g{j}', 'ones'], w=['bank2'])
            P.op(A, lambda e: e.activation(out=rinv[:, 0:NB], in_=B[2][:, 0:NB], func=AF.Sqrt, scale=1.0 / D,
                                           bias=self.eps6[:, 0:1]), r=['bank2', 'eps'], w=['g2'])
            P.op(V, lambda e: e.reciprocal(rinv[:, 0:NB], rinv[:, 0:NB]), r=['g2'], w=['g2'])
            for kt in range(8):
                P.op(V, lambda e, kt=kt, hb=hb: e.scalar_tensor_tensor(
                    out=hb[:, kt, :], in0=hb[:, kt, :], scalar=fw[:, kt:kt + 1], in1=rinv[:, 0:NB], op0=ALU.mult,
                    op1=ALU.mult), r=[f'hb{i}', 'g2', 'fnw'], w=[f'hb{i}'])
            for sub in range(2):
                oi = (blk * 2 + sub) % 2
                for half in range(2):
                    bk = B[3 + half]

                    def fn(e, hb=hb, sub=sub, half=half, bk=bk):
                        ins = None
                        for q in range(4):
                            kt = half * 4 + q
                            ins = e.transpose(bk[:, q * 128:(q + 1) * 128], hb[:, kt, sub * 128:(sub + 1) * 128], self.ident[:])
                        return ins
                    P.op(T, fn, r=[f'hb{i}', 'ident'], w=[f'bank{3 + half}'])
                    o = ot[oi * 2 + half]
                    if half == 0:
                        P.op(V, lambda e, o=o, bk=bk: e.tensor_copy(o[:], bk[:, :]), r=['bank3'], w=[f'g{4 + oi * 2 + half}'])
                    else:
                        P.op(A, lambda e, o=o, bk=bk: e.copy(o[:], bk[:, :]), r=['bank4'], w=[f'g{4 + oi * 2 + half}'])
                    r0 = t0 + sub * 128
                    P.dma(self.out[r0:r0 + 128, half * 512:(half + 1) * 512], o[:], r=[f'g{4 + oi * 2 + half}'], q='gpsimd')

    def even_layer(self, layer):
        li = layer // 2
        self.norm_phase(layer)
        self.phase()
        self.rwkv_alloc()
        self.rwkv(li)
        self.phase()
        self.s5_alloc()
        self.s5(li)
        self.phase()
        self.mem_attn(layer, self.I['ev_in_w'][li], 4224, 6528, 2048)
        self.out_proj(self.I['ev_out_w'][li])
        self.phase()


    def ssd(self, li):
        P = self.P
        I = self.I
        B = self.bank
        g = self.g
        V, A, G, T = 'vector', 'scalar', 'gpsimd', 'tensor'
        inw = I['od_in_w'][li]
        gk = lambda i: f'g{i}'

        def ts(eng, out, in0, s1, s2, op0, op1, r, w):
            if op1 is None:
                P.op(eng, lambda e: e.tensor_scalar(out=out, in0=in0, scalar1=s1, scalar2=None, op0=op0), r=r, w=w)
            else:
                P.op(eng, lambda e: e.tensor_scalar(out=out, in0=in0, scalar1=s1, scalar2=s2, op0=op0, op1=op1), r=r, w=w)

        def tt(eng, out, in0, in1, op, r, w):
            P.op(eng, lambda e: e.tensor_tensor(out=out, in0=in0, in1=in1, op=op), r=r, w=w)

        def stt(eng, out, in0, sc, in1, op0, op1, r, w):
            P.op(eng, lambda e: e.scalar_tensor_tensor(out=out, in0=in0, scalar=sc, in1=in1, op0=op0, op1=op1), r=r, w=w)

        def act(out, in_, func, r, w, **kw):
            P.op(A, lambda e: e.activation(out=out, in_=in_, func=func, **kw), r=r, w=w)
        c = self.carve
        raw = [c("sd_raw0", [128, 515]), c("sd_raw1", [128, 515])]
        carry = c("sd_carry", [128, 12, 3])
        cw = c("sd_cw", [128, 12, 4])
        cb = c("sd_cb", [128, 12])
        state = c("sd_state", [128, 2, 512])
        stack = c("sd_stack", [16, 3, 512])
        dtmp = c("sd_dtmp", [16, 512])
        prm = c("sd_prm", [16, 4])
        dg = c("sd_dg", [16, 16])
        sel = c("sd_sel", [16, 16, 128])
        nwb = c("sd_nwb", [128, 1024])
        dskb = c("sd_dskb", [128, 16])
        smT = c("sd_smT", [128, 4, 16])
        dec = c("sd_dec", [128, 16])
        M1 = [c("sd_M10", [128, 128]), c("sd_M11", [128, 128])]
        Lt = [c("sd_L0", [128, 128]), c("sd_L1", [128, 128])]
        CBm = c("sd_CBm", [128, 128])
        mui = c("sd_mui", [128, 128])
        ssq = c("sd_ssq", [128, 4])
        yoT = c("sd_yoT", [128, 4, 128], BF16)
        wsl = c("sd_wsl", [128, 8, 128], BF16)
        L4b = c("sd_L4b", [128, 512])
        wzc = self.ybig[:, 0:8192].rearrange("p (k m) -> p k m", m=1024)
        KP = 'sdp'
        for k_ in range(4):
            P.dma(cw[:, :, k_], I['m2_conv_w'][li][k_].rearrange("(c p) -> p c", p=128), w=[KP], allow_slow_non_contiguous=True)
        P.dma(cb, I['m2_conv_b'][li].rearrange("(c p) -> p c", p=128), w=[KP], allow_slow_non_contiguous=True)
        P.dma(prm[:, 0:1], I['m2_dt_bias'][li].rearrange("(p o) -> p o", o=1), w=[KP], allow_slow_non_contiguous=True)
        P.dma(prm[:, 1:2], I['m2_a_log'][li].rearrange("(p o) -> p o", o=1), w=[KP], allow_slow_non_contiguous=True)
        act(prm[:, 1:2], prm[:, 1:2], AF.Exp, [KP], [KP])
        ts(V, prm[:, 1:2], prm[:, 1:2], -1.0, None, ALU.mult, None, [KP], [KP])
        P.dma(dskb, I['m2_d'][li].rearrange("(o h) -> o h", o=1).to_broadcast([128, 16]), w=[KP], allow_slow_non_contiguous=True)
        P.dma(nwb, I['m2_norm_w'][li].rearrange("(o h) -> o h", o=1).to_broadcast([128, 1024]), w=[KP],
              allow_slow_non_contiguous=True)
        P.dma(sel, I['sel16'].rearrange("h (k s) -> h k s", s=128), w=[KP])
        P.dma(mui, I['mui128'], w=[KP])
        P.op(G, lambda e: e.memset(carry, 0.0), w=['sd_carry'])
        P.op(G, lambda e: e.memset(state, 0.0), w=['sd_state'])
        for q in range(8):
            self.load_w_to(inw, 2512 + q * 128, 128, wzc[:, :, q * 128:(q + 1) * 128], 'big')
        dstg = [g[i] for i in range(12)]
        for blk in range(8):
            t0 = blk * 512
            for sl in range(12):
                self.load_w(inw, sl * 128, 128, wsl, 'sd_wsl')
                rw_ = raw[sl % 2]
                rk = f'sd_raw{sl % 2}'
                ps, pk = self.proj(wsl, 'sd_wsl', 128, t0)
                P.op(G, lambda e, rw_=rw_, sl=sl: e.tensor_copy(rw_[:, 0:3], carry[:, sl, :]), r=['sd_carry'], w=[rk])
                P.op(A, lambda e, rw_=rw_, ps=ps: e.copy(rw_[:, 3:515], ps), r=[pk], w=[rk])
                P.op(G, lambda e, rw_=rw_, sl=sl: e.tensor_copy(carry[:, sl, :], rw_[:, 512:515]), r=[rk], w=['sd_carry'])
                d_ = dstg[sl]
                dk = gk(sl)
                ts(V, d_[:], rw_[:, 0:512], cw[:, sl, 0:1], cb[:, sl:sl + 1], ALU.mult, ALU.add, [rk, KP], [dk])
                for k_ in range(1, 4):
                    stt(V, d_[:], rw_[:, k_:k_ + 512], cw[:, sl, k_:k_ + 1], d_[:], ALU.mult, ALU.add,
                        [rk, KP, dk], [dk])
                act(d_[:], d_[:], AF.Silu, [dk], [dk])
            self.load_w(inw, 1536, 16, wsl, 'sd_wsl')
            ps, pk = self.proj(wsl, 'sd_wsl', 16, t0)
            ts(V, dtmp, ps, prm[:, 0:1], None, ALU.add, None, [pk, KP], ['sd_dtmp'])
            act(stack[:, 2, :], dtmp, AF.Abs, ['sd_dtmp'], ['sd_stack'])
            act(stack[:, 2, :], stack[:, 2, :], AF.Exp, ['sd_stack'], ['sd_stack'], scale=-1.0)
            act(stack[:, 2, :], stack[:, 2, :], AF.Ln, ['sd_stack', 'eps'], ['sd_stack'], bias=self.eps6[0:16, 3:4])
            stt(V, stack[:, 0, :], dtmp, 0.0, stack[:, 2, :], ALU.max, ALU.add, ['sd_dtmp', 'sd_stack'], ['sd_stack'])
            ts(V, dtmp, stack[:, 0, :], prm[:, 1:2], None, ALU.mult, None, ['sd_stack', KP], ['sd_dtmp'])
            P.op(V, lambda e: e.tensor_tensor_scan(out=stack[:, 1, :], data0=self.m128[0:16, :], data1=dtmp, initial=0.0,
                                                   op0=ALU.mult, op1=ALU.add), r=['m128', 'sd_dtmp'], w=['sd_stack'])
            ac3 = stack[:, 1, :].rearrange("p (c t) -> p c t", t=128)
            tt(V, stack[:, 2, :].rearrange("p (c t) -> p c t", t=128), ac3[:, :, 127:128].to_broadcast([16, 4, 128]), ac3,
               ALU.subtract, ['sd_stack'], ['sd_stack'])
            act(stack[:, 2, :], stack[:, 2, :], AF.Exp, ['sd_stack'], ['sd_stack'])
            for cc in range(4):
                tc0 = cc * 128
                cs = slice(tc0, tc0 + 128)
                tg = t0 + tc0
                zt = [g[19], g[20]]
                for half in range(2):
                    i = self.pj_i
                    self.pj_i ^= 1

                    def fz(e, i=i, half=half, tg=tg):
                        ins = None
                        for kt in range(8):
                            ins = e.matmul(B[i][:, :], self.xnT[:, kt, tg:tg + 128], wzc[:, kt, half * 512:(half + 1) * 512],
                                           start=(kt == 0), stop=(kt == 7))
                        return ins
                    P.op(T, fz, r=['big', 'xnT'], w=[f'bank{i}'])
                    act(zt[half][:], B[i][:, :], AF.Silu, [f'bank{i}'], [gk(19 + half)])
                xsT = [g[12], g[13]]
                for half in range(2):
                    def ftx(e, half=half, cs=cs):
                        ins = None
                        for q in range(4):
                            ins = e.transpose(B[2][:, q * 128:(q + 1) * 128], dstg[half * 4 + q][:, cs], self.ident[:])
                        return ins
                    P.op(T, ftx, r=[gk(half * 4 + q) for q in range(4)] + ['ident'], w=['bank2'])
                    P.op(V, lambda e, half=half: e.tensor_copy(xsT[half][:], B[2][:, :]), r=['bank2'], w=[gk(12 + half)])

                def ftb(e, cs=cs):
                    e.transpose(B[3][:, 0:128], dstg[8][:, cs], self.ident[:])
                    e.transpose(B[3][:, 128:256], dstg[9][:, cs], self.ident[:])
                    ins = None
                    for q in range(3):
                        ins = e.transpose(B[3][:, 256 + q * 16:256 + (q + 1) * 16], stack[:, q, cs], self.ident[0:16, 0:16])
                    return ins
                P.op(T, ftb, r=[gk(8), gk(9), 'sd_stack', 'ident'], w=['bank3'])
                Bt = g[18]
                P.op(V, lambda e: e.tensor_copy(Bt[:, 0:256], B[3][:, 0:256]), r=['bank3'], w=[gk(18)])
                P.op(V, lambda e: e.tensor_copy(smT[:, 0:3, :], B[3][:, 256:304].rearrange("p (q h) -> p q h", h=16)),
                     r=['bank3'], w=['sd_smT'])
                act(smT[:, 3, :], smT[:, 1, :], AF.Exp, ['sd_smT'], ['sd_smT'])
                xdt = [g[14], g[15]]
                xdd = [g[16], g[17]]
                v8 = lambda t: t[:].rearrange("p (h q) -> p h q", q=64)
                for half in range(2):
                    hs_ = slice(half * 8, (half + 1) * 8)
                    tt(V, v8(xdt[half]), v8(xsT[half]), smT[:, 0, hs_].unsqueeze(2).to_broadcast([128, 8, 64]), ALU.mult,
                       [gk(12 + half), 'sd_smT'], [gk(14 + half)])
                    tt(G, v8(xdd[half]), v8(xdt[half]), smT[:, 2, hs_].unsqueeze(2).to_broadcast([128, 8, 64]), ALU.mult,
                       [gk(14 + half), 'sd_smT'], [gk(16 + half)])
                ce = tc0 + 127
                ts(V, dg, self.ident[0:16, 0:16], stack[:, 1, ce:ce + 1], None, ALU.mult, None, ['ident', 'sd_stack'], ['sd_dg'])
                P.op(T, lambda e: e.matmul(B[4][:, 0:16], self.ones[0:16, :], dg, start=True, stop=True),
                     r=['ones', 'sd_dg'], w=['bank4'])
                act(dec, B[4][:, 0:16], AF.Exp, ['bank4'], ['sd_dec'])
                for gq in range(2):
                    BTf, CTf = dstg[8 + gq], dstg[10 + gq]
                    P.op(T, lambda e, BTf=BTf, CTf=CTf, cs=cs: e.matmul(B[4][:, 128:256], BTf[:, cs], CTf[:, cs], start=True,
                                                                        stop=True), r=[gk(8 + gq), gk(10 + gq)], w=['bank4'])
                    tt(V, CBm, B[4][:, 128:256], mui, ALU.mult, ['bank4', KP], ['sd_CBm'])
                    P.op(T, lambda e, CTf=CTf, cs=cs, gq=gq: e.matmul(B[5][:, :], CTf[:, cs], state[:, gq, :], start=True,
                                                                      stop=True), r=[gk(10 + gq), 'sd_state'], w=['bank5'])
                    v4h = lambda t: t[:].rearrange("p (h l) -> p h l", l=128)
                    for hb_ in range(2):
                        h0 = gq * 8 + hb_ * 4
                        L4, M4 = (g[23], g[22]) if hb_ == 0 else (L4b, g[22])
                        kL, kM = (gk(23), gk(22)) if hb_ == 0 else ('sd_L4b', gk(22))

                        def fsel(e, h0=h0, cs=cs):
                            ins = None
                            for q in range(4):
                                ins = e.matmul(B[7][:, q * 128:(q + 1) * 128], sel[:, h0 + q, :], stack[:, 1, cs], start=True, stop=True)
                            return ins
                        P.op(T, fsel, r=[KP, 'sd_stack'], w=['bank7'])
                        tt(V, v4h(L4), B[7][:, :].rearrange("p (h l) -> p h l", l=128),
                           smT[:, 1, h0:h0 + 4].unsqueeze(2).to_broadcast([128, 4, 128]), ALU.subtract, ['bank7', 'sd_smT'], [kL])
                        ts(V, L4[:], L4[:], 0.0, None, ALU.min, None, [kL], [kL])
                        act(L4[:], L4[:], AF.Exp, [kL], [kL])
                        tt(G, v4h(M4), v4h(L4), CBm.unsqueeze(1).to_broadcast([128, 4, 128]), ALU.mult, [kL, 'sd_CBm'], [kM])

                        def fyd(e, hb_=hb_, gq=gq, M4=M4):
                            ins = None
                            for q in range(4):
                                hl = hb_ * 4 + q
                                ins = e.matmul(B[6][:, hl * 64:(hl + 1) * 64], v4h(M4)[:, q, :], v8(xdt[gq])[:, hl, :], start=True,
                                               stop=True)
                            return ins
                        P.op(T, fyd, r=[kM, gk(14 + gq)], w=['bank6'])
                    yg, tmpy = g[21], g[22]
                    hs_ = slice(gq * 8, (gq + 1) * 8)
                    tt(V, v8(yg), B[5][:, :].rearrange("p (h q) -> p h q", q=64),
                       smT[:, 3, hs_].unsqueeze(2).to_broadcast([128, 8, 64]), ALU.mult, ['bank5', 'sd_smT'], [gk(21)])
                    tt(V, yg[:], yg[:], B[6][:, :], ALU.add, [gk(21), 'bank6'], [gk(21)])
                    tt(G, v8(tmpy), v8(xsT[gq]), dskb[:, hs_].unsqueeze(2).to_broadcast([128, 8, 64]), ALU.mult,
                       [gk(12 + gq), KP], [gk(22)])
                    tt(V, yg[:], yg[:], tmpy[:], ALU.add, [gk(21), gk(22)], [gk(21)])
                    tt(V, yg[:], yg[:], zt[gq][:], ALU.mult, [gk(21), gk(19 + gq)], [gk(21)])
                    P.op(T, lambda e, gq=gq: e.matmul(B[5][:, :], Bt[:, gq * 128:(gq + 1) * 128], xdd[gq][:], start=True,
                                                      stop=True), r=[gk(18), gk(16 + gq)], w=['bank5'])
                    stg = state[:, gq, :].rearrange("p (h q) -> p h q", q=64)
                    tt(V, stg, stg, dec[:, hs_].unsqueeze(2).to_broadcast([128, 8, 64]), ALU.mult, ['sd_state', 'sd_dec'],
                       ['sd_state'])
                    tt(V, state[:, gq, :], state[:, gq, :], B[5][:, :], ALU.add, ['sd_state', 'bank5'], ['sd_state'])
                    act(tmpy[:], yg[:], AF.Square, [gk(21)], [gk(22), 'sd_ssq'], accum_out=ssq[:, 0:1])
                    act(ssq[:, 1:2], ssq[:, 0:1], AF.Sqrt, ['sd_ssq', 'eps'], ['sd_ssq'], scale=1.0 / 512,
                        bias=self.eps6[:, 0:1])
                    P.op(V, lambda e: e.reciprocal(ssq[:, 1:2], ssq[:, 1:2]), r=['sd_ssq'], w=['sd_ssq'])
                    stt(V, yg[:], yg[:], ssq[:, 1:2], nwb[:, gq * 512:(gq + 1) * 512], ALU.mult, ALU.mult,
                        [gk(21), 'sd_ssq', KP], [gk(21)])

                    def fty(e):
                        ins = None
                        for q in range(4):
                            ins = e.transpose(B[2][:, q * 128:(q + 1) * 128], yg[:, q * 128:(q + 1) * 128], self.ident[:])
                        return ins
                    P.op(T, fty, r=[gk(21), 'ident'], w=['bank2'])
                    if self.debug and self.dbgsel == 'c_out' and li == 0:
                        P.op(V, lambda e: e.tensor_copy(tmpy[:], B[2][:, :]), r=['bank2'], w=[gk(22)])
                        P.dma(self.dbg[gq * 512:(gq + 1) * 512, tg:tg + 128].rearrange("(q p) t -> p q t", p=128),
                              tmpy[:].rearrange("p (q t) -> p q t", t=128), r=[gk(22)])
                    P.op(V, lambda e: e.tensor_copy(yoT, B[2][:, :].rearrange("p (q t) -> p q t", t=128)), r=['bank2'],
                         w=['sd_yoT'])
                    P.dma(self.ycT[gq * 512:(gq + 1) * 512, tg:tg + 128].rearrange("(q p) t -> p q t", p=128), yoT,
                          r=['sd_yoT'], w=['ycT'])

    def rope_setup(self):
        P = self.P
        V, A, G, T = 'vector', 'scalar', 'gpsimd', 'tensor'
        self.csT = P.dram("csT", [2, 64, S], F32).ap()
        self.phase()
        c = self.carve
        posi = c("rp_posi", [64, 512], I32)
        posf = c("rp_posf", [64, 512])
        ang = c("rp_ang", [64, 512])
        sn = c("rp_sn", [64, 512])
        tmp = c("rp_tmp", [64, 512])
        tmi = c("rp_tmi", [64, 512], I32)
        invf = c("rp_invf", [64, 1])
        P.dma(invf, self.I['invf64'], w=['rp_invf'])
        for blk in range(8):
            t0 = blk * 512
            P.dma(posi, self.I['positions'][0:1, t0:t0 + 512].to_broadcast([64, 512]), w=['rp_posi'],
                  allow_slow_non_contiguous=True)
            P.op(V, lambda e: e.tensor_copy(posf, posi), r=['rp_posi'], w=['rp_posf'])
            P.op(V, lambda e: e.tensor_scalar(out=ang, in0=posf, scalar1=invf[:, 0:1], scalar2=None, op0=ALU.mult),
                 r=['rp_posf', 'rp_invf'], w=['rp_ang'])
            for q, sh in [(0, math.pi / 2), (1, 0.0)]:
                self.sin_rr(sn, ang, None, tmp, tmi, ['rp_ang'], 'rp_sn', 'rp_tmp', 'rp_tmi', shift=sh)
                P.dma(self.csT[q, :, t0:t0 + 512], sn, r=['rp_sn'], w=['csT'])

    def mla(self, li):
        P = self.P
        I = self.I
        B = self.bank
        g = self.g
        V, A, G, T = 'vector', 'scalar', 'gpsimd', 'tensor'
        inw = I['od_in_w'][li]
        gk = lambda i: f'g{i}'
        SC = 192 ** -0.5

        def ts(eng, out, in0, s1, s2, op0, op1, r, w):
            if op1 is None:
                P.op(eng, lambda e: e.tensor_scalar(out=out, in0=in0, scalar1=s1, scalar2=None, op0=op0), r=r, w=w)
            else:
                P.op(eng, lambda e: e.tensor_scalar(out=out, in0=in0, scalar1=s1, scalar2=s2, op0=op0, op1=op1), r=r, w=w)

        def tt(eng, out, in0, in1, op, r, w):
            P.op(eng, lambda e: e.tensor_tensor(out=out, in0=in0, in1=in1, op=op), r=r, w=w)

        def stt(eng, out, in0, sc, in1, op0, op1, r, w):
            P.op(eng, lambda e: e.scalar_tensor_tensor(out=out, in0=in0, scalar=sc, in1=in1, op0=op0, op1=op1), r=r, w=w)

        def act(out, in_, func, r, w, **kw):
            P.op(A, lambda e: e.activation(out=out, in_=in_, func=func, **kw), r=r, w=w)
        if not hasattr(self, 'cqnT'):
            self.cqnT = P.dram("cqnT", [384, S], BF16).ap()
            self.ckvnT = P.dram("ckvnT", [256, S], BF16).ap()
        c = self.carve
        kpe = c("ml_kpe", [64, S], BF16)
        qpe = c("ml_qpe", [64, S], BF16)
        vv = c("ml_v", [128, 32, 128], BF16)
        qn = self.ybig[:, 0:4096]
        kn = self.ybig[:, 4096:8192]
        wsl = c("ml_wsl", [128, 8, 128], BF16)
        wkr = c("ml_wkr", [128, 8, 64], BF16)
        wkrot = c("ml_wkrot", [128, 8, 64], BF16)
        nrm = c("ml_nrm", [128, 5])
        cs_c, cs_s = g[8][0:64, :], g[9][0:64, :]
        P.alias['ml_cs'] = 'g8'
        cqb = c("ml_cqb", [128, 3, 512], BF16)
        ckb = c("ml_ckb", [128, 2, 512], BF16)
        wq_st = self.wst[0][:].rearrange("p a b -> p (a b)")[:, 0:576].rearrange("p (a b) -> p a b", b=192)
        wkv_st = self.wst[1][:].rearrange("p a b -> p (a b)")[:, 0:512].rearrange("p (a b) -> p a b", b=256)
        P.alias['ml_wqst'] = 'wst0'
        P.alias['ml_wkvst'] = 'wst1'
        wqn = c("ml_wqn", [128, 3, 128], BF16)
        wqp = c("ml_wqp", [128, 3, 64], BF16)
        wqrot = c("ml_wqrot", [128, 3, 64], BF16)
        wkn = c("ml_wkn", [128, 2, 128], BF16)
        wv = c("ml_wv", [128, 2, 128], BF16)
        identb = c("ml_identb", [128, 128], BF16)
        mneg = g[21][:, 0:128]
        mx2 = c("ml_mx", [128, 48])
        rs2 = c("ml_rs", [128, 48])
        PT = [g[16][:].bitcast(BF16)[:, 0:512].rearrange("p (j q) -> p j q", q=128),
              g[17][:].bitcast(BF16)[:, 0:512].rearrange("p (j q) -> p j q", q=128)]
        Pb = [g[14][:].bitcast(BF16)[:, 0:512], g[15][:].bitcast(BF16)[:, 0:512]]
        sd = g[18][:, 0:128]
        Osb = g[19][:, 0:128]
        yo = g[20][:].bitcast(BF16)[:, 0:512]
        for a_, b_ in [('ml_PT0', 'g16'), ('ml_PT1', 'g17'), ('ml_P0', 'g14'), ('ml_P1', 'g15'), ('ml_sd', 'g18'),
                       ('ml_O', 'g19'), ('ml_yo', 'g20')]:
            P.alias[a_] = b_
        KP = 'mlp'
        P.op(V, lambda e: e.tensor_copy(identb, self.ident[:]), r=['ident'], w=[KP])
        P.dma(mneg, I['mneg128'], w=[KP, 'g21'])
        P.dma(nrm[:, 0:3], I['mla_q_norm_w'][li].rearrange("(c p) -> p c", p=128), w=[KP], allow_slow_non_contiguous=True)
        P.dma(nrm[:, 3:5], I['mla_kv_norm_w'][li].rearrange("(c p) -> p c", p=128), w=[KP], allow_slow_non_contiguous=True)
        st = self.wst[0]
        P.dma(st[:, :, 0:64], inw[:, 2192:2256].rearrange("(kt p) m -> p kt m", p=128), w=['wst0'])
        P.op(G, lambda e: e.tensor_copy(wkr, st[:, :, 0:64]), r=['wst0'], w=[KP])
        P.op(V, lambda e: e.tensor_scalar(out=wkrot[:, :, 0:32], in0=st[:, :, 32:64], scalar1=-1.0, scalar2=None, op0=ALU.mult),
             r=['wst0'], w=[KP])
        P.op(G, lambda e: e.tensor_copy(wkrot[:, :, 32:64], st[:, :, 0:32]), r=['wst0'], w=[KP])

        def rope_pair(wa, wb, nk, rhs_fn, rkeys, dst, dkey):
            def f1(e):
                ins = None
                for kt in range(nk):
                    ins = e.matmul(B[2][0:64, :], wa[:, kt, :], rhs_fn(kt), start=(kt == 0), stop=(kt == nk - 1))
                return ins

            def f2(e):
                ins = None
                for kt in range(nk):
                    ins = e.matmul(B[3][0:64, :], wb[:, kt, :], rhs_fn(kt), start=(kt == 0), stop=(kt == nk - 1))
                return ins
            P.op(T, f1, r=rkeys + [KP], w=['bank2'])
            P.op(T, f2, r=rkeys + [KP], w=['bank3'])
            t1, t2 = g[10], g[11]
            tt(V, t1[0:64, :], B[2][0:64, :], cs_c, ALU.mult, ['bank2', 'g8'], [gk(10)])
            tt(V, t2[0:64, :], B[3][0:64, :], cs_s, ALU.mult, ['bank3', 'g9'], [gk(11)])
            tt(G, dst, t1[0:64, :], t2[0:64, :], ALU.add, [gk(10), gk(11)], [dkey])

        for blk in range(8):
            t0 = blk * 512
            P.dma(cs_c, self.csT[0, :, t0:t0 + 512], r=['csT'], w=['g8'])
            P.dma(cs_s, self.csT[1, :, t0:t0 + 512], r=['csT'], w=['g9'])
            for (c0, ns, nw0, dstT, den) in [(1552, 3, 0, self.cqnT, 384.0), (1936, 2, 3, self.ckvnT, 256.0)]:
                for s_ in range(ns):
                    self.load_w(inw, c0 + s_ * 128, 128, wsl, 'ml_wsl')
                    ps, pk = self.proj(wsl, 'ml_wsl', 128, t0)
                    P.op(A, lambda e, ps=ps, s_=s_: e.copy(g[s_][:], ps), r=[pk], w=[gk(s_)])
                    act(g[4 + (s_ % 2)][:], g[s_][:], AF.Square, [gk(s_)], [gk(4 + (s_ % 2))])
                    P.op(T, lambda e, s_=s_, ns=ns: e.matmul(B[2][:, :], self.ones[:], g[4 + (s_ % 2)][:], start=(s_ == 0),
                                                            stop=(s_ == ns - 1)), r=['ones', gk(4 + (s_ % 2))], w=['bank2'])
                act(g[6][:], B[2][:, :], AF.Sqrt, ['bank2', 'eps'], [gk(6)], scale=1.0 / den, bias=self.eps6[:, 0:1])
                P.op(V, lambda e: e.reciprocal(g[6][:], g[6][:]), r=[gk(6)], w=[gk(6)])
                ob = cqb if ns == 3 else ckb
                okey = 'ml_cqb' if ns == 3 else 'ml_ckb'
                for s_ in range(ns):
                    stt(V, ob[:, s_, :], g[s_][:], nrm[:, nw0 + s_:nw0 + s_ + 1], g[6][:], ALU.mult, ALU.mult,
                        [gk(s_), gk(6), KP], [okey])
                P.dma(dstT[:, t0:t0 + 512].rearrange("(kt p) t -> p kt t", p=128), ob, r=[okey], w=['cqnT' if ns == 3 else 'ckvnT'])
            rope_pair(wkr, wkrot, 8, lambda kt, t0=t0: self.xnT[:, kt, t0:t0 + 512], ['xnT'], kpe[:, t0:t0 + 512], 'ml_kpe')
        for h in range(8):
            P.dma(wq_st, I['mla_wq_up'][li][:, h * 192:(h + 1) * 192].rearrange("(kt p) m -> p kt m", p=128), w=['ml_wqst'])
            P.dma(wkv_st, I['mla_wkv_up'][li][:, h * 256:(h + 1) * 256].rearrange("(kt p) m -> p kt m", p=128), w=['ml_wkvst'])
            P.op(G, lambda e: e.tensor_copy(wqn, wq_st[:, :, 0:128]), r=['ml_wqst'], w=['ml_wh'])
            P.op(G, lambda e: e.tensor_copy(wqp, wq_st[:, :, 128:192]), r=['ml_wqst'], w=['ml_wh'])
            P.op(V, lambda e: e.tensor_scalar(out=wqrot[:, :, 0:32], in0=wq_st[:, :, 160:192], scalar1=-1.0, scalar2=None,
                                              op0=ALU.mult), r=['ml_wqst'], w=['ml_wh'])
            P.op(G, lambda e: e.tensor_copy(wqrot[:, :, 32:64], wq_st[:, :, 128:160]), r=['ml_wqst'], w=['ml_wh'])
            P.op(G, lambda e: e.tensor_copy(wkn, wkv_st[:, :, 0:128]), r=['ml_wkvst'], w=['ml_wh'])
            P.op(G, lambda e: e.tensor_copy(wv, wkv_st[:, :, 128:256]), r=['ml_wkvst'], w=['ml_wh'])
            self.load_w(inw, 3536 + h * 128, 128, wsl, 'ml_wsl')
            for blk in range(8):
                t0 = blk * 512
                P.dma(cs_c, self.csT[0, :, t0:t0 + 512], r=['csT'], w=['g8'])
                P.dma(cs_s, self.csT[1, :, t0:t0 + 512], r=['csT'], w=['g9'])
                P.dma(cqb, self.cqnT[:, t0:t0 + 512].rearrange("(kt p) t -> p kt t", p=128), r=['cqnT'], w=['ml_cqb'])
                P.dma(ckb, self.ckvnT[:, t0:t0 + 512].rearrange("(kt p) t -> p kt t", p=128), r=['ckvnT'], w=['ml_ckb'])

                def fq(e):
                    ins = None
                    for kt in range(3):
                        ins = e.matmul(B[4][:, :], wqn[:, kt, :], cqb[:, kt, :], start=(kt == 0), stop=(kt == 2))
                    return ins
                P.op(T, fq, r=['ml_wh', 'ml_cqb'], w=['bank4'])
                P.op(A, lambda e, t0=t0: e.copy(qn[:, t0:t0 + 512], B[4][:, :]), r=['bank4'], w=['big'])

                def fk(e):
                    ins = None
                    for kt in range(2):
                        ins = e.matmul(B[5][:, :], wkn[:, kt, :], ckb[:, kt, :], start=(kt == 0), stop=(kt == 1))
                    return ins
                P.op(T, fk, r=['ml_wh', 'ml_ckb'], w=['bank5'])
                P.op(A, lambda e, t0=t0: e.copy(kn[:, t0:t0 + 512], B[5][:, :]), r=['bank5'], w=['big'])
                rope_pair(wqp, wqrot, 3, lambda kt: cqb[:, kt, :], ['ml_cqb', 'ml_wh'], qpe[:, t0:t0 + 512], 'ml_qpe')
                for sb in range(4):
                    def fv(e, sb=sb):
                        ins = None
                        for kt in range(2):
                            ins = e.matmul(B[6][:, sb * 128:(sb + 1) * 128], ckb[:, kt, sb * 128:(sb + 1) * 128], wv[:, kt, :],
                                           start=(kt == 0), stop=(kt == 1))
                        return ins
                    P.op(T, fv, r=['ml_wh', 'ml_ckb'], w=['bank6'])
                P.op(V, lambda e, blk=blk: e.tensor_copy(vv[:, blk * 4:(blk + 1) * 4, :],
                                                        B[6][:, :].rearrange("p (s d) -> p s d", d=128)), r=['bank6'], w=['ml_v'])
            for qb in range(32):
                qs = slice(qb * 128, (qb + 1) * 128)
                nkb = qb + 1
                mx = mx2[:, (qb % 2) * 24:(qb % 2) * 24 + 24]
                rs = rs2[:, (qb % 2) * 24:(qb % 2) * 24 + 24]
                kmx, krs = f'ml_mx{qb % 2}', f'ml_rs{qb % 2}'
                NG = (nkb + 3) // 4

                def scores(bi, gi, qs=qs, nkb=nkb):
                    ncols = min(512, nkb * 128 - gi * 512)
                    k0 = gi * 512

                    def f(e):
                        e.matmul(B[bi][:, 0:ncols], qn[:, qs], kn[:, k0:k0 + ncols], start=True, stop=False)
                        return e.matmul(B[bi][:, 0:ncols], qpe[:, qs], kpe[:, k0:k0 + ncols], start=False, stop=True)
                    P.op(T, f, r=['big', 'ml_qpe', 'ml_kpe'], w=[f'bank{bi}'])
                    return ncols
                nm = 0
                for gi in range(NG):
                    bi = 2 + (gi % 2)
                    ncols = scores(bi, gi)
                    last = (gi == NG - 1)
                    nfull = ncols - 128 if last else ncols
                    if nfull > 0:
                        P.op(V, lambda e, bi=bi, nfull=nfull, nm=nm, mx=mx: e.tensor_reduce(out=mx[:, nm:nm + 1], in_=B[bi][:, 0:nfull],
                                                                                   axis=AX.X, op=ALU.max),
                             r=[f'bank{bi}'], w=[kmx])
                        nm += 1
                    if last:
                        tt(V, sd, B[bi][:, ncols - 128:ncols], mneg, ALU.add, [f'bank{bi}', KP, 'g21'], ['ml_sd'])
                        P.op(V, lambda e, nm=nm, mx=mx: e.tensor_reduce(out=mx[:, nm:nm + 1], in_=sd, axis=AX.X, op=ALU.max),
                             r=['ml_sd'], w=[kmx])
                        nm += 1
                P.op(V, lambda e, nm=nm, mx=mx: e.tensor_reduce(out=mx[:, 23:24], in_=mx[:, 0:nm], axis=AX.X, op=ALU.max),
                     r=[kmx], w=[kmx])
                ts(V, mx[:, 22:23], mx[:, 23:24], -SC, None, ALU.mult, None, [kmx], [kmx])
                nr = 0
                kbi = 0
                for gi in range(NG):
                    bi = 2 + (gi % 2)
                    pi = gi % 2
                    ncols = scores(bi, gi)
                    last = (gi == NG - 1)
                    nfull = ncols - 128 if last else ncols
                    Pt = Pb[pi]
                    if nfull > 0:
                        act(Pt[:, 0:nfull], B[bi][:, 0:nfull], AF.Exp, [f'bank{bi}', kmx], [f'ml_P{pi}', krs], scale=SC,
                            bias=mx[:, 22:23], accum_out=rs[:, nr:nr + 1])
                        nr += 1
                    if last:
                        tt(V, sd, B[bi][:, ncols - 128:ncols], mneg, ALU.add, [f'bank{bi}', KP, 'g21'], ['ml_sd'])
                        act(Pt[:, nfull:ncols], sd, AF.Exp, ['ml_sd', kmx], [f'ml_P{pi}', krs], scale=SC, bias=mx[:, 22:23],
                            accum_out=rs[:, nr:nr + 1])
                        nr += 1
                    nb_ = ncols // 128
                    b4 = B[4][:, :].bitcast(BF16)

                    def ftp(e, Pt=Pt, nb_=nb_):
                        ins = None
                        for j in range(nb_):
                            ins = e.transpose(b4[:, j * 128:(j + 1) * 128], Pt[:, j * 128:(j + 1) * 128], identb)
                        return ins
                    P.op(T, ftp, r=[f'ml_P{pi}', KP], w=['bank4'])
                    P.op(V, lambda e, pi=pi, nb_=nb_: e.tensor_copy(PT[pi][:, 0:nb_, :],
                                                                   b4[:, 0:nb_ * 128].rearrange("p (j q) -> p j q", q=128)),
                         r=['bank4'], w=[f'ml_PT{pi}'])

                    def fpv(e, pi=pi, nb_=nb_, kbi=kbi, nkb=nkb):
                        ins = None
                        for j in range(nb_):
                            ins = e.matmul(B[5][:, 0:128], PT[pi][:, j, :], vv[:, kbi + j, :], start=(kbi + j == 0),
                                           stop=(kbi + j == nkb - 1))
                        return ins
                    P.op(T, fpv, r=[f'ml_PT{pi}', 'ml_v'], w=['bank5'])
                    kbi += nb_
                P.op(V, lambda e, nr=nr, rs=rs: e.tensor_reduce(out=rs[:, 23:24], in_=rs[:, 0:nr], axis=AX.X, op=ALU.add),
                     r=[krs], w=[krs])
                P.op(V, lambda e, rs=rs: e.reciprocal(rs[:, 23:24], rs[:, 23:24]), r=[krs], w=[krs])
                ts(V, Osb, B[5][:, 0:128], rs[:, 23:24], None, ALU.mult, None, ['bank5', krs], ['ml_O'])
                q4 = qb % 4
                P.op(T, lambda e, q4=q4: e.transpose(B[6][:, q4 * 128:(q4 + 1) * 128], Osb, self.ident[:]), r=['ml_O', 'ident'],
                     w=['bank6'])
                if q4 == 3:
                    t0 = (qb // 4) * 512
                    ps, pk = self.proj(wsl, 'ml_wsl', 128, t0)
                    zs = g[12]
                    act(zs[:], ps, AF.Silu, [pk], [gk(12)])
                    if self.debug and self.dbgsel == 'd_raw' and li == 0:
                        P.op(V, lambda e: e.tensor_copy(g[13][:], B[6][:, :]), r=['bank6'], w=[gk(13)])
                        P.dma(self.dbg[h * 128:(h + 1) * 128, t0:t0 + 512], g[13][:], r=[gk(13)])
                    tt(V, yo, B[6][:, :], zs[:], ALU.mult, ['bank6', gk(12)], ['ml_yo'])
                    P.dma(self.ycT[1024 + h * 128:1024 + (h + 1) * 128, t0:t0 + 512], yo, r=['ml_yo'], w=['ycT'])

    def odd_layer(self, layer):
        li = layer // 2
        self.norm_phase(layer)
        if not hasattr(self, 'csT'):
            self.rope_setup()
        self.phase()
        self.ssd(li)
        self.phase()
        self.mla(li)
        self.mem_attn(layer, self.I['od_in_w'][li], 2256, 4560, 2048)
        self.out_proj(self.I['od_out_w'][li])
        self.phase()


def build(shapes, nlayers=4, layers=None):
    k = K(shapes, nlayers=nlayers)
    k.consts_small()
    k.stage0()
    k.mem_setup()
    for layer in (layers if layers is not None else range(nlayers)):
        if layer % 2 == 0:
            k.even_layer(layer)
        else:
            k.odd_layer(layer)
    k.final_phase()
    return k


_CACHE = {}


def kernel(**inputs):
    consts = host_consts()
    x = np.ascontiguousarray(np.asarray(inputs['x'], dtype=np.float32))
    mem = np.ascontiguousarray(np.asarray(inputs['mem'], dtype=np.float32))
    pos = np.ascontiguousarray(np.asarray(inputs['positions']).astype(np.int32))
    nb = x.shape[0]
    shared = {n: np.ascontiguousarray(np.asarray(inputs[n], dtype=np.float32)) for n in WNAMES}
    shared.update(consts)
    shapes = {'x': ([S, D], F32), 'mem': ([256, D], F32), 'positions': ([1, S], I32)}
    for n, v in shared.items():
        shapes[n] = (list(v.shape), F32)
    if 'nc' not in _CACHE:
        k = build(shapes)
        _CACHE['nc'] = k.P.emit()
    nc = _CACHE['nc']
    in_maps = []
    for b in range(nb):
        m = {'x': x[b], 'mem': mem[b], 'positions': pos[b:b + 1]}
        m.update(shared)
        in_maps.append(m)
    res = run_bass_kernel_spmd(nc, in_maps, core_ids=list(range(nb)))
    return np.stack([np.asarray(r['out'], dtype=np.float32) for r in res.results], axis=0)
```

```python
import contextlib
import math
import numpy as np
import concourse.bass as bass
import concourse.mybir as mybir
from concourse.bass_utils import run_bass_kernel_spmd

F32 = mybir.dt.float32
BF16 = mybir.dt.bfloat16
I32 = mybir.dt.int32
AF = mybir.ActivationFunctionType
ALU = mybir.AluOpType
AX = mybir.AxisListType

ENGS = ['tensor', 'vector', 'scalar', 'gpsimd', 'sync']
NDMASEM = 12

S = 4096
D = 1024
EVEN_PROJ = 6784
ODD_PROJ = 4816
C0 = math.exp(-0.5)


class Prog:
    def __init__(self):
        self.nc = bass.Bass("TRN2", target_bir_lowering=False)
        self.stack = contextlib.ExitStack()
        self.ops = {e: [] for e in ENGS}
        self.cnt = {e: 0 for e in ENGS}
        self.lastw = {}
        self.readers = {}
        self.dma_rr = {'sync': 0, 'gpsimd': 0}
        self.dma_val = {}
        self.dma_last = {}
        self.all_dma_tokens = []
        self.alias = {'b5a': 'bank5', 'b5b': 'bank5', 'b6a': 'bank7', 'b6b': 'bank6', 'b6c': 'bank7', 'b6d': 'bank7',
                      'b7a': 'bank6', 'b7b': 'bank2', 'b7c': 'bank2', 'b7d': 'bank2', 'b7e': 'bank2'}

    def dram(self, name, shape, dt, kind="Internal"):
        return self.nc.dram_tensor(name, list(shape), dt, kind=kind)

    def sb(self, name, shape, dt=F32):
        return self.stack.enter_context(self.nc.sbuf_tensor("t_" + name, list(shape), dt))

    def ps(self, name, shape, dt=F32):
        return self.stack.enter_context(self.nc.psum_tensor("p_" + name, list(shape), dt))

    def _deps(self, r, w, eng=None):
        r = [self.alias.get(k, k) for k in r]
        w = [self.alias.get(k, k) for k in w]
        deps = []
        skip_same = eng in ('vector', 'scalar')
        for k in r:
            t = self.lastw.get(k)
            if t is not None:
                deps.append(t)
        for k in w:
            t = self.lastw.get(k)
            if t is not None and not (skip_same and t[0] == 'c' and t[1] == eng):
                deps.append(t)
            for t2 in self.readers.get(k, []):
                if not (skip_same and t2[0] == 'c' and t2[1] == eng):
                    deps.append(t2)
        return deps

    def _commit(self, tok, r, w):
        r = [self.alias.get(k, k) for k in r]
        w = [self.alias.get(k, k) for k in w]
        for k in w:
            self.lastw[k] = tok
            self.readers[k] = []
        for k in r:
            if k in w:
                continue
            lst = self.readers.setdefault(k, [])
            if tok[0] == 'c':
                lst[:] = [t for t in lst if not (t[0] == 'c' and t[1] == tok[1])]
            lst.append(tok)

    def op(self, eng, fn, r=(), w=()):
        r = [self.alias.get(k, k) for k in r]
        w = [self.alias.get(k, k) for k in w]
        pb = [k for k in list(r) + list(w) if k.startswith('bank')]
        r = list(r) + pb
        w = list(w) + [k for k in pb if k not in w]
        deps = self._deps(r, w, eng) + list(getattr(self, '_bar_extra', []))
        self.cnt[eng] += 1
        tok = ('c', eng, self.cnt[eng])
        self.ops[eng].append((fn, deps, 'c', tok))
        self._commit(tok, r, w)
        return tok

    def barrier(self, scratch):
        toks = [('c', e, self.cnt[e]) for e in ENGS if self.cnt[e] > 0]
        last_d = {}
        for t in self.all_dma_tokens:
            last_d[t[1]] = t
        toks += list(last_d.values())
        for i, eng in enumerate(['vector', 'scalar', 'gpsimd']):
            self.cnt[eng] += 1
            tok = ('c', eng, self.cnt[eng])
            if eng == 'scalar':
                fn = lambda e, i=i: e.copy(scratch[:, i:i + 1], scratch[:, i + 4:i + 5])
            else:
                fn = lambda e, i=i: e.tensor_copy(scratch[:, i:i + 1], scratch[:, i + 4:i + 5])
            self.ops[eng].append((fn, list(toks), 'c', tok))
        self.bar_toks = [('c', e, self.cnt[e]) for e in ['vector', 'scalar', 'gpsimd']]
        for k in list(self.lastw.keys()):
            pass
        self.lastw['__bar__'] = self.bar_toks[0]
        self._bar_extra = list(self.bar_toks)

    def dma(self, out, in_, r=(), w=(), q='sync', **kw):
        deps = self._deps(r, w) + list(getattr(self, '_bar_extra', []))
        i = self.dma_rr[q]
        self.dma_rr[q] = (i + 1) % NDMASEM
        key = (q, i)
        prev = self.dma_last.get(key)
        if prev is not None:
            deps.append(prev)
        v = self.dma_val.get(key, 0) + 16
        self.dma_val[key] = v
        tok = ('d', key, v)
        self.dma_last[key] = tok

        def fn(e, out=out, in_=in_, kw=kw):
            return e.dma_start(out=out, in_=in_, **kw)
        self.ops[q].append((fn, deps, 'd', tok))
        self._commit(tok, r, w)
        self.all_dma_tokens.append(tok)
        return tok

    def emit(self):
        nc = self.nc
        EPOCH = 16000
        sems = {}
        for e in ENGS:
            nep = max(1, (self.cnt[e] + EPOCH - 1) // EPOCH)
            for ep in range(nep):
                sems[(e, ep)] = self.stack.enter_context(nc.semaphore(f"s_{e}_{ep}"))
        dsems = {}
        for q in ('sync', 'gpsimd'):
            for i in range(NDMASEM):
                dsems[(q, i)] = self.stack.enter_context(nc.semaphore(f"d_{q}_{i}"))
        prog = self

        def csem(eng, n):
            return sems[(eng, (n - 1) // EPOCH)], (n - 1) % EPOCH + 1

        def run(ename, e):
            waited = {}
            for (fn, deps, kind, tok) in prog.ops[ename]:
                need = {}
                for d in deps:
                    if d[0] == 'c':
                        if d[1] == ename and ename == 'tensor':
                            continue
                        k = ('c', d[1])
                    else:
                        k = ('d', d[1])
                    if d[2] > need.get(k, 0):
                        need[k] = d[2]
                for k, v in need.items():
                    if waited.get(k, 0) >= v:
                        continue
                    waited[k] = v
                    if k[0] == 'c':
                        s_, val = csem(k[1], v)
                        e.wait_ge(s_, val)
                    else:
                        e.wait_ge(dsems[k[1]], v)
                ins = fn(e)
                if kind == 'c':
                    s_, _ = csem(ename, tok[2])
                    ins.then_inc(s_, 1)
                else:
                    ins.then_inc(dsems[tok[1]], 16)
            if ename == 'sync':
                fin = {}
                for t in prog.all_dma_tokens:
                    fin[t[1]] = max(fin.get(t[1], 0), t[2])
                for k, v in fin.items():
                    e.wait_ge(dsems[k], v)
                for en in ENGS:
                    if en != 'sync' and prog.cnt[en] > 0:
                        s_, val = csem(en, prog.cnt[en])
                        e.wait_ge(s_, val)

        with nc.Block() as block:
            @block.sync
            def _(e):
                run('sync', e)

            @block.tensor
            def _(e):
                run('tensor', e)

            @block.vector
            def _(e):
                run('vector', e)

            @block.scalar
            def _(e):
                run('scalar', e)

            @block.gpsimd
            def _(e):
                run('gpsimd', e)
        self.stack.close()
        return nc


def host_consts():
    c = {}
    c['ident'] = np.eye(128, dtype=np.float32)
    s = np.arange(128)[:, None]
    t = np.arange(128)[None, :]
    same = (s // 64) == (t // 64)
    MU = (same & (s < t)).astype(np.float32)
    MUI = (same & (s <= t)).astype(np.float32)
    c['mm'] = np.concatenate([MU, MUI], axis=1)
    c['ml'] = (same & (s > t)).astype(np.float32)
    m01 = np.ones((128, 512), np.float32)
    m01[:, ::64] = 0.0
    c['m01'] = m01
    m128 = np.ones((128, 512), np.float32)
    m128[:, ::128] = 0.0
    c['m128'] = m128
    c['iota'] = np.tile(np.arange(128, dtype=np.float32)[None, :], (128, 1))
    sel = np.zeros((16, 16, 128), np.float32)
    for h in range(16):
        sel[h, h, :] = 1.0
    c['sel16'] = sel.reshape(16, 2048)
    c['mui128'] = (np.arange(128)[:, None] <= np.arange(128)[None, :]).astype(np.float32)
    c['mneg128'] = np.where(np.arange(128)[None, :] > np.arange(128)[:, None], -30000.0, 0.0).astype(np.float32)
    inv = 1.0 / (10000.0 ** (np.arange(0, 64, 2, dtype=np.float32) / 64.0))
    c['invf64'] = np.concatenate([inv, inv]).astype(np.float32).reshape(64, 1)
    return c


WNAMES = ['norm_w', 'mem_norm_w', 'final_norm_w', 'mem_kv_w', 'ev_in_w', 'ev_out_w', 'rw_mu', 'rw_w0',
          'rw_w2', 'rw_a0', 'rw_a2', 'rw_k_k', 'rw_k_a', 'rw_r_k', 'rw_ln_w', 'rw_ln_b',
          's5_lambda_re', 's5_lambda_im', 's5_b_re', 's5_b_im', 's5_c_re', 's5_c_im', 's5_d',
          's5_log_dt', 's5_glu_w', 's5_glu_b', 'od_in_w', 'od_out_w', 'm2_conv_w', 'm2_conv_b',
          'm2_dt_bias', 'm2_a_log', 'm2_d', 'm2_norm_w', 'mla_q_norm_w', 'mla_wq_up',
          'mla_kv_norm_w', 'mla_wkv_up']


class K:
    def __init__(self, shapes, nlayers=4, debug=None):
        self.P = P = Prog()
        self.nlayers = nlayers
        self.debug = debug
        self.dbgsel = None
        self.heads = range(16)
        self.segs = range(8)
        self.stop = None
        self.I = {}
        for n, (shp, dt) in shapes.items():
            self.I[n] = P.dram(n, shp, dt, kind="ExternalInput").ap()
        self.out = P.dram("out", [S, D], F32, kind="ExternalOutput").ap()
        self.hT = P.dram("hT", [D, S], F32).ap()
        self.ycT = P.dram("ycT", [2304, S], BF16).ap()
        if debug:
            self.dbg = P.dram("dbg", list(debug), F32, kind="ExternalOutput").ap()
        self.xnT = P.sb("xnT", [128, 8, S], BF16)
        self.ident = P.sb("ident", [128, 128])
        self.mm = P.sb("mm", [128, 256])
        self.ml = P.sb("ml", [128, 128])
        self.m01 = P.sb("m01", [128, 512])
        self.ones = P.sb("ones", [128, 128])
        self.bank = [P.ps(f"bank{i}", [128, 512]) for i in range(8)]
        P.dma(self.ident[:], self.I['ident'], w=['ident'])
        P.dma(self.mm[:], self.I['mm'], w=['mm'])
        P.dma(self.ml[:], self.I['ml'], w=['ml'])
        P.dma(self.m01[:], self.I['m01'], w=['m01'])
        self.m128 = P.sb("m128", [128, 512])
        self.iota = P.sb("iota", [128, 128])
        P.dma(self.m128[:], self.I['m128'], w=['m128'])
        P.dma(self.iota[:], self.I['iota'], w=['iota'])
        P.op('gpsimd', lambda e: e.memset(self.ones[:], 1.0), w=['ones'])
        self.wst = [P.sb(f"wst{i}", [128, 8, 128]) for i in range(2)]
        self.g = [P.sb(f"g{i}", [128, 512]) for i in range(24)]
        self.big = P.sb("big", [128, S + 1])
        self.ARENA = 9728
        self.arena = P.sb("arena", [128, self.ARENA])
        self.arena_off = 0
        self.ybig = self.big[:, 0:4096].bitcast(BF16)
        self.wst_i = 0
        self.pj_i = 0

    def carve(self, name, shape, dt=F32):
        p = shape[0]
        n = 1
        for d_ in shape[1:]:
            n *= d_
        words = n if dt in (F32, I32) else (n + 1) // 2
        off = self.arena_off
        self.arena_off += words
        assert self.arena_off <= self.ARENA, (name, self.arena_off)
        ap = self.arena[0:p, off:off + words]
        if dt != F32:
            ap = ap.bitcast(dt)[:, 0:n]
        if len(shape) == 3:
            ap = ap.rearrange("p (a b) -> p a b", b=shape[2])
        elif len(shape) == 4:
            ap = ap.rearrange("p (a b c) -> p a b c", b=shape[2], c=shape[3])
        return ap

    def phase(self):
        self.P.barrier(self.barscr)
        self.arena_off = 0

    def load_w(self, wap, c0, M, dst, key):
        P = self.P
        i = self.wst_i
        self.wst_i ^= 1
        st = self.wst[i]
        P.dma(st[:, :, 0:M], wap[:, c0:c0 + M].rearrange("(kt p) m -> p kt m", p=128),
              w=[f'wst{i}'])
        P.op('gpsimd', lambda e: e.tensor_copy(dst[:, :, 0:M], st[:, :, 0:M]), r=[f'wst{i}'], w=[key])

    def proj(self, wt, wkey, M, t0):
        P = self.P
        i = self.pj_i
        self.pj_i ^= 1
        bk = self.bank[i]
        xn = self.xnT

        def fn(e):
            ins = None
            for kt in range(8):
                ins = e.matmul(bk[0:M, :], wt[:, kt, 0:M], xn[:, kt, t0:t0 + 512],
                               start=(kt == 0), stop=(kt == 7))
            return ins
        P.op('tensor', fn, r=[wkey, 'xnT'], w=[f'bank{i}'])
        return bk[0:M, :], f'bank{i}'

    def stage0(self):
        P = self.P
        x = self.I['x']
        self.hb = [P.sb(f"hb{i}", [128, 8, 256]) for i in range(2)]
        xin = [self.hb[i][:, 0:4, :].rearrange("p a b -> p (a b)") for i in range(2)]
        xo = [self.hb[i][:, 4:8, :].rearrange("p a (b c) -> p (a b) c", c=128) for i in range(2)]
        for tb in range(S // 128):
            i = tb % 2
            P.dma(xin[i], x[tb * 128:(tb + 1) * 128, :], w=[f'hb{i}'])
            for half in range(2):
                bk = self.bank[2 + half]

                def fn(e, i=i, half=half, bk=bk):
                    ins = None
                    for q in range(4):
                        kt = half * 4 + q
                        ins = e.transpose(bk[:, q * 128:(q + 1) * 128], xin[i][:, kt * 128:(kt + 1) * 128],
                                          self.ident[:])
                    return ins
                P.op('tensor', fn, r=[f'hb{i}', 'ident'], w=[f'bank{2 + half}'])
                eng = 'vector' if half == 0 else 'scalar'
                if half == 0:
                    P.op('vector', lambda e, i=i, bk=bk: e.tensor_copy(
                        xo[i][:, 0:4, :], bk[:].rearrange("p (q t) -> p q t", t=128)),
                        r=['bank2'], w=[f'hb{i}'])
                else:
                    P.op('scalar', lambda e, i=i, bk=bk: e.copy(
                        xo[i][:, 4:8, :], bk[:].rearrange("p (q t) -> p q t", t=128)),
                        r=['bank3'], w=[f'hb{i}'])
            P.dma(self.hT[:, tb * 128:(tb + 1) * 128].rearrange("(kt p) t -> p kt t", p=128), xo[i],
                  r=[f'hb{i}'], w=['hT'])

    def norm_phase(self, layer):
        P = self.P
        if not hasattr(self, 'nw'):
            self.nw = P.sb("nw", [128, 4, 8])
            P.dma(self.nw[:], self.I['norm_w'].rearrange("l (kt p) -> p l kt", p=128), w=['nw'],
                  allow_slow_non_contiguous=True)
            self.sq = [self.g[0], self.g[1]]
            self.rinv = self.g[2]
        NB = 256
        for blk in range(S // NB):
            i = blk % 2
            t0 = blk * NB
            hb = self.hb[i]
            P.dma(hb[:], self.hT[:, t0:t0 + NB].rearrange("(kt p) t -> p kt t", p=128), r=['hT'],
                  w=[f'hb{i}'])
            bk = self.bank[2]
            for kt in range(8):
                j = kt % 2
                P.op('scalar', lambda e, kt=kt, j=j, hb=hb: e.activation(
                    out=self.sq[j][:, 0:NB], in_=hb[:, kt, :], func=AF.Square), r=[f'hb{i}'], w=[f'g{j}'])
                P.op('tensor', lambda e, kt=kt, j=j, bk=bk: e.matmul(
                    bk[:, 0:NB], self.ones[:], self.sq[j][:, 0:NB], start=(kt == 0), stop=(kt == 7)),
                    r=[f'g{j}', 'ones'], w=['bank2'])
            P.op('scalar', lambda e, bk=bk: e.activation(out=self.rinv[:, 0:NB], in_=bk[:, 0:NB], func=AF.Sqrt,
                                                         scale=1.0 / D, bias=self.eps6[:, 0:1]),
                 r=['bank2', 'eps'], w=['g2'])
            P.op('vector', lambda e: e.reciprocal(self.rinv[:, 0:NB], self.rinv[:, 0:NB]), r=['g2'], w=['g2'])
            for kt in range(8):
                eng = 'vector' if kt % 2 == 0 else 'gpsimd'
                if eng == 'vector':
                    P.op('vector', lambda e, kt=kt, hb=hb, t0=t0: e.scalar_tensor_tensor(
                        out=self.xnT[:, kt, t0:t0 + NB], in0=hb[:, kt, :], scalar=self.nw[:, layer, kt:kt + 1],
                        in1=self.rinv[:, 0:NB], op0=ALU.mult, op1=ALU.mult), r=[f'hb{i}', 'g2', 'nw'], w=['xnT'])
                else:
                    P.op('gpsimd', lambda e, kt=kt, hb=hb: e.tensor_scalar(
                        out=hb[:, kt, :], in0=hb[:, kt, :], scalar1=self.nw[:, layer, kt:kt + 1], scalar2=None,
                        op0=ALU.mult), r=[f'hb{i}', 'nw'], w=[f'hb{i}'])
                    P.op('gpsimd', lambda e, kt=kt, hb=hb, t0=t0: e.tensor_tensor(
                        out=self.xnT[:, kt, t0:t0 + NB], in0=hb[:, kt, :], in1=self.rinv[:, 0:NB], op=ALU.mult),
                        r=[f'hb{i}', 'g2'], w=['xnT'])

    def consts_small(self):
        P = self.P
        self.eps6 = P.sb("eps6", [128, 4])
        self.barscr = P.sb("barscr", [128, 8])
        P.op('gpsimd', lambda e: e.memset(self.barscr[:], 0.0), w=['barscr'])
        P.op('gpsimd', lambda e: e.memset(self.eps6[:, 0:1], 1e-6), w=['eps'])
        P.op('gpsimd', lambda e: e.memset(self.eps6[:, 1:2], 64e-5), w=['eps'])
        P.op('gpsimd', lambda e: e.memset(self.eps6[:, 2:3], 0.0), w=['eps'])
        P.op('gpsimd', lambda e: e.memset(self.eps6[:, 3:4], 1.0), w=['eps'])

    def rwkv_alloc(self):
        P = self.P
        a = self.rw = {}
        self.rwk = {}
        for n in ['rr', 'kr', 'vr']:
            a[n] = self.carve("rw_" + n, [64, 513])
        for gi, n in enumerate(['rp', 'kp', 'vp', 'sg', 'ic', 'kk', 'k2', 'bb', 'csg', 'e1', 'e2', 'e3', 'e4', 'tmp',
                                'Af', 'Rf', 'Kf', 'Bf', 'KHf', 'BHf', 'bon']):
            a[n] = self.g[gi][0:64, :]
            self.rwk[n] = f'g{gi}'
            P.alias['rw_' + n] = f'g{gi}'
        a['zs'] = a['e3']
        a['yf'] = a['e4']
        P.alias['rw_zs'] = P.alias['rw_e3']
        P.alias['rw_yf'] = P.alias['rw_e4']
        P.alias['rw_latraw'] = 'big'
        P.alias['rw_lat'] = 'big'
        a['yo'] = self.carve("rw_yo", [64, 512], BF16)
        a['latraw'] = self.big
        a['lat'] = self.big[:, 1:S + 1]
        a['Zt'] = self.carve("rw_Zt", [128, 4, 128])
        a['KBt'] = self.carve("rw_KBt", [128, 2, 4, 64])
        a['Vt'] = self.carve("rw_Vt", [128, 4, 64])
        v4g = lambda gi: self.g[gi][:].rearrange("p (j t) -> p j t", t=128)
        for n, gi in [('X0', 0), ('X1', 1), ('XT0', 2), ('XT1', 3), ('RbT', 4), ('AkT', 5), ('RkT', 6)]:
            a[n] = v4g(gi)
            P.alias['rw_' + n] = f'g{gi}'
        a['QT'] = self.g[7][0:64, :]
        a['GT'] = self.g[10][0:64, :].rearrange("p (c n) -> p c n", n=64)
        a['H'] = self.g[13][0:64, :].rearrange("p (c n) -> p c n", n=64)
        P.alias['rw_QT'] = 'g7'
        P.alias['rw_GT'] = 'g10'
        P.alias['rw_H'] = 'g13'
        a['St'] = self.carve("rw_St", [64, 2, 64])
        a['Yt'] = self.g[21][0:64, :].rearrange("p (c i) -> p c i", i=64)
        a['Ysq'] = self.g[22][0:64, :].rearrange("p (c i) -> p c i", i=64)
        P.alias['rw_Yt'] = 'g21'
        P.alias['rw_Ysq'] = 'g22'
        a['st'] = self.carve("rw_st", [64, 4, 8])
        a['wr'] = self.carve("rw_wr", [128, 8, 64], BF16)
        a['wk'] = self.carve("rw_wk", [128, 8, 64], BF16)
        a['wv'] = self.carve("rw_wv", [128, 8, 64], BF16)
        a['wz'] = self.carve("rw_wz", [128, 8, 64], BF16)
        a['wlat'] = self.carve("rw_wlat", [128, 8, 128], BF16)
        a['w2a2'] = self.carve("rw_w2a2", [128, 1024])
        a['mu'] = self.carve("rw_mu", [64, 50])
        a['omu'] = self.carve("rw_omu", [64, 50])
        a['mulat'] = self.carve("rw_mulat", [128, 1])
        a['pv'] = self.carve("rw_pv", [64, 7, 16])

    def rwkv(self, li):
        P = self.P
        a = self.rw
        I = self.I
        B = self.bank
        V, A, G, T = 'vector', 'scalar', 'gpsimd', 'tensor'
        inw = I['ev_in_w'][li]
        P.dma(a['mu'][:], I['rw_mu'][li].rearrange("(c p) -> p c", p=64), w=['rw_mu'],
              allow_slow_non_contiguous=True)
        P.op(V, lambda e: e.tensor_scalar(out=a['omu'][:], in0=a['mu'][:], scalar1=-1.0, scalar2=1.0,
                                          op0=ALU.mult, op1=ALU.add), r=['rw_mu'], w=['rw_omu'])
        P.dma(a['mulat'][:], I['rw_mu'][li][3072:3200].rearrange("(p o) -> p o", o=1), w=['rw_mulat'],
              allow_slow_non_contiguous=True)
        for j, n in enumerate(['rw_w0', 'rw_a0', 'rw_k_k', 'rw_k_a', 'rw_ln_w', 'rw_ln_b']):
            P.dma(a['pv'][:, j, :], I[n][li].rearrange("(h p) -> p h", p=64), w=['rw_pv'],
                  allow_slow_non_contiguous=True)
        P.dma(a['pv'][:, 6, :], I['rw_r_k'][li].rearrange("h p -> p h"), w=['rw_pv'],
              allow_slow_non_contiguous=True)
        P.dma(a['w2a2'][0:64, :], I['rw_w2'][li], w=['rw_w2a2'])
        P.dma(a['w2a2'][64:128, :], I['rw_a2'][li], w=['rw_w2a2'])
        P.op(G, lambda e: e.memset(a['latraw'][:, 0:1], 0.0), w=['rw_latraw'])
        self.load_w(inw, 3072, 128, a['wlat'], 'rw_wlat')
        for blk in range(8):
            t0 = blk * 512
            ps, pk = self.proj(a['wlat'], 'rw_wlat', 128, t0)
            P.op(A, lambda e, ps=ps, t0=t0: e.copy(a['latraw'][:, 1 + t0:1 + t0 + 512], ps), r=[pk],
                 w=['rw_latraw'])
        tmpd = self.g[23]
        for blk in reversed(range(8)):
            t0 = blk * 512
            P.op(V, lambda e, t0=t0: e.tensor_tensor(out=tmpd[:], in0=a['latraw'][:, t0:t0 + 512],
                                                     in1=a['latraw'][:, t0 + 1:t0 + 513], op=ALU.subtract),
                 r=['big'], w=['g23'])
            P.op(V, lambda e, t0=t0: e.scalar_tensor_tensor(
                out=a['latraw'][:, t0 + 1:t0 + 513], in0=tmpd[:], scalar=a['mulat'][:, 0:1],
                in1=a['latraw'][:, t0 + 1:t0 + 513], op0=ALU.mult, op1=ALU.add),
                r=['g23', 'big', 'rw_mulat'], w=['big'])
            P.op(A, lambda e, t0=t0: e.activation(out=a['lat'][0:64, t0:t0 + 512], in_=a['lat'][0:64, t0:t0 + 512],
                                                  func=AF.Tanh), r=['big'], w=['big'])
        for h in self.heads:
            self.load_w(inw, h * 64, 64, a['wr'], 'rw_wr')
            self.load_w(inw, 1024 + h * 64, 64, a['wk'], 'rw_wk')
            self.load_w(inw, 2048 + h * 64, 64, a['wv'], 'rw_wv')
            self.load_w(inw, 4480 + h * 64, 64, a['wz'], 'rw_wz')
            P.op(G, lambda e: e.memset(a['St'][:, 0, :], 0.0), w=['rw_St0'])
            for n in ['rr', 'kr', 'vr']:
                P.op(G, lambda e, n=n: e.memset(a[n][:, 512:513], 0.0), w=['rw_' + n])
            for seg in self.segs:
                self.rwkv_seg(li, h, seg)

    def rwkv_seg(self, li, h, seg):
        P = self.P
        a = self.rw
        B = self.bank
        V, A, G, T = 'vector', 'scalar', 'gpsimd', 'tensor'
        t0 = seg * 512
        pv = a['pv']
        w0, a0, k_k, k_a, ln_w, ln_b, r_k = [pv[:, j, h:h + 1] for j in range(7)]
        ident = self.ident
        hs = slice(h * 64, (h + 1) * 64)

        def tt(eng, out, in0, in1, op, r, w):
            P.op(eng, lambda e: e.tensor_tensor(out=out, in0=in0, in1=in1, op=op), r=r, w=w)

        def stt(eng, out, in0, sc, in1, op0, op1, r, w):
            P.op(eng, lambda e: e.scalar_tensor_tensor(out=out, in0=in0, scalar=sc, in1=in1, op0=op0, op1=op1),
                 r=r, w=w)

        def act(out, in_, func, r, w, scale=None, bias=None):
            kw = {}
            if scale is not None:
                kw['scale'] = scale
            if bias is not None:
                kw['bias'] = bias
            P.op(A, lambda e: e.activation(out=out, in_=in_, func=func, **kw), r=r, w=w)

        for n, wn, mc, tmn in [('r', 'wr', h, 'Af'), ('k', 'wk', 16 + h, 'Rf'), ('v', 'wv', 32 + h, 'Kf')]:
            raw = a[n + 'r']
            rk = 'rw_' + n + 'r'
            ltmp, ltk = a[tmn], 'rw_' + tmn
            P.op(G, lambda e, raw=raw: e.tensor_copy(raw[:, 0:1], raw[:, 512:513]), r=[rk], w=[rk])
            ps, pk = self.proj(a[wn], 'rw_' + wn, 64, t0)
            P.op(A, lambda e, raw=raw, ps=ps: e.copy(raw[:, 1:513], ps), r=[pk], w=[rk])
            P.op(G, lambda e, raw=raw, mc=mc, ltmp=ltmp: e.tensor_scalar(out=ltmp[:], in0=raw[:, 0:512],
                                                                        scalar1=a['mu'][:, mc:mc + 1], scalar2=None,
                                                                        op0=ALU.mult), r=[rk, 'rw_mu'], w=[ltk])
            stt(V, a[n + 'p'][:], ps, a['omu'][:, mc:mc + 1], ltmp[:], ALU.mult, ALU.add,
                [pk, 'rw_omu', ltk], ['rw_' + n + 'p'])
        if self.stop == 'lerp':
            return
        bz = B[2]
        P.op(T, lambda e: e.matmul(bz[0:64, :], a['w2a2'][0:64, hs], a['lat'][0:64, t0:t0 + 512], start=True,
                                   stop=True), r=['rw_w2a2', 'rw_lat'], w=['bank2'])
        act(a['sg'][:], bz[0:64, :], AF.Sigmoid, ['bank2', 'rw_pv'], ['rw_sg'], bias=w0)
        P.op(T, lambda e: e.matmul(bz[0:64, :], a['w2a2'][64:128, hs], a['lat'][64:128, t0:t0 + 512], start=True,
                                   stop=True), r=['rw_w2a2', 'rw_lat'], w=['bank2'])
        act(a['ic'][:], bz[0:64, :], AF.Sigmoid, ['bank2', 'rw_pv'], ['rw_ic'], bias=a0)
        P.op(V, lambda e: e.tensor_scalar(out=a['kk'][:], in0=a['kp'][:], scalar1=k_k, scalar2=None, op0=ALU.mult),
             r=['rw_kp', 'rw_pv'], w=['rw_kk'])
        act(a['tmp'][:], a['kk'][:], AF.Square, ['rw_kk'], ['rw_tmp'])
        P.op(T, lambda e: e.matmul(bz[0:64, :], self.ones[0:64, 0:64], a['tmp'][:], start=True, stop=True),
             r=['ones', 'rw_tmp'], w=['bank2'])
        P.op(V, lambda e: e.tensor_scalar(out=a['tmp'][:], in0=bz[0:64, :], scalar1=1e-24, scalar2=None,
                                          op0=ALU.max), r=['bank2'], w=['rw_tmp'])
        act(a['tmp'][:], a['tmp'][:], AF.Sqrt, ['rw_tmp'], ['rw_tmp'])
        P.op(V, lambda e: e.reciprocal(a['tmp'][:], a['tmp'][:]), r=['rw_tmp'], w=['rw_tmp'])
        tt(G, a['kk'][:], a['kk'][:], a['tmp'][:], ALU.mult, ['rw_kk', 'rw_tmp'], ['rw_kk'])
        P.op(V, lambda e: e.tensor_scalar(out=a['k2'][:], in0=a['ic'][:], scalar1=-1.0, scalar2=k_a, op0=ALU.add,
                                          op1=ALU.mult), r=['rw_ic', 'rw_pv'], w=['rw_k2'])
        stt(V, a['k2'][:], a['k2'][:], 1.0, a['kp'][:], ALU.add, ALU.mult, ['rw_k2', 'rw_kp'], ['rw_k2'])
        tt(G, a['bb'][:], a['kk'][:], a['ic'][:], ALU.mult, ['rw_kk', 'rw_ic'], ['rw_bb'])
        if self.stop == 'kk':
            return
        P.op(V, lambda e: e.tensor_tensor_scan(out=a['csg'][:], data0=self.m01[0:64, :], data1=a['sg'][:],
                                               initial=0.0, op0=ALU.mult, op1=ALU.add),
             r=['m01', 'rw_sg'], w=['rw_csg'])
        act(a['e1'][:], a['csg'][:], AF.Exp, ['rw_csg'], ['rw_e1'], scale=-C0)
        act(a['e2'][:], a['csg'][:], AF.Exp, ['rw_csg'], ['rw_e2'], scale=C0)
        tt(G, a['e3'][:], a['csg'][:], a['sg'][:], ALU.subtract, ['rw_csg', 'rw_sg'], ['rw_e3'])
        act(a['e3'][:], a['e3'][:], AF.Exp, ['rw_e3'], ['rw_e3'], scale=-C0)
        c3 = a['csg'][:].rearrange("p (c t) -> p c t", t=64)
        tt(V, a['e4'][:].rearrange("p (c t) -> p c t", t=64), c3[:, :, 63:64].to_broadcast([64, 8, 64]), c3,
           ALU.subtract, ['rw_csg'], ['rw_e4'])
        act(a['e4'][:], a['e4'][:], AF.Exp, ['rw_e4'], ['rw_e4'], scale=-C0)
        stt(V, a['Af'][:], a['kk'][:], -1.0, a['e3'][:], ALU.mult, ALU.mult, ['rw_kk', 'rw_e3'], ['rw_Af'])
        tt(G, a['Rf'][:], a['rp'][:], a['e1'][:], ALU.mult, ['rw_rp', 'rw_e1'], ['rw_Rf'])
        tt(V, a['Kf'][:], a['k2'][:], a['e2'][:], ALU.mult, ['rw_k2', 'rw_e2'], ['rw_Kf'])
        tt(G, a['Bf'][:], a['bb'][:], a['e2'][:], ALU.mult, ['rw_bb', 'rw_e2'], ['rw_Bf'])
        tt(V, a['KHf'][:], a['k2'][:], a['e4'][:], ALU.mult, ['rw_k2', 'rw_e4'], ['rw_KHf'])
        tt(G, a['BHf'][:], a['bb'][:], a['e4'][:], ALU.mult, ['rw_bb', 'rw_e4'], ['rw_BHf'])
        stt(V, a['tmp'][:], a['rp'][:], r_k, a['k2'][:], ALU.mult, ALU.mult, ['rw_rp', 'rw_k2', 'rw_pv'], ['rw_tmp'])
        P.op(T, lambda e: e.matmul(bz[0:64, :], self.ones[0:64, 0:64], a['tmp'][:], start=True, stop=True),
             r=['ones', 'rw_tmp'], w=['bank2'])
        tt(V, a['bon'][:], bz[0:64, :], a['vp'][:], ALU.mult, ['bank2', 'rw_vp'], ['rw_bon'])
        if self.stop == 'prep':
            return
        def trn(e, srcs, bk):
            ins = None
            for q, src in enumerate(srcs):
                for j in range(4):
                    c = (q * 4 + j) * 64
                    ins = e.transpose(bk[:, c:c + 64], src[0:64, j * 128:(j + 1) * 128], ident[0:64, 0:64])
            return ins
        P.op(T, lambda e: trn(e, [a['Af'], a['vp']], B[3]), r=['rw_Af', 'rw_vp', 'ident'], w=['bank3'])
        if self.stop == 'trans1':
            return
        P.op(T, lambda e: trn(e, [a['KHf'], a['BHf']], B[4]), r=['rw_KHf', 'rw_BHf', 'ident'], w=['bank4'])
        P.op(V, lambda e: e.tensor_copy(a['Zt'][:, :, 0:64], B[3][:, 0:256].rearrange("p (j c) -> p j c", c=64)),
             r=['bank3'], w=['rw_Zt'])
        if self.stop == 'trans2':
            return
        P.op(V, lambda e: e.tensor_copy(a['Vt'][:], B[3][:, 256:512].rearrange("p (j c) -> p j c", c=64)),
             r=['bank3'], w=['rw_Vt'])
        if self.stop == 'trans3':
            return
        P.op(A, lambda e: e.copy(a['KBt'][:], B[4][:, :].rearrange("p (q j c) -> p q j c", q=2, c=64)),
             r=['bank4'], w=['rw_KBt'])
        if self.stop == 'trans':
            return
        Zt = a['Zt']
        mu_b = self.mm[:, 0:128].unsqueeze(1).to_broadcast([128, 4, 128])
        mui_b = self.mm[:, 128:256].unsqueeze(1).to_broadcast([128, 4, 128])
        ml_b = self.ml[:].unsqueeze(1).to_broadcast([128, 4, 128])
        b3v = lambda bk: bk[:, :].rearrange("p (j t) -> p j t", t=128)
        blk = lambda j: slice(j * 128, (j + 1) * 128)

        for (bi_, L, R) in [(5, 'Bf', 'Af'), (3, 'Bf', 'Rf'), (4, 'Kf', 'Af'), (6, 'Kf', 'Rf'), (7, 'Af', 'Bf')]:
            def sc(e, bi_=bi_, L=L, R=R):
                ins = None
                for j in range(4):
                    ins = e.matmul(B[bi_][:, blk(j)], a[L][:, blk(j)], a[R][:, blk(j)], start=True, stop=True)
                return ins
            P.op(T, sc, r=['rw_' + L, 'rw_' + R], w=[f'bank{bi_}'])
        tt(V, a['X0'], b3v(B[5]), mu_b, ALU.mult, ['bank5', 'mm'], ['rw_X0'])
        tt(V, a['RbT'], b3v(B[3]), mui_b, ALU.mult, ['bank3', 'mm'], ['rw_RbT'])
        tt(V, a['AkT'], b3v(B[4]), mu_b, ALU.mult, ['bank4', 'mm'], ['rw_AkT'])
        tt(V, a['RkT'], b3v(B[6]), mui_b, ALU.mult, ['bank6', 'mm'], ['rw_RkT'])
        tt(V, a['XT0'], b3v(B[7]), ml_b, ALU.mult, ['bank7', 'ml'], ['rw_XT0'])

        if self.stop == 'c1':
            return

        def akv(e):
            ins = None
            for j in range(4):
                ins = e.matmul(B[5][:, j * 128:j * 128 + 64], a['AkT'][:, j, :], a['Vt'][:, j, :], start=True, stop=True)
            return ins
        P.op(T, akv, r=['rw_AkT', 'rw_Vt'], w=['bank5'])
        P.op(V, lambda e: e.tensor_copy(Zt[:, :, 64:128], b3v(B[5])[:, :, 0:64]), r=['bank5'], w=['rw_Zt'])
        for lv in range(6):
            Xc, XTc = a[f'X{lv % 2}'], a[f'XT{lv % 2}']
            Xn, XTn = a[f'X{(lv + 1) % 2}'], a[f'XT{(lv + 1) % 2}']
            kc, ktc = f'rw_X{lv % 2}', f'rw_XT{lv % 2}'
            kn, ktn = f'rw_X{(lv + 1) % 2}', f'rw_XT{(lv + 1) % 2}'

            if lv < 5:
                def fx(e, Xc=Xc, XTc=XTc):
                    ins = None
                    for j in range(4):
                        ins = e.matmul(B[6][:, blk(j)], XTc[:, j, :], Xc[:, j, :], start=True, stop=True)
                    return ins

                def fxt(e, Xc=Xc, XTc=XTc):
                    ins = None
                    for j in range(4):
                        ins = e.matmul(B[7][:, blk(j)], Xc[:, j, :], XTc[:, j, :], start=True, stop=True)
                    return ins
                P.op(T, fx, r=[kc, ktc], w=['bank6'])
                P.op(T, fxt, r=[kc, ktc], w=['bank7'])
                P.op(A, lambda e, Xn=Xn: e.copy(Xn, b3v(B[6])), r=['bank6'], w=[kn])
                P.op(A, lambda e, XTn=XTn: e.copy(XTn, b3v(B[7])), r=['bank7'], w=[ktn])

            def fz(e, Xc=Xc):
                ins = None
                for j in range(4):
                    ins = e.matmul(B[5][:, blk(j)], Xc[:, j, :], Zt[:, j, :], start=True, stop=True)
                return ins
            P.op(T, fz, r=[kc, 'rw_Zt'], w=['bank5'])
            tt(V, Zt[:], Zt[:], b3v(B[5]), ALU.add, ['rw_Zt', 'bank5'], ['rw_Zt'])

        if self.stop == 'c2':
            return

        def fq(e):
            ins = None
            for j in range(4):
                ins = e.matmul(B[3][0:64, blk(j)], Zt[:, j, 0:64], a['RbT'][:, j, :], start=True, stop=True)
            return ins
        P.op(T, fq, r=['rw_Zt', 'rw_RbT'], w=['bank3'])
        tt(V, a['QT'], B[3][0:64, :], a['Rf'][:], ALU.add, ['bank3', 'rw_Rf'], ['rw_QT'])

        if self.stop == 'c2a':
            return

        def fg(e):
            ins = None
            for par, bk in [(0, B[4]), (1, B[7])]:
                rs_ = slice(par * 64, par * 64 + 64)
                for j in range(4):
                    ins = e.matmul(bk[0:64, j * 64:(j + 1) * 64], Zt[rs_, j, 0:64], a['KBt'][rs_, 1, j, :], start=True, stop=True)
            return ins
        P.op(T, fg, r=['rw_Zt', 'rw_KBt'], w=['bank4', 'bank7'])
        if self.stop == 'c2f':
            return
        for ci in range(8):
            ce = ci * 64 + 63
            bk = B[4] if ci % 2 == 0 else B[7]
            j = ci // 2
            stt(V, a['GT'][:, ci, :], ident[0:64, 0:64], a['e1'][:, ce:ce + 1], bk[0:64, j * 64:(j + 1) * 64], ALU.mult, ALU.add,
                ['ident', 'rw_e1', 'bank4' if ci % 2 == 0 else 'bank7'], ['rw_GT'])

        if self.stop == 'c2b':
            return

        def fh(e):
            ins = None
            for par, bk in [(0, B[6]), (1, B[5])]:
                rs_ = slice(par * 64, par * 64 + 64)
                for j in range(4):
                    e.matmul(bk[0:64, j * 64:(j + 1) * 64], a['KBt'][rs_, 0, j, :], a['Vt'][rs_, j, :], start=True, stop=False)
                    ins = e.matmul(bk[0:64, j * 64:(j + 1) * 64], a['KBt'][rs_, 1, j, :], Zt[rs_, j, 64:128], start=False, stop=True)
            return ins
        P.op(T, fh, r=['rw_KBt', 'rw_Vt', 'rw_Zt'], w=['bank6', 'bank5'])
        H4 = self.g[13][0:64, :].rearrange("p (j par n) -> p j par n", par=2, n=64)
        P.op(A, lambda e: e.copy(H4[:, :, 0, :], B[6][0:64, 0:256].rearrange("p (j n) -> p j n", n=64)), r=['bank6'], w=['rw_H'])
        P.op(A, lambda e: e.copy(H4[:, :, 1, :], B[5][0:64, 0:256].rearrange("p (j n) -> p j n", n=64)), r=['bank5'], w=['rw_H'])
        if self.stop == 'c3':
            return
        for ci in range(8):
            j, ccs = ci // 2, slice((ci % 2) * 64, (ci % 2) * 64 + 64)
            qcs = slice(ci * 64, (ci + 1) * 64)
            St, Sn = a['St'][:, ci % 2, :], a['St'][:, (ci + 1) % 2, :]
            kS, kSn = f'rw_St{ci % 2}', f'rw_St{(ci + 1) % 2}'
            P.op(T, lambda e, ci=ci, St=St: e.matmul(B[2][0:64, 0:64], a['GT'][:, ci, :], St, start=True, stop=True),
                 r=['rw_GT', kS], w=['bank2'])
            tt(V, Sn, B[2][0:64, 0:64], a['H'][:, ci, :], ALU.add, ['bank2', 'rw_H'], [kSn])

            def ym(e, j=j, ccs=ccs, qcs=qcs, ci=ci, St=St):
                e.matmul(B[3][0:64, ci * 64:(ci + 1) * 64], a['RkT'][:, j, ccs], a['Vt'][:, j, :], start=True, stop=False)
                e.matmul(B[3][0:64, ci * 64:(ci + 1) * 64], a['RbT'][:, j, ccs], Zt[:, j, 64:128], start=False, stop=False)
                return e.matmul(B[3][0:64, ci * 64:(ci + 1) * 64], a['QT'][:, qcs], St, start=False, stop=True)
            P.op(T, ym, r=['rw_RkT', 'rw_RbT', 'rw_Vt', 'rw_Zt', 'rw_QT', kS], w=['bank3'])
        P.op(V, lambda e: e.tensor_copy(a['Yt'], B[3][0:64, :].rearrange("p (c i) -> p c i", i=64)), r=['bank3'], w=['rw_Yt'])
        if self.stop == 'chunk':
            return
        Yt, Ysq, st = a['Yt'], a['Ysq'], a['st']
        P.op(V, lambda e: e.tensor_reduce(out=st[:, 0, :], in_=Yt[:], axis=AX.X, op=ALU.add), r=['rw_Yt'], w=['rw_st'])
        act(Ysq[:], Yt[:], AF.Square, ['rw_Yt'], ['rw_Ysq'])
        P.op(V, lambda e: e.tensor_reduce(out=st[:, 1, :], in_=Ysq[:], axis=AX.X, op=ALU.add), r=['rw_Ysq'], w=['rw_st'])
        P.op(V, lambda e: e.tensor_scalar(out=st[:, 0, :], in0=st[:, 0, :], scalar1=1.0 / 64, scalar2=None, op0=ALU.mult),
             r=['rw_st'], w=['rw_st'])
        tt(V, st[:, 2, :], st[:, 0, :], st[:, 0, :], ALU.mult, ['rw_st'], ['rw_st'])
        stt(V, st[:, 2, :], st[:, 1, :], 1.0 / 64, st[:, 2, :], ALU.mult, ALU.subtract, ['rw_st'], ['rw_st'])
        act(st[:, 2, :], st[:, 2, :], AF.Sqrt, ['rw_st', 'eps'], ['rw_st'], bias=self.eps6[0:64, 1:2])
        P.op(V, lambda e: e.reciprocal(st[:, 2, :], st[:, 2, :]), r=['rw_st'], w=['rw_st'])
        tt(V, Yt[:], Yt[:], st[:, 0, :].unsqueeze(2).to_broadcast([64, 8, 64]), ALU.subtract, ['rw_Yt', 'rw_st'], ['rw_Yt'])
        tt(V, Yt[:], Yt[:], st[:, 2, :].unsqueeze(2).to_broadcast([64, 8, 64]), ALU.mult, ['rw_Yt', 'rw_st'], ['rw_Yt'])

        def ytr(e):
            ins = None
            for ci in range(8):
                ins = e.transpose(B[3][0:64, ci * 64:(ci + 1) * 64], Yt[:, ci, :], ident[0:64, 0:64])
            return ins
        P.op(T, ytr, r=['rw_Yt', 'ident'], w=['bank3'])
        act(a['yf'][:], B[3][0:64, :], AF.Identity, ['bank3', 'rw_pv'], ['rw_yf'], scale=ln_w, bias=ln_b)
        tt(V, a['yf'][:], a['yf'][:], a['bon'][:], ALU.add, ['rw_yf', 'rw_bon'], ['rw_yf'])
        if self.debug and li == 0 and self.dbgsel == 'a_out':
            P.dma(self.dbg[h * 64:(h + 1) * 64, t0:t0 + 512], a['yf'][:], r=['rw_yf'])
        ps, pk = self.proj(a['wz'], 'rw_wz', 64, t0)
        act(a['zs'][:], ps, AF.Silu, [pk], ['rw_zs'])
        tt(V, a['yo'][:], a['yf'][:], a['zs'][:], ALU.mult, ['rw_yf', 'rw_zs'], ['rw_yo'])
        P.dma(self.ycT[h * 64:(h + 1) * 64, t0:t0 + 512], a['yo'][:], r=['rw_yo'], w=['ycT'])

    def mem_setup(self):
        P = self.P
        V, A, G, T = 'vector', 'scalar', 'gpsimd', 'tensor'
        self.memT = P.sb("memT", [128, 8, 256], BF16)
        self.mnw = P.sb("mnw", [128, 8])
        self.wkv = self.ybig[:, 4096:8192].rearrange("p (k m) -> p k m", m=512)
        P.alias['wkv'] = 'big'
        self.mst = P.sb("mst", [128, 8])
        P.dma(self.mnw[:], self.I['mem_norm_w'].rearrange("(kt p) -> p kt", p=128), w=['mnw'],
              allow_slow_non_contiguous=True)
        mt_ = [self.g[0], self.g[1]]
        for mt in range(2):
            m2 = self.hb[mt][:, 0:4, :].rearrange("p a b -> p (a b)")
            P.dma(m2, self.I['mem'][mt * 128:(mt + 1) * 128, :], w=[f'hb{mt}'])
            P.op(A, lambda e, m2=m2, mt=mt: e.activation(out=self.hb[mt][:, 4:8, :].rearrange("p a b -> p (a b)"), in_=m2,
                                                       func=AF.Square, accum_out=self.mst[:, mt:mt + 1]),
                 r=[f'hb{mt}'], w=[f'hb{mt}', 'mst'])
            P.op(A, lambda e, mt=mt: e.activation(out=self.mst[:, mt:mt + 1], in_=self.mst[:, mt:mt + 1], func=AF.Sqrt,
                                                  scale=1.0 / D, bias=self.eps6[:, 0:1]), r=['mst', 'eps'], w=['mst'])
            P.op(V, lambda e, mt=mt: e.reciprocal(self.mst[:, mt:mt + 1], self.mst[:, mt:mt + 1]), r=['mst'], w=['mst'])
            P.op(V, lambda e, m2=m2, mt=mt: e.tensor_scalar(out=m2, in0=m2, scalar1=self.mst[:, mt:mt + 1], scalar2=None,
                                                          op0=ALU.mult), r=[f'hb{mt}', 'mst'], w=[f'hb{mt}'])
            for half in range(2):
                bk = self.bank[3 + half]

                def fn(e, m2=m2, half=half, bk=bk):
                    ins = None
                    for q in range(4):
                        kt = half * 4 + q
                        ins = e.transpose(bk[:, q * 128:(q + 1) * 128], m2[:, kt * 128:(kt + 1) * 128], self.ident[:])
                    return ins
                P.op(T, fn, r=[f'hb{mt}', 'ident'], w=[f'bank{3 + half}'])
                for q in range(4):
                    kt = half * 4 + q
                    P.op(V, lambda e, kt=kt, q=q, bk=bk, mt=mt: e.tensor_scalar(
                        out=self.memT[:, kt, mt * 128:(mt + 1) * 128], in0=bk[:, q * 128:(q + 1) * 128],
                        scalar1=self.mnw[:, kt:kt + 1], scalar2=None, op0=ALU.mult),
                        r=[f'bank{3 + half}', 'mnw'], w=['memT'])

    def load_w_to(self, wap, c0, M, dst_ap, key):
        P = self.P
        i = self.wst_i
        self.wst_i ^= 1
        st = self.wst[i]
        P.dma(st[:, :, 0:M], wap[:, c0:c0 + M].rearrange("(kt p) m -> p kt m", p=128), w=[f'wst{i}'])
        P.op('gpsimd', lambda e: e.tensor_copy(dst_ap, st[:, :, 0:M]), r=[f'wst{i}'], w=[key])

    def mem_attn(self, layer, inw, qcol, zcol, ycrow):
        P = self.P
        V, A, G, T = 'vector', 'scalar', 'gpsimd', 'tensor'
        B = self.bank
        self.phase()
        self.kT = self.carve("kT", [64, 4, 256])
        self.vm = self.carve("vm", [128, 2, 256])
        self.wq = self.carve("wq", [128, 8, 64], BF16)
        self.wzm = self.carve("wzm", [128, 8, 64], BF16)
        self.mo = self.carve("mo", [64, 512], BF16)
        wkvd = self.I['mem_kv_w'][layer]
        for c in range(4):
            self.load_w_to(wkvd, c * 128, 128, self.wkv[:, :, c * 128:(c + 1) * 128], 'wkv')
        for h in range(4):
            def fk(e, h=h):
                ins = None
                for kt in range(8):
                    ins = e.matmul(B[2][0:64, 0:256], self.wkv[:, kt, h * 64:(h + 1) * 64], self.memT[:, kt, :],
                                   start=(kt == 0), stop=(kt == 7))
                return ins
            P.op(T, fk, r=['wkv', 'memT'], w=['bank2'])
            P.op(V, lambda e, h=h: e.tensor_copy(self.kT[:, h, :], B[2][0:64, 0:256]), r=['bank2'], w=['kT'])
        for mt in range(2):
            def fv(e, mt=mt):
                ins = None
                for kt in range(8):
                    ins = e.matmul(B[2][:, 0:256], self.memT[:, kt, mt * 128:(mt + 1) * 128], self.wkv[:, kt, 256:512],
                                   start=(kt == 0), stop=(kt == 7))
                return ins
            P.op(T, fv, r=['wkv', 'memT'], w=['bank2'])
            P.op(V, lambda e, mt=mt: e.tensor_copy(self.vm[:, mt, :], B[2][:, 0:256]), r=['bank2'], w=['vm'])
        qf, pr, prT, zs = self.g[0], self.g[1], self.g[2], self.g[3]
        for h in range(4):
            self.load_w(inw, qcol + h * 64, 64, self.wq, 'wq')
            self.load_w(inw, zcol + h * 64, 64, self.wzm, 'wzm')
            for blk in range(8):
                t0 = blk * 512
                ps, pk = self.proj(self.wq, 'wq', 64, t0)
                P.op(A, lambda e, ps=ps: e.copy(qf[0:64, :], ps), r=[pk], w=['g0'])
                for sb in range(4):
                    ts = slice(sb * 128, (sb + 1) * 128)
                    P.op(T, lambda e, ts=ts, h=h: e.matmul(B[3][:, 0:256], qf[0:64, ts], self.kT[:, h, :], start=True,
                                                           stop=True), r=['g0', 'kT'], w=['bank3'])
                    P.op(V, lambda e: e.tensor_reduce(out=self.mst[:, 2:3], in_=B[3][:, 0:256], axis=AX.X, op=ALU.max),
                         r=['bank3'], w=['mst'])
                    P.op(V, lambda e: e.tensor_scalar(out=self.mst[:, 2:3], in0=self.mst[:, 2:3], scalar1=-0.125,
                                                      scalar2=None, op0=ALU.mult), r=['mst'], w=['mst'])
                    P.op(A, lambda e: e.activation(out=pr[:, 0:256], in_=B[3][:, 0:256], func=AF.Exp, scale=0.125,
                                                   bias=self.mst[:, 2:3], accum_out=self.mst[:, 3:4]),
                         r=['bank3', 'mst'], w=['g1', 'mst'])
                    P.op(V, lambda e: e.reciprocal(self.mst[:, 3:4], self.mst[:, 3:4]), r=['mst'], w=['mst'])
                    P.op(V, lambda e: e.tensor_scalar(out=pr[:, 0:256], in0=pr[:, 0:256], scalar1=self.mst[:, 3:4],
                                                      scalar2=None, op0=ALU.mult), r=['g1', 'mst'], w=['g1'])

                    def ftr(e):
                        e.transpose(B[4][:, 0:128], pr[:, 0:128], self.ident[:])
                        return e.transpose(B[4][:, 128:256], pr[:, 128:256], self.ident[:])
                    P.op(T, ftr, r=['g1', 'ident'], w=['bank4'])
                    P.op(V, lambda e: e.tensor_copy(prT[:, 0:256], B[4][:, 0:256]), r=['bank4'], w=['g2'])

                    def fpv(e, ts=ts, h=h):
                        e.matmul(B[5][0:64, ts], self.vm[:, 0, h * 64:(h + 1) * 64], prT[:, 0:128], start=True, stop=False)
                        return e.matmul(B[5][0:64, ts], self.vm[:, 1, h * 64:(h + 1) * 64], prT[:, 128:256], start=False,
                                        stop=True)
                    P.op(T, fpv, r=['vm', 'g2'], w=['bank5'])
                ps, pk = self.proj(self.wzm, 'wzm', 64, t0)
                P.op(A, lambda e, ps=ps: e.activation(out=zs[0:64, :], in_=ps, func=AF.Silu), r=[pk], w=['g3'])
                if self.debug and self.dbgsel == 'm_out' and layer == 0:
                    P.op(V, lambda e: e.tensor_copy(qf[0:64, :], B[5][0:64, :]), r=['bank5'], w=['g0'])
                    P.dma(self.dbg[h * 64:(h + 1) * 64, t0:t0 + 512], qf[0:64, :], r=['g0'])
                P.op(V, lambda e: e.tensor_tensor(out=self.mo[:], in0=B[5][0:64, :], in1=zs[0:64, :], op=ALU.mult),
                     r=['bank5', 'g3'], w=['mo'])
                P.dma(self.ycT[ycrow + h * 64:ycrow + (h + 1) * 64, t0:t0 + 512], self.mo[:], r=['mo'], w=['ycT'])

    def out_proj(self, wout):
        P = self.P
        V, A, G, T = 'vector', 'scalar', 'gpsimd', 'tensor'
        B = self.bank
        self.phase()
        wo = self.carve("wo", [128, 18, 1024], BF16)
        yb = self.ybig[:, 0:18 * 256].rearrange("p (k t) -> p k t", t=256)
        for c in range(6):
            for q in range(8):
                st = self.wst[(c * 8 + q) % 2]
                sk = f'wst{(c * 8 + q) % 2}'
                P.dma(st[:, 0:3, :], wout[c * 384:(c + 1) * 384, q * 128:(q + 1) * 128].rearrange("(kt p) m -> p kt m", p=128),
                      w=[sk])
                P.op(G, lambda e, st=st, c=c, q=q: e.tensor_copy(wo[:, c * 3:(c + 1) * 3, q * 128:(q + 1) * 128], st[:, 0:3, :]),
                     r=[sk], w=['wo'])
        for blk in range(16):
            t0 = blk * 256
            P.dma(yb, self.ycT[:, t0:t0 + 256].rearrange("(kt p) t -> p kt t", p=128), r=['ycT'], w=['big'])
            for dt_ in range(8):
                i = self.pj_i
                self.pj_i ^= 1

                def fn(e, i=i, dt_=dt_):
                    ins = None
                    for kt in range(18):
                        ins = e.matmul(B[i][:, 0:256], wo[:, kt, dt_ * 128:(dt_ + 1) * 128], yb[:, kt, :], start=(kt == 0),
                                       stop=(kt == 17))
                    return ins
                P.op(T, fn, r=['wo', 'big'], w=[f'bank{i}'])
                gi = 8 + (dt_ % 4)
                hs = self.g[gi]
                hk = f'hT_{dt_}_{blk}'
                P.dma(hs[:, 0:256], self.hT[dt_ * 128:(dt_ + 1) * 128, t0:t0 + 256], r=[hk], w=[f'g{gi}'])
                P.op(V, lambda e, hs=hs, i=i: e.tensor_tensor(out=hs[:, 0:256], in0=hs[:, 0:256], in1=B[i][:, 0:256], op=ALU.add),
                     r=[f'g{gi}', f'bank{i}'], w=[f'g{gi}'])
                P.dma(self.hT[dt_ * 128:(dt_ + 1) * 128, t0:t0 + 256], hs[:, 0:256], r=[f'g{gi}'], w=[hk], q='gpsimd')

    def sin_rr(self, out, x, shape, tmp, tmpi, keys_r, key_out, key_tmp, key_tmpi, shift=0.0):
        P = self.P
        V, A = 'vector', 'scalar'
        TWO_PI = 2.0 * math.pi
        C1 = 6.28125
        C2 = TWO_PI - C1
        P.op(V, lambda e: e.tensor_scalar(out=out, in0=x, scalar1=shift, scalar2=None, op0=ALU.add), r=keys_r, w=[key_out])
        P.op(V, lambda e: e.tensor_scalar(out=tmp, in0=out, scalar1=1.0 / TWO_PI, scalar2=None, op0=ALU.mult),
             r=[key_out], w=[key_tmp])
        P.op(V, lambda e: e.tensor_copy(tmpi, tmp), r=[key_tmp], w=[key_tmpi])
        P.op(V, lambda e: e.tensor_copy(tmp, tmpi), r=[key_tmpi], w=[key_tmp])
        P.op(V, lambda e: e.scalar_tensor_tensor(out=out, in0=tmp, scalar=-C1, in1=out, op0=ALU.mult, op1=ALU.add),
             r=[key_tmp, key_out], w=[key_out])
        P.op(V, lambda e: e.scalar_tensor_tensor(out=out, in0=tmp, scalar=-C2, in1=out, op0=ALU.mult, op1=ALU.add),
             r=[key_tmp, key_out], w=[key_out])
        P.op(V, lambda e: e.tensor_scalar(out=tmp, in0=out, scalar1=math.pi, scalar2=None, op0=ALU.is_gt),
             r=[key_out], w=[key_tmp])
        P.op(V, lambda e: e.scalar_tensor_tensor(out=out, in0=tmp, scalar=-TWO_PI, in1=out, op0=ALU.mult, op1=ALU.add),
             r=[key_tmp, key_out], w=[key_out])
        P.op(V, lambda e: e.tensor_scalar(out=tmp, in0=out, scalar1=-math.pi, scalar2=None, op0=ALU.is_lt),
             r=[key_out], w=[key_tmp])
        P.op(V, lambda e: e.scalar_tensor_tensor(out=out, in0=tmp, scalar=TWO_PI, in1=out, op0=ALU.mult, op1=ALU.add),
             r=[key_tmp, key_out], w=[key_out])
        P.op(V, lambda e: e.tensor_scalar(out=out, in0=out, scalar1=-3.1415925, scalar2=3.1415925, op0=ALU.max,
                                          op1=ALU.min), r=[key_out], w=[key_out])
        P.op(A, lambda e: e.activation(out=out, in_=out, func=AF.Sin), r=[key_out], w=[key_out])

    def s5_alloc(self):
        P = self.P
        s = self.s5d = {}
        for n in ['LR', 'LI', 'LD', 'lrd', 'lid', 'mag', 'abr', 'abi', 'den', 'fr', 'fi', 'a128r', 'a128i', 'na128i',
                  'nabi', 't0', 't1', 't2']:
            s[n] = self.carve("s5_" + n, [128, 32])
        s['ti'] = self.carve("s5_ti", [128, 512], I32)
        v16 = lambda t: t[:].rearrange("p (j q) -> p j q", q=16)
        s['br'], s['bi'], s['bt'], s['bbr'], s['bbi'] = v16(self.g[8]), v16(self.g[9]), v16(self.g[10]), v16(self.g[22]), v16(self.g[23])
        s['cn'] = self.carve("s5_cn", [128, 2, 64])
        s['H'] = self.carve("s5_H", [128, 2, 4, 5])
        s['e4'] = self.carve("s5_e4", [128, 2, 4, 4])
        s['gg4'] = self.carve("s5_gg4", [128, 2, 4, 4])
        s['hm'] = self.carve("s5_hm", [128, 4, 4])
        s['hm2'] = self.carve("s5_hm2", [128, 2, 4, 4])
        s['x'] = [self.carve(f"s5_x{i}", [128, 512]) for i in range(6)]
        s['p127'] = self.carve("s5_p127", [128, 4, 4])
        s['dsk'] = self.carve("s5_dsk", [128, 8])
        s['glb'] = self.carve("s5_glb", [128, 8])
        s['wu'] = self.carve("s5_wu", [128, 8, 128], BF16)
        s['wg'] = self.carve("s5_wg", [128, 8, 128], BF16)
        s['yo'] = self.carve("s5_yo", [128, 512], BF16)
        if not hasattr(self, 's5yT'):
            self.s5yT = P.dram("s5yT", [1024, S], BF16).ap()

    def s5(self, li):
        P = self.P
        s = self.s5d
        I = self.I
        B = self.bank
        g = self.g
        V, A, G, T = 'vector', 'scalar', 'gpsimd', 'tensor'
        inw = I['ev_in_w'][li]

        def ts(eng, out, in0, s1, s2, op0, op1, r, w):
            if op1 is None:
                P.op(eng, lambda e: e.tensor_scalar(out=out, in0=in0, scalar1=s1, scalar2=None, op0=op0), r=r, w=w)
            else:
                P.op(eng, lambda e: e.tensor_scalar(out=out, in0=in0, scalar1=s1, scalar2=s2, op0=op0, op1=op1), r=r, w=w)

        def tt(eng, out, in0, in1, op, r, w):
            P.op(eng, lambda e: e.tensor_tensor(out=out, in0=in0, in1=in1, op=op), r=r, w=w)

        def stt(eng, out, in0, sc, in1, op0, op1, r, w):
            P.op(eng, lambda e: e.scalar_tensor_tensor(out=out, in0=in0, scalar=sc, in1=in1, op0=op0, op1=op1), r=r, w=w)

        def act(out, in_, func, r, w, **kw):
            P.op(A, lambda e: e.activation(out=out, in_=in_, func=func, **kw), r=r, w=w)
        K = 's5p'
        for kk_ in ['g8', 'g9', 'g10', 'g22', 'g23']:
            pass
        P.dma(s['LR'][:], I['s5_lambda_re'][li].rearrange("(j gl) n -> (gl n) j", gl=2), w=[K], allow_slow_non_contiguous=True)
        P.dma(s['LI'][:], I['s5_lambda_im'][li].rearrange("(j gl) n -> (gl n) j", gl=2), w=[K], allow_slow_non_contiguous=True)
        ld2 = I['s5_log_dt'][li].rearrange("(j gl) -> gl j", gl=2)
        for gl in range(2):
            P.dma(s['LD'][gl * 64:(gl + 1) * 64, :], ld2[gl:gl + 1, :].to_broadcast([64, 32]), w=[K],
                  allow_slow_non_contiguous=True)
        P.dma(s['br'][:], I['s5_b_re'][li].rearrange("(j gl) n q -> (gl n) j q", gl=2), w=[K, 'g8', 'g9', 'g10', 'g22', 'g23'], allow_slow_non_contiguous=True)
        P.dma(s['bi'][:], I['s5_b_im'][li].rearrange("(j gl) n q -> (gl n) j q", gl=2), w=[K, 'g8', 'g9', 'g10', 'g22', 'g23'], allow_slow_non_contiguous=True)
        P.dma(s['dsk'][:], I['s5_d'][li].rearrange("(c p) -> p c", p=128), w=[K], allow_slow_non_contiguous=True)
        P.dma(s['glb'][:], I['s5_glu_b'][li].rearrange("(c p) -> p c", p=128), w=[K], allow_slow_non_contiguous=True)
        act(s['LD'][:], s['LD'][:], AF.Exp, [K], [K])
        tt(V, s['lrd'][:], s['LR'][:], s['LD'][:], ALU.mult, [K], [K])
        tt(V, s['lid'][:], s['LI'][:], s['LD'][:], ALU.mult, [K], [K])
        act(s['mag'][:], s['lrd'][:], AF.Exp, [K], [K])
        ti32 = s['ti'][:, 0:32]
        self.sin_rr(s['abi'][:], s['lid'][:], None, s['t0'][:], ti32, [K], K, K, K)
        self.sin_rr(s['abr'][:], s['lid'][:], None, s['t0'][:], ti32, [K], K, K, K, shift=math.pi / 2)
        tt(V, s['abr'][:], s['abr'][:], s['mag'][:], ALU.mult, [K], [K])
        tt(V, s['abi'][:], s['abi'][:], s['mag'][:], ALU.mult, [K], [K])
        ts(V, s['nabi'][:], s['abi'][:], -1.0, None, ALU.mult, None, [K], [K])
        tt(V, s['den'][:], s['LR'][:], s['LR'][:], ALU.mult, [K], [K])
        tt(V, s['t1'][:], s['LI'][:], s['LI'][:], ALU.mult, [K], [K])
        tt(V, s['den'][:], s['den'][:], s['t1'][:], ALU.add, [K], [K])
        P.op(V, lambda e: e.reciprocal(s['den'][:], s['den'][:]), r=[K], w=[K])
        ts(V, s['t1'][:], s['abr'][:], -1.0, None, ALU.add, None, [K], [K])
        tt(V, s['fr'][:], s['t1'][:], s['LR'][:], ALU.mult, [K], [K])
        tt(V, s['t2'][:], s['abi'][:], s['LI'][:], ALU.mult, [K], [K])
        tt(V, s['fr'][:], s['fr'][:], s['t2'][:], ALU.add, [K], [K])
        tt(V, s['fr'][:], s['fr'][:], s['den'][:], ALU.mult, [K], [K])
        tt(V, s['fi'][:], s['abi'][:], s['LR'][:], ALU.mult, [K], [K])
        tt(V, s['t2'][:], s['t1'][:], s['LI'][:], ALU.mult, [K], [K])
        tt(V, s['fi'][:], s['fi'][:], s['t2'][:], ALU.subtract, [K], [K])
        tt(V, s['fi'][:], s['fi'][:], s['den'][:], ALU.mult, [K], [K])
        bc = lambda t: t[:].unsqueeze(2).to_broadcast([128, 32, 16])
        tt(V, s['bbr'][:], s['br'][:], bc(s['fr']), ALU.mult, [K], [K, 'g8', 'g9', 'g10', 'g22', 'g23'])
        tt(V, s['bt'][:], s['bi'][:], bc(s['fi']), ALU.mult, [K], [K, 'g8', 'g9', 'g10', 'g22', 'g23'])
        tt(V, s['bbr'][:], s['bbr'][:], s['bt'][:], ALU.subtract, [K], [K, 'g8', 'g9', 'g10', 'g22', 'g23'])
        tt(V, s['bbi'][:], s['bi'][:], bc(s['fr']), ALU.mult, [K], [K, 'g8', 'g9', 'g10', 'g22', 'g23'])
        tt(V, s['bt'][:], s['br'][:], bc(s['fi']), ALU.mult, [K], [K, 'g8', 'g9', 'g10', 'g22', 'g23'])
        tt(V, s['bbi'][:], s['bbi'][:], s['bt'][:], ALU.add, [K], [K, 'g8', 'g9', 'g10', 'g22', 'g23'])
        ts(V, s['t1'][:], s['lrd'][:], 128.0, None, ALU.mult, None, [K], [K])
        act(s['t1'][:], s['t1'][:], AF.Exp, [K], [K])
        ts(V, s['t2'][:], s['lid'][:], 128.0, None, ALU.mult, None, [K], [K])
        self.sin_rr(s['a128i'][:], s['t2'][:], None, s['t0'][:], ti32, [K], K, K, K)
        self.sin_rr(s['a128r'][:], s['t2'][:], None, s['t0'][:], ti32, [K], K, K, K, shift=math.pi / 2)
        tt(V, s['a128r'][:], s['a128r'][:], s['t1'][:], ALU.mult, [K], [K])
        tt(V, s['a128i'][:], s['a128i'][:], s['t1'][:], ALU.mult, [K], [K])
        ts(V, s['na128i'][:], s['a128i'][:], -1.0, None, ALU.mult, None, [K], [K])

        PiR, PiI, PoR, PoI, LBr, LBi, CLr, CLi = [g[i] for i in range(8)]
        gk = lambda i: f'g{i}'
        v4 = lambda t: t[:].rearrange("p (j t) -> p j t", t=128)
        iota = self.iota
        for sl in getattr(self, 's5_slices', range(8)):
            j0 = sl * 4
            if getattr(self, 'use_bar', False):
                P.barrier(self.barscr)
            self.load_w(inw, 3200 + sl * 128, 128, s['wu'], 's5_wu')
            for blk in range(8):
                ps, pk = self.proj(s['wu'], 's5_wu', 128, blk * 512)
                P.op(A, lambda e, ps=ps, blk=blk: e.copy(self.big[:, blk * 512:(blk + 1) * 512], ps), r=[pk], w=['big'])
            ang, Sn, Cs, mo, mi, tmp = [g[i] for i in range(8, 14)]
            iob = iota[:].unsqueeze(1).to_broadcast([128, 4, 128])
            tt(V, v4(ang), iob, s['lid'][:, j0:j0 + 4].unsqueeze(2).to_broadcast([128, 4, 128]), ALU.mult,
               ['iota', K], [gk(8)])
            self.sin_rr(Sn[:], ang[:], None, tmp[:], s['ti'][:], [gk(8)], gk(9), gk(13), K)
            self.sin_rr(Cs[:], ang[:], None, tmp[:], s['ti'][:], [gk(8)], gk(10), gk(13), K, shift=math.pi / 2)
            tt(V, v4(ang), iob, s['lrd'][:, j0:j0 + 4].unsqueeze(2).to_broadcast([128, 4, 128]), ALU.mult,
               ['iota', K], [gk(8)])
            act(mo[:], ang[:], AF.Exp, [gk(8)], [gk(11)])
            act(mi[:], ang[:], AF.Exp, [gk(8)], [gk(12)], scale=-1.0)
            tt(V, PoR[:], mo[:], Cs[:], ALU.mult, [gk(11), gk(10)], [gk(2)])
            tt(G, PoI[:], mo[:], Sn[:], ALU.mult, [gk(11), gk(9)], [gk(3)])
            tt(V, PiR[:], mi[:], Cs[:], ALU.mult, [gk(12), gk(10)], [gk(0)])
            stt(V, PiI[:], mi[:], -1.0, Sn[:], ALU.mult, ALU.mult, [gk(12), gk(9)], [gk(1)])
            P.op(V, lambda e: e.tensor_copy(s['p127'][:, :, 0], v4(PoR)[:, :, 127]), r=[gk(2)], w=['s5_p127'])
            P.op(V, lambda e: e.tensor_copy(s['p127'][:, :, 1], v4(PoI)[:, :, 127]), r=[gk(3)], w=['s5_p127'])
            ts(V, s['p127'][:, :, 2], s['p127'][:, :, 1], -1.0, None, ALU.mult, None, ['s5_p127'], ['s5_p127'])
            Xr, Xi = g[8], g[9]
            P.op(G, lambda e: e.memset(Xr[:], 0.0), w=[gk(8)])
            P.op(G, lambda e: e.memset(Xi[:], 0.0), w=[gk(9)])
            for jj in range(4):
                for gl in range(2):
                    off = (2 * jj + gl) * 16
                    ps_ = slice(gl * 64, (gl + 1) * 64)
                    P.op(V, lambda e, jj=jj, off=off, ps_=ps_, j0=j0: e.tensor_copy(v4(Xr)[ps_, jj, off:off + 16], s['bbr'][ps_, j0 + jj, :]),
                         r=[K, 'g22', 'g23'], w=[gk(8)])
                    P.op(V, lambda e, jj=jj, off=off, ps_=ps_, j0=j0: e.tensor_copy(v4(Xi)[ps_, jj, off:off + 16], s['bbi'][ps_, j0 + jj, :]),
                         r=[K, 'g22', 'g23'], w=[gk(9)])
            for X, LBx, kx, ko in [(Xr, LBr, gk(8), gk(4)), (Xi, LBi, gk(9), gk(5))]:
                def ftr(e, X=X):
                    ins = None
                    for jj in range(4):
                        ins = e.transpose(B[4][:, jj * 128:(jj + 1) * 128], v4(X)[:, jj, :], self.ident[:])
                    return ins
                P.op(T, ftr, r=[kx, 'ident'], w=['bank4'])
                P.op(V, lambda e, LBx=LBx: e.tensor_copy(LBx[:], B[4][:, :]), r=['bank4'], w=[ko])
            P.op(G, lambda e: e.memset(CLr[:], 0.0), w=[gk(6)])
            P.op(G, lambda e: e.memset(CLi[:], 0.0), w=[gk(7)])
            for ci, (cname, CLx, kc, sgn) in enumerate([('s5_c_re', CLr, gk(6), 1.0), ('s5_c_im', CLi, gk(7), -1.0)]):
                P.dma(s['cn'][:, ci, :], I[cname][li].rearrange("g p n -> (g p) n")[sl * 128:(sl + 1) * 128, :], w=['s5_cn'])
                P.op(T, lambda e, ci=ci: e.transpose(B[4][0:64, 0:128], s['cn'][:, ci, :], self.ident[:]),
                     r=['s5_cn', 'ident'], w=['bank4'])
                for jj in range(4):
                    c0 = 2 * jj * 16
                    ts(V, v4(CLx)[0:64, jj, c0:c0 + 16], B[4][0:64, c0:c0 + 16], sgn, None, ALU.mult, None, ['bank4'], [kc])
                    ts(V, v4(CLx)[64:128, jj, c0 + 16:c0 + 32], B[4][0:64, c0 + 16:c0 + 32], sgn, None, ALU.mult, None,
                       ['bank4'], [kc])
            P.op(G, lambda e: e.memset(s['H'][:], 0.0), w=['s5_H'])
            AR, AI = s['a128r'][:, j0:j0 + 4], s['a128i'][:, j0:j0 + 4]
            ABR, ABI = s['abr'][:, j0:j0 + 4], s['abi'][:, j0:j0 + 4]
            H = s['H']
            E = s['e4']
            GGt = s['gg4']
            m_ = s['hm']
            for blk in range(8):
                t0 = blk * 512
                ub = self.big[:, t0:t0 + 512]
                sets = [([g[16 + i] for i in range(6)], [gk(16 + i) for i in range(6)]),
                        (s['x'], [f's5_x{i}' for i in range(6)])]
                for jj in range(4):
                    (t1, t2, u1, u2, zr, zi), (k1, k2, ku1, ku2, kzr, kzi) = sets[jj % 2]
                    sr, si = g[8 + 2 * jj], g[9 + 2 * jj]
                    ksr, ksi = gk(8 + 2 * jj), gk(9 + 2 * jj)
                    bR, bI = (B[2], B[3]) if jj % 2 == 0 else (B[4], B[5])
                    kR, kI = ('bank2', 'bank3') if jj % 2 == 0 else ('bank4', 'bank5')
                    P.op(T, lambda e, jj=jj, ub=ub, bR=bR: e.matmul(bR[:, :], v4(LBr)[:, jj, :], ub, start=True, stop=True),
                         r=[gk(4), 'big'], w=[kR])
                    P.op(T, lambda e, jj=jj, ub=ub, bI=bI: e.matmul(bI[:, :], v4(LBi)[:, jj, :], ub, start=True, stop=True),
                         r=[gk(5), 'big'], w=[kI])
                    b2 = bR[:, :].rearrange("p (c t) -> p c t", t=128)
                    b3 = bI[:, :].rearrange("p (c t) -> p c t", t=128)
                    tb = lambda tab, jj=jj: v4(tab)[:, jj, :].unsqueeze(1).to_broadcast([128, 4, 128])
                    tt(V, v4(t1), b2, tb(PiR), ALU.mult, [kR, gk(0)], [k1])
                    tt(V, v4(t2), b3, tb(PiI), ALU.mult, [kI, gk(1)], [k2])
                    tt(G, zr[:], t1[:], t2[:], ALU.subtract, [k1, k2], [kzr])
                    tt(V, v4(u1), b2, tb(PiI), ALU.mult, [kR, gk(1)], [ku1])
                    tt(V, v4(u2), b3, tb(PiR), ALU.mult, [kI, gk(0)], [ku2])
                    tt(G, zi[:], u1[:], u2[:], ALU.add, [ku1, ku2], [kzi])
                    P.op(V, lambda e, zr=zr, sr=sr: e.tensor_tensor_scan(out=sr[:], data0=self.m128[:], data1=zr[:], initial=0.0,
                                                                       op0=ALU.mult, op1=ALU.add), r=['m128', kzr], w=[ksr])
                    P.op(V, lambda e, zi=zi, si=si: e.tensor_tensor_scan(out=si[:], data0=self.m128[:], data1=zi[:], initial=0.0,
                                                                       op0=ALU.mult, op1=ALU.add), r=['m128', kzi], w=[ksi])
                    pr_, pi_, npi_ = s['p127'][:, jj, 0:1], s['p127'][:, jj, 1:2], s['p127'][:, jj, 2:3]
                    ts(V, E[:, 0, jj, :], v4(sr)[:, :, 127], pr_, None, ALU.mult, None, [ksr, 's5_p127'], ['s5_e'])
                    stt(V, E[:, 0, jj, :], v4(si)[:, :, 127], npi_, E[:, 0, jj, :], ALU.mult, ALU.add, [ksi, 's5_p127', 's5_e'], ['s5_e'])
                    ts(V, E[:, 1, jj, :], v4(sr)[:, :, 127], pi_, None, ALU.mult, None, [ksr, 's5_p127'], ['s5_e'])
                    stt(V, E[:, 1, jj, :], v4(si)[:, :, 127], pr_, E[:, 1, jj, :], ALU.mult, ALU.add, [ksi, 's5_p127', 's5_e'], ['s5_e'])
                KH = ['s5_H', 's5_e', K, 's5_hm']
                P.op(V, lambda e: e.tensor_copy(H[:, :, :, 0], H[:, :, :, 4]), r=['s5_H'], w=['s5_H'])
                for c_ in range(4):
                    hr, hi = H[:, 0, :, c_], H[:, 1, :, c_]
                    hrn, hin = H[:, 0, :, c_ + 1], H[:, 1, :, c_ + 1]
                    tt(V, m_[:, 0, :], hr, AR, ALU.mult, KH, ['s5_hm'])
                    tt(V, m_[:, 1, :], hi, AI, ALU.mult, KH, ['s5_hm'])
                    tt(V, m_[:, 2, :], hi, AR, ALU.mult, KH, ['s5_hm'])
                    tt(V, m_[:, 3, :], hr, AI, ALU.mult, KH, ['s5_hm'])
                    tt(V, hrn, m_[:, 0, :], m_[:, 1, :], ALU.subtract, KH, ['s5_H'])
                    tt(V, hrn, hrn, E[:, 0, :, c_], ALU.add, KH, ['s5_H'])
                    tt(V, hin, m_[:, 2, :], m_[:, 3, :], ALU.add, KH, ['s5_H'])
                    tt(V, hin, hin, E[:, 1, :, c_], ALU.add, KH, ['s5_H'])
                bc4 = lambda t: t.unsqueeze(2).to_broadcast([128, 4, 4])
                KG = ['s5_H', K, 's5_gg', 's5_hm2']
                m2_ = s['hm2']
                tt(V, GGt[:, 0, :, :], H[:, 0, :, 0:4], bc4(ABR), ALU.mult, KG, ['s5_gg'])
                tt(V, m2_[:, 0, :, :], H[:, 1, :, 0:4], bc4(ABI), ALU.mult, KG, ['s5_hm2'])
                tt(V, GGt[:, 0, :, :], GGt[:, 0, :, :], m2_[:, 0, :, :], ALU.subtract, KG, ['s5_gg'])
                tt(V, GGt[:, 1, :, :], H[:, 1, :, 0:4], bc4(ABR), ALU.mult, KG, ['s5_gg'])
                tt(V, m2_[:, 1, :, :], H[:, 0, :, 0:4], bc4(ABI), ALU.mult, KG, ['s5_hm2'])
                tt(V, GGt[:, 1, :, :], GGt[:, 1, :, :], m2_[:, 1, :, :], ALU.add, KG, ['s5_gg'])
                for jj in range(4):
                    (t1, t2, u1, u2, zr, zi), (k1, k2, ku1, ku2, kzr, kzi) = sets[jj % 2]
                    sr, si = g[8 + 2 * jj], g[9 + 2 * jj]
                    ksr, ksi = gk(8 + 2 * jj), gk(9 + 2 * jj)
                    tb = lambda tab, jj=jj: v4(tab)[:, jj, :].unsqueeze(1).to_broadcast([128, 4, 128])
                    tt(G, v4(sr), v4(sr), GGt[:, 0, jj, :].unsqueeze(2).to_broadcast([128, 4, 128]), ALU.add, [ksr, 's5_gg'], [ksr])
                    tt(G, v4(si), v4(si), GGt[:, 1, jj, :].unsqueeze(2).to_broadcast([128, 4, 128]), ALU.add, [ksi, 's5_gg'], [ksi])
                    tt(V, v4(t1), v4(sr), tb(PoR), ALU.mult, [ksr, gk(2)], [k1])
                    tt(G, v4(t2), v4(si), tb(PoI), ALU.mult, [ksi, gk(3)], [k2])
                    tt(V, zr[:], t1[:], t2[:], ALU.subtract, [k1, k2], [kzr])
                    tt(V, v4(u1), v4(sr), tb(PoI), ALU.mult, [ksr, gk(3)], [ku1])
                    tt(G, v4(u2), v4(si), tb(PoR), ALU.mult, [ksi, gk(2)], [ku2])
                    tt(V, zi[:], u1[:], u2[:], ALU.add, [ku1, ku2], [kzi])

                    def fy(e, jj=jj, zr=zr, zi=zi):
                        e.matmul(B[6][:, :], v4(CLr)[:, jj, :], zr[:], start=(jj == 0), stop=False)
                        return e.matmul(B[6][:, :], v4(CLi)[:, jj, :], zi[:], start=False, stop=(jj == 3))
                    P.op(T, fy, r=[gk(6), gk(7), kzr, kzi], w=['bank6'])
                y, y2 = g[16], g[17]
                stt(V, y[:], ub, s['dsk'][:, sl:sl + 1], B[6][:, :], ALU.mult, ALU.add, ['big', K, 'bank6'], [gk(16)])
                if self.debug and self.dbgsel == 's5y' and li == 0:
                    P.dma(self.dbg[sl * 128:(sl + 1) * 128, t0:t0 + 512], y[:], r=[gk(16)])
                tt(G, y2[:], y[:], y[:], ALU.mult, [gk(16)], [gk(17)])
                ts(V, y2[:], y2[:], 0.044715, 1.0, ALU.mult, ALU.add, [gk(17)], [gk(17)])
                tt(G, y2[:], y2[:], y[:], ALU.mult, [gk(17), gk(16)], [gk(17)])
                act(y2[:], y2[:], AF.Sigmoid, [gk(17)], [gk(17)], scale=2.0 * math.sqrt(2.0 / math.pi))
                tt(V, s['yo'][:], y[:], y2[:], ALU.mult, [gk(16), gk(17)], ['s5_yo'])
                P.dma(self.s5yT[sl * 128:(sl + 1) * 128, t0:t0 + 512], s['yo'][:], r=['s5_yo'], w=['s5yT'])
        yb = self.ybig[:, 0:4096].rearrange("p (k t) -> p k t", t=512)
        for f in range(8):
            self.load_w(I['s5_glu_w'][li], f * 128, 128, s['wg'], 's5_wg')
            self.load_w(inw, 5504 + f * 128, 128, s['wu'], 's5_wu')
            for blk in range(8):
                t0 = blk * 512
                P.dma(yb, self.s5yT[:, t0:t0 + 512].rearrange("(kt p) t -> p kt t", p=128), r=['s5yT'], w=['big'])

                def fg(e):
                    ins = None
                    for kt in range(8):
                        ins = e.matmul(B[2][:, :], s['wg'][:, kt, :], yb[:, kt, :], start=(kt == 0), stop=(kt == 7))
                    return ins
                P.op(T, fg, r=['s5_wg', 'big'], w=['bank2'])
                sg, zs = g[14], g[15]
                act(sg[:], B[2][:, :], AF.Sigmoid, ['bank2', K], [gk(14)], bias=s['glb'][:, f:f + 1])
                tt(V, sg[:], sg[:], yb[:, f, :], ALU.mult, [gk(14), 'big'], [gk(14)])
                ps, pk = self.proj(s['wu'], 's5_wu', 128, t0)
                act(zs[:], ps, AF.Silu, [pk], [gk(15)])
                if self.debug and self.dbgsel == 'b_out' and li == 0:
                    P.dma(self.dbg[f * 128:(f + 1) * 128, t0:t0 + 512], sg[:], r=[gk(14)])
                tt(V, s['yo'][:], sg[:], zs[:], ALU.mult, [gk(14), gk(15)], ['s5_yo'])
                P.dma(self.ycT[1024 + f * 128:1024 + (f + 1) * 128, t0:t0 + 512], s['yo'][:], r=['s5_yo'], w=['ycT'])

    def final_phase(self):
        P = self.P
        V, A, G, T = 'vector', 'scalar', 'gpsimd', 'tensor'
        B = self.bank
        fw = P.sb("fnw", [128, 8])
        P.dma(fw[:], self.I['final_norm_w'].rearrange("(kt p) -> p kt", p=128), w=['fnw'], allow_slow_non_contiguous=True)
        sq = [self.g[0], self.g[1]]
        rinv = self.g[2]
        ot = [self.g[4 + i] for i in range(4)]
        NB = 256
        for blk in range(S // NB):
            i = blk % 2
            t0 = blk * NB
            hb = self.hb[i]
            P.dma(hb[:], self.hT[:, t0:t0 + NB].rearrange("(kt p) t -> p kt t", p=128), r=['hT'], w=[f'hb{i}'])
            for kt in range(8):
                j = kt % 2
                P.op(A, lambda e, kt=kt, j=j, hb=hb: e.activation(out=sq[j][:, 0:NB], in_=hb[:, kt, :], func=AF.Square),
                     r=[f'hb{i}'], w=[f'g{j}'])
                P.op(T, lambda e, kt=kt, j=j: e.matmul(B[2][:, 0:NB], self.ones[:], sq[j][:, 0:NB], start=(kt == 0),
                                                       stop=(kt == 7)), r=[f'g{j}', 'ones'], w=['bank2'])
            P.op(A, lambda e: e.activation(out=rinv[:, 0:NB], in_=B[2][:, 0:NB], func=AF.Sqrt, scale=1.0 / D,
                                           bias=self.eps6[:, 0:1]), r=['bank2', 'eps'], w=['g2'])
            P.op(V, lambda e: e.reciprocal(rinv[:, 0:NB], rinv[:, 0:NB]), r=['g2'], w=['g2'])
            for kt in range(8):
                P.op(V, lambda e, kt=kt, hb=hb: e.scalar_tensor_tensor(
                    out=hb[:, kt, :], in0=hb[:, kt, :], scalar=fw[:, kt:kt + 1], in1=rinv[:, 0:NB], op0=ALU.mult,
                    op1=ALU.mult), r=[f'hb{i}', 'g2', 'fnw'], w=[f'hb{i}'])
            for sub in range(2):
                oi = (blk * 2 + sub) % 2
                for half in range(2):
                    bk = B[3 + half]

                    def fn(e, hb=hb, sub=sub, half=half, bk=bk):
                        ins = None
                        for q in range(4):
                            kt = half * 4 + q
                            ins = e.transpose(bk[:, q * 128:(q + 1) * 128], hb[:, kt, sub * 128:(sub + 1) * 128], self.ident[:])
                        return ins
                    P.op(T, fn, r=[f'hb{i}', 'ident'], w=[f'bank{3 + half}'])
                    o = ot[oi * 2 + half]
                    if half == 0:
                        P.op(V, lambda e, o=o, bk=bk: e.tensor_copy(o[:], bk[:, :]), r=['bank3'], w=[f'g{4 + oi * 2 + half}'])
                    else:
                        P.op(A, lambda e, o=o, bk=bk: e.copy(o[:], bk[:, :]), r=['bank4'], w=[f'g{4 + oi * 2 + half}'])
                    r0 = t0 + sub * 128
                    P.dma(self.out[r0:r0 + 128, half * 512:(half + 1) * 512], o[:], r=[f'g{4 + oi * 2 + half}'], q='gpsimd')

    def even_layer(self, layer):
        li = layer // 2
        self.norm_phase(layer)
        self.phase()
        self.rwkv_alloc()
        self.rwkv(li)
        self.phase()
        self.s5_alloc()
        self.s5(li)
        self.phase()
        self.mem_attn(layer, self.I['ev_in_w'][li], 4224, 6528, 2048)
        self.out_proj(self.I['ev_out_w'][li])
        self.phase()


    def ssd(self, li):
        P = self.P
        I = self.I
        B = self.bank
        g = self.g
        V, A, G, T = 'vector', 'scalar', 'gpsimd', 'tensor'
        inw = I['od_in_w'][li]
        gk = lambda i: f'g{i}'

        def ts(eng, out, in0, s1, s2, op0, op1, r, w):
            if op1 is None:
                P.op(eng, lambda e: e.tensor_scalar(out=out, in0=in0, scalar1=s1, scalar2=None, op0=op0), r=r, w=w)
            else:
                P.op(eng, lambda e: e.tensor_scalar(out=out, in0=in0, scalar1=s1, scalar2=s2, op0=op0, op1=op1), r=r, w=w)

        def tt(eng, out, in0, in1, op, r, w):
            P.op(eng, lambda e: e.tensor_tensor(out=out, in0=in0, in1=in1, op=op), r=r, w=w)

        def stt(eng, out, in0, sc, in1, op0, op1, r, w):
            P.op(eng, lambda e: e.scalar_tensor_tensor(out=out, in0=in0, scalar=sc, in1=in1, op0=op0, op1=op1), r=r, w=w)

        def act(out, in_, func, r, w, **kw):
            P.op(A, lambda e: e.activation(out=out, in_=in_, func=func, **kw), r=r, w=w)
        c = self.carve
        raw = [c("sd_raw0", [128, 515]), c("sd_raw1", [128, 515])]
        carry = c("sd_carry", [128, 12, 3])
        cw = c("sd_cw", [128, 12, 4])
        cb = c("sd_cb", [128, 12])
        state = c("sd_state", [128, 2, 512])
        stack = c("sd_stack", [16, 3, 512])
        dtmp = c("sd_dtmp", [16, 512])
        prm = c("sd_prm", [16, 4])
        dg = c("sd_dg", [16, 16])
        sel = c("sd_sel", [16, 16, 128])
        nwb = c("sd_nwb", [128, 1024])
        dskb = c("sd_dskb", [128, 16])
        smT = c("sd_smT", [128, 4, 16])
        dec = c("sd_dec", [128, 16])
        M1 = [c("sd_M10", [128, 128]), c("sd_M11", [128, 128])]
        Lt = [c("sd_L0", [128, 128]), c("sd_L1", [128, 128])]
        CBm = c("sd_CBm", [128, 128])
        mui = c("sd_mui", [128, 128])
        ssq = c("sd_ssq", [128, 4])
        yoT = c("sd_yoT", [128, 4, 128], BF16)
        wsl = c("sd_wsl", [128, 8, 128], BF16)
        L4b = c("sd_L4b", [128, 512])
        wzc = self.ybig[:, 0:8192].rearrange("p (k m) -> p k m", m=1024)
        KP = 'sdp'
        for k_ in range(4):
            P.dma(cw[:, :, k_], I['m2_conv_w'][li][k_].rearrange("(c p) -> p c", p=128), w=[KP], allow_slow_non_contiguous=True)
        P.dma(cb, I['m2_conv_b'][li].rearrange("(c p) -> p c", p=128), w=[KP], allow_slow_non_contiguous=True)
        P.dma(prm[:, 0:1], I['m2_dt_bias'][li].rearrange("(p o) -> p o", o=1), w=[KP], allow_slow_non_contiguous=True)
        P.dma(prm[:, 1:2], I['m2_a_log'][li].rearrange("(p o) -> p o", o=1), w=[KP], allow_slow_non_contiguous=True)
        act(prm[:, 1:2], prm[:, 1:2], AF.Exp, [KP], [KP])
        ts(V, prm[:, 1:2], prm[:, 1:2], -1.0, None, ALU.mult, None, [KP], [KP])
        P.dma(dskb, I['m2_d'][li].rearrange("(o h) -> o h", o=1).to_broadcast([128, 16]), w=[KP], allow_slow_non_contiguous=True)
        P.dma(nwb, I['m2_norm_w'][li].rearrange("(o h) -> o h", o=1).to_broadcast([128, 1024]), w=[KP],
              allow_slow_non_contiguous=True)
        P.dma(sel, I['sel16'].rearrange("h (k s) -> h k s", s=128), w=[KP])
        P.dma(mui, I['mui128'], w=[KP])
        P.op(G, lambda e: e.memset(carry, 0.0), w=['sd_carry'])
        P.op(G, lambda e: e.memset(state, 0.0), w=['sd_state'])
        for q in range(8):
            self.load_w_to(inw, 2512 + q * 128, 128, wzc[:, :, q * 128:(q + 1) * 128], 'big')
        dstg = [g[i] for i in range(12)]
        for blk in range(8):
            t0 = blk * 512
            for sl in range(12):
                self.load_w(inw, sl * 128, 128, wsl, 'sd_wsl')
                rw_ = raw[sl % 2]
                rk = f'sd_raw{sl % 2}'
                ps, pk = self.proj(wsl, 'sd_wsl', 128, t0)
                P.op(G, lambda e, rw_=rw_, sl=sl: e.tensor_copy(rw_[:, 0:3], carry[:, sl, :]), r=['sd_carry'], w=[rk])
                P.op(A, lambda e, rw_=rw_, ps=ps: e.copy(rw_[:, 3:515], ps), r=[pk], w=[rk])
                P.op(G, lambda e, rw_=rw_, sl=sl: e.tensor_copy(carry[:, sl, :], rw_[:, 512:515]), r=[rk], w=['sd_carry'])
                d_ = dstg[sl]
                dk = gk(sl)
                ts(V, d_[:], rw_[:, 0:512], cw[:, sl, 0:1], cb[:, sl:sl + 1], ALU.mult, ALU.add, [rk, KP], [dk])
                for k_ in range(1, 4):
                    stt(V, d_[:], rw_[:, k_:k_ + 512], cw[:, sl, k_:k_ + 1], d_[:], ALU.mult, ALU.add,
                        [rk, KP, dk], [dk])
                act(d_[:], d_[:], AF.Silu, [dk], [dk])
            self.load_w(inw, 1536, 16, wsl, 'sd_wsl')
            ps, pk = self.proj(wsl, 'sd_wsl', 16, t0)
            ts(V, dtmp, ps, prm[:, 0:1], None, ALU.add, None, [pk, KP], ['sd_dtmp'])
            act(stack[:, 2, :], dtmp, AF.Abs, ['sd_dtmp'], ['sd_stack'])
            act(stack[:, 2, :], stack[:, 2, :], AF.Exp, ['sd_stack'], ['sd_stack'], scale=-1.0)
            act(stack[:, 2, :], stack[:, 2, :], AF.Ln, ['sd_stack', 'eps'], ['sd_stack'], bias=self.eps6[0:16, 3:4])
            stt(V, stack[:, 0, :], dtmp, 0.0, stack[:, 2, :], ALU.max, ALU.add, ['sd_dtmp', 'sd_stack'], ['sd_stack'])
            ts(V, dtmp, stack[:, 0, :], prm[:, 1:2], None, ALU.mult, None, ['sd_stack', KP], ['sd_dtmp'])
            P.op(V, lambda e: e.tensor_tensor_scan(out=stack[:, 1, :], data0=self.m128[0:16, :], data1=dtmp, initial=0.0,
                                                   op0=ALU.mult, op1=ALU.add), r=['m128', 'sd_dtmp'], w=['sd_stack'])
            ac3 = stack[:, 1, :].rearrange("p (c t) -> p c t", t=128)
            tt(V, stack[:, 2, :].rearrange("p (c t) -> p c t", t=128), ac3[:, :, 127:128].to_broadcast([16, 4, 128]), ac3,
               ALU.subtract, ['sd_stack'], ['sd_stack'])
            act(stack[:, 2, :], stack[:, 2, :], AF.Exp, ['sd_stack'], ['sd_stack'])
            for cc in range(4):
                tc0 = cc * 128
                cs = slice(tc0, tc0 + 128)
                tg = t0 + tc0
                zt = [g[19], g[20]]
                for half in range(2):
                    i = self.pj_i
                    self.pj_i ^= 1

                    def fz(e, i=i, half=half, tg=tg):
                        ins = None
                        for kt in range(8):
                            ins = e.matmul(B[i][:, :], self.xnT[:, kt, tg:tg + 128], wzc[:, kt, half * 512:(half + 1) * 512],
                                           start=(kt == 0), stop=(kt == 7))
                        return ins
                    P.op(T, fz, r=['big', 'xnT'], w=[f'bank{i}'])
                    act(zt[half][:], B[i][:, :], AF.Silu, [f'bank{i}'], [gk(19 + half)])
                xsT = [g[12], g[13]]
                for half in range(2):
                    def ftx(e, half=half, cs=cs):
                        ins = None
                        for q in range(4):
                            ins = e.transpose(B[2][:, q * 128:(q + 1) * 128], dstg[half * 4 + q][:, cs], self.ident[:])
                        return ins
                    P.op(T, ftx, r=[gk(half * 4 + q) for q in range(4)] + ['ident'], w=['bank2'])
                    P.op(V, lambda e, half=half: e.tensor_copy(xsT[half][:], B[2][:, :]), r=['bank2'], w=[gk(12 + half)])

                def ftb(e, cs=cs):
                    e.transpose(B[3][:, 0:128], dstg[8][:, cs], self.ident[:])
                    e.transpose(B[3][:, 128:256], dstg[9][:, cs], self.ident[:])
                    ins = None
                    for q in range(3):
                        ins = e.transpose(B[3][:, 256 + q * 16:256 + (q + 1) * 16], stack[:, q, cs], self.ident[0:16, 0:16])
                    return ins
                P.op(T, ftb, r=[gk(8), gk(9), 'sd_stack', 'ident'], w=['bank3'])
                Bt = g[18]
                P.op(V, lambda e: e.tensor_copy(Bt[:, 0:256], B[3][:, 0:256]), r=['bank3'], w=[gk(18)])
                P.op(V, lambda e: e.tensor_copy(smT[:, 0:3, :], B[3][:, 256:304].rearrange("p (q h) -> p q h", h=16)),
                     r=['bank3'], w=['sd_smT'])
                act(smT[:, 3, :], smT[:, 1, :], AF.Exp, ['sd_smT'], ['sd_smT'])
                xdt = [g[14], g[15]]
                xdd = [g[16], g[17]]
                v8 = lambda t: t[:].rearrange("p (h q) -> p h q", q=64)
                for half in range(2):
                    hs_ = slice(half * 8, (half + 1) * 8)
                    tt(V, v8(xdt[half]), v8(xsT[half]), smT[:, 0, hs_].unsqueeze(2).to_broadcast([128, 8, 64]), ALU.mult,
                       [gk(12 + half), 'sd_smT'], [gk(14 + half)])
                    tt(G, v8(xdd[half]), v8(xdt[half]), smT[:, 2, hs_].unsqueeze(2).to_broadcast([128, 8, 64]), ALU.mult,
                       [gk(14 + half), 'sd_smT'], [gk(16 + half)])
                ce = tc0 + 127
                ts(V, dg, self.ident[0:16, 0:16], stack[:, 1, ce:ce + 1], None, ALU.mult, None, ['ident', 'sd_stack'], ['sd_dg'])
                P.op(T, lambda e: e.matmul(B[4][:, 0:16], self.ones[0:16, :], dg, start=True, stop=True),
                     r=['ones', 'sd_dg'], w=['bank4'])
                act(dec, B[4][:, 0:16], AF.Exp, ['bank4'], ['sd_dec'])
                for gq in range(2):
                    BTf, CTf = dstg[8 + gq], dstg[10 + gq]
                    P.op(T, lambda e, BTf=BTf, CTf=CTf, cs=cs: e.matmul(B[4][:, 128:256], BTf[:, cs], CTf[:, cs], start=True,
                                                                        stop=True), r=[gk(8 + gq), gk(10 + gq)], w=['bank4'])
                    tt(V, CBm, B[4][:, 128:256], mui, ALU.mult, ['bank4', KP], ['sd_CBm'])
                    P.op(T, lambda e, CTf=CTf, cs=cs, gq=gq: e.matmul(B[5][:, :], CTf[:, cs], state[:, gq, :], start=True,
                                                                      stop=True), r=[gk(10 + gq), 'sd_state'], w=['bank5'])
                    v4h = lambda t: t[:].rearrange("p (h l) -> p h l", l=128)
                    for hb_ in range(2):
                        h0 = gq * 8 + hb_ * 4
                        L4, M4 = (g[23], g[22]) if hb_ == 0 else (L4b, g[22])
                        kL, kM = (gk(23), gk(22)) if hb_ == 0 else ('sd_L4b', gk(22))

                        def fsel(e, h0=h0, cs=cs):
                            ins = None
                            for q in range(4):
                                ins = e.matmul(B[7][:, q * 128:(q + 1) * 128], sel[:, h0 + q, :], stack[:, 1, cs], start=True, stop=True)
                            return ins
                        P.op(T, fsel, r=[KP, 'sd_stack'], w=['bank7'])
                        tt(V, v4h(L4), B[7][:, :].rearrange("p (h l) -> p h l", l=128),
                           smT[:, 1, h0:h0 + 4].unsqueeze(2).to_broadcast([128, 4, 128]), ALU.subtract, ['bank7', 'sd_smT'], [kL])
                        ts(V, L4[:], L4[:], 0.0, None, ALU.min, None, [kL], [kL])
                        act(L4[:], L4[:], AF.Exp, [kL], [kL])
                        tt(G, v4h(M4), v4h(L4), CBm.unsqueeze(1).to_broadcast([128, 4, 128]), ALU.mult, [kL, 'sd_CBm'], [kM])

                        def fyd(e, hb_=hb_, gq=gq, M4=M4):
                            ins = None
                            for q in range(4):
                                hl = hb_ * 4 + q
                                ins = e.matmul(B[6][:, hl * 64:(hl + 1) * 64], v4h(M4)[:, q, :], v8(xdt[gq])[:, hl, :], start=True,
                                               stop=True)
                            return ins
                        P.op(T, fyd, r=[kM, gk(14 + gq)], w=['bank6'])
                    yg, tmpy = g[21], g[22]
                    hs_ = slice(gq * 8, (gq + 1) * 8)
                    tt(V, v8(yg), B[5][:, :].rearrange("p (h q) -> p h q", q=64),
                       smT[:, 3, hs_].unsqueeze(2).to_broadcast([128, 8, 64]), ALU.mult, ['bank5', 'sd_smT'], [gk(21)])
                    tt(V, yg[:], yg[:], B[6][:, :], ALU.add, [gk(21), 'bank6'], [gk(21)])
                    tt(G, v8(tmpy), v8(xsT[gq]), dskb[:, hs_].unsqueeze(2).to_broadcast([128, 8, 64]), ALU.mult,
                       [gk(12 + gq), KP], [gk(22)])
                    tt(V, yg[:], yg[:], tmpy[:], ALU.add, [gk(21), gk(22)], [gk(21)])
                    tt(V, yg[:], yg[:], zt[gq][:], ALU.mult, [gk(21), gk(19 + gq)], [gk(21)])
                    P.op(T, lambda e, gq=gq: e.matmul(B[5][:, :], Bt[:, gq * 128:(gq + 1) * 128], xdd[gq][:], start=True,
                                                      stop=True), r=[gk(18), gk(16 + gq)], w=['bank5'])
                    stg = state[:, gq, :].rearrange("p (h q) -> p h q", q=64)
                    tt(V, stg, stg, dec[:, hs_].unsqueeze(2).to_broadcast([128, 8, 64]), ALU.mult, ['sd_state', 'sd_dec'],
                       ['sd_state'])
                    tt(V, state[:, gq, :], state[:, gq, :], B[5][:, :], ALU.add, ['sd_state', 'bank5'], ['sd_state'])
                    act(tmpy[:], yg[:], AF.Square, [gk(21)], [gk(22), 'sd_ssq'], accum_out=ssq[:, 0:1])
                    act(ssq[:, 1:2], ssq[:, 0:1], AF.Sqrt, ['sd_ssq', 'eps'], ['sd_ssq'], scale=1.0 / 512,
                        bias=self.eps6[:, 0:1])
                    P.op(V, lambda e: e.reciprocal(ssq[:, 1:2], ssq[:, 1:2]), r=['sd_ssq'], w=['sd_ssq'])
                    stt(V, yg[:], yg[:], ssq[:, 1:2], nwb[:, gq * 512:(gq + 1) * 512], ALU.mult, ALU.mult,
                        [gk(21), 'sd_ssq', KP], [gk(21)])

                    def fty(e):
                        ins = None
                        for q in range(4):
                            ins = e.transpose(B[2][:, q * 128:(q + 1) * 128], yg[:, q * 128:(q + 1) * 128], self.ident[:])
                        return ins
                    P.op(T, fty, r=[gk(21), 'ident'], w=['bank2'])
                    if self.debug and self.dbgsel == 'c_out' and li == 0:
                        P.op(V, lambda e: e.tensor_copy(tmpy[:], B[2][:, :]), r=['bank2'], w=[gk(22)])
                        P.dma(self.dbg[gq * 512:(gq + 1) * 512, tg:tg + 128].rearrange("(q p) t -> p q t", p=128),
                              tmpy[:].rearrange("p (q t) -> p q t", t=128), r=[gk(22)])
                    P.op(V, lambda e: e.tensor_copy(yoT, B[2][:, :].rearrange("p (q t) -> p q t", t=128)), r=['bank2'],
                         w=['sd_yoT'])
                    P.dma(self.ycT[gq * 512:(gq + 1) * 512, tg:tg + 128].rearrange("(q p) t -> p q t", p=128), yoT,
                          r=['sd_yoT'], w=['ycT'])

    def rope_setup(self):
        P = self.P
        V, A, G, T = 'vector', 'scalar', 'gpsimd', 'tensor'
        self.csT = P.dram("csT", [2, 64, S], F32).ap()
        self.phase()
        c = self.carve
        posi = c("rp_posi", [64, 512], I32)
        posf = c("rp_posf", [64, 512])
        ang = c("rp_ang", [64, 512])
        sn = c("rp_sn", [64, 512])
        tmp = c("rp_tmp", [64, 512])
        tmi = c("rp_tmi", [64, 512], I32)
        invf = c("rp_invf", [64, 1])
        P.dma(invf, self.I['invf64'], w=['rp_invf'])
        for blk in range(8):
            t0 = blk * 512
            P.dma(posi, self.I['positions'][0:1, t0:t0 + 512].to_broadcast([64, 512]), w=['rp_posi'],
                  allow_slow_non_contiguous=True)
            P.op(V, lambda e: e.tensor_copy(posf, posi), r=['rp_posi'], w=['rp_posf'])
            P.op(V, lambda e: e.tensor_scalar(out=ang, in0=posf, scalar1=invf[:, 0:1], scalar2=None, op0=ALU.mult),
                 r=['rp_posf', 'rp_invf'], w=['rp_ang'])
            for q, sh in [(0, math.pi / 2), (1, 0.0)]:
                self.sin_rr(sn, ang, None, tmp, tmi, ['rp_ang'], 'rp_sn', 'rp_tmp', 'rp_tmi', shift=sh)
                P.dma(self.csT[q, :, t0:t0 + 512], sn, r=['rp_sn'], w=['csT'])

    def mla(self, li):
        P = self.P
        I = self.I
        B = self.bank
        g = self.g
        V, A, G, T = 'vector', 'scalar', 'gpsimd', 'tensor'
        inw = I['od_in_w'][li]
        gk = lambda i: f'g{i}'
        SC = 192 ** -0.5

        def ts(eng, out, in0, s1, s2, op0, op1, r, w):
            if op1 is None:
                P.op(eng, lambda e: e.tensor_scalar(out=out, in0=in0, scalar1=s1, scalar2=None, op0=op0), r=r, w=w)
            else:
                P.op(eng, lambda e: e.tensor_scalar(out=out, in0=in0, scalar1=s1, scalar2=s2, op0=op0, op1=op1), r=r, w=w)

        def tt(eng, out, in0, in1, op, r, w):
            P.op(eng, lambda e: e.tensor_tensor(out=out, in0=in0, in1=in1, op=op), r=r, w=w)

        def stt(eng, out, in0, sc, in1, op0, op1, r, w):
            P.op(eng, lambda e: e.scalar_tensor_tensor(out=out, in0=in0, scalar=sc, in1=in1, op0=op0, op1=op1), r=r, w=w)

        def act(out, in_, func, r, w, **kw):
            P.op(A, lambda e: e.activation(out=out, in_=in_, func=func, **kw), r=r, w=w)
        if not hasattr(self, 'cqnT'):
            self.cqnT = P.dram("cqnT", [384, S], BF16).ap()
            self.ckvnT = P.dram("ckvnT", [256, S], BF16).ap()
        c = self.carve
        kpe = c("ml_kpe", [64, S], BF16)
        qpe = c("ml_qpe", [64, S], BF16)
        vv = c("ml_v", [128, 32, 128], BF16)
        qn = self.ybig[:, 0:4096]
        kn = self.ybig[:, 4096:8192]
        wsl = c("ml_wsl", [128, 8, 128], BF16)
        wkr = c("ml_wkr", [128, 8, 64], BF16)
        wkrot = c("ml_wkrot", [128, 8, 64], BF16)
        nrm = c("ml_nrm", [128, 5])
        cs_c, cs_s = g[8][0:64, :], g[9][0:64, :]
        P.alias['ml_cs'] = 'g8'
        cqb = c("ml_cqb", [128, 3, 512], BF16)
        ckb = c("ml_ckb", [128, 2, 512], BF16)
        wq_st = self.wst[0][:].rearrange("p a b -> p (a b)")[:, 0:576].rearrange("p (a b) -> p a b", b=192)
        wkv_st = self.wst[1][:].rearrange("p a b -> p (a b)")[:, 0:512].rearrange("p (a b) -> p a b", b=256)
        P.alias['ml_wqst'] = 'wst0'
        P.alias['ml_wkvst'] = 'wst1'
        wqn = c("ml_wqn", [128, 3, 128], BF16)
        wqp = c("ml_wqp", [128, 3, 64], BF16)
        wqrot = c("ml_wqrot", [128, 3, 64], BF16)
        wkn = c("ml_wkn", [128, 2, 128], BF16)
        wv = c("ml_wv", [128, 2, 128], BF16)
        identb = c("ml_identb", [128, 128], BF16)
        mneg = g[21][:, 0:128]
        mx2 = c("ml_mx", [128, 48])
        rs2 = c("ml_rs", [128, 48])
        PT = [g[16][:].bitcast(BF16)[:, 0:512].rearrange("p (j q) -> p j q", q=128),
              g[17][:].bitcast(BF16)[:, 0:512].rearrange("p (j q) -> p j q", q=128)]
        Pb = [g[14][:].bitcast(BF16)[:, 0:512], g[15][:].bitcast(BF16)[:, 0:512]]
        sd = g[18][:, 0:128]
        Osb = g[19][:, 0:128]
        yo = g[20][:].bitcast(BF16)[:, 0:512]
        for a_, b_ in [('ml_PT0', 'g16'), ('ml_PT1', 'g17'), ('ml_P0', 'g14'), ('ml_P1', 'g15'), ('ml_sd', 'g18'),
                       ('ml_O', 'g19'), ('ml_yo', 'g20')]:
            P.alias[a_] = b_
        KP = 'mlp'
        P.op(V, lambda e: e.tensor_copy(identb, self.ident[:]), r=['ident'], w=[KP])
        P.dma(mneg, I['mneg128'], w=[KP, 'g21'])
        P.dma(nrm[:, 0:3], I['mla_q_norm_w'][li].rearrange("(c p) -> p c", p=128), w=[KP], allow_slow_non_contiguous=True)
        P.dma(nrm[:, 3:5], I['mla_kv_norm_w'][li].rearrange("(c p) -> p c", p=128), w=[KP], allow_slow_non_contiguous=True)
        st = self.wst[0]
        P.dma(st[:, :, 0:64], inw[:, 2192:2256].rearrange("(kt p) m -> p kt m", p=128), w=['wst0'])
        P.op(G, lambda e: e.tensor_copy(wkr, st[:, :, 0:64]), r=['wst0'], w=[KP])
        P.op(V, lambda e: e.tensor_scalar(out=wkrot[:, :, 0:32], in0=st[:, :, 32:64], scalar1=-1.0, scalar2=None, op0=ALU.mult),
             r=['wst0'], w=[KP])
        P.op(G, lambda e: e.tensor_copy(wkrot[:, :, 32:64], st[:, :, 0:32]), r=['wst0'], w=[KP])

        def rope_pair(wa, wb, nk, rhs_fn, rkeys, dst, dkey):
            def f1(e):
                ins = None
                for kt in range(nk):
                    ins = e.matmul(B[2][0:64, :], wa[:, kt, :], rhs_fn(kt), start=(kt == 0), stop=(kt == nk - 1))
                return ins

            def f2(e):
                ins = None
                for kt in range(nk):
                    ins = e.matmul(B[3][0:64, :], wb[:, kt, :], rhs_fn(kt), start=(kt == 0), stop=(kt == nk - 1))
                return ins
            P.op(T, f1, r=rkeys + [KP], w=['bank2'])
            P.op(T, f2, r=rkeys + [KP], w=['bank3'])
            t1, t2 = g[10], g[11]
            tt(V, t1[0:64, :], B[2][0:64, :], cs_c, ALU.mult, ['bank2', 'g8'], [gk(10)])
            tt(V, t2[0:64, :], B[3][0:64, :], cs_s, ALU.mult, ['bank3', 'g9'], [gk(11)])
            tt(G, dst, t1[0:64, :], t2[0:64, :], ALU.add, [gk(10), gk(11)], [dkey])

        for blk in range(8):
            t0 = blk * 512
            P.dma(cs_c, self.csT[0, :, t0:t0 + 512], r=['csT'], w=['g8'])
            P.dma(cs_s, self.csT[1, :, t0:t0 + 512], r=['csT'], w=['g9'])
            for (c0, ns, nw0, dstT, den) in [(1552, 3, 0, self.cqnT, 384.0), (1936, 2, 3, self.ckvnT, 256.0)]:
                for s_ in range(ns):
                    self.load_w(inw, c0 + s_ * 128, 128, wsl, 'ml_wsl')
                    ps, pk = self.proj(wsl, 'ml_wsl', 128, t0)
                    P.op(A, lambda e, ps=ps, s_=s_: e.copy(g[s_][:], ps), r=[pk], w=[gk(s_)])
                    act(g[4 + (s_ % 2)][:], g[s_][:], AF.Square, [gk(s_)], [gk(4 + (s_ % 2))])
                    P.op(T, lambda e, s_=s_, ns=ns: e.matmul(B[2][:, :], self.ones[:], g[4 + (s_ % 2)][:], start=(s_ == 0),
                                                            stop=(s_ == ns - 1)), r=['ones', gk(4 + (s_ % 2))], w=['bank2'])
                act(g[6][:], B[2][:, :], AF.Sqrt, ['bank2', 'eps'], [gk(6)], scale=1.0 / den, bias=self.eps6[:, 0:1])
                P.op(V, lambda e: e.reciprocal(g[6][:], g[6][:]), r=[gk(6)], w=[gk(6)])
                ob = cqb if ns == 3 else ckb
                okey = 'ml_cqb' if ns == 3 else 'ml_ckb'
                for s_ in range(ns):
                    stt(V, ob[:, s_, :], g[s_][:], nrm[:, nw0 + s_:nw0 + s_ + 1], g[6][:], ALU.mult, ALU.mult,
                        [gk(s_), gk(6), KP], [okey])
                P.dma(dstT[:, t0:t0 + 512].rearrange("(kt p) t -> p kt t", p=128), ob, r=[okey], w=['cqnT' if ns == 3 else 'ckvnT'])
            rope_pair(wkr, wkrot, 8, lambda kt, t0=t0: self.xnT[:, kt, t0:t0 + 512], ['xnT'], kpe[:, t0:t0 + 512], 'ml_kpe')
        for h in range(8):
            P.dma(wq_st, I['mla_wq_up'][li][:, h * 192:(h + 1) * 192].rearrange("(kt p) m -> p kt m", p=128), w=['ml_wqst'])
            P.dma(wkv_st, I['mla_wkv_up'][li][:, h * 256:(h + 1) * 256].rearrange("(kt p) m -> p kt m", p=128), w=['ml_wkvst'])
            P.op(G, lambda e: e.tensor_copy(wqn, wq_st[:, :, 0:128]), r=['ml_wqst'], w=['ml_wh'])
            P.op(G, lambda e: e.tensor_copy(wqp, wq_st[:, :, 128:192]), r=['ml_wqst'], w=['ml_wh'])
            P.op(V, lambda e: e.tensor_scalar(out=wqrot[:, :, 0:32], in0=wq_st[:, :, 160:192], scalar1=-1.0, scalar2=None,
                                              op0=ALU.mult), r=['ml_wqst'], w=['ml_wh'])
            P.op(G, lambda e: e.tensor_copy(wqrot[:, :, 32:64], wq_st[:, :, 128:160]), r=['ml_wqst'], w=['ml_wh'])
            P.op(G, lambda e: e.tensor_copy(wkn, wkv_st[:, :, 0:128]), r=['ml_wkvst'], w=['ml_wh'])
            P.op(G, lambda e: e.tensor_copy(wv, wkv_st[:, :, 128:256]), r=['ml_wkvst'], w=['ml_wh'])
            self.load_w(inw, 3536 + h * 128, 128, wsl, 'ml_wsl')
            for blk in range(8):
                t0 = blk * 512
                P.dma(cs_c, self.csT[0, :, t0:t0 + 512], r=['csT'], w=['g8'])
                P.dma(cs_s, self.csT[1, :, t0:t0 + 512], r=['csT'], w=['g9'])
                P.dma(cqb, self.cqnT[:, t0:t0 + 512].rearrange("(kt p) t -> p kt t", p=128), r=['cqnT'], w=['ml_cqb'])
                P.dma(ckb, self.ckvnT[:, t0:t0 + 512].rearrange("(kt p) t -> p kt t", p=128), r=['ckvnT'], w=['ml_ckb'])

                def fq(e):
                    ins = None
                    for kt in range(3):
                        ins = e.matmul(B[4][:, :], wqn[:, kt, :], cqb[:, kt, :], start=(kt == 0), stop=(kt == 2))
                    return ins
                P.op(T, fq, r=['ml_wh', 'ml_cqb'], w=['bank4'])
                P.op(A, lambda e, t0=t0: e.copy(qn[:, t0:t0 + 512], B[4][:, :]), r=['bank4'], w=['big'])

                def fk(e):
                    ins = None
                    for kt in range(2):
                        ins = e.matmul(B[5][:, :], wkn[:, kt, :], ckb[:, kt, :], start=(kt == 0), stop=(kt == 1))
                    return ins
                P.op(T, fk, r=['ml_wh', 'ml_ckb'], w=['bank5'])
                P.op(A, lambda e, t0=t0: e.copy(kn[:, t0:t0 + 512], B[5][:, :]), r=['bank5'], w=['big'])
                rope_pair(wqp, wqrot, 3, lambda kt: cqb[:, kt, :], ['ml_cqb', 'ml_wh'], qpe[:, t0:t0 + 512], 'ml_qpe')
                for sb in range(4):
                    def fv(e, sb=sb):
                        ins = None
                        for kt in range(2):
                            ins = e.matmul(B[6][:, sb * 128:(sb + 1) * 128], ckb[:, kt, sb * 128:(sb + 1) * 128], wv[:, kt, :],
                                           start=(kt == 0), stop=(kt == 1))
                        return ins
                    P.op(T, fv, r=['ml_wh', 'ml_ckb'], w=['bank6'])
                P.op(V, lambda e, blk=blk: e.tensor_copy(vv[:, blk * 4:(blk + 1) * 4, :],
                                                        B[6][:, :].rearrange("p (s d) -> p s d", d=128)), r=['bank6'], w=['ml_v'])
            for qb in range(32):
                qs = slice(qb * 128, (qb + 1) * 128)
                nkb = qb + 1
                mx = mx2[:, (qb % 2) * 24:(qb % 2) * 24 + 24]
                rs = rs2[:, (qb % 2) * 24:(qb % 2) * 24 + 24]
                kmx, krs = f'ml_mx{qb % 2}', f'ml_rs{qb % 2}'
                NG = (nkb + 3) // 4

                def scores(bi, gi, qs=qs, nkb=nkb):
                    ncols = min(512, nkb * 128 - gi * 512)
                    k0 = gi * 512

                    def f(e):
                        e.matmul(B[bi][:, 0:ncols], qn[:, qs], kn[:, k0:k0 + ncols], start=True, stop=False)
                        return e.matmul(B[bi][:, 0:ncols], qpe[:, qs], kpe[:, k0:k0 + ncols], start=False, stop=True)
                    P.op(T, f, r=['big', 'ml_qpe', 'ml_kpe'], w=[f'bank{bi}'])
                    return ncols
                nm = 0
                for gi in range(NG):
                    bi = 2 + (gi % 2)
                    ncols = scores(bi, gi)
                    last = (gi == NG - 1)
                    nfull = ncols - 128 if last else ncols
                    if nfull > 0:
                        P.op(V, lambda e, bi=bi, nfull=nfull, nm=nm, mx=mx: e.tensor_reduce(out=mx[:, nm:nm + 1], in_=B[bi][:, 0:nfull],
                                                                                   axis=AX.X, op=ALU.max),
                             r=[f'bank{bi}'], w=[kmx])
                        nm += 1
                    if last:
                        tt(V, sd, B[bi][:, ncols - 128:ncols], mneg, ALU.add, [f'bank{bi}', KP, 'g21'], ['ml_sd'])
                        P.op(V, lambda e, nm=nm, mx=mx: e.tensor_reduce(out=mx[:, nm:nm + 1], in_=sd, axis=AX.X, op=ALU.max),
                             r=['ml_sd'], w=[kmx])
                        nm += 1
                P.op(V, lambda e, nm=nm, mx=mx: e.tensor_reduce(out=mx[:, 23:24], in_=mx[:, 0:nm], axis=AX.X, op=ALU.max),
                     r=[kmx], w=[kmx])
                ts(V, mx[:, 22:23], mx[:, 23:24], -SC, None, ALU.mult, None, [kmx], [kmx])
                nr = 0
                kbi = 0
                for gi in range(NG):
                    bi = 2 + (gi % 2)
                    pi = gi % 2
                    ncols = scores(bi, gi)
                    last = (gi == NG - 1)
                    nfull = ncols - 128 if last else ncols
                    Pt = Pb[pi]
                    if nfull > 0:
                        act(Pt[:, 0:nfull], B[bi][:, 0:nfull], AF.Exp, [f'bank{bi}', kmx], [f'ml_P{pi}', krs], scale=SC,
                            bias=mx[:, 22:23], accum_out=rs[:, nr:nr + 1])
                        nr += 1
                    if last:
                        tt(V, sd, B[bi][:, ncols - 128:ncols], mneg, ALU.add, [f'bank{bi}', KP, 'g21'], ['ml_sd'])
                        act(Pt[:, nfull:ncols], sd, AF.Exp, ['ml_sd', kmx], [f'ml_P{pi}', krs], scale=SC, bias=mx[:, 22:23],
                            accum_out=rs[:, nr:nr + 1])
                        nr += 1
                    nb_ = ncols // 128
                    b4 = B[4][:, :].bitcast(BF16)

                    def ftp(e, Pt=Pt, nb_=nb_):
                        ins = None
                        for j in range(nb_):
                            ins = e.transpose(b4[:, j * 128:(j + 1) * 128], Pt[:, j * 128:(j + 1) * 128], identb)
                        return ins
                    P.op(T, ftp, r=[f'ml_P{pi}', KP], w=['bank4'])
                    P.op(V, lambda e, pi=pi, nb_=nb_: e.tensor_copy(PT[pi][:, 0:nb_, :],
                                                                   b4[:, 0:nb_ * 128].rearrange("p (j q) -> p j q", q=128)),
                         r=['bank4'], w=[f'ml_PT{pi}'])

                    def fpv(e, pi=pi, nb_=nb_, kbi=kbi, nkb=nkb):
                        ins = None
                        for j in range(nb_):
                            ins = e.matmul(B[5][:, 0:128], PT[pi][:, j, :], vv[:, kbi + j, :], start=(kbi + j == 0),
                                           stop=(kbi + j == nkb - 1))
                        return ins
                    P.op(T, fpv, r=[f'ml_PT{pi}', 'ml_v'], w=['bank5'])
                    kbi += nb_
                P.op(V, lambda e, nr=nr, rs=rs: e.tensor_reduce(out=rs[:, 23:24], in_=rs[:, 0:nr], axis=AX.X, op=ALU.add),
                     r=[krs], w=[krs])
                P.op(V, lambda e, rs=rs: e.reciprocal(rs[:, 23:24], rs[:, 23:24]), r=[krs], w=[krs])
                ts(V, Osb, B[5][:, 0:128], rs[:, 23:24], None, ALU.mult, None, ['bank5', krs], ['ml_O'])
                q4 = qb % 4
                P.op(T, lambda e, q4=q4: e.transpose(B[6][:, q4 * 128:(q4 + 1) * 128], Osb, self.ident[:]), r=['ml_O', 'ident'],
                     w=['bank6'])
                if q4 == 3:
                    t0 = (qb // 4) * 512
                    ps, pk = self.proj(wsl, 'ml_wsl', 128, t0)
                    zs = g[12]
                    act(zs[:], ps, AF.Silu, [pk], [gk(12)])
                    if self.debug and self.dbgsel == 'd_raw' and li == 0:
                        P.op(V, lambda e: e.tensor_copy(g[13][:], B[6][:, :]), r=['bank6'], w=[gk(13)])
                        P.dma(self.dbg[h * 128:(h + 1) * 128, t0:t0 + 512], g[13][:], r=[gk(13)])
                    tt(V, yo, B[6][:, :], zs[:], ALU.mult, ['bank6', gk(12)], ['ml_yo'])
                    P.dma(self.ycT[1024 + h * 128:1024 + (h + 1) * 128, t0:t0 + 512], yo, r=['ml_yo'], w=['ycT'])

    def odd_layer(self, layer):
        li = layer // 2
        self.norm_phase(layer)
        if not hasattr(self, 'csT'):
            self.rope_setup()
        self.phase()
        self.ssd(li)
        self.phase()
        self.mla(li)
        self.mem_attn(layer, self.I['od_in_w'][li], 2256, 4560, 2048)
        self.out_proj(self.I['od_out_w'][li])
        self.phase()


def build(shapes, nlayers=4, layers=None):
    k = K(shapes, nlayers=nlayers)
    k.consts_small()
    k.stage0()
    k.mem_setup()
    for layer in (layers if layers is not None else range(nlayers)):
        if layer % 2 == 0:
            k.even_layer(layer)
        else:
            k.odd_layer(layer)
    k.final_phase()
    return k


_CACHE = {}


def kernel(**inputs):
    consts = host_consts()
    x = np.ascontiguousarray(np.asarray(inputs['x'], dtype=np.float32))
    mem = np.ascontiguousarray(np.asarray(inputs['mem'], dtype=np.float32))
    pos = np.ascontiguousarray(np.asarray(inputs['positions']).astype(np.int32))
    nb = x.shape[0]
    shared = {n: np.ascontiguousarray(np.asarray(inputs[n], dtype=np.float32)) for n in WNAMES}
    shared.update(consts)
    shapes = {'x': ([S, D], F32), 'mem': ([256, D], F32), 'positions': ([1, S], I32)}
    for n, v in shared.items():
        shapes[n] = (list(v.shape), F32)
    if 'nc' not in _CACHE:
        k = build(shapes)
        _CACHE['nc'] = k.P.emit()
    nc = _CACHE['nc']
    in_maps = []
    for b in range(nb):
        m = {'x': x[b], 'mem': mem[b], 'positions': pos[b:b + 1]}
        m.update(shared)
        in_maps.append(m)
    res = run_bass_kernel_spmd(nc, in_maps, core_ids=list(range(nb)))
    return np.stack([np.asarray(r['out'], dtype=np.float32) for r in res.results], axis=0)
```

```python
import contextlib
import math
import numpy as np
import concourse.bass as bass
import concourse.mybir as mybir
from concourse.bass_utils import run_bass_kernel_spmd

F32 = mybir.dt.float32
BF16 = mybir.dt.bfloat16
I32 = mybir.dt.int32
AF = mybir.ActivationFunctionType
ALU = mybir.AluOpType
AX = mybir.AxisListType

ENGS = ['tensor', 'vector', 'scalar', 'gpsimd', 'sync']
NDMASEM = 12

S = 4096
D = 1024
EVEN_PROJ = 6784
ODD_PROJ = 4816
C0 = math.exp(-0.5)


class Prog:
    def __init__(self):
        self.nc = bass.Bass("TRN2", target_bir_lowering=False)
        self.stack = contextlib.ExitStack()
        self.ops = {e: [] for e in ENGS}
        self.cnt = {e: 0 for e in ENGS}
        self.lastw = {}
        self.readers = {}
        self.dma_rr = {'sync': 0, 'gpsimd': 0}
        self.dma_val = {}
        self.dma_last = {}
        self.all_dma_tokens = []
        self.bank_ro = {}
        self.alias = {'b5a': 'bank5', 'b5b': 'bank5', 'b6a': 'bank7', 'b6b': 'bank6', 'b6c': 'bank7', 'b6d': 'bank7',
                      'b7a': 'bank6', 'b7b': 'bank2', 'b7c': 'bank2', 'b7d': 'bank2', 'b7e': 'bank2'}

    def dram(self, name, shape, dt, kind="Internal"):
        return self.nc.dram_tensor(name, list(shape), dt, kind=kind)

    def sb(self, name, shape, dt=F32):
        return self.stack.enter_context(self.nc.sbuf_tensor("t_" + name, list(shape), dt))

    def ps(self, name, shape, dt=F32):
        return self.stack.enter_context(self.nc.psum_tensor("p_" + name, list(shape), dt))

    def _deps(self, r, w, eng=None, ro_skip=()):
        r = [self.alias.get(k, k) for k in r]
        w = [self.alias.get(k, k) for k in w]
        deps = []
        skip_same = eng in ('vector', 'scalar')
        for k in r:
            t = self.lastw.get(k)
            if t is not None and not (k in ro_skip and t[0] == 'c' and t[1] == eng):
                deps.append(t)
        for k in w:
            t = self.lastw.get(k)
            if t is not None and not (skip_same and t[0] == 'c' and t[1] == eng):
                deps.append(t)
            for t2 in self.readers.get(k, []):
                if not (skip_same and t2[0] == 'c' and t2[1] == eng):
                    deps.append(t2)
        return deps

    def _commit(self, tok, r, w):
        r = [self.alias.get(k, k) for k in r]
        w = [self.alias.get(k, k) for k in w]
        for k in w:
            self.lastw[k] = tok
            self.readers[k] = []
        for k in r:
            if k in w:
                continue
            lst = self.readers.setdefault(k, [])
            if tok[0] == 'c':
                lst[:] = [t for t in lst if not (t[0] == 'c' and t[1] == tok[1])]
            lst.append(tok)

    def op(self, eng, fn, r=(), w=()):
        r = [self.alias.get(k, k) for k in r]
        w = [self.alias.get(k, k) for k in w]
        pb = [k for k in list(r) + list(w) if k.startswith('bank')]
        orig_w = set(w)
        r = list(r) + pb
        w = list(w) + [k for k in pb if k not in w]
        ro_skip = set()
        if eng in ('vector', 'scalar'):
            ro_skip = set(k for k in pb if k not in orig_w and self.bank_ro.get(k, False))
        deps = self._deps(r, w, eng, ro_skip) + list(getattr(self, '_bar_extra', []))
        for k in set(pb):
            self.bank_ro[k] = (k not in orig_w)
        self.cnt[eng] += 1
        tok = ('c', eng, self.cnt[eng])
        self.ops[eng].append((fn, deps, 'c', tok))
        self._commit(tok, r, w)
        return tok

    def barrier(self, scratch):
        toks = [('c', e, self.cnt[e]) for e in ENGS if self.cnt[e] > 0]
        last_d = {}
        for t in self.all_dma_tokens:
            last_d[t[1]] = t
        toks += list(last_d.values())
        for i, eng in enumerate(['vector', 'scalar', 'gpsimd']):
            self.cnt[eng] += 1
            tok = ('c', eng, self.cnt[eng])
            if eng == 'scalar':
                fn = lambda e, i=i: e.copy(scratch[:, i:i + 1], scratch[:, i + 4:i + 5])
            else:
                fn = lambda e, i=i: e.tensor_copy(scratch[:, i:i + 1], scratch[:, i + 4:i + 5])
            self.ops[eng].append((fn, list(toks), 'c', tok))
        self.bar_toks = [('c', e, self.cnt[e]) for e in ['vector', 'scalar', 'gpsimd']]
        for k in list(self.lastw.keys()):
            pass
        self.lastw['__bar__'] = self.bar_toks[0]
        self._bar_extra = list(self.bar_toks)

    def dma(self, out, in_, r=(), w=(), q='sync', **kw):
        deps = self._deps(r, w) + list(getattr(self, '_bar_extra', []))
        i = self.dma_rr[q]
        self.dma_rr[q] = (i + 1) % NDMASEM
        key = (q, i)
        prev = self.dma_last.get(key)
        if prev is not None:
            deps.append(prev)
        v = self.dma_val.get(key, 0) + 16
        self.dma_val[key] = v
        tok = ('d', key, v)
        self.dma_last[key] = tok

        def fn(e, out=out, in_=in_, kw=kw):
            return e.dma_start(out=out, in_=in_, **kw)
        self.ops[q].append((fn, deps, 'd', tok))
        self._commit(tok, r, w)
        self.all_dma_tokens.append(tok)
        return tok

    def emit(self):
        nc = self.nc
        EPOCH = 16000
        sems = {}
        for e in ENGS:
            nep = max(1, (self.cnt[e] + EPOCH - 1) // EPOCH)
            for ep in range(nep):
                sems[(e, ep)] = self.stack.enter_context(nc.semaphore(f"s_{e}_{ep}"))
        dsems = {}
        for q in ('sync', 'gpsimd'):
            for i in range(NDMASEM):
                dsems[(q, i)] = self.stack.enter_context(nc.semaphore(f"d_{q}_{i}"))
        prog = self

        def csem(eng, n):
            return sems[(eng, (n - 1) // EPOCH)], (n - 1) % EPOCH + 1

        def run(ename, e):
            waited = {}
            for (fn, deps, kind, tok) in prog.ops[ename]:
                need = {}
                for d in deps:
                    if d[0] == 'c':
                        if d[1] == ename and ename == 'tensor':
                            continue
                        k = ('c', d[1])
                    else:
                        k = ('d', d[1])
                    if d[2] > need.get(k, 0):
                        need[k] = d[2]
                for k, v in need.items():
                    if waited.get(k, 0) >= v:
                        continue
                    waited[k] = v
                    if k[0] == 'c':
                        s_, val = csem(k[1], v)
                        e.wait_ge(s_, val)
                    else:
                        e.wait_ge(dsems[k[1]], v)
                ins = fn(e)
                if kind == 'c':
                    s_, _ = csem(ename, tok[2])
                    ins.then_inc(s_, 1)
                else:
                    ins.then_inc(dsems[tok[1]], 16)
            if ename == 'sync':
                fin = {}
                for t in prog.all_dma_tokens:
                    fin[t[1]] = max(fin.get(t[1], 0), t[2])
                for k, v in fin.items():
                    e.wait_ge(dsems[k], v)
                for en in ENGS:
                    if en != 'sync' and prog.cnt[en] > 0:
                        s_, val = csem(en, prog.cnt[en])
                        e.wait_ge(s_, val)

        with nc.Block() as block:
            @block.sync
            def _(e):
                run('sync', e)

            @block.tensor
            def _(e):
                run('tensor', e)

            @block.vector
            def _(e):
                run('vector', e)

            @block.scalar
            def _(e):
                run('scalar', e)

            @block.gpsimd
            def _(e):
                run('gpsimd', e)
        self.stack.close()
        return nc


def host_consts():
    c = {}
    c['ident'] = np.eye(128, dtype=np.float32)
    s = np.arange(128)[:, None]
    t = np.arange(128)[None, :]
    same = (s // 64) == (t // 64)
    MU = (same & (s < t)).astype(np.float32)
    MUI = (same & (s <= t)).astype(np.float32)
    c['mm'] = np.concatenate([MU, MUI], axis=1)
    c['ml'] = (same & (s > t)).astype(np.float32)
    m01 = np.ones((128, 512), np.float32)
    m01[:, ::64] = 0.0
    c['m01'] = m01
    m128 = np.ones((128, 512), np.float32)
    m128[:, ::128] = 0.0
    c['m128'] = m128
    c['iota'] = np.tile(np.arange(128, dtype=np.float32)[None, :], (128, 1))
    sel = np.zeros((16, 16, 128), np.float32)
    for h in range(16):
        sel[h, h, :] = 1.0
    c['sel16'] = sel.reshape(16, 2048)
    c['mui128'] = (np.arange(128)[:, None] <= np.arange(128)[None, :]).astype(np.float32)
    c['mneg128'] = np.where(np.arange(128)[None, :] > np.arange(128)[:, None], -30000.0, 0.0).astype(np.float32)
    inv = 1.0 / (10000.0 ** (np.arange(0, 64, 2, dtype=np.float32) / 64.0))
    c['invf64'] = np.concatenate([inv, inv]).astype(np.float32).reshape(64, 1)
    return c


WNAMES = ['norm_w', 'mem_norm_w', 'final_norm_w', 'mem_kv_w', 'ev_in_w', 'ev_out_w', 'rw_mu', 'rw_w0',
          'rw_w2', 'rw_a0', 'rw_a2', 'rw_k_k', 'rw_k_a', 'rw_r_k', 'rw_ln_w', 'rw_ln_b',
          's5_lambda_re', 's5_lambda_im', 's5_b_re', 's5_b_im', 's5_c_re', 's5_c_im', 's5_d',
          's5_log_dt', 's5_glu_w', 's5_glu_b', 'od_in_w', 'od_out_w', 'm2_conv_w', 'm2_conv_b',
          'm2_dt_bias', 'm2_a_log', 'm2_d', 'm2_norm_w', 'mla_q_norm_w', 'mla_wq_up',
          'mla_kv_norm_w', 'mla_wkv_up']


class K:
    def __init__(self, shapes, nlayers=4, debug=None):
        self.P = P = Prog()
        self.nlayers = nlayers
        self.debug = debug
        self.dbgsel = None
        self.heads = range(16)
        self.segs = range(8)
        self.stop = None
        self.I = {}
        for n, (shp, dt) in shapes.items():
            self.I[n] = P.dram(n, shp, dt, kind="ExternalInput").ap()
        self.out = P.dram("out", [S, D], F32, kind="ExternalOutput").ap()
        self.hT = P.dram("hT", [D, S], F32).ap()
        self.ycT = P.dram("ycT", [2304, S], BF16).ap()
        if debug:
            self.dbg = P.dram("dbg", list(debug), F32, kind="ExternalOutput").ap()
        self.xnT = P.sb("xnT", [128, 8, S], BF16)
        self.ident = P.sb("ident", [128, 128])
        self.mm = P.sb("mm", [128, 256])
        self.ml = P.sb("ml", [128, 128])
        self.m01 = P.sb("m01", [128, 512])
        self.ones = P.sb("ones", [128, 128])
        self.bank = [P.ps(f"bank{i}", [128, 512]) for i in range(8)]
        P.dma(self.ident[:], self.I['ident'], w=['ident'])
        P.dma(self.mm[:], self.I['mm'], w=['mm'])
        P.dma(self.ml[:], self.I['ml'], w=['ml'])
        P.dma(self.m01[:], self.I['m01'], w=['m01'])
        self.m128 = P.sb("m128", [128, 512])
        self.iota = P.sb("iota", [128, 128])
        P.dma(self.m128[:], self.I['m128'], w=['m128'])
        P.dma(self.iota[:], self.I['iota'], w=['iota'])
        P.op('gpsimd', lambda e: e.memset(self.ones[:], 1.0), w=['ones'])
        self.wst = [P.sb(f"wst{i}", [128, 8, 128]) for i in range(2)]
        self.g = [P.sb(f"g{i}", [128, 512]) for i in range(24)]
        self.big = P.sb("big", [128, S + 1])
        self.ARENA = 9728
        self.arena = P.sb("arena", [128, self.ARENA])
        self.arena_off = 0
        self.ybig = self.big[:, 0:4096].bitcast(BF16)
        self.wst_i = 0
        self.pj_i = 0

    def carve(self, name, shape, dt=F32):
        p = shape[0]
        n = 1
        for d_ in shape[1:]:
            n *= d_
        words = n if dt in (F32, I32) else (n + 1) // 2
        off = self.arena_off
        self.arena_off += words
        assert self.arena_off <= self.ARENA, (name, self.arena_off)
        ap = self.arena[0:p, off:off + words]
        if dt != F32:
            ap = ap.bitcast(dt)[:, 0:n]
        if len(shape) == 3:
            ap = ap.rearrange("p (a b) -> p a b", b=shape[2])
        elif len(shape) == 4:
            ap = ap.rearrange("p (a b c) -> p a b c", b=shape[2], c=shape[3])
        return ap

    def phase(self):
        self.P.barrier(self.barscr)
        self.arena_off = 0

    def load_w(self, wap, c0, M, dst, key):
        P = self.P
        i = self.wst_i
        self.wst_i ^= 1
        st = self.wst[i]
        P.dma(st[:, :, 0:M], wap[:, c0:c0 + M].rearrange("(kt p) m -> p kt m", p=128),
              w=[f'wst{i}'])
        P.op('gpsimd', lambda e: e.tensor_copy(dst[:, :, 0:M], st[:, :, 0:M]), r=[f'wst{i}'], w=[key])

    def proj(self, wt, wkey, M, t0):
        P = self.P
        i = self.pj_i
        self.pj_i ^= 1
        bk = self.bank[i]
        xn = self.xnT

        def fn(e):
            ins = None
            for kt in range(8):
                ins = e.matmul(bk[0:M, :], wt[:, kt, 0:M], xn[:, kt, t0:t0 + 512],
                               start=(kt == 0), stop=(kt == 7))
            return ins
        P.op('tensor', fn, r=[wkey, 'xnT'], w=[f'bank{i}'])
        return bk[0:M, :], f'bank{i}'

    def stage0(self):
        P = self.P
        x = self.I['x']
        self.hb = [P.sb(f"hb{i}", [128, 8, 256]) for i in range(2)]
        xin = [self.hb[i][:, 0:4, :].rearrange("p a b -> p (a b)") for i in range(2)]
        xo = [self.hb[i][:, 4:8, :].rearrange("p a (b c) -> p (a b) c", c=128) for i in range(2)]
        for tb in range(S // 128):
            i = tb % 2
            P.dma(xin[i], x[tb * 128:(tb + 1) * 128, :], w=[f'hb{i}'])
            for half in range(2):
                bk = self.bank[2 + half]

                def fn(e, i=i, half=half, bk=bk):
                    ins = None
                    for q in range(4):
                        kt = half * 4 + q
                        ins = e.transpose(bk[:, q * 128:(q + 1) * 128], xin[i][:, kt * 128:(kt + 1) * 128],
                                          self.ident[:])
                    return ins
                P.op('tensor', fn, r=[f'hb{i}', 'ident'], w=[f'bank{2 + half}'])
                eng = 'vector' if half == 0 else 'scalar'
                if half == 0:
                    P.op('vector', lambda e, i=i, bk=bk: e.tensor_copy(
                        xo[i][:, 0:4, :], bk[:].rearrange("p (q t) -> p q t", t=128)),
                        r=['bank2'], w=[f'hb{i}'])
                else:
                    P.op('scalar', lambda e, i=i, bk=bk: e.copy(
                        xo[i][:, 4:8, :], bk[:].rearrange("p (q t) -> p q t", t=128)),
                        r=['bank3'], w=[f'hb{i}'])
            P.dma(self.hT[:, tb * 128:(tb + 1) * 128].rearrange("(kt p) t -> p kt t", p=128), xo[i],
                  r=[f'hb{i}'], w=['hT'])

    def norm_phase(self, layer):
        P = self.P
        if not hasattr(self, 'nw'):
            self.nw = P.sb("nw", [128, 4, 8])
            P.dma(self.nw[:], self.I['norm_w'].rearrange("l (kt p) -> p l kt", p=128), w=['nw'],
                  allow_slow_non_contiguous=True)
            self.sq = [self.g[0], self.g[1]]
            self.rinv = self.g[2]
        NB = 256
        for blk in range(S // NB):
            i = blk % 2
            t0 = blk * NB
            hb = self.hb[i]
            P.dma(hb[:], self.hT[:, t0:t0 + NB].rearrange("(kt p) t -> p kt t", p=128), r=['hT'],
                  w=[f'hb{i}'])
            bk = self.bank[2]
            for kt in range(8):
                j = kt % 2
                P.op('scalar', lambda e, kt=kt, j=j, hb=hb: e.activation(
                    out=self.sq[j][:, 0:NB], in_=hb[:, kt, :], func=AF.Square), r=[f'hb{i}'], w=[f'g{j}'])
                P.op('tensor', lambda e, kt=kt, j=j, bk=bk: e.matmul(
                    bk[:, 0:NB], self.ones[:], self.sq[j][:, 0:NB], start=(kt == 0), stop=(kt == 7)),
                    r=[f'g{j}', 'ones'], w=['bank2'])
            P.op('scalar', lambda e, bk=bk: e.activation(out=self.rinv[:, 0:NB], in_=bk[:, 0:NB], func=AF.Sqrt,
                                                         scale=1.0 / D, bias=self.eps6[:, 0:1]),
                 r=['bank2', 'eps'], w=['g2'])
            P.op('vector', lambda e: e.reciprocal(self.rinv[:, 0:NB], self.rinv[:, 0:NB]), r=['g2'], w=['g2'])
            for kt in range(8):
                eng = 'vector' if kt % 2 == 0 else 'gpsimd'
                if eng == 'vector':
                    P.op('vector', lambda e, kt=kt, hb=hb, t0=t0: e.scalar_tensor_tensor(
                        out=self.xnT[:, kt, t0:t0 + NB], in0=hb[:, kt, :], scalar=self.nw[:, layer, kt:kt + 1],
                        in1=self.rinv[:, 0:NB], op0=ALU.mult, op1=ALU.mult), r=[f'hb{i}', 'g2', 'nw'], w=['xnT'])
                else:
                    P.op('gpsimd', lambda e, kt=kt, hb=hb: e.tensor_scalar(
                        out=hb[:, kt, :], in0=hb[:, kt, :], scalar1=self.nw[:, layer, kt:kt + 1], scalar2=None,
                        op0=ALU.mult), r=[f'hb{i}', 'nw'], w=[f'hb{i}'])
                    P.op('gpsimd', lambda e, kt=kt, hb=hb, t0=t0: e.tensor_tensor(
                        out=self.xnT[:, kt, t0:t0 + NB], in0=hb[:, kt, :], in1=self.rinv[:, 0:NB], op=ALU.mult),
                        r=[f'hb{i}', 'g2'], w=['xnT'])

    def consts_small(self):
        P = self.P
        self.eps6 = P.sb("eps6", [128, 4])
        self.barscr = P.sb("barscr", [128, 8])
        P.op('gpsimd', lambda e: e.memset(self.barscr[:], 0.0), w=['barscr'])
        P.op('gpsimd', lambda e: e.memset(self.eps6[:, 0:1], 1e-6), w=['eps'])
        P.op('gpsimd', lambda e: e.memset(self.eps6[:, 1:2], 64e-5), w=['eps'])
        P.op('gpsimd', lambda e: e.memset(self.eps6[:, 2:3], 0.0), w=['eps'])
        P.op('gpsimd', lambda e: e.memset(self.eps6[:, 3:4], 1.0), w=['eps'])

    def rwkv_alloc(self):
        P = self.P
        a = self.rw = {}
        self.rwk = {}
        for n in ['rr', 'kr', 'vr']:
            a[n] = self.carve("rw_" + n, [64, 513])
        for gi, n in enumerate(['rp', 'kp', 'vp', 'sg', 'ic', 'kk', 'k2', 'bb', 'csg', 'e1', 'e2', 'e3', 'e4', 'tmp',
                                'Af', 'Rf', 'Kf', 'Bf', 'KHf', 'BHf', 'bon']):
            a[n] = self.g[gi][0:64, :]
            self.rwk[n] = f'g{gi}'
            P.alias['rw_' + n] = f'g{gi}'
        a['zs'] = a['e3']
        a['yf'] = a['e4']
        P.alias['rw_zs'] = P.alias['rw_e3']
        P.alias['rw_yf'] = P.alias['rw_e4']
        P.alias['rw_latraw'] = 'big'
        P.alias['rw_lat'] = 'big'
        a['yo'] = self.carve("rw_yo", [64, 512], BF16)
        a['latraw'] = self.big
        a['lat'] = self.big[:, 1:S + 1]
        a['Zt'] = self.carve("rw_Zt", [128, 4, 128])
        a['KBt'] = self.carve("rw_KBt", [128, 2, 4, 64])
        a['Vt'] = self.carve("rw_Vt", [128, 4, 64])
        v4g = lambda gi: self.g[gi][:].rearrange("p (j t) -> p j t", t=128)
        for n, gi in [('X0', 0), ('X1', 1), ('XT0', 2), ('XT1', 3), ('RbT', 4), ('AkT', 5), ('RkT', 6)]:
            a[n] = v4g(gi)
            P.alias['rw_' + n] = f'g{gi}'
        a['QT'] = self.g[7][0:64, :]
        a['GT'] = self.g[10][0:64, :].rearrange("p (c n) -> p c n", n=64)
        a['H'] = self.g[13][0:64, :].rearrange("p (c n) -> p c n", n=64)
        P.alias['rw_QT'] = 'g7'
        P.alias['rw_GT'] = 'g10'
        P.alias['rw_H'] = 'g13'
        a['St'] = self.carve("rw_St", [64, 2, 64])
        a['Yt'] = self.g[21][0:64, :].rearrange("p (c i) -> p c i", i=64)
        a['Ysq'] = self.g[22][0:64, :].rearrange("p (c i) -> p c i", i=64)
        P.alias['rw_Yt'] = 'g21'
        P.alias['rw_Ysq'] = 'g22'
        a['st'] = self.carve("rw_st", [64, 4, 8])
        a['wr'] = self.carve("rw_wr", [128, 8, 64], BF16)
        a['wk'] = self.carve("rw_wk", [128, 8, 64], BF16)
        a['wv'] = self.carve("rw_wv", [128, 8, 64], BF16)
        a['wz'] = self.carve("rw_wz", [128, 8, 64], BF16)
        a['wlat'] = self.carve("rw_wlat", [128, 8, 128], BF16)
        a['w2a2'] = self.carve("rw_w2a2", [128, 1024])
        a['mu'] = self.carve("rw_mu", [64, 50])
        a['omu'] = self.carve("rw_omu", [64, 50])
        a['mulat'] = self.carve("rw_mulat", [128, 1])
        a['pv'] = self.carve("rw_pv", [64, 7, 16])

    def rwkv(self, li):
        P = self.P
        a = self.rw
        I = self.I
        B = self.bank
        V, A, G, T = 'vector', 'scalar', 'gpsimd', 'tensor'
        inw = I['ev_in_w'][li]
        P.dma(a['mu'][:], I['rw_mu'][li].rearrange("(c p) -> p c", p=64), w=['rw_mu'],
              allow_slow_non_contiguous=True)
        P.op(V, lambda e: e.tensor_scalar(out=a['omu'][:], in0=a['mu'][:], scalar1=-1.0, scalar2=1.0,
                                          op0=ALU.mult, op1=ALU.add), r=['rw_mu'], w=['rw_omu'])
        P.dma(a['mulat'][:], I['rw_mu'][li][3072:3200].rearrange("(p o) -> p o", o=1), w=['rw_mulat'],
              allow_slow_non_contiguous=True)
        for j, n in enumerate(['rw_w0', 'rw_a0', 'rw_k_k', 'rw_k_a', 'rw_ln_w', 'rw_ln_b']):
            P.dma(a['pv'][:, j, :], I[n][li].rearrange("(h p) -> p h", p=64), w=['rw_pv'],
                  allow_slow_non_contiguous=True)
        P.dma(a['pv'][:, 6, :], I['rw_r_k'][li].rearrange("h p -> p h"), w=['rw_pv'],
              allow_slow_non_contiguous=True)
        P.dma(a['w2a2'][0:64, :], I['rw_w2'][li], w=['rw_w2a2'])
        P.dma(a['w2a2'][64:128, :], I['rw_a2'][li], w=['rw_w2a2'])
        P.op(G, lambda e: e.memset(a['latraw'][:, 0:1], 0.0), w=['rw_latraw'])
        self.load_w(inw, 3072, 128, a['wlat'], 'rw_wlat')
        for blk in range(8):
            t0 = blk * 512
            ps, pk = self.proj(a['wlat'], 'rw_wlat', 128, t0)
            P.op(A, lambda e, ps=ps, t0=t0: e.copy(a['latraw'][:, 1 + t0:1 + t0 + 512], ps), r=[pk],
                 w=['rw_latraw'])
        tmpd = self.g[23]
        for blk in reversed(range(8)):
            t0 = blk * 512
            P.op(V, lambda e, t0=t0: e.tensor_tensor(out=tmpd[:], in0=a['latraw'][:, t0:t0 + 512],
                                                     in1=a['latraw'][:, t0 + 1:t0 + 513], op=ALU.subtract),
                 r=['big'], w=['g23'])
            P.op(V, lambda e, t0=t0: e.scalar_tensor_tensor(
                out=a['latraw'][:, t0 + 1:t0 + 513], in0=tmpd[:], scalar=a['mulat'][:, 0:1],
                in1=a['latraw'][:, t0 + 1:t0 + 513], op0=ALU.mult, op1=ALU.add),
                r=['g23', 'big', 'rw_mulat'], w=['big'])
            P.op(A, lambda e, t0=t0: e.activation(out=a['lat'][0:64, t0:t0 + 512], in_=a['lat'][0:64, t0:t0 + 512],
                                                  func=AF.Tanh), r=['big'], w=['big'])
        for h in self.heads:
            self.load_w(inw, h * 64, 64, a['wr'], 'rw_wr')
            self.load_w(inw, 1024 + h * 64, 64, a['wk'], 'rw_wk')
            self.load_w(inw, 2048 + h * 64, 64, a['wv'], 'rw_wv')
            self.load_w(inw, 4480 + h * 64, 64, a['wz'], 'rw_wz')
            P.op(G, lambda e: e.memset(a['St'][:, 0, :], 0.0), w=['rw_St0'])
            for n in ['rr', 'kr', 'vr']:
                P.op(G, lambda e, n=n: e.memset(a[n][:, 512:513], 0.0), w=['rw_' + n])
            for seg in self.segs:
                self.rwkv_seg(li, h, seg)

    def rwkv_seg(self, li, h, seg):
        P = self.P
        a = self.rw
        B = self.bank
        V, A, G, T = 'vector', 'scalar', 'gpsimd', 'tensor'
        t0 = seg * 512
        pv = a['pv']
        w0, a0, k_k, k_a, ln_w, ln_b, r_k = [pv[:, j, h:h + 1] for j in range(7)]
        ident = self.ident
        hs = slice(h * 64, (h + 1) * 64)

        def tt(eng, out, in0, in1, op, r, w):
            P.op(eng, lambda e: e.tensor_tensor(out=out, in0=in0, in1=in1, op=op), r=r, w=w)

        def stt(eng, out, in0, sc, in1, op0, op1, r, w):
            P.op(eng, lambda e: e.scalar_tensor_tensor(out=out, in0=in0, scalar=sc, in1=in1, op0=op0, op1=op1),
                 r=r, w=w)

        def act(out, in_, func, r, w, scale=None, bias=None):
            kw = {}
            if scale is not None:
                kw['scale'] = scale
            if bias is not None:
                kw['bias'] = bias
            P.op(A, lambda e: e.activation(out=out, in_=in_, func=func, **kw), r=r, w=w)

        for n, wn, mc, tmn in [('r', 'wr', h, 'Af'), ('k', 'wk', 16 + h, 'Rf'), ('v', 'wv', 32 + h, 'Kf')]:
            raw = a[n + 'r']
            rk = 'rw_' + n + 'r'
            ltmp, ltk = a[tmn], 'rw_' + tmn
            P.op(G, lambda e, raw=raw: e.tensor_copy(raw[:, 0:1], raw[:, 512:513]), r=[rk], w=[rk])
            ps, pk = self.proj(a[wn], 'rw_' + wn, 64, t0)
            P.op(A, lambda e, raw=raw, ps=ps: e.copy(raw[:, 1:513], ps), r=[pk], w=[rk])
            P.op(G, lambda e, raw=raw, mc=mc, ltmp=ltmp: e.tensor_scalar(out=ltmp[:], in0=raw[:, 0:512],
                                                                        scalar1=a['mu'][:, mc:mc + 1], scalar2=None,
                                                                        op0=ALU.mult), r=[rk, 'rw_mu'], w=[ltk])
            stt(V, a[n + 'p'][:], ps, a['omu'][:, mc:mc + 1], ltmp[:], ALU.mult, ALU.add,
                [pk, 'rw_omu', ltk], ['rw_' + n + 'p'])
        if self.stop == 'lerp':
            return
        bz = B[2]
        P.op(T, lambda e: e.matmul(bz[0:64, :], a['w2a2'][0:64, hs], a['lat'][0:64, t0:t0 + 512], start=True,
                                   stop=True), r=['rw_w2a2', 'rw_lat'], w=['bank2'])
        act(a['sg'][:], bz[0:64, :], AF.Sigmoid, ['bank2', 'rw_pv'], ['rw_sg'], bias=w0)
        P.op(T, lambda e: e.matmul(bz[0:64, :], a['w2a2'][64:128, hs], a['lat'][64:128, t0:t0 + 512], start=True,
                                   stop=True), r=['rw_w2a2', 'rw_lat'], w=['bank2'])
        act(a['ic'][:], bz[0:64, :], AF.Sigmoid, ['bank2', 'rw_pv'], ['rw_ic'], bias=a0)
        P.op(V, lambda e: e.tensor_scalar(out=a['kk'][:], in0=a['kp'][:], scalar1=k_k, scalar2=None, op0=ALU.mult),
             r=['rw_kp', 'rw_pv'], w=['rw_kk'])
        act(a['tmp'][:], a['kk'][:], AF.Square, ['rw_kk'], ['rw_tmp'])
        P.op(T, lambda e: e.matmul(bz[0:64, :], self.ones[0:64, 0:64], a['tmp'][:], start=True, stop=True),
             r=['ones', 'rw_tmp'], w=['bank2'])
        P.op(V, lambda e: e.tensor_scalar(out=a['tmp'][:], in0=bz[0:64, :], scalar1=1e-24, scalar2=None,
                                          op0=ALU.max), r=['bank2'], w=['rw_tmp'])
        act(a['tmp'][:], a['tmp'][:], AF.Sqrt, ['rw_tmp'], ['rw_tmp'])
        P.op(V, lambda e: e.reciprocal(a['tmp'][:], a['tmp'][:]), r=['rw_tmp'], w=['rw_tmp'])
        tt(G, a['kk'][:], a['kk'][:], a['tmp'][:], ALU.mult, ['rw_kk', 'rw_tmp'], ['rw_kk'])
        P.op(V, lambda e: e.tensor_scalar(out=a['k2'][:], in0=a['ic'][:], scalar1=-1.0, scalar2=k_a, op0=ALU.add,
                                          op1=ALU.mult), r=['rw_ic', 'rw_pv'], w=['rw_k2'])
        stt(V, a['k2'][:], a['k2'][:], 1.0, a['kp'][:], ALU.add, ALU.mult, ['rw_k2', 'rw_kp'], ['rw_k2'])
        tt(G, a['bb'][:], a['kk'][:], a['ic'][:], ALU.mult, ['rw_kk', 'rw_ic'], ['rw_bb'])
        if self.stop == 'kk':
            return
        P.op(V, lambda e: e.tensor_tensor_scan(out=a['csg'][:], data0=self.m01[0:64, :], data1=a['sg'][:],
                                               initial=0.0, op0=ALU.mult, op1=ALU.add),
             r=['m01', 'rw_sg'], w=['rw_csg'])
        act(a['e1'][:], a['csg'][:], AF.Exp, ['rw_csg'], ['rw_e1'], scale=-C0)
        act(a['e2'][:], a['csg'][:], AF.Exp, ['rw_csg'], ['rw_e2'], scale=C0)
        tt(G, a['e3'][:], a['csg'][:], a['sg'][:], ALU.subtract, ['rw_csg', 'rw_sg'], ['rw_e3'])
        act(a['e3'][:], a['e3'][:], AF.Exp, ['rw_e3'], ['rw_e3'], scale=-C0)
        c3 = a['csg'][:].rearrange("p (c t) -> p c t", t=64)
        tt(V, a['e4'][:].rearrange("p (c t) -> p c t", t=64), c3[:, :, 63:64].to_broadcast([64, 8, 64]), c3,
           ALU.subtract, ['rw_csg'], ['rw_e4'])
        act(a['e4'][:], a['e4'][:], AF.Exp, ['rw_e4'], ['rw_e4'], scale=-C0)
        stt(V, a['Af'][:], a['kk'][:], -1.0, a['e3'][:], ALU.mult, ALU.mult, ['rw_kk', 'rw_e3'], ['rw_Af'])
        tt(G, a['Rf'][:], a['rp'][:], a['e1'][:], ALU.mult, ['rw_rp', 'rw_e1'], ['rw_Rf'])
        tt(V, a['Kf'][:], a['k2'][:], a['e2'][:], ALU.mult, ['rw_k2', 'rw_e2'], ['rw_Kf'])
        tt(G, a['Bf'][:], a['bb'][:], a['e2'][:], ALU.mult, ['rw_bb', 'rw_e2'], ['rw_Bf'])
        tt(V, a['KHf'][:], a['k2'][:], a['e4'][:], ALU.mult, ['rw_k2', 'rw_e4'], ['rw_KHf'])
        tt(G, a['BHf'][:], a['bb'][:], a['e4'][:], ALU.mult, ['rw_bb', 'rw_e4'], ['rw_BHf'])
        stt(V, a['tmp'][:], a['rp'][:], r_k, a['k2'][:], ALU.mult, ALU.mult, ['rw_rp', 'rw_k2', 'rw_pv'], ['rw_tmp'])
        P.op(T, lambda e: e.matmul(bz[0:64, :], self.ones[0:64, 0:64], a['tmp'][:], start=True, stop=True),
             r=['ones', 'rw_tmp'], w=['bank2'])
        tt(V, a['bon'][:], bz[0:64, :], a['vp'][:], ALU.mult, ['bank2', 'rw_vp'], ['rw_bon'])
        if self.stop == 'prep':
            return
        def trn(e, srcs, bk):
            ins = None
            for q, src in enumerate(srcs):
                for j in range(4):
                    c = (q * 4 + j) * 64
                    ins = e.transpose(bk[:, c:c + 64], src[0:64, j * 128:(j + 1) * 128], ident[0:64, 0:64])
            return ins
        P.op(T, lambda e: trn(e, [a['Af'], a['vp']], B[3]), r=['rw_Af', 'rw_vp', 'ident'], w=['bank3'])
        if self.stop == 'trans1':
            return
        P.op(T, lambda e: trn(e, [a['KHf'], a['BHf']], B[4]), r=['rw_KHf', 'rw_BHf', 'ident'], w=['bank4'])
        P.op(V, lambda e: e.tensor_copy(a['Zt'][:, :, 0:64], B[3][:, 0:256].rearrange("p (j c) -> p j c", c=64)),
             r=['bank3'], w=['rw_Zt'])
        if self.stop == 'trans2':
            return
        P.op(V, lambda e: e.tensor_copy(a['Vt'][:], B[3][:, 256:512].rearrange("p (j c) -> p j c", c=64)),
             r=['bank3'], w=['rw_Vt'])
        if self.stop == 'trans3':
            return
        P.op(A, lambda e: e.copy(a['KBt'][:], B[4][:, :].rearrange("p (q j c) -> p q j c", q=2, c=64)),
             r=['bank4'], w=['rw_KBt'])
        if self.stop == 'trans':
            return
        Zt = a['Zt']
        mu_b = self.mm[:, 0:128].unsqueeze(1).to_broadcast([128, 4, 128])
        mui_b = self.mm[:, 128:256].unsqueeze(1).to_broadcast([128, 4, 128])
        ml_b = self.ml[:].unsqueeze(1).to_broadcast([128, 4, 128])
        b3v = lambda bk: bk[:, :].rearrange("p (j t) -> p j t", t=128)
        blk = lambda j: slice(j * 128, (j + 1) * 128)

        for (bi_, L, R) in [(5, 'Bf', 'Af'), (3, 'Bf', 'Rf'), (4, 'Kf', 'Af'), (6, 'Kf', 'Rf'), (7, 'Af', 'Bf')]:
            def sc(e, bi_=bi_, L=L, R=R):
                ins = None
                for j in range(4):
                    ins = e.matmul(B[bi_][:, blk(j)], a[L][:, blk(j)], a[R][:, blk(j)], start=True, stop=True)
                return ins
            P.op(T, sc, r=['rw_' + L, 'rw_' + R], w=[f'bank{bi_}'])
        tt(V, a['X0'], b3v(B[5]), mu_b, ALU.mult, ['bank5', 'mm'], ['rw_X0'])
        tt(V, a['RbT'], b3v(B[3]), mui_b, ALU.mult, ['bank3', 'mm'], ['rw_RbT'])
        tt(V, a['AkT'], b3v(B[4]), mu_b, ALU.mult, ['bank4', 'mm'], ['rw_AkT'])
        tt(V, a['RkT'], b3v(B[6]), mui_b, ALU.mult, ['bank6', 'mm'], ['rw_RkT'])
        tt(V, a['XT0'], b3v(B[7]), ml_b, ALU.mult, ['bank7', 'ml'], ['rw_XT0'])

        if self.stop == 'c1':
            return

        def akv(e):
            ins = None
            for j in range(4):
                ins = e.matmul(B[5][:, j * 128:j * 128 + 64], a['AkT'][:, j, :], a['Vt'][:, j, :], start=True, stop=True)
            return ins
        P.op(T, akv, r=['rw_AkT', 'rw_Vt'], w=['bank5'])
        P.op(V, lambda e: e.tensor_copy(Zt[:, :, 64:128], b3v(B[5])[:, :, 0:64]), r=['bank5'], w=['rw_Zt'])
        for lv in range(6):
            Xc, XTc = a[f'X{lv % 2}'], a[f'XT{lv % 2}']
            Xn, XTn = a[f'X{(lv + 1) % 2}'], a[f'XT{(lv + 1) % 2}']
            kc, ktc = f'rw_X{lv % 2}', f'rw_XT{lv % 2}'
            kn, ktn = f'rw_X{(lv + 1) % 2}', f'rw_XT{(lv + 1) % 2}'

            if lv < 5:
                def fx(e, Xc=Xc, XTc=XTc):
                    ins = None
                    for j in range(4):
                        ins = e.matmul(B[6][:, blk(j)], XTc[:, j, :], Xc[:, j, :], start=True, stop=True)
                    return ins

                def fxt(e, Xc=Xc, XTc=XTc):
                    ins = None
                    for j in range(4):
                        ins = e.matmul(B[7][:, blk(j)], Xc[:, j, :], XTc[:, j, :], start=True, stop=True)
                    return ins
                P.op(T, fx, r=[kc, ktc], w=['bank6'])
                P.op(T, fxt, r=[kc, ktc], w=['bank7'])
                P.op(A, lambda e, Xn=Xn: e.copy(Xn, b3v(B[6])), r=['bank6'], w=[kn])
                P.op(A, lambda e, XTn=XTn: e.copy(XTn, b3v(B[7])), r=['bank7'], w=[ktn])

            def fz(e, Xc=Xc):
                ins = None
                for j in range(4):
                    ins = e.matmul(B[5][:, blk(j)], Xc[:, j, :], Zt[:, j, :], start=True, stop=True)
                return ins
            P.op(T, fz, r=[kc, 'rw_Zt'], w=['bank5'])
            tt(V, Zt[:], Zt[:], b3v(B[5]), ALU.add, ['rw_Zt', 'bank5'], ['rw_Zt'])

        if self.stop == 'c2':
            return

        def fq(e):
            ins = None
            for j in range(4):
                ins = e.matmul(B[3][0:64, blk(j)], Zt[:, j, 0:64], a['RbT'][:, j, :], start=True, stop=True)
            return ins
        P.op(T, fq, r=['rw_Zt', 'rw_RbT'], w=['bank3'])
        tt(V, a['QT'], B[3][0:64, :], a['Rf'][:], ALU.add, ['bank3', 'rw_Rf'], ['rw_QT'])

        if self.stop == 'c2a':
            return

        def fg(e):
            ins = None
            for par, bk in [(0, B[4]), (1, B[7])]:
                rs_ = slice(par * 64, par * 64 + 64)
                for j in range(4):
                    ins = e.matmul(bk[0:64, j * 64:(j + 1) * 64], Zt[rs_, j, 0:64], a['KBt'][rs_, 1, j, :], start=True, stop=True)
            return ins
        P.op(T, fg, r=['rw_Zt', 'rw_KBt'], w=['bank4', 'bank7'])
        if self.stop == 'c2f':
            return
        for ci in range(8):
            ce = ci * 64 + 63
            bk = B[4] if ci % 2 == 0 else B[7]
            j = ci // 2
            stt(V, a['GT'][:, ci, :], ident[0:64, 0:64], a['e1'][:, ce:ce + 1], bk[0:64, j * 64:(j + 1) * 64], ALU.mult, ALU.add,
                ['ident', 'rw_e1', 'bank4' if ci % 2 == 0 else 'bank7'], ['rw_GT'])

        if self.stop == 'c2b':
            return

        def fh(e):
            ins = None
            for par, bk in [(0, B[6]), (1, B[5])]:
                rs_ = slice(par * 64, par * 64 + 64)
                for j in range(4):
                    e.matmul(bk[0:64, j * 64:(j + 1) * 64], a['KBt'][rs_, 0, j, :], a['Vt'][rs_, j, :], start=True, stop=False)
                    ins = e.matmul(bk[0:64, j * 64:(j + 1) * 64], a['KBt'][rs_, 1, j, :], Zt[rs_, j, 64:128], start=False, stop=True)
            return ins
        P.op(T, fh, r=['rw_KBt', 'rw_Vt', 'rw_Zt'], w=['bank6', 'bank5'])
        H4 = self.g[13][0:64, :].rearrange("p (j par n) -> p j par n", par=2, n=64)
        P.op(A, lambda e: e.copy(H4[:, :, 0, :], B[6][0:64, 0:256].rearrange("p (j n) -> p j n", n=64)), r=['bank6'], w=['rw_H'])
        P.op(A, lambda e: e.copy(H4[:, :, 1, :], B[5][0:64, 0:256].rearrange("p (j n) -> p j n", n=64)), r=['bank5'], w=['rw_H'])
        if self.stop == 'c3':
            return
        for ci in range(8):
            j, ccs = ci // 2, slice((ci % 2) * 64, (ci % 2) * 64 + 64)
            qcs = slice(ci * 64, (ci + 1) * 64)
            St, Sn = a['St'][:, ci % 2, :], a['St'][:, (ci + 1) % 2, :]
            kS, kSn = f'rw_St{ci % 2}', f'rw_St{(ci + 1) % 2}'
            P.op(T, lambda e, ci=ci, St=St: e.matmul(B[2][0:64, 0:64], a['GT'][:, ci, :], St, start=True, stop=True),
                 r=['rw_GT', kS], w=['bank2'])
            tt(V, Sn, B[2][0:64, 0:64], a['H'][:, ci, :], ALU.add, ['bank2', 'rw_H'], [kSn])

            def ym(e, j=j, ccs=ccs, qcs=qcs, ci=ci, St=St):
                e.matmul(B[3][0:64, ci * 64:(ci + 1) * 64], a['RkT'][:, j, ccs], a['Vt'][:, j, :], start=True, stop=False)
                e.matmul(B[3][0:64, ci * 64:(ci + 1) * 64], a['RbT'][:, j, ccs], Zt[:, j, 64:128], start=False, stop=False)
                return e.matmul(B[3][0:64, ci * 64:(ci + 1) * 64], a['QT'][:, qcs], St, start=False, stop=True)
            P.op(T, ym, r=['rw_RkT', 'rw_RbT', 'rw_Vt', 'rw_Zt', 'rw_QT', kS], w=['bank3'])
        P.op(V, lambda e: e.tensor_copy(a['Yt'], B[3][0:64, :].rearrange("p (c i) -> p c i", i=64)), r=['bank3'], w=['rw_Yt'])
        if self.stop == 'chunk':
            return
        Yt, Ysq, st = a['Yt'], a['Ysq'], a['st']
        P.op(V, lambda e: e.tensor_reduce(out=st[:, 0, :], in_=Yt[:], axis=AX.X, op=ALU.add), r=['rw_Yt'], w=['rw_st'])
        act(Ysq[:], Yt[:], AF.Square, ['rw_Yt'], ['rw_Ysq'])
        P.op(V, lambda e: e.tensor_reduce(out=st[:, 1, :], in_=Ysq[:], axis=AX.X, op=ALU.add), r=['rw_Ysq'], w=['rw_st'])
        P.op(V, lambda e: e.tensor_scalar(out=st[:, 0, :], in0=st[:, 0, :], scalar1=1.0 / 64, scalar2=None, op0=ALU.mult),
             r=['rw_st'], w=['rw_st'])
        tt(V, st[:, 2, :], st[:, 0, :], st[:, 0, :], ALU.mult, ['rw_st'], ['rw_st'])
        stt(V, st[:, 2, :], st[:, 1, :], 1.0 / 64, st[:, 2, :], ALU.mult, ALU.subtract, ['rw_st'], ['rw_st'])
        act(st[:, 2, :], st[:, 2, :], AF.Sqrt, ['rw_st', 'eps'], ['rw_st'], bias=self.eps6[0:64, 1:2])
        P.op(V, lambda e: e.reciprocal(st[:, 2, :], st[:, 2, :]), r=['rw_st'], w=['rw_st'])
        tt(V, Yt[:], Yt[:], st[:, 0, :].unsqueeze(2).to_broadcast([64, 8, 64]), ALU.subtract, ['rw_Yt', 'rw_st'], ['rw_Yt'])
        tt(V, Yt[:], Yt[:], st[:, 2, :].unsqueeze(2).to_broadcast([64, 8, 64]), ALU.mult, ['rw_Yt', 'rw_st'], ['rw_Yt'])

        def ytr(e):
            ins = None
            for ci in range(8):
                ins = e.transpose(B[3][0:64, ci * 64:(ci + 1) * 64], Yt[:, ci, :], ident[0:64, 0:64])
            return ins
        P.op(T, ytr, r=['rw_Yt', 'ident'], w=['bank3'])
        act(a['yf'][:], B[3][0:64, :], AF.Identity, ['bank3', 'rw_pv'], ['rw_yf'], scale=ln_w, bias=ln_b)
        tt(V, a['yf'][:], a['yf'][:], a['bon'][:], ALU.add, ['rw_yf', 'rw_bon'], ['rw_yf'])
        if self.debug and li == 0 and self.dbgsel == 'a_out':
            P.dma(self.dbg[h * 64:(h + 1) * 64, t0:t0 + 512], a['yf'][:], r=['rw_yf'])
        ps, pk = self.proj(a['wz'], 'rw_wz', 64, t0)
        act(a['zs'][:], ps, AF.Silu, [pk], ['rw_zs'])
        tt(V, a['yo'][:], a['yf'][:], a['zs'][:], ALU.mult, ['rw_yf', 'rw_zs'], ['rw_yo'])
        P.dma(self.ycT[h * 64:(h + 1) * 64, t0:t0 + 512], a['yo'][:], r=['rw_yo'], w=['ycT'])

    def mem_setup(self):
        P = self.P
        V, A, G, T = 'vector', 'scalar', 'gpsimd', 'tensor'
        self.memT = P.sb("memT", [128, 8, 256], BF16)
        self.mnw = P.sb("mnw", [128, 8])
        self.wkv = self.ybig[:, 4096:8192].rearrange("p (k m) -> p k m", m=512)
        P.alias['wkv'] = 'big'
        self.mst = P.sb("mst", [128, 8])
        P.dma(self.mnw[:], self.I['mem_norm_w'].rearrange("(kt p) -> p kt", p=128), w=['mnw'],
              allow_slow_non_contiguous=True)
        mt_ = [self.g[0], self.g[1]]
        for mt in range(2):
            m2 = self.hb[mt][:, 0:4, :].rearrange("p a b -> p (a b)")
            P.dma(m2, self.I['mem'][mt * 128:(mt + 1) * 128, :], w=[f'hb{mt}'])
            P.op(A, lambda e, m2=m2, mt=mt: e.activation(out=self.hb[mt][:, 4:8, :].rearrange("p a b -> p (a b)"), in_=m2,
                                                       func=AF.Square, accum_out=self.mst[:, mt:mt + 1]),
                 r=[f'hb{mt}'], w=[f'hb{mt}', 'mst'])
            P.op(A, lambda e, mt=mt: e.activation(out=self.mst[:, mt:mt + 1], in_=self.mst[:, mt:mt + 1], func=AF.Sqrt,
                                                  scale=1.0 / D, bias=self.eps6[:, 0:1]), r=['mst', 'eps'], w=['mst'])
            P.op(V, lambda e, mt=mt: e.reciprocal(self.mst[:, mt:mt + 1], self.mst[:, mt:mt + 1]), r=['mst'], w=['mst'])
            P.op(V, lambda e, m2=m2, mt=mt: e.tensor_scalar(out=m2, in0=m2, scalar1=self.mst[:, mt:mt + 1], scalar2=None,
                                                          op0=ALU.mult), r=[f'hb{mt}', 'mst'], w=[f'hb{mt}'])
            for half in range(2):
                bk = self.bank[3 + half]

                def fn(e, m2=m2, half=half, bk=bk):
                    ins = None
                    for q in range(4):
                        kt = half * 4 + q
                        ins = e.transpose(bk[:, q * 128:(q + 1) * 128], m2[:, kt * 128:(kt + 1) * 128], self.ident[:])
                    return ins
                P.op(T, fn, r=[f'hb{mt}', 'ident'], w=[f'bank{3 + half}'])
                for q in range(4):
                    kt = half * 4 + q
                    P.op(V, lambda e, kt=kt, q=q, bk=bk, mt=mt: e.tensor_scalar(
                        out=self.memT[:, kt, mt * 128:(mt + 1) * 128], in0=bk[:, q * 128:(q + 1) * 128],
                        scalar1=self.mnw[:, kt:kt + 1], scalar2=None, op0=ALU.mult),
                        r=[f'bank{3 + half}', 'mnw'], w=['memT'])

    def load_w_to(self, wap, c0, M, dst_ap, key):
        P = self.P
        i = self.wst_i
        self.wst_i ^= 1
        st = self.wst[i]
        P.dma(st[:, :, 0:M], wap[:, c0:c0 + M].rearrange("(kt p) m -> p kt m", p=128), w=[f'wst{i}'])
        P.op('gpsimd', lambda e: e.tensor_copy(dst_ap, st[:, :, 0:M]), r=[f'wst{i}'], w=[key])

    def mem_attn(self, layer, inw, qcol, zcol, ycrow):
        P = self.P
        V, A, G, T = 'vector', 'scalar', 'gpsimd', 'tensor'
        B = self.bank
        self.phase()
        self.kT = self.carve("kT", [64, 4, 256])
        self.vm = self.carve("vm", [128, 2, 256])
        self.wq = self.carve("wq", [128, 8, 64], BF16)
        self.wzm = self.carve("wzm", [128, 8, 64], BF16)
        self.mo = self.carve("mo", [64, 512], BF16)
        wkvd = self.I['mem_kv_w'][layer]
        for c in range(4):
            self.load_w_to(wkvd, c * 128, 128, self.wkv[:, :, c * 128:(c + 1) * 128], 'wkv')
        for h in range(4):
            def fk(e, h=h):
                ins = None
                for kt in range(8):
                    ins = e.matmul(B[2][0:64, 0:256], self.wkv[:, kt, h * 64:(h + 1) * 64], self.memT[:, kt, :],
                                   start=(kt == 0), stop=(kt == 7))
                return ins
            P.op(T, fk, r=['wkv', 'memT'], w=['bank2'])
            P.op(V, lambda e, h=h: e.tensor_copy(self.kT[:, h, :], B[2][0:64, 0:256]), r=['bank2'], w=['kT'])
        for mt in range(2):
            def fv(e, mt=mt):
                ins = None
                for kt in range(8):
                    ins = e.matmul(B[2][:, 0:256], self.memT[:, kt, mt * 128:(mt + 1) * 128], self.wkv[:, kt, 256:512],
                                   start=(kt == 0), stop=(kt == 7))
                return ins
            P.op(T, fv, r=['wkv', 'memT'], w=['bank2'])
            P.op(V, lambda e, mt=mt: e.tensor_copy(self.vm[:, mt, :], B[2][:, 0:256]), r=['bank2'], w=['vm'])
        qf, pr, prT, zs = self.g[0], self.g[1], self.g[2], self.g[3]
        for h in range(4):
            self.load_w(inw, qcol + h * 64, 64, self.wq, 'wq')
            self.load_w(inw, zcol + h * 64, 64, self.wzm, 'wzm')
            for blk in range(8):
                t0 = blk * 512
                ps, pk = self.proj(self.wq, 'wq', 64, t0)
                P.op(A, lambda e, ps=ps: e.copy(qf[0:64, :], ps), r=[pk], w=['g0'])
                for sb in range(4):
                    ts = slice(sb * 128, (sb + 1) * 128)
                    P.op(T, lambda e, ts=ts, h=h: e.matmul(B[3][:, 0:256], qf[0:64, ts], self.kT[:, h, :], start=True,
                                                           stop=True), r=['g0', 'kT'], w=['bank3'])
                    P.op(V, lambda e: e.tensor_reduce(out=self.mst[:, 2:3], in_=B[3][:, 0:256], axis=AX.X, op=ALU.max),
                         r=['bank3'], w=['mst'])
                    P.op(V, lambda e: e.tensor_scalar(out=self.mst[:, 2:3], in0=self.mst[:, 2:3], scalar1=-0.125,
                                                      scalar2=None, op0=ALU.mult), r=['mst'], w=['mst'])
                    P.op(A, lambda e: e.activation(out=pr[:, 0:256], in_=B[3][:, 0:256], func=AF.Exp, scale=0.125,
                                                   bias=self.mst[:, 2:3], accum_out=self.mst[:, 3:4]),
                         r=['bank3', 'mst'], w=['g1', 'mst'])
                    P.op(V, lambda e: e.reciprocal(self.mst[:, 3:4], self.mst[:, 3:4]), r=['mst'], w=['mst'])
                    P.op(V, lambda e: e.tensor_scalar(out=pr[:, 0:256], in0=pr[:, 0:256], scalar1=self.mst[:, 3:4],
                                                      scalar2=None, op0=ALU.mult), r=['g1', 'mst'], w=['g1'])

                    def ftr(e):
                        e.transpose(B[4][:, 0:128], pr[:, 0:128], self.ident[:])
                        return e.transpose(B[4][:, 128:256], pr[:, 128:256], self.ident[:])
                    P.op(T, ftr, r=['g1', 'ident'], w=['bank4'])
                    P.op(V, lambda e: e.tensor_copy(prT[:, 0:256], B[4][:, 0:256]), r=['bank4'], w=['g2'])

                    def fpv(e, ts=ts, h=h):
                        e.matmul(B[5][0:64, ts], self.vm[:, 0, h * 64:(h + 1) * 64], prT[:, 0:128], start=True, stop=False)
                        return e.matmul(B[5][0:64, ts], self.vm[:, 1, h * 64:(h + 1) * 64], prT[:, 128:256], start=False,
                                        stop=True)
                    P.op(T, fpv, r=['vm', 'g2'], w=['bank5'])
                ps, pk = self.proj(self.wzm, 'wzm', 64, t0)
                P.op(A, lambda e, ps=ps: e.activation(out=zs[0:64, :], in_=ps, func=AF.Silu), r=[pk], w=['g3'])
                if self.debug and self.dbgsel == 'm_out' and layer == 0:
                    P.op(V, lambda e: e.tensor_copy(qf[0:64, :], B[5][0:64, :]), r=['bank5'], w=['g0'])
                    P.dma(self.dbg[h * 64:(h + 1) * 64, t0:t0 + 512], qf[0:64, :], r=['g0'])
                P.op(V, lambda e: e.tensor_tensor(out=self.mo[:], in0=B[5][0:64, :], in1=zs[0:64, :], op=ALU.mult),
                     r=['bank5', 'g3'], w=['mo'])
                P.dma(self.ycT[ycrow + h * 64:ycrow + (h + 1) * 64, t0:t0 + 512], self.mo[:], r=['mo'], w=['ycT'])

    def out_proj(self, wout):
        P = self.P
        V, A, G, T = 'vector', 'scalar', 'gpsimd', 'tensor'
        B = self.bank
        self.phase()
        wo = self.carve("wo", [128, 18, 1024], BF16)
        yb = self.ybig[:, 0:18 * 256].rearrange("p (k t) -> p k t", t=256)
        for c in range(6):
            for q in range(8):
                st = self.wst[(c * 8 + q) % 2]
                sk = f'wst{(c * 8 + q) % 2}'
                P.dma(st[:, 0:3, :], wout[c * 384:(c + 1) * 384, q * 128:(q + 1) * 128].rearrange("(kt p) m -> p kt m", p=128),
                      w=[sk])
                P.op(G, lambda e, st=st, c=c, q=q: e.tensor_copy(wo[:, c * 3:(c + 1) * 3, q * 128:(q + 1) * 128], st[:, 0:3, :]),
                     r=[sk], w=['wo'])
        for blk in range(16):
            t0 = blk * 256
            P.dma(yb, self.ycT[:, t0:t0 + 256].rearrange("(kt p) t -> p kt t", p=128), r=['ycT'], w=['big'])
            for dt_ in range(8):
                i = self.pj_i
                self.pj_i ^= 1

                def fn(e, i=i, dt_=dt_):
                    ins = None
                    for kt in range(18):
                        ins = e.matmul(B[i][:, 0:256], wo[:, kt, dt_ * 128:(dt_ + 1) * 128], yb[:, kt, :], start=(kt == 0),
                                       stop=(kt == 17))
                    return ins
                P.op(T, fn, r=['wo', 'big'], w=[f'bank{i}'])
                gi = 8 + (dt_ % 4)
                hs = self.g[gi]
                hk = f'hT_{dt_}_{blk}'
                P.dma(hs[:, 0:256], self.hT[dt_ * 128:(dt_ + 1) * 128, t0:t0 + 256], r=[hk], w=[f'g{gi}'])
                P.op(V, lambda e, hs=hs, i=i: e.tensor_tensor(out=hs[:, 0:256], in0=hs[:, 0:256], in1=B[i][:, 0:256], op=ALU.add),
                     r=[f'g{gi}', f'bank{i}'], w=[f'g{gi}'])
                P.dma(self.hT[dt_ * 128:(dt_ + 1) * 128, t0:t0 + 256], hs[:, 0:256], r=[f'g{gi}'], w=[hk], q='gpsimd')

    def sin_rr(self, out, x, shape, tmp, tmpi, keys_r, key_out, key_tmp, key_tmpi, shift=0.0):
        P = self.P
        V, A = 'vector', 'scalar'
        TWO_PI = 2.0 * math.pi
        C1 = 6.28125
        C2 = TWO_PI - C1
        P.op(V, lambda e: e.tensor_scalar(out=out, in0=x, scalar1=shift, scalar2=None, op0=ALU.add), r=keys_r, w=[key_out])
        P.op(V, lambda e: e.tensor_scalar(out=tmp, in0=out, scalar1=1.0 / TWO_PI, scalar2=None, op0=ALU.mult),
             r=[key_out], w=[key_tmp])
        P.op(V, lambda e: e.tensor_copy(tmpi, tmp), r=[key_tmp], w=[key_tmpi])
        P.op(V, lambda e: e.tensor_copy(tmp, tmpi), r=[key_tmpi], w=[key_tmp])
        P.op(V, lambda e: e.scalar_tensor_tensor(out=out, in0=tmp, scalar=-C1, in1=out, op0=ALU.mult, op1=ALU.add),
             r=[key_tmp, key_out], w=[key_out])
        P.op(V, lambda e: e.scalar_tensor_tensor(out=out, in0=tmp, scalar=-C2, in1=out, op0=ALU.mult, op1=ALU.add),
             r=[key_tmp, key_out], w=[key_out])
        P.op(V, lambda e: e.tensor_scalar(out=tmp, in0=out, scalar1=math.pi, scalar2=None, op0=ALU.is_gt),
             r=[key_out], w=[key_tmp])
        P.op(V, lambda e: e.scalar_tensor_tensor(out=out, in0=tmp, scalar=-TWO_PI, in1=out, op0=ALU.mult, op1=ALU.add),
             r=[key_tmp, key_out], w=[key_out])
        P.op(V, lambda e: e.tensor_scalar(out=tmp, in0=out, scalar1=-math.pi, scalar2=None, op0=ALU.is_lt),
             r=[key_out], w=[key_tmp])
        P.op(V, lambda e: e.scalar_tensor_tensor(out=out, in0=tmp, scalar=TWO_PI, in1=out, op0=ALU.mult, op1=ALU.add),
             r=[key_tmp, key_out], w=[key_out])
        P.op(V, lambda e: e.tensor_scalar(out=out, in0=out, scalar1=-3.1415925, scalar2=3.1415925, op0=ALU.max,
                                          op1=ALU.min), r=[key_out], w=[key_out])
        P.op(A, lambda e: e.activation(out=out, in_=out, func=AF.Sin), r=[key_out], w=[key_out])

    def s5_alloc(self):
        P = self.P
        s = self.s5d = {}
        for n in ['LR', 'LI', 'LD', 'lrd', 'lid', 'mag', 'abr', 'abi', 'den', 'fr', 'fi', 'a128r', 'a128i', 'na128i',
                  'nabi', 't0', 't1', 't2']:
            s[n] = self.carve("s5_" + n, [128, 32])
        s['ti'] = self.carve("s5_ti", [128, 512], I32)
        v16 = lambda t: t[:].rearrange("p (j q) -> p j q", q=16)
        s['br'], s['bi'], s['bt'], s['bbr'], s['bbi'] = v16(self.g[8]), v16(self.g[9]), v16(self.g[10]), v16(self.g[22]), v16(self.g[23])
        s['cn'] = self.carve("s5_cn", [128, 2, 64])
        s['H'] = self.carve("s5_H", [128, 2, 4, 5])
        s['e4'] = self.carve("s5_e4", [128, 2, 4, 4])
        s['gg4'] = self.carve("s5_gg4", [128, 2, 4, 4])
        s['hm'] = self.carve("s5_hm", [128, 4, 4])
        s['hm2'] = self.carve("s5_hm2", [128, 2, 4, 4])
        s['x'] = [self.carve(f"s5_x{i}", [128, 512]) for i in range(6)]
        s['p127'] = self.carve("s5_p127", [128, 4, 4])
        s['dsk'] = self.carve("s5_dsk", [128, 8])
        s['glb'] = self.carve("s5_glb", [128, 8])
        s['wu'] = self.carve("s5_wu", [128, 8, 128], BF16)
        s['wg'] = self.carve("s5_wg", [128, 8, 128], BF16)
        s['yo'] = self.carve("s5_yo", [128, 512], BF16)
        if not hasattr(self, 's5yT'):
            self.s5yT = P.dram("s5yT", [1024, S], BF16).ap()

    def s5(self, li):
        P = self.P
        s = self.s5d
        I = self.I
        B = self.bank
        g = self.g
        V, A, G, T = 'vector', 'scalar', 'gpsimd', 'tensor'
        inw = I['ev_in_w'][li]

        def ts(eng, out, in0, s1, s2, op0, op1, r, w):
            if op1 is None:
                P.op(eng, lambda e: e.tensor_scalar(out=out, in0=in0, scalar1=s1, scalar2=None, op0=op0), r=r, w=w)
            else:
                P.op(eng, lambda e: e.tensor_scalar(out=out, in0=in0, scalar1=s1, scalar2=s2, op0=op0, op1=op1), r=r, w=w)

        def tt(eng, out, in0, in1, op, r, w):
            P.op(eng, lambda e: e.tensor_tensor(out=out, in0=in0, in1=in1, op=op), r=r, w=w)

        def stt(eng, out, in0, sc, in1, op0, op1, r, w):
            P.op(eng, lambda e: e.scalar_tensor_tensor(out=out, in0=in0, scalar=sc, in1=in1, op0=op0, op1=op1), r=r, w=w)

        def act(out, in_, func, r, w, **kw):
            P.op(A, lambda e: e.activation(out=out, in_=in_, func=func, **kw), r=r, w=w)
        K = 's5p'
        for kk_ in ['g8', 'g9', 'g10', 'g22', 'g23']:
            pass
        P.dma(s['LR'][:], I['s5_lambda_re'][li].rearrange("(j gl) n -> (gl n) j", gl=2), w=[K], allow_slow_non_contiguous=True)
        P.dma(s['LI'][:], I['s5_lambda_im'][li].rearrange("(j gl) n -> (gl n) j", gl=2), w=[K], allow_slow_non_contiguous=True)
        ld2 = I['s5_log_dt'][li].rearrange("(j gl) -> gl j", gl=2)
        for gl in range(2):
            P.dma(s['LD'][gl * 64:(gl + 1) * 64, :], ld2[gl:gl + 1, :].to_broadcast([64, 32]), w=[K],
                  allow_slow_non_contiguous=True)
        P.dma(s['br'][:], I['s5_b_re'][li].rearrange("(j gl) n q -> (gl n) j q", gl=2), w=[K, 'g8', 'g9', 'g10', 'g22', 'g23'], allow_slow_non_contiguous=True)
        P.dma(s['bi'][:], I['s5_b_im'][li].rearrange("(j gl) n q -> (gl n) j q", gl=2), w=[K, 'g8', 'g9', 'g10', 'g22', 'g23'], allow_slow_non_contiguous=True)
        P.dma(s['dsk'][:], I['s5_d'][li].rearrange("(c p) -> p c", p=128), w=[K], allow_slow_non_contiguous=True)
        P.dma(s['glb'][:], I['s5_glu_b'][li].rearrange("(c p) -> p c", p=128), w=[K], allow_slow_non_contiguous=True)
        act(s['LD'][:], s['LD'][:], AF.Exp, [K], [K])
        tt(V, s['lrd'][:], s['LR'][:], s['LD'][:], ALU.mult, [K], [K])
        tt(V, s['lid'][:], s['LI'][:], s['LD'][:], ALU.mult, [K], [K])
        act(s['mag'][:], s['lrd'][:], AF.Exp, [K], [K])
        ti32 = s['ti'][:, 0:32]
        self.sin_rr(s['abi'][:], s['lid'][:], None, s['t0'][:], ti32, [K], K, K, K)
        self.sin_rr(s['abr'][:], s['lid'][:], None, s['t0'][:], ti32, [K], K, K, K, shift=math.pi / 2)
        tt(V, s['abr'][:], s['abr'][:], s['mag'][:], ALU.mult, [K], [K])
        tt(V, s['abi'][:], s['abi'][:], s['mag'][:], ALU.mult, [K], [K])
        ts(V, s['nabi'][:], s['abi'][:], -1.0, None, ALU.mult, None, [K], [K])
        tt(V, s['den'][:], s['LR'][:], s['LR'][:], ALU.mult, [K], [K])
        tt(V, s['t1'][:], s['LI'][:], s['LI'][:], ALU.mult, [K], [K])
        tt(V, s['den'][:], s['den'][:], s['t1'][:], ALU.add, [K], [K])
        P.op(V, lambda e: e.reciprocal(s['den'][:], s['den'][:]), r=[K], w=[K])
        ts(V, s['t1'][:], s['abr'][:], -1.0, None, ALU.add, None, [K], [K])
        tt(V, s['fr'][:], s['t1'][:], s['LR'][:], ALU.mult, [K], [K])
        tt(V, s['t2'][:], s['abi'][:], s['LI'][:], ALU.mult, [K], [K])
        tt(V, s['fr'][:], s['fr'][:], s['t2'][:], ALU.add, [K], [K])
        tt(V, s['fr'][:], s['fr'][:], s['den'][:], ALU.mult, [K], [K])
        tt(V, s['fi'][:], s['abi'][:], s['LR'][:], ALU.mult, [K], [K])
        tt(V, s['t2'][:], s['t1'][:], s['LI'][:], ALU.mult, [K], [K])
        tt(V, s['fi'][:], s['fi'][:], s['t2'][:], ALU.subtract, [K], [K])
        tt(V, s['fi'][:], s['fi'][:], s['den'][:], ALU.mult, [K], [K])
        bc = lambda t: t[:].unsqueeze(2).to_broadcast([128, 32, 16])
        tt(V, s['bbr'][:], s['br'][:], bc(s['fr']), ALU.mult, [K], [K, 'g8', 'g9', 'g10', 'g22', 'g23'])
        tt(V, s['bt'][:], s['bi'][:], bc(s['fi']), ALU.mult, [K], [K, 'g8', 'g9', 'g10', 'g22', 'g23'])
        tt(V, s['bbr'][:], s['bbr'][:], s['bt'][:], ALU.subtract, [K], [K, 'g8', 'g9', 'g10', 'g22', 'g23'])
        tt(V, s['bbi'][:], s['bi'][:], bc(s['fr']), ALU.mult, [K], [K, 'g8', 'g9', 'g10', 'g22', 'g23'])
        tt(V, s['bt'][:], s['br'][:], bc(s['fi']), ALU.mult, [K], [K, 'g8', 'g9', 'g10', 'g22', 'g23'])
        tt(V, s['bbi'][:], s['bbi'][:], s['bt'][:], ALU.add, [K], [K, 'g8', 'g9', 'g10', 'g22', 'g23'])
        ts(V, s['t1'][:], s['lrd'][:], 128.0, None, ALU.mult, None, [K], [K])
        act(s['t1'][:], s['t1'][:], AF.Exp, [K], [K])
        ts(V, s['t2'][:], s['lid'][:], 128.0, None, ALU.mult, None, [K], [K])
        self.sin_rr(s['a128i'][:], s['t2'][:], None, s['t0'][:], ti32, [K], K, K, K)
        self.sin_rr(s['a128r'][:], s['t2'][:], None, s['t0'][:], ti32, [K], K, K, K, shift=math.pi / 2)
        tt(V, s['a128r'][:], s['a128r'][:], s['t1'][:], ALU.mult, [K], [K])
        tt(V, s['a128i'][:], s['a128i'][:], s['t1'][:], ALU.mult, [K], [K])
        ts(V, s['na128i'][:], s['a128i'][:], -1.0, None, ALU.mult, None, [K], [K])

        PiR, PiI, PoR, PoI, LBr, LBi, CLr, CLi = [g[i] for i in range(8)]
        gk = lambda i: f'g{i}'
        v4 = lambda t: t[:].rearrange("p (j t) -> p j t", t=128)
        iota = self.iota
        for sl in getattr(self, 's5_slices', range(8)):
            j0 = sl * 4
            if getattr(self, 'use_bar', False):
                P.barrier(self.barscr)
            self.load_w(inw, 3200 + sl * 128, 128, s['wu'], 's5_wu')
            for blk in range(8):
                ps, pk = self.proj(s['wu'], 's5_wu', 128, blk * 512)
                P.op(A, lambda e, ps=ps, blk=blk: e.copy(self.big[:, blk * 512:(blk + 1) * 512], ps), r=[pk], w=['big'])
            ang, Sn, Cs, mo, mi, tmp = [g[i] for i in range(8, 14)]
            iob = iota[:].unsqueeze(1).to_broadcast([128, 4, 128])
            tt(V, v4(ang), iob, s['lid'][:, j0:j0 + 4].unsqueeze(2).to_broadcast([128, 4, 128]), ALU.mult,
               ['iota', K], [gk(8)])
            self.sin_rr(Sn[:], ang[:], None, tmp[:], s['ti'][:], [gk(8)], gk(9), gk(13), K)
            self.sin_rr(Cs[:], ang[:], None, tmp[:], s['ti'][:], [gk(8)], gk(10), gk(13), K, shift=math.pi / 2)
            tt(V, v4(ang), iob, s['lrd'][:, j0:j0 + 4].unsqueeze(2).to_broadcast([128, 4, 128]), ALU.mult,
               ['iota', K], [gk(8)])
            act(mo[:], ang[:], AF.Exp, [gk(8)], [gk(11)])
            act(mi[:], ang[:], AF.Exp, [gk(8)], [gk(12)], scale=-1.0)
            tt(V, PoR[:], mo[:], Cs[:], ALU.mult, [gk(11), gk(10)], [gk(2)])
            tt(G, PoI[:], mo[:], Sn[:], ALU.mult, [gk(11), gk(9)], [gk(3)])
            tt(V, PiR[:], mi[:], Cs[:], ALU.mult, [gk(12), gk(10)], [gk(0)])
            stt(V, PiI[:], mi[:], -1.0, Sn[:], ALU.mult, ALU.mult, [gk(12), gk(9)], [gk(1)])
            P.op(V, lambda e: e.tensor_copy(s['p127'][:, :, 0], v4(PoR)[:, :, 127]), r=[gk(2)], w=['s5_p127'])
            P.op(V, lambda e: e.tensor_copy(s['p127'][:, :, 1], v4(PoI)[:, :, 127]), r=[gk(3)], w=['s5_p127'])
            ts(V, s['p127'][:, :, 2], s['p127'][:, :, 1], -1.0, None, ALU.mult, None, ['s5_p127'], ['s5_p127'])
            Xr, Xi = g[8], g[9]
            P.op(G, lambda e: e.memset(Xr[:], 0.0), w=[gk(8)])
            P.op(G, lambda e: e.memset(Xi[:], 0.0), w=[gk(9)])
            for jj in range(4):
                for gl in range(2):
                    off = (2 * jj + gl) * 16
                    ps_ = slice(gl * 64, (gl + 1) * 64)
                    P.op(V, lambda e, jj=jj, off=off, ps_=ps_, j0=j0: e.tensor_copy(v4(Xr)[ps_, jj, off:off + 16], s['bbr'][ps_, j0 + jj, :]),
                         r=[K, 'g22', 'g23'], w=[gk(8)])
                    P.op(V, lambda e, jj=jj, off=off, ps_=ps_, j0=j0: e.tensor_copy(v4(Xi)[ps_, jj, off:off + 16], s['bbi'][ps_, j0 + jj, :]),
                         r=[K, 'g22', 'g23'], w=[gk(9)])
            for X, LBx, kx, ko in [(Xr, LBr, gk(8), gk(4)), (Xi, LBi, gk(9), gk(5))]:
                def ftr(e, X=X):
                    ins = None
                    for jj in range(4):
                        ins = e.transpose(B[4][:, jj * 128:(jj + 1) * 128], v4(X)[:, jj, :], self.ident[:])
                    return ins
                P.op(T, ftr, r=[kx, 'ident'], w=['bank4'])
                P.op(V, lambda e, LBx=LBx: e.tensor_copy(LBx[:], B[4][:, :]), r=['bank4'], w=[ko])
            P.op(G, lambda e: e.memset(CLr[:], 0.0), w=[gk(6)])
            P.op(G, lambda e: e.memset(CLi[:], 0.0), w=[gk(7)])
            for ci, (cname, CLx, kc, sgn) in enumerate([('s5_c_re', CLr, gk(6), 1.0), ('s5_c_im', CLi, gk(7), -1.0)]):
                P.dma(s['cn'][:, ci, :], I[cname][li].rearrange("g p n -> (g p) n")[sl * 128:(sl + 1) * 128, :], w=['s5_cn'])
                P.op(T, lambda e, ci=ci: e.transpose(B[4][0:64, 0:128], s['cn'][:, ci, :], self.ident[:]),
                     r=['s5_cn', 'ident'], w=['bank4'])
                for jj in range(4):
                    c0 = 2 * jj * 16
                    ts(V, v4(CLx)[0:64, jj, c0:c0 + 16], B[4][0:64, c0:c0 + 16], sgn, None, ALU.mult, None, ['bank4'], [kc])
                    ts(V, v4(CLx)[64:128, jj, c0 + 16:c0 + 32], B[4][0:64, c0 + 16:c0 + 32], sgn, None, ALU.mult, None,
                       ['bank4'], [kc])
            P.op(G, lambda e: e.memset(s['H'][:], 0.0), w=['s5_H'])
            AR, AI = s['a128r'][:, j0:j0 + 4], s['a128i'][:, j0:j0 + 4]
            ABR, ABI = s['abr'][:, j0:j0 + 4], s['abi'][:, j0:j0 + 4]
            H = s['H']
            E = s['e4']
            GGt = s['gg4']
            m_ = s['hm']
            for blk in range(8):
                t0 = blk * 512
                ub = self.big[:, t0:t0 + 512]
                sets = [([g[16 + i] for i in range(6)], [gk(16 + i) for i in range(6)]),
                        (s['x'], [f's5_x{i}' for i in range(6)])]
                for jj in range(4):
                    (t1, t2, u1, u2, zr, zi), (k1, k2, ku1, ku2, kzr, kzi) = sets[jj % 2]
                    sr, si = g[8 + 2 * jj], g[9 + 2 * jj]
                    ksr, ksi = gk(8 + 2 * jj), gk(9 + 2 * jj)
                    bR, bI = (B[2], B[3]) if jj % 2 == 0 else (B[4], B[5])
                    kR, kI = ('bank2', 'bank3') if jj % 2 == 0 else ('bank4', 'bank5')
                    P.op(T, lambda e, jj=jj, ub=ub, bR=bR: e.matmul(bR[:, :], v4(LBr)[:, jj, :], ub, start=True, stop=True),
                         r=[gk(4), 'big'], w=[kR])
                    P.op(T, lambda e, jj=jj, ub=ub, bI=bI: e.matmul(bI[:, :], v4(LBi)[:, jj, :], ub, start=True, stop=True),
                         r=[gk(5), 'big'], w=[kI])
                    b2 = bR[:, :].rearrange("p (c t) -> p c t", t=128)
                    b3 = bI[:, :].rearrange("p (c t) -> p c t", t=128)
                    tb = lambda tab, jj=jj: v4(tab)[:, jj, :].unsqueeze(1).to_broadcast([128, 4, 128])
                    tt(V, v4(t1), b2, tb(PiR), ALU.mult, [kR, gk(0)], [k1])
                    tt(V, v4(t2), b3, tb(PiI), ALU.mult, [kI, gk(1)], [k2])
                    tt(G, zr[:], t1[:], t2[:], ALU.subtract, [k1, k2], [kzr])
                    tt(V, v4(u1), b2, tb(PiI), ALU.mult, [kR, gk(1)], [ku1])
                    tt(V, v4(u2), b3, tb(PiR), ALU.mult, [kI, gk(0)], [ku2])
                    tt(G, zi[:], u1[:], u2[:], ALU.add, [ku1, ku2], [kzi])
                    P.op(V, lambda e, zr=zr, sr=sr: e.tensor_tensor_scan(out=sr[:], data0=self.m128[:], data1=zr[:], initial=0.0,
                                                                       op0=ALU.mult, op1=ALU.add), r=['m128', kzr], w=[ksr])
                    P.op(V, lambda e, zi=zi, si=si: e.tensor_tensor_scan(out=si[:], data0=self.m128[:], data1=zi[:], initial=0.0,
                                                                       op0=ALU.mult, op1=ALU.add), r=['m128', kzi], w=[ksi])
                    pr_, pi_, npi_ = s['p127'][:, jj, 0:1], s['p127'][:, jj, 1:2], s['p127'][:, jj, 2:3]
                    ts(V, E[:, 0, jj, :], v4(sr)[:, :, 127], pr_, None, ALU.mult, None, [ksr, 's5_p127'], ['s5_e'])
                    stt(V, E[:, 0, jj, :], v4(si)[:, :, 127], npi_, E[:, 0, jj, :], ALU.mult, ALU.add, [ksi, 's5_p127', 's5_e'], ['s5_e'])
                    ts(V, E[:, 1, jj, :], v4(sr)[:, :, 127], pi_, None, ALU.mult, None, [ksr, 's5_p127'], ['s5_e'])
                    stt(V, E[:, 1, jj, :], v4(si)[:, :, 127], pr_, E[:, 1, jj, :], ALU.mult, ALU.add, [ksi, 's5_p127', 's5_e'], ['s5_e'])
                KH = ['s5_H', 's5_e', K, 's5_hm']
                P.op(V, lambda e: e.tensor_copy(H[:, :, :, 0], H[:, :, :, 4]), r=['s5_H'], w=['s5_H'])
                for c_ in range(4):
                    hr, hi = H[:, 0, :, c_], H[:, 1, :, c_]
                    hrn, hin = H[:, 0, :, c_ + 1], H[:, 1, :, c_ + 1]
                    tt(V, m_[:, 0, :], hr, AR, ALU.mult, KH, ['s5_hm'])
                    tt(V, m_[:, 1, :], hi, AI, ALU.mult, KH, ['s5_hm'])
                    tt(V, m_[:, 2, :], hi, AR, ALU.mult, KH, ['s5_hm'])
                    tt(V, m_[:, 3, :], hr, AI, ALU.mult, KH, ['s5_hm'])
                    tt(V, hrn, m_[:, 0, :], m_[:, 1, :], ALU.subtract, KH, ['s5_H'])
                    tt(V, hrn, hrn, E[:, 0, :, c_], ALU.add, KH, ['s5_H'])
                    tt(V, hin, m_[:, 2, :], m_[:, 3, :], ALU.add, KH, ['s5_H'])
                    tt(V, hin, hin, E[:, 1, :, c_], ALU.add, KH, ['s5_H'])
                bc4 = lambda t: t.unsqueeze(2).to_broadcast([128, 4, 4])
                KG = ['s5_H', K, 's5_gg', 's5_hm2']
                m2_ = s['hm2']
                tt(V, GGt[:, 0, :, :], H[:, 0, :, 0:4], bc4(ABR), ALU.mult, KG, ['s5_gg'])
                tt(V, m2_[:, 0, :, :], H[:, 1, :, 0:4], bc4(ABI), ALU.mult, KG, ['s5_hm2'])
                tt(V, GGt[:, 0, :, :], GGt[:, 0, :, :], m2_[:, 0, :, :], ALU.subtract, KG, ['s5_gg'])
                tt(V, GGt[:, 1, :, :], H[:, 1, :, 0:4], bc4(ABR), ALU.mult, KG, ['s5_gg'])
                tt(V, m2_[:, 1, :, :], H[:, 0, :, 0:4], bc4(ABI), ALU.mult, KG, ['s5_hm2'])
                tt(V, GGt[:, 1, :, :], GGt[:, 1, :, :], m2_[:, 1, :, :], ALU.add, KG, ['s5_gg'])
                for jj in range(4):
                    (t1, t2, u1, u2, zr, zi), (k1, k2, ku1, ku2, kzr, kzi) = sets[jj % 2]
                    sr, si = g[8 + 2 * jj], g[9 + 2 * jj]
                    ksr, ksi = gk(8 + 2 * jj), gk(9 + 2 * jj)
                    tb = lambda tab, jj=jj: v4(tab)[:, jj, :].unsqueeze(1).to_broadcast([128, 4, 128])
                    tt(G, v4(sr), v4(sr), GGt[:, 0, jj, :].unsqueeze(2).to_broadcast([128, 4, 128]), ALU.add, [ksr, 's5_gg'], [ksr])
                    tt(G, v4(si), v4(si), GGt[:, 1, jj, :].unsqueeze(2).to_broadcast([128, 4, 128]), ALU.add, [ksi, 's5_gg'], [ksi])
                    tt(V, v4(t1), v4(sr), tb(PoR), ALU.mult, [ksr, gk(2)], [k1])
                    tt(G, v4(t2), v4(si), tb(PoI), ALU.mult, [ksi, gk(3)], [k2])
                    tt(V, zr[:], t1[:], t2[:], ALU.subtract, [k1, k2], [kzr])
                    tt(V, v4(u1), v4(sr), tb(PoI), ALU.mult, [ksr, gk(3)], [ku1])
                    tt(G, v4(u2), v4(si), tb(PoR), ALU.mult, [ksi, gk(2)], [ku2])
                    tt(V, zi[:], u1[:], u2[:], ALU.add, [ku1, ku2], [kzi])

                    def fy(e, jj=jj, zr=zr, zi=zi):
                        e.matmul(B[6][:, :], v4(CLr)[:, jj, :], zr[:], start=(jj == 0), stop=False)
                        return e.matmul(B[6][:, :], v4(CLi)[:, jj, :], zi[:], start=False, stop=(jj == 3))
                    P.op(T, fy, r=[gk(6), gk(7), kzr, kzi], w=['bank6'])
                y, y2 = g[16], g[17]
                stt(V, y[:], ub, s['dsk'][:, sl:sl + 1], B[6][:, :], ALU.mult, ALU.add, ['big', K, 'bank6'], [gk(16)])
                if self.debug and self.dbgsel == 's5y' and li == 0:
                    P.dma(self.dbg[sl * 128:(sl + 1) * 128, t0:t0 + 512], y[:], r=[gk(16)])
                tt(G, y2[:], y[:], y[:], ALU.mult, [gk(16)], [gk(17)])
                ts(V, y2[:], y2[:], 0.044715, 1.0, ALU.mult, ALU.add, [gk(17)], [gk(17)])
                tt(G, y2[:], y2[:], y[:], ALU.mult, [gk(17), gk(16)], [gk(17)])
                act(y2[:], y2[:], AF.Sigmoid, [gk(17)], [gk(17)], scale=2.0 * math.sqrt(2.0 / math.pi))
                tt(V, s['yo'][:], y[:], y2[:], ALU.mult, [gk(16), gk(17)], ['s5_yo'])
                P.dma(self.s5yT[sl * 128:(sl + 1) * 128, t0:t0 + 512], s['yo'][:], r=['s5_yo'], w=['s5yT'])
        yb = self.ybig[:, 0:4096].rearrange("p (k t) -> p k t", t=512)
        for f in range(8):
            self.load_w(I['s5_glu_w'][li], f * 128, 128, s['wg'], 's5_wg')
            self.load_w(inw, 5504 + f * 128, 128, s['wu'], 's5_wu')
            for blk in range(8):
                t0 = blk * 512
                P.dma(yb, self.s5yT[:, t0:t0 + 512].rearrange("(kt p) t -> p kt t", p=128), r=['s5yT'], w=['big'])

                def fg(e):
                    ins = None
                    for kt in range(8):
                        ins = e.matmul(B[2][:, :], s['wg'][:, kt, :], yb[:, kt, :], start=(kt == 0), stop=(kt == 7))
                    return ins
                P.op(T, fg, r=['s5_wg', 'big'], w=['bank2'])
                sg, zs = g[14], g[15]
                act(sg[:], B[2][:, :], AF.Sigmoid, ['bank2', K], [gk(14)], bias=s['glb'][:, f:f + 1])
                tt(V, sg[:], sg[:], yb[:, f, :], ALU.mult, [gk(14), 'big'], [gk(14)])
                ps, pk = self.proj(s['wu'], 's5_wu', 128, t0)
                act(zs[:], ps, AF.Silu, [pk], [gk(15)])
                if self.debug and self.dbgsel == 'b_out' and li == 0:
                    P.dma(self.dbg[f * 128:(f + 1) * 128, t0:t0 + 512], sg[:], r=[gk(14)])
                tt(V, s['yo'][:], sg[:], zs[:], ALU.mult, [gk(14), gk(15)], ['s5_yo'])
                P.dma(self.ycT[1024 + f * 128:1024 + (f + 1) * 128, t0:t0 + 512], s['yo'][:], r=['s5_yo'], w=['ycT'])

    def final_phase(self):
        P = self.P
        V, A, G, T = 'vector', 'scalar', 'gpsimd', 'tensor'
        B = self.bank
        fw = P.sb("fnw", [128, 8])
        P.dma(fw[:], self.I['final_norm_w'].rearrange("(kt p) -> p kt", p=128), w=['fnw'], allow_slow_non_contiguous=True)
        sq = [self.g[0], self.g[1]]
        rinv = self.g[2]
        ot = [self.g[4 + i] for i in range(4)]
        NB = 256
        for blk in range(S // NB):
            i = blk % 2
            t0 = blk * NB
            hb = self.hb[i]
            P.dma(hb[:], self.hT[:, t0:t0 + NB].rearrange("(kt p) t -> p kt t", p=128), r=['hT'], w=[f'hb{i}'])
            for kt in range(8):
                j = kt % 2
                P.op(A, lambda e, kt=kt, j=j, hb=hb: e.activation(out=sq[j][:, 0:NB], in_=hb[:, kt, :], func=AF.Square),
                     r=[f'hb{i}'], w=[f'g{j}'])
                P.op(T, lambda e, kt=kt, j=j: e.matmul(B[2][:, 0:NB], self.ones[:], sq[j][:, 0:NB], start=(kt == 0),
                                                       stop=(kt == 7)), r=[f'g{j}', 'ones'], w=['bank2'])
            P.op(A, lambda e: e.activation(out=rinv[:, 0:NB], in_=B[2][:, 0:NB], func=AF.Sqrt, scale=1.0 / D,
                                           bias=self.eps6[:, 0:1]), r=['bank2', 'eps'], w=['g2'])
            P.op(V, lambda e: e.reciprocal(rinv[:, 0:NB], rinv[:, 0:NB]), r=['g2'], w=['g2'])
            for kt in range(8):
                P.op(V, lambda e, kt=kt, hb=hb: e.scalar_tensor_tensor(
                    out=hb[:, kt, :], in0=hb[:, kt, :], scalar=fw[:, kt:kt + 1], in1=rinv[:, 0:NB], op0=ALU.mult,
                    op1=ALU.mult), r=[f'hb{i}', 'g2', 'fnw'], w=[f'hb{i}'])
            for sub in range(2):
                oi = (blk * 2 + sub) % 2
                for half in range(2):
                    bk = B[3 + half]

                    def fn(e, hb=hb, sub=sub, half=half, bk=bk):
                        ins = None
                        for q in range(4):
                            kt = half * 4 + q
                            ins = e.transpose(bk[:, q * 128:(q + 1) * 128], hb[:, kt, sub * 128:(sub + 1) * 128], self.ident[:])
                        return ins
                    P.op(T, fn, r=[f'hb{i}', 'ident'], w=[f'bank{3 + half}'])
                    o = ot[oi * 2 + half]
                    if half == 0:
                        P.op(V, lambda e, o=o, bk=bk: e.tensor_copy(o[:], bk[:, :]), r=['bank3'], w=[f'g{4 + oi * 2 + half}'])
                    else:
                        P.op(A, lambda e, o=o, bk=bk: e.copy(o[:], bk[:, :]), r=['bank4'], w=[f'g{4 + oi * 2 + half}'])
                    r0 = t0 + sub * 128
                    P.dma(self.out[r0:r0 + 128, half * 512:(half + 1) * 512], o[:], r=[f'g{4 + oi * 2 + half}'], q='gpsimd')

    def even_layer(self, layer):
        li = layer // 2
        self.norm_phase(layer)
        self.phase()
        self.rwkv_alloc()
        self.rwkv(li)
        self.phase()
        self.s5_alloc()
        self.s5(li)
        self.phase()
        self.mem_attn(layer, self.I['ev_in_w'][li], 4224, 6528, 2048)
        self.out_proj(self.I['ev_out_w'][li])
        self.phase()


    def ssd(self, li):
        P = self.P
        I = self.I
        B = self.bank
        g = self.g
        V, A, G, T = 'vector', 'scalar', 'gpsimd', 'tensor'
        inw = I['od_in_w'][li]
        gk = lambda i: f'g{i}'

        def ts(eng, out, in0, s1, s2, op0, op1, r, w):
            if op1 is None:
                P.op(eng, lambda e: e.tensor_scalar(out=out, in0=in0, scalar1=s1, scalar2=None, op0=op0), r=r, w=w)
            else:
                P.op(eng, lambda e: e.tensor_scalar(out=out, in0=in0, scalar1=s1, scalar2=s2, op0=op0, op1=op1), r=r, w=w)

        def tt(eng, out, in0, in1, op, r, w):
            P.op(eng, lambda e: e.tensor_tensor(out=out, in0=in0, in1=in1, op=op), r=r, w=w)

        def stt(eng, out, in0, sc, in1, op0, op1, r, w):
            P.op(eng, lambda e: e.scalar_tensor_tensor(out=out, in0=in0, scalar=sc, in1=in1, op0=op0, op1=op1), r=r, w=w)

        def act(out, in_, func, r, w, **kw):
            P.op(A, lambda e: e.activation(out=out, in_=in_, func=func, **kw), r=r, w=w)
        c = self.carve
        raw = [c("sd_raw0", [128, 515]), c("sd_raw1", [128, 515])]
        carry = c("sd_carry", [128, 12, 3])
        cw = c("sd_cw", [128, 12, 4])
        cb = c("sd_cb", [128, 12])
        state = c("sd_state", [128, 2, 512])
        stack = c("sd_stack", [16, 3, 512])
        dtmp = c("sd_dtmp", [16, 512])
        prm = c("sd_prm", [16, 4])
        dg = c("sd_dg", [16, 16])
        sel = c("sd_sel", [16, 16, 128])
        nwb = c("sd_nwb", [128, 1024])
        dskb = c("sd_dskb", [128, 16])
        smT = c("sd_smT", [128, 4, 16])
        dec = c("sd_dec", [128, 16])
        M1 = [c("sd_M10", [128, 128]), c("sd_M11", [128, 128])]
        Lt = [c("sd_L0", [128, 128]), c("sd_L1", [128, 128])]
        CBm = c("sd_CBm", [128, 128])
        mui = c("sd_mui", [128, 128])
        ssq = c("sd_ssq", [128, 4])
        yoT = c("sd_yoT", [128, 4, 128], BF16)
        wsl = c("sd_wsl", [128, 8, 128], BF16)
        L4b = c("sd_L4b", [128, 512])
        wzc = self.ybig[:, 0:8192].rearrange("p (k m) -> p k m", m=1024)
        KP = 'sdp'
        for k_ in range(4):
            P.dma(cw[:, :, k_], I['m2_conv_w'][li][k_].rearrange("(c p) -> p c", p=128), w=[KP], allow_slow_non_contiguous=True)
        P.dma(cb, I['m2_conv_b'][li].rearrange("(c p) -> p c", p=128), w=[KP], allow_slow_non_contiguous=True)
        P.dma(prm[:, 0:1], I['m2_dt_bias'][li].rearrange("(p o) -> p o", o=1), w=[KP], allow_slow_non_contiguous=True)
        P.dma(prm[:, 1:2], I['m2_a_log'][li].rearrange("(p o) -> p o", o=1), w=[KP], allow_slow_non_contiguous=True)
        act(prm[:, 1:2], prm[:, 1:2], AF.Exp, [KP], [KP])
        ts(V, prm[:, 1:2], prm[:, 1:2], -1.0, None, ALU.mult, None, [KP], [KP])
        P.dma(dskb, I['m2_d'][li].rearrange("(o h) -> o h", o=1).to_broadcast([128, 16]), w=[KP], allow_slow_non_contiguous=True)
        P.dma(nwb, I['m2_norm_w'][li].rearrange("(o h) -> o h", o=1).to_broadcast([128, 1024]), w=[KP],
              allow_slow_non_contiguous=True)
        P.dma(sel, I['sel16'].rearrange("h (k s) -> h k s", s=128), w=[KP])
        P.dma(mui, I['mui128'], w=[KP])
        P.op(G, lambda e: e.memset(carry, 0.0), w=['sd_carry'])
        P.op(G, lambda e: e.memset(state, 0.0), w=['sd_state'])
        for q in range(8):
            self.load_w_to(inw, 2512 + q * 128, 128, wzc[:, :, q * 128:(q + 1) * 128], 'big')
        dstg = [g[i] for i in range(12)]
        for blk in range(8):
            t0 = blk * 512
            for sl in range(12):
                self.load_w(inw, sl * 128, 128, wsl, 'sd_wsl')
                rw_ = raw[sl % 2]
                rk = f'sd_raw{sl % 2}'
                ps, pk = self.proj(wsl, 'sd_wsl', 128, t0)
                P.op(G, lambda e, rw_=rw_, sl=sl: e.tensor_copy(rw_[:, 0:3], carry[:, sl, :]), r=['sd_carry'], w=[rk])
                P.op(A, lambda e, rw_=rw_, ps=ps: e.copy(rw_[:, 3:515], ps), r=[pk], w=[rk])
                P.op(G, lambda e, rw_=rw_, sl=sl: e.tensor_copy(carry[:, sl, :], rw_[:, 512:515]), r=[rk], w=['sd_carry'])
                d_ = dstg[sl]
                dk = gk(sl)
                ts(V, d_[:], rw_[:, 0:512], cw[:, sl, 0:1], cb[:, sl:sl + 1], ALU.mult, ALU.add, [rk, KP], [dk])
                for k_ in range(1, 4):
                    stt(V, d_[:], rw_[:, k_:k_ + 512], cw[:, sl, k_:k_ + 1], d_[:], ALU.mult, ALU.add,
                        [rk, KP, dk], [dk])
                act(d_[:], d_[:], AF.Silu, [dk], [dk])
            self.load_w(inw, 1536, 16, wsl, 'sd_wsl')
            ps, pk = self.proj(wsl, 'sd_wsl', 16, t0)
            ts(V, dtmp, ps, prm[:, 0:1], None, ALU.add, None, [pk, KP], ['sd_dtmp'])
            act(stack[:, 2, :], dtmp, AF.Abs, ['sd_dtmp'], ['sd_stack'])
            act(stack[:, 2, :], stack[:, 2, :], AF.Exp, ['sd_stack'], ['sd_stack'], scale=-1.0)
            act(stack[:, 2, :], stack[:, 2, :], AF.Ln, ['sd_stack', 'eps'], ['sd_stack'], bias=self.eps6[0:16, 3:4])
            stt(V, stack[:, 0, :], dtmp, 0.0, stack[:, 2, :], ALU.max, ALU.add, ['sd_dtmp', 'sd_stack'], ['sd_stack'])
            ts(V, dtmp, stack[:, 0, :], prm[:, 1:2], None, ALU.mult, None, ['sd_stack', KP], ['sd_dtmp'])
            P.op(V, lambda e: e.tensor_tensor_scan(out=stack[:, 1, :], data0=self.m128[0:16, :], data1=dtmp, initial=0.0,
                                                   op0=ALU.mult, op1=ALU.add), r=['m128', 'sd_dtmp'], w=['sd_stack'])
            ac3 = stack[:, 1, :].rearrange("p (c t) -> p c t", t=128)
            tt(V, stack[:, 2, :].rearrange("p (c t) -> p c t", t=128), ac3[:, :, 127:128].to_broadcast([16, 4, 128]), ac3,
               ALU.subtract, ['sd_stack'], ['sd_stack'])
            act(stack[:, 2, :], stack[:, 2, :], AF.Exp, ['sd_stack'], ['sd_stack'])
            for cc in range(4):
                tc0 = cc * 128
                cs = slice(tc0, tc0 + 128)
                tg = t0 + tc0
                zt = [g[19], g[20]]
                for half in range(2):
                    i = self.pj_i
                    self.pj_i ^= 1

                    def fz(e, i=i, half=half, tg=tg):
                        ins = None
                        for kt in range(8):
                            ins = e.matmul(B[i][:, :], self.xnT[:, kt, tg:tg + 128], wzc[:, kt, half * 512:(half + 1) * 512],
                                           start=(kt == 0), stop=(kt == 7))
                        return ins
                    P.op(T, fz, r=['big', 'xnT'], w=[f'bank{i}'])
                    act(zt[half][:], B[i][:, :], AF.Silu, [f'bank{i}'], [gk(19 + half)])
                xsT = [g[12], g[13]]
                for half in range(2):
                    def ftx(e, half=half, cs=cs):
                        ins = None
                        for q in range(4):
                            ins = e.transpose(B[2][:, q * 128:(q + 1) * 128], dstg[half * 4 + q][:, cs], self.ident[:])
                        return ins
                    P.op(T, ftx, r=[gk(half * 4 + q) for q in range(4)] + ['ident'], w=['bank2'])
                    P.op(V, lambda e, half=half: e.tensor_copy(xsT[half][:], B[2][:, :]), r=['bank2'], w=[gk(12 + half)])

                def ftb(e, cs=cs):
                    e.transpose(B[3][:, 0:128], dstg[8][:, cs], self.ident[:])
                    e.transpose(B[3][:, 128:256], dstg[9][:, cs], self.ident[:])
                    ins = None
                    for q in range(3):
                        ins = e.transpose(B[3][:, 256 + q * 16:256 + (q + 1) * 16], stack[:, q, cs], self.ident[0:16, 0:16])
                    return ins
                P.op(T, ftb, r=[gk(8), gk(9), 'sd_stack', 'ident'], w=['bank3'])
                Bt = g[18]
                P.op(V, lambda e: e.tensor_copy(Bt[:, 0:256], B[3][:, 0:256]), r=['bank3'], w=[gk(18)])
                P.op(V, lambda e: e.tensor_copy(smT[:, 0:3, :], B[3][:, 256:304].rearrange("p (q h) -> p q h", h=16)),
                     r=['bank3'], w=['sd_smT'])
                act(smT[:, 3, :], smT[:, 1, :], AF.Exp, ['sd_smT'], ['sd_smT'])
                xdt = [g[14], g[15]]
                xdd = [g[16], g[17]]
                v8 = lambda t: t[:].rearrange("p (h q) -> p h q", q=64)
                for half in range(2):
                    hs_ = slice(half * 8, (half + 1) * 8)
                    tt(V, v8(xdt[half]), v8(xsT[half]), smT[:, 0, hs_].unsqueeze(2).to_broadcast([128, 8, 64]), ALU.mult,
                       [gk(12 + half), 'sd_smT'], [gk(14 + half)])
                    tt(G, v8(xdd[half]), v8(xdt[half]), smT[:, 2, hs_].unsqueeze(2).to_broadcast([128, 8, 64]), ALU.mult,
                       [gk(14 + half), 'sd_smT'], [gk(16 + half)])
                ce = tc0 + 127
                ts(V, dg, self.ident[0:16, 0:16], stack[:, 1, ce:ce + 1], None, ALU.mult, None, ['ident', 'sd_stack'], ['sd_dg'])
                P.op(T, lambda e: e.matmul(B[4][:, 0:16], self.ones[0:16, :], dg, start=True, stop=True),
                     r=['ones', 'sd_dg'], w=['bank4'])
                act(dec, B[4][:, 0:16], AF.Exp, ['bank4'], ['sd_dec'])
                for gq in range(2):
                    BTf, CTf = dstg[8 + gq], dstg[10 + gq]
                    P.op(T, lambda e, BTf=BTf, CTf=CTf, cs=cs: e.matmul(B[4][:, 128:256], BTf[:, cs], CTf[:, cs], start=True,
                                                                        stop=True), r=[gk(8 + gq), gk(10 + gq)], w=['bank4'])
                    tt(V, CBm, B[4][:, 128:256], mui, ALU.mult, ['bank4', KP], ['sd_CBm'])
                    P.op(T, lambda e, CTf=CTf, cs=cs, gq=gq: e.matmul(B[5][:, :], CTf[:, cs], state[:, gq, :], start=True,
                                                                      stop=True), r=[gk(10 + gq), 'sd_state'], w=['bank5'])
                    v4h = lambda t: t[:].rearrange("p (h l) -> p h l", l=128)
                    for hb_ in range(2):
                        h0 = gq * 8 + hb_ * 4
                        L4, M4 = (g[23], g[22]) if hb_ == 0 else (L4b, g[22])
                        kL, kM = (gk(23), gk(22)) if hb_ == 0 else ('sd_L4b', gk(22))

                        def fsel(e, h0=h0, cs=cs):
                            ins = None
                            for q in range(4):
                                ins = e.matmul(B[7][:, q * 128:(q + 1) * 128], sel[:, h0 + q, :], stack[:, 1, cs], start=True, stop=True)
                            return ins
                        P.op(T, fsel, r=[KP, 'sd_stack'], w=['bank7'])
                        tt(V, v4h(L4), B[7][:, :].rearrange("p (h l) -> p h l", l=128),
                           smT[:, 1, h0:h0 + 4].unsqueeze(2).to_broadcast([128, 4, 128]), ALU.subtract, ['bank7', 'sd_smT'], [kL])
                        ts(V, L4[:], L4[:], 0.0, None, ALU.min, None, [kL], [kL])
                        act(L4[:], L4[:], AF.Exp, [kL], [kL])
                        tt(G, v4h(M4), v4h(L4), CBm.unsqueeze(1).to_broadcast([128, 4, 128]), ALU.mult, [kL, 'sd_CBm'], [kM])

                        def fyd(e, hb_=hb_, gq=gq, M4=M4):
                            ins = None
                            for q in range(4):
                                hl = hb_ * 4 + q
                                ins = e.matmul(B[6][:, hl * 64:(hl + 1) * 64], v4h(M4)[:, q, :], v8(xdt[gq])[:, hl, :], start=True,
                                               stop=True)
                            return ins
                        P.op(T, fyd, r=[kM, gk(14 + gq)], w=['bank6'])
                    yg, tmpy = g[21], g[22]
                    hs_ = slice(gq * 8, (gq + 1) * 8)
                    tt(V, v8(yg), B[5][:, :].rearrange("p (h q) -> p h q", q=64),
                       smT[:, 3, hs_].unsqueeze(2).to_broadcast([128, 8, 64]), ALU.mult, ['bank5', 'sd_smT'], [gk(21)])
                    tt(V, yg[:], yg[:], B[6][:, :], ALU.add, [gk(21), 'bank6'], [gk(21)])
                    tt(G, v8(tmpy), v8(xsT[gq]), dskb[:, hs_].unsqueeze(2).to_broadcast([128, 8, 64]), ALU.mult,
                       [gk(12 + gq), KP], [gk(22)])
                    tt(V, yg[:], yg[:], tmpy[:], ALU.add, [gk(21), gk(22)], [gk(21)])
                    tt(V, yg[:], yg[:], zt[gq][:], ALU.mult, [gk(21), gk(19 + gq)], [gk(21)])
                    P.op(T, lambda e, gq=gq: e.matmul(B[5][:, :], Bt[:, gq * 128:(gq + 1) * 128], xdd[gq][:], start=True,
                                                      stop=True), r=[gk(18), gk(16 + gq)], w=['bank5'])
                    stg = state[:, gq, :].rearrange("p (h q) -> p h q", q=64)
                    tt(V, stg, stg, dec[:, hs_].unsqueeze(2).to_broadcast([128, 8, 64]), ALU.mult, ['sd_state', 'sd_dec'],
                       ['sd_state'])
                    tt(V, state[:, gq, :], state[:, gq, :], B[5][:, :], ALU.add, ['sd_state', 'bank5'], ['sd_state'])
                    act(tmpy[:], yg[:], AF.Square, [gk(21)], [gk(22), 'sd_ssq'], accum_out=ssq[:, 0:1])
                    act(ssq[:, 1:2], ssq[:, 0:1], AF.Sqrt, ['sd_ssq', 'eps'], ['sd_ssq'], scale=1.0 / 512,
                        bias=self.eps6[:, 0:1])
                    P.op(V, lambda e: e.reciprocal(ssq[:, 1:2], ssq[:, 1:2]), r=['sd_ssq'], w=['sd_ssq'])
                    stt(V, yg[:], yg[:], ssq[:, 1:2], nwb[:, gq * 512:(gq + 1) * 512], ALU.mult, ALU.mult,
                        [gk(21), 'sd_ssq', KP], [gk(21)])

                    def fty(e):
                        ins = None
                        for q in range(4):
                            ins = e.transpose(B[2][:, q * 128:(q + 1) * 128], yg[:, q * 128:(q + 1) * 128], self.ident[:])
                        return ins
                    P.op(T, fty, r=[gk(21), 'ident'], w=['bank2'])
                    if self.debug and self.dbgsel == 'c_out' and li == 0:
                        P.op(V, lambda e: e.tensor_copy(tmpy[:], B[2][:, :]), r=['bank2'], w=[gk(22)])
                        P.dma(self.dbg[gq * 512:(gq + 1) * 512, tg:tg + 128].rearrange("(q p) t -> p q t", p=128),
                              tmpy[:].rearrange("p (q t) -> p q t", t=128), r=[gk(22)])
                    P.op(V, lambda e: e.tensor_copy(yoT, B[2][:, :].rearrange("p (q t) -> p q t", t=128)), r=['bank2'],
                         w=['sd_yoT'])
                    P.dma(self.ycT[gq * 512:(gq + 1) * 512, tg:tg + 128].rearrange("(q p) t -> p q t", p=128), yoT,
                          r=['sd_yoT'], w=['ycT'])

    def rope_setup(self):
        P = self.P
        V, A, G, T = 'vector', 'scalar', 'gpsimd', 'tensor'
        self.csT = P.dram("csT", [2, 64, S], F32).ap()
        self.phase()
        c = self.carve
        posi = c("rp_posi", [64, 512], I32)
        posf = c("rp_posf", [64, 512])
        ang = c("rp_ang", [64, 512])
        sn = c("rp_sn", [64, 512])
        tmp = c("rp_tmp", [64, 512])
        tmi = c("rp_tmi", [64, 512], I32)
        invf = c("rp_invf", [64, 1])
        P.dma(invf, self.I['invf64'], w=['rp_invf'])
        for blk in range(8):
            t0 = blk * 512
            P.dma(posi, self.I['positions'][0:1, t0:t0 + 512].to_broadcast([64, 512]), w=['rp_posi'],
                  allow_slow_non_contiguous=True)
            P.op(V, lambda e: e.tensor_copy(posf, posi), r=['rp_posi'], w=['rp_posf'])
            P.op(V, lambda e: e.tensor_scalar(out=ang, in0=posf, scalar1=invf[:, 0:1], scalar2=None, op0=ALU.mult),
                 r=['rp_posf', 'rp_invf'], w=['rp_ang'])
            for q, sh in [(0, math.pi / 2), (1, 0.0)]:
                self.sin_rr(sn, ang, None, tmp, tmi, ['rp_ang'], 'rp_sn', 'rp_tmp', 'rp_tmi', shift=sh)
                P.dma(self.csT[q, :, t0:t0 + 512], sn, r=['rp_sn'], w=['csT'])

    def mla(self, li):
        P = self.P
        I = self.I
        B = self.bank
        g = self.g
        V, A, G, T = 'vector', 'scalar', 'gpsimd', 'tensor'
        inw = I['od_in_w'][li]
        gk = lambda i: f'g{i}'
        SC = 192 ** -0.5

        def ts(eng, out, in0, s1, s2, op0, op1, r, w):
            if op1 is None:
                P.op(eng, lambda e: e.tensor_scalar(out=out, in0=in0, scalar1=s1, scalar2=None, op0=op0), r=r, w=w)
            else:
                P.op(eng, lambda e: e.tensor_scalar(out=out, in0=in0, scalar1=s1, scalar2=s2, op0=op0, op1=op1), r=r, w=w)

        def tt(eng, out, in0, in1, op, r, w):
            P.op(eng, lambda e: e.tensor_tensor(out=out, in0=in0, in1=in1, op=op), r=r, w=w)

        def stt(eng, out, in0, sc, in1, op0, op1, r, w):
            P.op(eng, lambda e: e.scalar_tensor_tensor(out=out, in0=in0, scalar=sc, in1=in1, op0=op0, op1=op1), r=r, w=w)

        def act(out, in_, func, r, w, **kw):
            P.op(A, lambda e: e.activation(out=out, in_=in_, func=func, **kw), r=r, w=w)
        if not hasattr(self, 'cqnT'):
            self.cqnT = P.dram("cqnT", [384, S], BF16).ap()
            self.ckvnT = P.dram("ckvnT", [256, S], BF16).ap()
        c = self.carve
        kpe = c("ml_kpe", [64, S], BF16)
        qpe = c("ml_qpe", [64, S], BF16)
        vv = c("ml_v", [128, 32, 128], BF16)
        qn = self.ybig[:, 0:4096]
        kn = self.ybig[:, 4096:8192]
        wsl = c("ml_wsl", [128, 8, 128], BF16)
        wkr = c("ml_wkr", [128, 8, 64], BF16)
        wkrot = c("ml_wkrot", [128, 8, 64], BF16)
        nrm = c("ml_nrm", [128, 5])
        cs_c, cs_s = g[8][0:64, :], g[9][0:64, :]
        P.alias['ml_cs'] = 'g8'
        cqb = c("ml_cqb", [128, 3, 512], BF16)
        ckb = c("ml_ckb", [128, 2, 512], BF16)
        wq_st = self.wst[0][:].rearrange("p a b -> p (a b)")[:, 0:576].rearrange("p (a b) -> p a b", b=192)
        wkv_st = self.wst[1][:].rearrange("p a b -> p (a b)")[:, 0:512].rearrange("p (a b) -> p a b", b=256)
        P.alias['ml_wqst'] = 'wst0'
        P.alias['ml_wkvst'] = 'wst1'
        wqn = c("ml_wqn", [128, 3, 128], BF16)
        wqp = c("ml_wqp", [128, 3, 64], BF16)
        wqrot = c("ml_wqrot", [128, 3, 64], BF16)
        wkn = c("ml_wkn", [128, 2, 128], BF16)
        wv = c("ml_wv", [128, 2, 128], BF16)
        identb = c("ml_identb", [128, 128], BF16)
        mneg = g[21][:, 0:128]
        mx2 = c("ml_mx", [128, 48])
        rs2 = c("ml_rs", [128, 48])
        PT = [g[16][:].bitcast(BF16)[:, 0:512].rearrange("p (j q) -> p j q", q=128),
              g[17][:].bitcast(BF16)[:, 0:512].rearrange("p (j q) -> p j q", q=128)]
        Pb = [g[14][:].bitcast(BF16)[:, 0:512], g[15][:].bitcast(BF16)[:, 0:512]]
        sd = g[18][:, 0:128]
        Osb = g[19][:, 0:128]
        yo = g[20][:].bitcast(BF16)[:, 0:512]
        for a_, b_ in [('ml_PT0', 'g16'), ('ml_PT1', 'g17'), ('ml_P0', 'g14'), ('ml_P1', 'g15'), ('ml_sd', 'g18'),
                       ('ml_O', 'g19'), ('ml_yo', 'g20')]:
            P.alias[a_] = b_
        KP = 'mlp'
        P.op(V, lambda e: e.tensor_copy(identb, self.ident[:]), r=['ident'], w=[KP])
        P.dma(mneg, I['mneg128'], w=[KP, 'g21'])
        P.dma(nrm[:, 0:3], I['mla_q_norm_w'][li].rearrange("(c p) -> p c", p=128), w=[KP], allow_slow_non_contiguous=True)
        P.dma(nrm[:, 3:5], I['mla_kv_norm_w'][li].rearrange("(c p) -> p c", p=128), w=[KP], allow_slow_non_contiguous=True)
        st = self.wst[0]
        P.dma(st[:, :, 0:64], inw[:, 2192:2256].rearrange("(kt p) m -> p kt m", p=128), w=['wst0'])
        P.op(G, lambda e: e.tensor_copy(wkr, st[:, :, 0:64]), r=['wst0'], w=[KP])
        P.op(V, lambda e: e.tensor_scalar(out=wkrot[:, :, 0:32], in0=st[:, :, 32:64], scalar1=-1.0, scalar2=None, op0=ALU.mult),
             r=['wst0'], w=[KP])
        P.op(G, lambda e: e.tensor_copy(wkrot[:, :, 32:64], st[:, :, 0:32]), r=['wst0'], w=[KP])

        def rope_pair(wa, wb, nk, rhs_fn, rkeys, dst, dkey):
            def f1(e):
                ins = None
                for kt in range(nk):
                    ins = e.matmul(B[2][0:64, :], wa[:, kt, :], rhs_fn(kt), start=(kt == 0), stop=(kt == nk - 1))
                return ins

            def f2(e):
                ins = None
                for kt in range(nk):
                    ins = e.matmul(B[3][0:64, :], wb[:, kt, :], rhs_fn(kt), start=(kt == 0), stop=(kt == nk - 1))
                return ins
            P.op(T, f1, r=rkeys + [KP], w=['bank2'])
            P.op(T, f2, r=rkeys + [KP], w=['bank3'])
            t1, t2 = g[10], g[11]
            tt(V, t1[0:64, :], B[2][0:64, :], cs_c, ALU.mult, ['bank2', 'g8'], [gk(10)])
            tt(V, t2[0:64, :], B[3][0:64, :], cs_s, ALU.mult, ['bank3', 'g9'], [gk(11)])
            tt(G, dst, t1[0:64, :], t2[0:64, :], ALU.add, [gk(10), gk(11)], [dkey])

        for blk in range(8):
            t0 = blk * 512
            P.dma(cs_c, self.csT[0, :, t0:t0 + 512], r=['csT'], w=['g8'])
            P.dma(cs_s, self.csT[1, :, t0:t0 + 512], r=['csT'], w=['g9'])
            for (c0, ns, nw0, dstT, den) in [(1552, 3, 0, self.cqnT, 384.0), (1936, 2, 3, self.ckvnT, 256.0)]:
                for s_ in range(ns):
                    self.load_w(inw, c0 + s_ * 128, 128, wsl, 'ml_wsl')
                    ps, pk = self.proj(wsl, 'ml_wsl', 128, t0)
                    P.op(A, lambda e, ps=ps, s_=s_: e.copy(g[s_][:], ps), r=[pk], w=[gk(s_)])
                    act(g[4 + (s_ % 2)][:], g[s_][:], AF.Square, [gk(s_)], [gk(4 + (s_ % 2))])
                    P.op(T, lambda e, s_=s_, ns=ns: e.matmul(B[2][:, :], self.ones[:], g[4 + (s_ % 2)][:], start=(s_ == 0),
                                                            stop=(s_ == ns - 1)), r=['ones', gk(4 + (s_ % 2))], w=['bank2'])
                act(g[6][:], B[2][:, :], AF.Sqrt, ['bank2', 'eps'], [gk(6)], scale=1.0 / den, bias=self.eps6[:, 0:1])
                P.op(V, lambda e: e.reciprocal(g[6][:], g[6][:]), r=[gk(6)], w=[gk(6)])
                ob = cqb if ns == 3 else ckb
                okey = 'ml_cqb' if ns == 3 else 'ml_ckb'
                for s_ in range(ns):
                    stt(V, ob[:, s_, :], g[s_][:], nrm[:, nw0 + s_:nw0 + s_ + 1], g[6][:], ALU.mult, ALU.mult,
                        [gk(s_), gk(6), KP], [okey])
                P.dma(dstT[:, t0:t0 + 512].rearrange("(kt p) t -> p kt t", p=128), ob, r=[okey], w=['cqnT' if ns == 3 else 'ckvnT'])
            rope_pair(wkr, wkrot, 8, lambda kt, t0=t0: self.xnT[:, kt, t0:t0 + 512], ['xnT'], kpe[:, t0:t0 + 512], 'ml_kpe')
        for h in range(8):
            P.dma(wq_st, I['mla_wq_up'][li][:, h * 192:(h + 1) * 192].rearrange("(kt p) m -> p kt m", p=128), w=['ml_wqst'])
            P.dma(wkv_st, I['mla_wkv_up'][li][:, h * 256:(h + 1) * 256].rearrange("(kt p) m -> p kt m", p=128), w=['ml_wkvst'])
            P.op(G, lambda e: e.tensor_copy(wqn, wq_st[:, :, 0:128]), r=['ml_wqst'], w=['ml_wh'])
            P.op(G, lambda e: e.tensor_copy(wqp, wq_st[:, :, 128:192]), r=['ml_wqst'], w=['ml_wh'])
            P.op(V, lambda e: e.tensor_scalar(out=wqrot[:, :, 0:32], in0=wq_st[:, :, 160:192], scalar1=-1.0, scalar2=None,
                                              op0=ALU.mult), r=['ml_wqst'], w=['ml_wh'])
            P.op(G, lambda e: e.tensor_copy(wqrot[:, :, 32:64], wq_st[:, :, 128:160]), r=['ml_wqst'], w=['ml_wh'])
            P.op(G, lambda e: e.tensor_copy(wkn, wkv_st[:, :, 0:128]), r=['ml_wkvst'], w=['ml_wh'])
            P.op(G, lambda e: e.tensor_copy(wv, wkv_st[:, :, 128:256]), r=['ml_wkvst'], w=['ml_wh'])
            self.load_w(inw, 3536 + h * 128, 128, wsl, 'ml_wsl')
            for blk in range(8):
                t0 = blk * 512
                P.dma(cs_c, self.csT[0, :, t0:t0 + 512], r=['csT'], w=['g8'])
                P.dma(cs_s, self.csT[1, :, t0:t0 + 512], r=['csT'], w=['g9'])
                P.dma(cqb, self.cqnT[:, t0:t0 + 512].rearrange("(kt p) t -> p kt t", p=128), r=['cqnT'], w=['ml_cqb'])
                P.dma(ckb, self.ckvnT[:, t0:t0 + 512].rearrange("(kt p) t -> p kt t", p=128), r=['ckvnT'], w=['ml_ckb'])

                def fq(e):
                    ins = None
                    for kt in range(3):
                        ins = e.matmul(B[4][:, :], wqn[:, kt, :], cqb[:, kt, :], start=(kt == 0), stop=(kt == 2))
                    return ins
                P.op(T, fq, r=['ml_wh', 'ml_cqb'], w=['bank4'])
                P.op(A, lambda e, t0=t0: e.copy(qn[:, t0:t0 + 512], B[4][:, :]), r=['bank4'], w=['big'])

                def fk(e):
                    ins = None
                    for kt in range(2):
                        ins = e.matmul(B[5][:, :], wkn[:, kt, :], ckb[:, kt, :], start=(kt == 0), stop=(kt == 1))
                    return ins
                P.op(T, fk, r=['ml_wh', 'ml_ckb'], w=['bank5'])
                P.op(A, lambda e, t0=t0: e.copy(kn[:, t0:t0 + 512], B[5][:, :]), r=['bank5'], w=['big'])
                rope_pair(wqp, wqrot, 3, lambda kt: cqb[:, kt, :], ['ml_cqb', 'ml_wh'], qpe[:, t0:t0 + 512], 'ml_qpe')
                for sb in range(4):
                    def fv(e, sb=sb):
                        ins = None
                        for kt in range(2):
                            ins = e.matmul(B[6][:, sb * 128:(sb + 1) * 128], ckb[:, kt, sb * 128:(sb + 1) * 128], wv[:, kt, :],
                                           start=(kt == 0), stop=(kt == 1))
                        return ins
                    P.op(T, fv, r=['ml_wh', 'ml_ckb'], w=['bank6'])
                P.op(V, lambda e, blk=blk: e.tensor_copy(vv[:, blk * 4:(blk + 1) * 4, :],
                                                        B[6][:, :].rearrange("p (s d) -> p s d", d=128)), r=['bank6'], w=['ml_v'])
            for qb in range(32):
                qs = slice(qb * 128, (qb + 1) * 128)
                nkb = qb + 1
                mx = mx2[:, (qb % 2) * 24:(qb % 2) * 24 + 24]
                rs = rs2[:, (qb % 2) * 24:(qb % 2) * 24 + 24]
                kmx, krs = f'ml_mx{qb % 2}', f'ml_rs{qb % 2}'
                NG = (nkb + 3) // 4

                def scores(bi, gi, qs=qs, nkb=nkb):
                    ncols = min(512, nkb * 128 - gi * 512)
                    k0 = gi * 512

                    def f(e):
                        e.matmul(B[bi][:, 0:ncols], qn[:, qs], kn[:, k0:k0 + ncols], start=True, stop=False)
                        return e.matmul(B[bi][:, 0:ncols], qpe[:, qs], kpe[:, k0:k0 + ncols], start=False, stop=True)
                    P.op(T, f, r=['big', 'ml_qpe', 'ml_kpe'], w=[f'bank{bi}'])
                    return ncols
                nm = 0
                for gi in range(NG):
                    bi = 2 + (gi % 2)
                    ncols = scores(bi, gi)
                    last = (gi == NG - 1)
                    nfull = ncols - 128 if last else ncols
                    if nfull > 0:
                        P.op(V, lambda e, bi=bi, nfull=nfull, nm=nm, mx=mx: e.tensor_reduce(out=mx[:, nm:nm + 1], in_=B[bi][:, 0:nfull],
                                                                                   axis=AX.X, op=ALU.max),
                             r=[f'bank{bi}'], w=[kmx])
                        nm += 1
                    if last:
                        tt(V, sd, B[bi][:, ncols - 128:ncols], mneg, ALU.add, [f'bank{bi}', KP, 'g21'], ['ml_sd'])
                        P.op(V, lambda e, nm=nm, mx=mx: e.tensor_reduce(out=mx[:, nm:nm + 1], in_=sd, axis=AX.X, op=ALU.max),
                             r=['ml_sd'], w=[kmx])
                        nm += 1
                P.op(V, lambda e, nm=nm, mx=mx: e.tensor_reduce(out=mx[:, 23:24], in_=mx[:, 0:nm], axis=AX.X, op=ALU.max),
                     r=[kmx], w=[kmx])
                ts(V, mx[:, 22:23], mx[:, 23:24], -SC, None, ALU.mult, None, [kmx], [kmx])
                nr = 0
                kbi = 0
                for gi in range(NG):
                    bi = 2 + (gi % 2)
                    pi = gi % 2
                    ncols = scores(bi, gi)
                    last = (gi == NG - 1)
                    nfull = ncols - 128 if last else ncols
                    Pt = Pb[pi]
                    if nfull > 0:
                        act(Pt[:, 0:nfull], B[bi][:, 0:nfull], AF.Exp, [f'bank{bi}', kmx], [f'ml_P{pi}', krs], scale=SC,
                            bias=mx[:, 22:23], accum_out=rs[:, nr:nr + 1])
                        nr += 1
                    if last:
                        tt(V, sd, B[bi][:, ncols - 128:ncols], mneg, ALU.add, [f'bank{bi}', KP, 'g21'], ['ml_sd'])
                        act(Pt[:, nfull:ncols], sd, AF.Exp, ['ml_sd', kmx], [f'ml_P{pi}', krs], scale=SC, bias=mx[:, 22:23],
                            accum_out=rs[:, nr:nr + 1])
                        nr += 1
                    nb_ = ncols // 128
                    b4 = B[4][:, :].bitcast(BF16)

                    def ftp(e, Pt=Pt, nb_=nb_):
                        ins = None
                        for j in range(nb_):
                            ins = e.transpose(b4[:, j * 128:(j + 1) * 128], Pt[:, j * 128:(j + 1) * 128], identb)
                        return ins
                    P.op(T, ftp, r=[f'ml_P{pi}', KP], w=['bank4'])
                    P.op(V, lambda e, pi=pi, nb_=nb_: e.tensor_copy(PT[pi][:, 0:nb_, :],
                                                                   b4[:, 0:nb_ * 128].rearrange("p (j q) -> p j q", q=128)),
                         r=['bank4'], w=[f'ml_PT{pi}'])

                    def fpv(e, pi=pi, nb_=nb_, kbi=kbi, nkb=nkb):
                        ins = None
                        for j in range(nb_):
                            ins = e.matmul(B[5][:, 0:128], PT[pi][:, j, :], vv[:, kbi + j, :], start=(kbi + j == 0),
                                           stop=(kbi + j == nkb - 1))
                        return ins
                    P.op(T, fpv, r=[f'ml_PT{pi}', 'ml_v'], w=['bank5'])
                    kbi += nb_
                P.op(V, lambda e, nr=nr, rs=rs: e.tensor_reduce(out=rs[:, 23:24], in_=rs[:, 0:nr], axis=AX.X, op=ALU.add),
                     r=[krs], w=[krs])
                P.op(V, lambda e, rs=rs: e.reciprocal(rs[:, 23:24], rs[:, 23:24]), r=[krs], w=[krs])
                ts(V, Osb, B[5][:, 0:128], rs[:, 23:24], None, ALU.mult, None, ['bank5', krs], ['ml_O'])
                q4 = qb % 4
                P.op(T, lambda e, q4=q4: e.transpose(B[6][:, q4 * 128:(q4 + 1) * 128], Osb, self.ident[:]), r=['ml_O', 'ident'],
                     w=['bank6'])
                if q4 == 3:
                    t0 = (qb // 4) * 512
                    ps, pk = self.proj(wsl, 'ml_wsl', 128, t0)
                    zs = g[12]
                    act(zs[:], ps, AF.Silu, [pk], [gk(12)])
                    if self.debug and self.dbgsel == 'd_raw' and li == 0:
                        P.op(V, lambda e: e.tensor_copy(g[13][:], B[6][:, :]), r=['bank6'], w=[gk(13)])
                        P.dma(self.dbg[h * 128:(h + 1) * 128, t0:t0 + 512], g[13][:], r=[gk(13)])
                    tt(V, yo, B[6][:, :], zs[:], ALU.mult, ['bank6', gk(12)], ['ml_yo'])
                    P.dma(self.ycT[1024 + h * 128:1024 + (h + 1) * 128, t0:t0 + 512], yo, r=['ml_yo'], w=['ycT'])

    def odd_layer(self, layer):
        li = layer // 2
        self.norm_phase(layer)
        if not hasattr(self, 'csT'):
            self.rope_setup()
        self.phase()
        self.ssd(li)
        self.phase()
        self.mla(li)
        self.mem_attn(layer, self.I['od_in_w'][li], 2256, 4560, 2048)
        self.out_proj(self.I['od_out_w'][li])
        self.phase()


def build(shapes, nlayers=4, layers=None):
    k = K(shapes, nlayers=nlayers)
    k.consts_small()
    k.stage0()
    k.mem_setup()
    for layer in (layers if layers is not None else range(nlayers)):
        if layer % 2 == 0:
            k.even_layer(layer)
        else:
            k.odd_layer(layer)
    k.final_phase()
    return k


_CACHE = {}


def kernel(**inputs):
    consts = host_consts()
    x = np.ascontiguousarray(np.asarray(inputs['x'], dtype=np.float32))
    mem = np.ascontiguousarray(np.asarray(inputs['mem'], dtype=np.float32))
    pos = np.ascontiguousarray(np.asarray(inputs['positions']).astype(np.int32))
    nb = x.shape[0]
    shared = {n: np.ascontiguousarray(np.asarray(inputs[n], dtype=np.float32)) for n in WNAMES}
    shared.update(consts)
    shapes = {'x': ([S, D], F32), 'mem': ([256, D], F32), 'positions': ([1, S], I32)}
    for n, v in shared.items():
        shapes[n] = (list(v.shape), F32)
    if 'nc' not in _CACHE:
        k = build(shapes)
        _CACHE['nc'] = k.P.emit()
    nc = _CACHE['nc']
    in_maps = []
    for b in range(nb):
        m = {'x': x[b], 'mem': mem[b], 'positions': pos[b:b + 1]}
        m.update(shared)
        in_maps.append(m)
    res = run_bass_kernel_spmd(nc, in_maps, core_ids=list(range(nb)))
    return np.stack([np.asarray(r['out'], dtype=np.float32) for r in res.results], axis=0)
```
